# Optimizing a Trainium2 kernel written in Bass

```python
import math
import numpy as np
import jax
import jax.numpy as jnp
from jax import lax

D_MODEL = 1024
BATCH = 8
SEQ = 4096
DEPTH = 4

GRID_W = 64
CTX_LEN = 256
EXPAND = 2
MIX_W = EXPAND * D_MODEL
BRANCH_W = MIX_W // 2
DA_HEADS = 8
DA_V = BRANCH_W // DA_HEADS
DA_QK = DA_V // 2
DA_QK_W = DA_HEADS * 2 * DA_QK
SC_WIDTH = 3
NA_HEADS = 16
NA_DIM = BRANCH_W // NA_HEADS
WIN_ROWS = 8
WIN_COLS = 16
COL_QBLOCK = 16
COL_BAND = 2 * WIN_COLS
GLA_HEADS = 4
GLA_K_W = BRANCH_W // 2
GLA_DK = GLA_K_W // GLA_HEADS
GLA_DV = BRANCH_W // GLA_HEADS
GLA_RANK = 16
GLA_TAU = 16.0
GLA_CHUNK = 64
QBLOCK = 128
ROPE_BASE = 10000.0
EPS = 1e-6
N_EVEN = (DEPTH + 1) // 2
N_ODD = DEPTH // 2
EVEN_SIZES = (DA_QK_W, BRANCH_W, DA_QK_W, BRANCH_W, BRANCH_W, BRANCH_W, BRANCH_W, BRANCH_W)
EVEN_CTX_N = 2
ODD_SIZES = (BRANCH_W, BRANCH_W, GLA_K_W, BRANCH_W, 2 * GLA_RANK, BRANCH_W, GLA_K_W, BRANCH_W, BRANCH_W)
ODD_CTX_N = 5

kernel_name = 'hybrid_diffattn_shortconv_natten_gla'


def rmsnorm(x, g):
    xf = x.astype(jnp.float32)
    y = xf * lax.rsqrt(jnp.mean(xf * xf, axis=-1, keepdims=True) + EPS)
    return (y * g.astype(jnp.float32)).astype(x.dtype)


def split_cols(p, sizes):
    idx = np.cumsum(sizes)[:-1].tolist()
    return jnp.split(p, idx, axis=-1)


def heads(t, hd):
    return t.reshape(t.shape[0], t.shape[1], -1, hd)


def to_bht(t, hd):
    return heads(t, hd).transpose(0, 2, 1, 3)


def flip_t(t):
    return None if t is None else t[:, :, ::-1]


def axial_rope(seq, dtype):
    half = DA_QK // 2
    inv = 1.0 / (ROPE_BASE ** (jnp.arange(0, half, 2, dtype=jnp.float32) / half))
    t = jnp.arange(seq)
    row = (t // GRID_W).astype(jnp.float32)[:, None] * inv
    col = (t % GRID_W).astype(jnp.float32)[:, None] * inv
    ang = jnp.concatenate([row, row, col, col], axis=-1)
    return (jnp.cos(ang).astype(dtype)[:, None, None, :], jnp.sin(ang).astype(dtype)[:, None, None, :])


def apply_rope(x, cos, sin):
    half = DA_QK // 2
    q4 = half // 2
    def rot(t):
        return jnp.concatenate([-t[..., q4:], t[..., :q4]], axis=-1)
    r = jnp.concatenate([rot(x[..., :half]), rot(x[..., half:])], axis=-1)
    return x * cos + r * sin


def query_blocks(fn, q):
    bsz, t = q.shape[:2]
    qb = jnp.moveaxis(q.reshape((bsz, t // QBLOCK, QBLOCK) + q.shape[2:]), 1, 0)
    ob = jnp.moveaxis(lax.map(fn, qb), 0, 1)
    return ob.reshape((bsz, t) + ob.shape[3:])


def diff_attention(q, k, v, lam):
    scale = DA_QK ** -0.5
    def block(qb):
        s = jnp.einsum('bqhmd,bkhmd->bhmqk', qb, k).astype(jnp.float32) * scale
        p = jax.nn.softmax(s, axis=-1)
        a = (p[:, :, 0] - lam * p[:, :, 1]).astype(v.dtype)
        return jnp.einsum('bhqk,bkhe->bqhe', a, v)
    return query_blocks(block, q)


def dense_attention(q, k, v):
    s = jnp.einsum('bqhd,bkhd->bhqk', q, k).astype(jnp.float32) * (q.shape[-1] ** -0.5)
    p = jax.nn.softmax(s, axis=-1).astype(v.dtype)
    return jnp.einsum('bhqk,bkhd->bqhd', p, v)


def short_conv(u, w):
    ch = u.shape[-1]
    return lax.conv_general_dilated(
        u, w[:, None, :].astype(u.dtype), window_strides=(1,),
        padding=[(SC_WIDTH // 2, SC_WIDTH // 2)],
        dimension_numbers=('NWC', 'WIO', 'NWC'), feature_group_count=ch)


def neighbourhood_attention(q, k, v, kc, vc, rpb):
    bsz, seq, nh, d = q.shape
    rows = seq // GRID_W
    wr = min(WIN_ROWS, rows)
    n_cb = GRID_W // COL_QBLOCK
    scale = d ** -0.5
    r = np.arange(rows)
    row_start = np.clip(r - wr // 2, 0, rows - wr)
    row_off = row_start[:, None] + np.arange(wr)[None, :] - r[:, None] + (WIN_ROWS - 1)
    qcol = np.arange(GRID_W).reshape(n_cb, COL_QBLOCK)
    band_start = np.clip(np.arange(n_cb) * COL_QBLOCK - WIN_COLS // 2, 0, GRID_W - COL_BAND)
    kcol = band_start[:, None] + np.arange(COL_BAND)[None, :]
    win_start = np.clip(qcol - WIN_COLS // 2, 0, GRID_W - WIN_COLS)
    kc3 = kcol[:, None, :]
    col_valid = (kc3 >= win_start[..., None]) & (kc3 < win_start[..., None] + WIN_COLS)
    col_off = np.clip(kc3 - qcol[..., None] + (WIN_COLS - 1), 0, 2 * WIN_COLS - 2)
    col_mask = jnp.asarray(col_valid)[:, :, None, :]
    rpb_cols = rpb[:, :, col_off].astype(jnp.float32)
    kg = k.reshape(bsz, rows, GRID_W, nh, d)
    vg = v.reshape(bsz, rows, GRID_W, nh, d)
    qg = jnp.moveaxis(q.reshape(bsz, rows, n_cb, COL_QBLOCK, nh, d), 1, 0)
    n_loc = wr * COL_BAND

    def row_block(inp):
        q_r, rs, ro = inp
        k_band = lax.dynamic_slice_in_dim(kg, rs, wr, axis=1)[:, :, kcol]
        v_band = lax.dynamic_slice_in_dim(vg, rs, wr, axis=1)[:, :, kcol]
        s_loc = jnp.einsum('bnqhd,brnjhd->bhnqrj', q_r, k_band).astype(jnp.float32) * scale
        bias = jnp.transpose(jnp.take(rpb_cols, ro, axis=1), (0, 2, 3, 1, 4))
        s_loc = jnp.where(col_mask, s_loc + bias, -jnp.inf).reshape(bsz, nh, n_cb, COL_QBLOCK, n_loc)
        s_ctx = jnp.einsum('bnqhd,bkhd->bhnqk', q_r, kc).astype(jnp.float32) * scale
        p = jax.nn.softmax(jnp.concatenate([s_loc, s_ctx], axis=-1), axis=-1).astype(v.dtype)
        p_loc = p[..., :n_loc].reshape(bsz, nh, n_cb, COL_QBLOCK, wr, COL_BAND)
        return (jnp.einsum('bhnqrj,brnjhd->bnqhd', p_loc, v_band)
                + jnp.einsum('bhnqk,bkhd->bnqhd', p[..., n_loc:], vc))

    out = lax.map(row_block, (qg, jnp.asarray(row_start, jnp.int32), jnp.asarray(row_off, jnp.int32)))
    return jnp.moveaxis(out, 0, 1).reshape(bsz, seq, nh, d)


def gla_log_gates(lr, gate_up, gate_bias):
    out = []
    for dirn, l in enumerate(jnp.split(lr, 2, axis=-1)):
        z = (l @ gate_up[dirn] + gate_bias[dirn]).astype(jnp.float32)
        out.append(to_bht(jax.nn.log_sigmoid(z) / GLA_TAU, GLA_DK))
    return out


def gla_scan(q, k, v, logg, s0):
    bsz, nh, t, _ = k.shape
    dv = v.shape[-1]
    n = t // GLA_CHUNK
    want = q is not None
    tri = jnp.tril(jnp.ones((GLA_CHUNK, GLA_CHUNK), bool))[:, :, None]

    def chunks(a):
        return jnp.moveaxis(a.reshape(bsz, nh, n, GLA_CHUNK, a.shape[-1]), 2, 0)

    def step(state, inp):
        if want:
            qc, kc, vc, gc = inp
        else:
            kc, vc, gc = inp
        b = jnp.cumsum(gc, axis=-2)
        b_end = b[:, :, -1:, :]
        new = (jnp.exp(b_end[:, :, 0, :])[..., None] * state
               + jnp.einsum('bhsd,bhse->bhde', kc * jnp.exp(b_end - b), vc))
        if not want:
            return new, None
        qf = qc.astype(jnp.float32)
        o_inter = jnp.einsum('bhtd,bhde->bhte', qf * jnp.exp(b), state)
        decay = jnp.exp(jnp.where(tri, b[:, :, :, None, :] - b[:, :, None, :, :], -jnp.inf))
        att = jnp.einsum('bhtd,bhsd,bhtsd->bhts', qf, kc.astype(jnp.float32), decay)
        o = o_inter + jnp.einsum('bhts,bhse->bhte', att, vc.astype(jnp.float32))
        return new, o.astype(v.dtype)

    xs = tuple(chunks(a) for a in ((q, k, v, logg) if want else (k, v, logg)))
    s_fin, o = lax.scan(step, s0, xs)
    if want:
        o = jnp.moveaxis(o, 0, 2).reshape(bsz, nh, t, dv)
    return s_fin, o


def gla_out(o, g):
    o = rmsnorm(o.transpose(0, 2, 1, 3), g)
    return o.reshape(o.shape[0], o.shape[1], -1)


def even_mixer(h, hc, w_in, lam_p, subln, conv_w, lam_init, ctx_out, cos, sin):
    bsz, seq, _ = h.shape
    clen = hc.shape[1]
    ak, av, aq, ag, bh, bb, bc, bg = split_cols(h @ w_in, EVEN_SIZES)
    csz = EVEN_SIZES if ctx_out else EVEN_SIZES[:EVEN_CTX_N]
    cp = split_cols(hc @ w_in[:, :sum(csz)], csz)
    lam = (jnp.exp(jnp.sum(lam_p[0] * lam_p[1])) - jnp.exp(jnp.sum(lam_p[2] * lam_p[3])) + lam_init).astype(jnp.float32)
    qk_shape = (DA_HEADS, 2, DA_QK)
    q = apply_rope(aq.reshape(bsz, seq, *qk_shape), cos, sin)
    k = apply_rope(ak.reshape(bsz, seq, *qk_shape), cos, sin)
    kc = cp[0].reshape(bsz, clen, *qk_shape)
    vc = heads(cp[1], DA_V)
    k_all = jnp.concatenate([k, kc], axis=1)
    v_all = jnp.concatenate([heads(av, DA_V), vc], axis=1)
    oa = rmsnorm(diff_attention(q, k_all, v_all, lam), subln) * (1.0 - lam_init)
    ua = oa.reshape(bsz, seq, BRANCH_W) * jax.nn.silu(ag)
    ub = bb * short_conv(bc * bh, conv_w) * jax.nn.silu(bg)
    u = jnp.concatenate([ua, ub], axis=-1)
    if not ctx_out:
        return u, None
    oac = rmsnorm(diff_attention(cp[2].reshape(bsz, clen, *qk_shape), kc, vc, lam), subln) * (1.0 - lam_init)
    uac = oac.reshape(bsz, clen, BRANCH_W) * jax.nn.silu(cp[3])
    ubc = cp[5] * short_conv(cp[6] * cp[4], conv_w) * jax.nn.silu(cp[7])
    return u, jnp.concatenate([uac, ubc], axis=-1)


def odd_mixer(h, hc, w_in, rpb, gate_up, gate_bias, gnorm, ctx_out):
    bsz, seq, _ = h.shape
    clen = hc.shape[1]
    ck, cv, dk, dv, dlr, cq, dq, cg, dg = split_cols(h @ w_in, ODD_SIZES)
    csz = ODD_SIZES if ctx_out else ODD_SIZES[:ODD_CTX_N]
    cp = split_cols(hc @ w_in[:, :sum(csz)], csz)
    kc_na, vc_na = heads(cp[0], NA_DIM), heads(cp[1], NA_DIM)
    o_na = neighbourhood_attention(heads(cq, NA_DIM), heads(ck, NA_DIM), heads(cv, NA_DIM), kc_na, vc_na, rpb)
    qscale = GLA_DK ** -0.5
    s0 = jnp.zeros((bsz, GLA_HEADS, GLA_DK, GLA_DV), jnp.float32)
    gcf, gcb = gla_log_gates(cp[4], gate_up, gate_bias)
    kch, vch = to_bht(cp[2], GLA_DK), to_bht(cp[3], GLA_DV)
    qch = to_bht(cp[6], GLA_DK) * qscale if ctx_out else None
    s_f, ocf = gla_scan(qch, kch, vch, gcf, s0)
    s_b, ocb = gla_scan(flip_t(qch), flip_t(kch), flip_t(vch), flip_t(gcb), s0)
    gf, gb = gla_log_gates(dlr, gate_up, gate_bias)
    qh, kh, vh = to_bht(dq, GLA_DK) * qscale, to_bht(dk, GLA_DK), to_bht(dv, GLA_DV)
    _, of = gla_scan(qh, kh, vh, gf, s_f)
    _, ob = gla_scan(flip_t(qh), flip_t(kh), flip_t(vh), flip_t(gb), s_b)
    o_gla = gla_out(of + flip_t(ob), gnorm)
    u = jnp.concatenate([o_na.reshape(bsz, seq, BRANCH_W) * jax.nn.silu(cg), o_gla * jax.nn.silu(dg)], axis=-1)
    if not ctx_out:
        return u, None
    o_na_c = dense_attention(heads(cp[5], NA_DIM), kc_na, vc_na)
    o_gla_c = gla_out(ocf + flip_t(ocb), gnorm)
    uc = jnp.concatenate([o_na_c.reshape(bsz, clen, BRANCH_W) * jax.nn.silu(cp[7]), o_gla_c * jax.nn.silu(cp[8])], axis=-1)
    return u, uc


def setup_inputs(seed: int = 0) -> dict:
    key = jax.random.key(seed)
    ks = jax.random.split(key, 18)
    def nrm(k, shape, s):
        return jax.random.normal(k, shape, jnp.float32) * s
    ew = sum(EVEN_SIZES)
    ow = sum(ODD_SIZES)
    return {
        'x': nrm(ks[0], (BATCH, SEQ, D_MODEL), 1.0),
        'c': nrm(ks[1], (BATCH, D_MODEL), 1.0),
        'ctx': nrm(ks[2], (BATCH, CTX_LEN, D_MODEL), 1.0),
        'c_ctx': nrm(ks[3], (D_MODEL,), 1.0),
        'w_mod': nrm(ks[4], (DEPTH, D_MODEL, 3 * D_MODEL), 0.5 * D_MODEL ** -0.5),
        'b_mod': nrm(ks[5], (DEPTH, 3 * D_MODEL), 0.02),
        'g_pre': 1.0 + nrm(ks[6], (DEPTH, D_MODEL), 0.02),
        'g_post': 1.0 + nrm(ks[7], (DEPTH, D_MODEL), 0.02),
        'w_out': nrm(ks[8], (DEPTH, MIX_W, D_MODEL), MIX_W ** -0.5),
        'ev_w_in': nrm(ks[9], (N_EVEN, D_MODEL, ew), D_MODEL ** -0.5),
        'ev_lambda': nrm(ks[10], (N_EVEN, 4, DA_QK), 0.1),
        'ev_subln': 1.0 + nrm(ks[11], (N_EVEN, DA_V), 0.02),
        'ev_conv': nrm(ks[12], (N_EVEN, SC_WIDTH, BRANCH_W), SC_WIDTH ** -0.5),
        'od_w_in': nrm(ks[13], (N_ODD, D_MODEL, ow), D_MODEL ** -0.5),
        'od_rpb': nrm(ks[14], (N_ODD, NA_HEADS, 2 * WIN_ROWS - 1, 2 * WIN_COLS - 1), 0.1),
        'od_gate_up': nrm(ks[15], (N_ODD, 2, GLA_RANK, GLA_K_W), GLA_RANK ** -0.5),
        'od_gate_bias': nrm(ks[16], (N_ODD, 2, GLA_K_W), 0.5),
        'od_gnorm': 1.0 + nrm(ks[17], (N_ODD, GLA_DV), 0.02),
    }


def reference(x, c, ctx, c_ctx, w_mod, b_mod, g_pre, g_post, w_out, ev_w_in, ev_lambda, ev_subln, ev_conv,
              od_w_in, od_rpb, od_gate_up, od_gate_bias, od_gnorm):
    seq = x.shape[1]
    cos, sin = axial_rope(seq, x.dtype)
    xc = ctx
    for l in range(DEPTH):
        last = l == DEPTH - 1
        mod = jax.nn.silu(c) @ w_mod[l] + b_mod[l]
        modc = jax.nn.silu(c_ctx) @ w_mod[l] + b_mod[l]
        shift, scale, gate = jnp.split(mod[:, None, :], 3, axis=-1)
        shift_c, scale_c, gate_c = jnp.split(modc, 3, axis=-1)
        h = rmsnorm(x, g_pre[l]) * (1.0 + scale) + shift
        hc = rmsnorm(xc, g_pre[l]) * (1.0 + scale_c) + shift_c
        j = l // 2
        if l % 2 == 0:
            lam_init = 0.8 - 0.6 * math.exp(-0.3 * l)
            u, uc = even_mixer(h, hc, ev_w_in[j], ev_lambda[j], ev_subln[j], ev_conv[j], lam_init, not last, cos, sin)
        else:
            u, uc = odd_mixer(h, hc, od_w_in[j], od_rpb[j], od_gate_up[j], od_gate_bias[j], od_gnorm[j], not last)
        x = x + gate * rmsnorm(u @ w_out[l], g_post[l])
        if not last:
            xc = xc + gate_c * rmsnorm(uc @ w_out[l], g_post[l])
    return x
```

```python
import math
from contextlib import ExitStack
import numpy as np
import concourse.bass as bass
import concourse.mybir as mybir
from concourse.bass_utils import run_bass_kernel_spmd

F32 = mybir.dt.float32
BF16 = mybir.dt.bfloat16
AF = mybir.ActivationFunctionType
ALU = mybir.AluOpType
AX = mybir.AxisListType

D = 1024
SEQ = 4096
CTX = 256
T = SEQ + CTX
NTT = T // 128
DEPTH = 4
EPS = 1e-6
GRID = 64
SAME_ENG_SYNC = True


class Fw:
    SEM_LIMIT = 30000

    def __init__(self, nc, es, n_dma_slots=12):
        self.nc = nc
        self.es = es
        self.E = {'pe': nc.tensor, 'dve': nc.vector, 'act': nc.scalar, 'pool': nc.gpsimd, 'sp': nc.sync}
        self.cur = {}
        self.nsem = 0
        for e in self.E:
            self.cur[e] = [self._newsem(e), 0]
        self.known = {e: {} for e in self.E}
        self.last_w = {}
        self.readers = {}
        self.slots = {}
        for q in ('sp', 'pool', 'act'):
            n = n_dma_slots if q != 'act' else 4
            self.slots[q] = [[self._newsem('d' + q), 0] for _ in range(n)]
        self.slot_rr = {q: 0 for q in self.slots}
        self.nops = 0

    def _newsem(self, tag):
        self.nsem += 1
        return self.es.enter_context(self.nc.semaphore("s%s%d" % (tag, self.nsem)))

    def _wait(self, eng, tok):
        sem, val, teng = tok
        kn = self.known[eng]
        k = id(sem)
        if kn.get(k, 0) >= val:
            return
        self.E[eng].wait_ge(sem, val)
        kn[k] = val

    def _deps(self, eng, reads, writes):
        deps = {}

        def addtok(t, hazard):
            if t is None:
                return
            sem, val, teng = t
            if teng == eng and teng != 'dma':
                if eng == 'pe' or not SAME_ENG_SYNC:
                    return
            k = id(sem)
            if k not in deps or deps[k][1] < val:
                deps[k] = t
        for r in reads:
            addtok(self.last_w.get(r), 'raw')
        for w in writes:
            addtok(self.last_w.get(w), 'waw')
            for t in self.readers.get(w, ()):
                addtok(t, 'war')
        return deps.values()

    def _commit(self, tok, reads, writes):
        for r in reads:
            self.readers.setdefault(r, []).append(tok)
        for w in writes:
            self.last_w[w] = tok
            self.readers[w] = []

    def op(self, eng, fn, reads=(), writes=()):
        for t in self._deps(eng, reads, writes):
            self._wait(eng, t)
        ins = fn(self.E[eng])
        c = self.cur[eng]
        if c[1] >= self.SEM_LIMIT:
            c[0] = self._newsem(eng)
            c[1] = 0
        c[1] += 1
        ins.then_inc(c[0], 1)
        tok = (c[0], c[1], eng)
        self._commit(tok, reads, writes)
        self.nops += 1
        return tok

    def dma(self, q, out, in_, reads=(), writes=()):
        for t in self._deps(q, reads, writes):
            self._wait(q, t)
        sl = self.slots[q]
        i = self.slot_rr[q]
        self.slot_rr[q] = (i + 1) % len(sl)
        s = sl[i]
        if s[1] > 0:
            self._wait(q, (s[0], s[1], 'dma'))
        if s[1] >= self.SEM_LIMIT:
            s[0] = self._newsem('d' + q)
            s[1] = 0
        ins = self.E[q].dma_start(out=out, in_=in_)
        s[1] += 16
        ins.then_inc(s[0], 16)
        tok = (s[0], s[1], 'dma')
        self._commit(tok, reads, writes)
        self.nops += 1
        return tok

    def barrier(self):
        toks = []
        for e in self.E:
            c = self.cur[e]
            if c[1] > 0:
                toks.append((c[0], c[1], e))
        for q in self.slots:
            for s in self.slots[q]:
                if s[1] > 0:
                    toks.append((s[0], s[1], 'dma'))
        for e in self.E:
            for t in toks:
                if t[2] == e:
                    continue
                self._wait(e, t)
        self.last_w = {}
        self.readers = {}


def _rope_tables():
    half = 32
    inv = (1.0 / (10000.0 ** (np.arange(0, half, 2, dtype=np.float32) / np.float32(half)))).astype(np.float32)
    t = np.arange(SEQ)
    row = (t // GRID).astype(np.float32)[:, None] * inv
    col = (t % GRID).astype(np.float32)[:, None] * inv
    ang = np.concatenate([row, row, col, col], axis=-1).astype(np.float32)
    cos = np.cos(ang).astype(np.float32).T
    sin = np.sin(ang).astype(np.float32).T
    sign = np.ones(64, np.float32)
    src = np.zeros(64, np.int64)
    for f in range(64):
        blk = (f // 32) * 32
        o = f % 32
        if o < 16:
            src[f] = blk + o + 16
            sign[f] = -1.0
        else:
            src[f] = blk + o - 16
            sign[f] = 1.0
    sin_s = sin * sign[:, None]
    cosT = np.concatenate([cos, cos], axis=0)
    sinT = np.concatenate([sin_s, sin_s], axis=0)
    rm = np.zeros((128, 128), np.float32)
    for m in range(2):
        for f in range(64):
            rm[m * 64 + src[f], m * 64 + f] = 1.0
    return np.ascontiguousarray(cosT), np.ascontiguousarray(sinT), rm


def _na_tables(rpb):
    NEG = np.float32(-30000.0)
    kc = np.arange(64)[:, None]
    c = np.arange(64)[None, :]
    ws = np.clip(c - 8, 0, 48)
    valid = (kc >= ws) & (kc < ws + 16)
    coff = np.clip(kc - c + 15, 0, 30)
    out = np.full((2, 16, 2, 64, 16, 64), NEG, np.float32)
    for ti in range(16):
        for par in range(2):
            if ti < 14:
                dr = ti + par
            elif ti == 14:
                dr = 3 if par == 1 else None
            else:
                dr = 10 if par == 0 else None
            if dr is None or dr > 14:
                continue
            g = rpb[:, :, dr, :][:, :, coff]
            out[:, :, par, :, ti, :] = np.where(valid[None, None], g, NEG)
    return np.ascontiguousarray(out.reshape(2, 16, 128, 16 * 64))


def _col(v):
    return np.ascontiguousarray(v.reshape(-1, 128).T)


class _Stop(Exception):
    pass


def build(n_layers=DEPTH, dbg=False, stop_after=None, first_layer=0):
    nc = bass.Bass("TRN2", target_bir_lowering=False)

    def din(name, shape, dt=F32):
        return nc.dram_tensor(name, list(shape), dt, kind="ExternalInput").ap()

    def dscr(name, shape, dt):
        return nc.dram_tensor(name, list(shape), dt, kind="Internal").ap()

    x_in = din("x", [SEQ, D])
    ctx_in = din("ctx", [CTX, D])
    cvec = din("cvec", [128, 16])
    w_mod = din("w_mod", [DEPTH, D, 3 * D])
    bmod_col = din("bmod_col", [128, DEPTH * 24])
    bmod_gate = din("bmod_gate", [DEPTH, 128, D])
    gpre_col = din("gpre_col", [128, DEPTH * 8])
    gpost_b = din("gpost_b", [DEPTH, 128, D])
    w_out = din("w_out", [DEPTH, 2 * D, D])
    ev_w_in = din("ev_w_in", [2, D, 8 * D])
    lam_b = din("lam_b", [128, 2 * 256])
    subln_col = din("subln_col", [128, 2])
    conv_col = din("conv_col", [128, 2 * 3 * 8])
    cosT_d = din("cosT", [128, SEQ])
    sinT_d = din("sinT", [128, SEQ])
    rm_d = din("rm", [128, 128])
    ident_d = din("ident", [128, 128])
    od_w_in = din("od_w_in", [2, D, 7200])
    tt_d = din("tt_tab", [2, 16, 128, 16 * 64])
    gu_d = din("gate_up", [2, 2, 16, 512])
    gb_col = din("gb_col", [128, 16])
    gnorm_b = din("gnorm_b", [2, 128, 256])
    cmask_d = din("cmask", [128, 512])
    tri_d = din("tri", [64, 128])
    out_d = nc.dram_tensor("out", [SEQ, D], F32, kind="ExternalOutput").ap()
    xc_d = nc.dram_tensor("xc_out", [CTX, D], F32, kind="ExternalOutput" if dbg else "Internal").ap()

    KT_s = dscr("KT_s", [8, 128, T], BF16)
    QT_s = dscr("QT_s", [8, 128, T], BF16)
    V_s = dscr("V_s", [T, D], BF16)
    GA_s = dscr("GA_s", [8, 128, T], F32)
    UT_s = dscr("UT_s", [16, 128, T], BF16)
    VN_s = dscr("VN_s", [T, 16 * 65], BF16)
    CG_s = dscr("CG_s", [T, D], F32)
    DG_s = dscr("DG_s", [T, D], F32)
    OF_s = dscr("OF_s", [T, D], F32)
    OB_s = dscr("OB_s", [T, D], F32)

    es_top = ExitStack()
    with es_top as es:
        fw = Fw(nc, es)

        uid = [0]

        def sb(name, shape, dt, stack=None):
            uid[0] += 1
            return (stack or es).enter_context(nc.sbuf_tensor("sb_%s_%d" % (name, uid[0]), list(shape), dt))

        ps = [es.enter_context(nc.psum_tensor("ps%d" % i, [128, 512], F32)) for i in range(8)]
        PSK = [('ps', i) for i in range(8)]

        ident = sb("ident", [128, 128], F32)
        ones_f = sb("ones_f", [128, 128], F32)
        ones_b = sb("ones_b", [128, 128], BF16)
        rm_b = sb("rm_b", [128, 128], BF16)
        cs = sb("cs", [128, 16], F32)
        csb = sb("csb", [128, 16, 128], F32)
        bmodc = sb("bmodc", [128, DEPTH * 24], F32)
        gprec = sb("gprec", [128, DEPTH * 8], F32)
        lamt = sb("lamt", [128, 512], F32)
        lam2 = sb("lam2", [128, 4, 64], F32)
        lsum = sb("lsum", [128, 4], F32)
        lam_c = sb("lam_c", [128, 2], F32)
        sublc = sb("sublc", [128, 2], F32)
        convc = sb("convc", [128, 48], F32)
        A_m = sb("A_m", [128, 2, 8], F32)
        B_m = sb("B_m", [128, 2, 8], F32)
        sc_m = sb("sc_m", [128, 2, 8], F32)
        GG = sb("GG", [128, 2, D], F32)

        identb = sb("identb", [128, 128], BF16)
        cmask = sb("cmask", [128, 512], F32)
        tri = sb("tri", [64, 128], F32)
        gbc = sb("gbc", [128, 16], F32)
        fw.dma('pool', identb[:], ident_d[:, :], writes=['identb'])
        fw.dma('sp', cmask[:], cmask_d[:, :], writes=['cmask'])
        fw.dma('sp', tri[:], tri_d[:, :], writes=['tri'])
        fw.dma('sp', gbc[:], gb_col[:, :], writes=['gbc'])
        fw.op('dve', lambda e: e.tensor_scalar(out=gbc[:], in0=gbc[:], scalar1=-1.0, scalar2=None, op0=ALU.mult),
              reads=['gbc'], writes=['gbc'])
        fw.dma('sp', ident[:], ident_d[:, :], writes=['ident'])
        fw.dma('pool', rm_b[:], rm_d[:, :], writes=['rm_b'])
        fw.dma('sp', cs[:], cvec[:, :], writes=['cs'])
        fw.dma('sp', bmodc[:], bmod_col[:, :], writes=['bmodc'])
        fw.dma('sp', gprec[:], gpre_col[:, :], writes=['gprec'])
        fw.dma('sp', lamt[:], lam_b[:, :], writes=['lamt'])
        fw.dma('sp', sublc[:], subln_col[:, :], writes=['sublc'])
        fw.dma('sp', convc[:], conv_col[:, :], writes=['convc'])
        fw.op('dve', lambda e: e.memset(ones_f[:], 1.0), writes=['ones_f'])
        fw.op('dve', lambda e: e.memset(ones_b[:], 1.0), writes=['ones_b'])
        fw.op('act', lambda e: e.activation(out=cs[:], in_=cs[:], func=AF.Silu), reads=['cs'], writes=['cs'])
        for k in range(16):
            fw.op('dve', lambda e, k=k: e.tensor_scalar(out=csb[:, k, :], in0=ones_f[:], scalar1=cs[:, k:k + 1],
                                                        scalar2=None, op0=ALU.mult),
                  reads=['ones_f', 'cs'], writes=[('csb', k)])
        lt = lamt[:].rearrange("p (j a d) -> p j a d", j=2, a=4)
        for j in range(2):
            for a in range(2):
                fw.op('dve', lambda e, j=j, a=a: e.tensor_tensor(out=lam2[:, j * 2 + a, :], in0=lt[:, j, 2 * a, :],
                                                                 in1=lt[:, j, 2 * a + 1, :], op=ALU.mult),
                      reads=['lamt'], writes=[('lam2', j, a)])
                fw.op('dve', lambda e, j=j, a=a: e.tensor_reduce(out=lsum[:, j * 2 + a:j * 2 + a + 1],
                                                                 in_=lam2[:, j * 2 + a, :], axis=AX.X, op=ALU.add),
                      reads=[('lam2', j, a)], writes=[('lsum', j, a)])
        fw.op('act', lambda e: e.activation(out=lsum[:], in_=lsum[:], func=AF.Exp),
              reads=[('lsum', j, a) for j in range(2) for a in range(2)], writes=['lsume'])
        for j in range(2):
            lam_init = 0.8 - 0.6 * math.exp(-0.3 * (2 * j))
            fw.op('dve', lambda e, j=j, li=lam_init: e.scalar_tensor_tensor(
                out=lam_c[:, j:j + 1], in0=lsum[:, 2 * j + 1:2 * j + 2], scalar=-li, in1=lsum[:, 2 * j:2 * j + 1],
                op0=ALU.add, op1=ALU.subtract), reads=['lsume'], writes=[('lam_c', j)])
            fw.op('dve', lambda e, j=j, li=lam_init: e.tensor_scalar(
                out=sublc[:, j:j + 1], in0=sublc[:, j:j + 1], scalar1=(1.0 - li), scalar2=None, op0=ALU.mult),
                reads=['sublc'], writes=['sublc'])

        def emit_mod(l):
            with ExitStack() as st:
                wm = [sb("wm%d" % i, [128, 8, 512], F32, st) for i in range(2)]
                bg = sb("bg", [128, D], F32, st)
                gp = sb("gp", [128, D], F32, st)
                tmpg = sb("tmpg", [128, 512], F32, st)
                fw.dma('sp', bg[:], bmod_gate[l, :, :], writes=['bg'])
                fw.dma('sp', gp[:], gpost_b[l, :, :], writes=['gp'])
                wv = w_mod[l, :, :].rearrange("(kc p) n -> p kc n", p=128)
                for blk in range(6):
                    b = blk % 2
                    fw.dma('sp', wm[b][:], wv[:, :, blk * 512:(blk + 1) * 512], writes=[('wm', b)])
                    if blk < 4:
                        for n4 in range(4):
                            n = blk * 4 + n4

                            def mm(e, n=n, n4=n4, b=b):
                                r = None
                                for kc in range(8):
                                    r = e.matmul(ps[0][:, 2 * n:2 * n + 2], lhsT=wm[b][:, kc, n4 * 128:(n4 + 1) * 128],
                                                 rhs=cs[:, kc:16:8], start=(kc == 0), stop=(kc == 7))
                                return r
                            fw.op('pe', mm, reads=[('wm', b), 'cs'], writes=[('modps', n)] + ([PSK[0]] if n == 0 else []))
                    else:
                        nb = blk - 4
                        for j in range(2):
                            bank = 1 + j

                            def mm(e, j=j, b=b, bank=bank):
                                r = None
                                for kc in range(8):
                                    r = e.matmul(ps[bank][:, :], lhsT=csb[:, j * 8 + kc, :], rhs=wm[b][:, kc, :],
                                                 start=(kc == 0), stop=(kc == 7))
                                return r
                            fw.op('pe', mm, reads=[('wm', b)] + [('csb', j * 8 + kc) for kc in range(8)], writes=[PSK[bank]])
                            fw.op('dve', lambda e, nb=nb, bank=bank: e.tensor_tensor(
                                out=tmpg[:], in0=ps[bank][:, :], in1=bg[:, nb * 512:(nb + 1) * 512], op=ALU.add),
                                reads=[PSK[bank], 'bg'], writes=['tmpg'])
                            fw.op('dve', lambda e, nb=nb, j=j: e.tensor_tensor(
                                out=GG[:, j, nb * 512:(nb + 1) * 512], in0=tmpg[:], in1=gp[:, nb * 512:(nb + 1) * 512],
                                op=ALU.mult), reads=['tmpg', 'gp'], writes=[('GG', j, nb)])
                pv = ps[0][:, 0:32].rearrange("p (n j) -> p n j", j=2)
                allmod = [('modps', n) for n in range(16)]
                for j in range(2):
                    fw.op('dve', lambda e, j=j: e.tensor_tensor(out=B_m[:, j, :], in0=pv[:, 0:8, j],
                                                                in1=bmodc[:, l * 24:l * 24 + 8], op=ALU.add),
                          reads=allmod + ['bmodc'], writes=[('B_m', j)])
                    fw.op('dve', lambda e, j=j: e.tensor_tensor(out=sc_m[:, j, :], in0=pv[:, 8:16, j],
                                                                in1=bmodc[:, l * 24 + 8:l * 24 + 16], op=ALU.add),
                          reads=allmod + ['bmodc'], writes=[('sc_m', j)])
                    fw.op('dve', lambda e, j=j: e.scalar_tensor_tensor(
                        out=A_m[:, j, :], in0=sc_m[:, j, :], scalar=1.0, in1=gprec[:, l * 8:(l + 1) * 8],
                        op0=ALU.add, op1=ALU.mult), reads=[('sc_m', j), 'gprec'], writes=[('A_m', j)])
                fw.barrier()

        def emit_hT(l, hT, st):
            xt = [sb("xt%d" % i, [128, D], F32, st) for i in range(2)]
            xn = [sb("xn%d" % i, [128, D], F32, st) for i in range(2)]
            junk = sb("junk", [128, D], BF16, st)
            ssq = sb("ssq", [128, 2], F32, st)
            rsd = sb("rsd", [128, 2], F32, st)
            acs = sb("acs", [128, 2], F32, st)

            def load(tt):
                b = tt % 2
                if tt < 32:
                    src = (x_in if l == first_layer else out_d)[tt * 128:(tt + 1) * 128, :]
                else:
                    src = (ctx_in if l == first_layer else xc_d)[(tt - 32) * 128:(tt - 31) * 128, :]
                fw.dma('sp', xt[b][:], src, reads=[('X', tt)], writes=[('xt', b)])
            load(0)
            for tt in range(NTT):
                b = tt % 2
                j = 0 if tt < 32 else 1
                if tt + 1 < NTT:
                    load(tt + 1)
                def sqa(e, b=b):
                    return e.activation(out=junk[:], in_=xt[b][:], func=AF.Square, accum_out=ssq[:, b:b + 1])
                fw.op('act', sqa, reads=[('xt', b)], writes=['junk', ('ssq', b)])
                fw.op('act', lambda e, b=b: e.activation(out=rsd[:, b:b + 1], in_=ssq[:, b:b + 1], func=AF.Sqrt,
                                                         scale=1.0 / D, bias=EPS),
                      reads=[('ssq', b)], writes=[('rsd', b)])
                fw.op('dve', lambda e, b=b: e.reciprocal(out=rsd[:, b:b + 1], in_=rsd[:, b:b + 1]),
                      reads=[('rsd', b)], writes=[('rsd', b)])
                fw.op('dve', lambda e, b=b: e.tensor_scalar(out=xn[b][:], in0=xt[b][:], scalar1=rsd[:, b:b + 1],
                                                            scalar2=None, op0=ALU.mult),
                      reads=[('xt', b), ('rsd', b)], writes=[('xn', b)])
                for half in range(2):
                    bank = 2 * b + half

                    def tr(e, b=b, half=half, bank=bank):
                        r = None
                        for q4 in range(4):
                            kc = half * 4 + q4
                            r = e.transpose(out=ps[bank][:, q4 * 128:(q4 + 1) * 128],
                                            in_=xn[b][:, kc * 128:(kc + 1) * 128], identity=ident[:])
                        return r
                    fw.op('pe', tr, reads=[('xn', b), 'ident'], writes=[PSK[bank]])
                    for q4 in range(4):
                        kc = half * 4 + q4
                        fw.op('dve', lambda e, kc=kc, q4=q4, bank=bank, tt=tt, j=j: e.tensor_scalar(
                            out=hT[:, kc, tt * 128:(tt + 1) * 128], in0=ps[bank][:, q4 * 128:(q4 + 1) * 128],
                            scalar1=A_m[:, j, kc:kc + 1], scalar2=B_m[:, j, kc:kc + 1], op0=ALU.mult, op1=ALU.add),
                            reads=[PSK[bank], ('A_m', j), ('B_m', j)], writes=[('hT', kc, tt)])

        TB = [(i * 512, 512) for i in range(8)] + [(SEQ, CTX)]

        def hT_keys(t0, n):
            return [('hT', kc, tt) for kc in range(8) for tt in range(t0 // 128, (t0 + n) // 128)]

        def emit_even(l, last):
            je = l // 2
            w_in = ev_w_in[je, :, :].rearrange("(kc p) f -> p kc f", p=128)
            with ExitStack() as st1:
                hT = sb("hT", [128, 8, T], BF16, st1)
                with ExitStack() as st:
                    emit_hT(l, hT, st)
                    fw.barrier()
                if chk('hT'):
                    return
                with ExitStack() as st:
                    cosT = sb("cosT", [128, SEQ], F32, st)
                    sinT = sb("sinT", [128, SEQ], F32, st)
                    fw.dma('sp', cosT[:], cosT_d[:, :], writes=['cosT'])
                    fw.dma('sp', sinT[:], sinT_d[:, :], writes=['sinT'])
                    wt = [sb("wt%d" % i, [128, 8, 512], BF16, st) for i in range(2)]
                    qb = [sb("qb%d" % i, [128, 512], BF16, st) for i in range(2)]
                    t1 = [sb("t1_%d" % i, [128, 512], F32, st) for i in range(2)]
                    t2 = [sb("t2_%d" % i, [128, 512], F32, st) for i in range(2)]
                    ob = [sb("ob%d" % i, [128, 512], BF16, st) for i in range(2)]
                    gs = [sb("gs%d" % i, [128, 512], F32, st) for i in range(2)]
                    vs = [sb("vs%d" % i, [128, D], BF16, st) for i in range(2)]
                    cnt = [0]
                    wcnt = [0]

                    def load_w(c0):
                        b = wcnt[0] % 2
                        wcnt[0] += 1
                        fw.dma('pool', wt[b][:], w_in[:, :, c0:c0 + 512], writes=[('wt', b)])
                        return b

                    def proj(b, ft, t0, n, bank):
                        def mm(e):
                            r = None
                            for kc in range(8):
                                r = e.matmul(ps[bank][:, 0:n], lhsT=wt[b][:, kc, ft * 128:(ft + 1) * 128],
                                             rhs=hT[:, kc, t0:t0 + n], start=(kc == 0), stop=(kc == 7))
                            return r
                        fw.op('pe', mm, reads=[('wt', b)] + hT_keys(t0, n), writes=[PSK[bank]])

                    plan = [('k', 0), ('k', 512), ('q', 2048), ('q', 2560), ('g', 3072), ('g', 3584)]
                    import os as _os
                    _kinds = _os.environ.get("KINDS")
                    if _kinds is not None:
                        plan = [p for p in plan if p[0] in _kinds]
                    nextb = load_w(plan[0][1])
                    for pi, (kind, c0) in enumerate(plan):
                        b = nextb
                        if pi + 1 < len(plan):
                            nextb = load_w(plan[pi + 1][1])
                        for ft in range(4):
                            hd = ((c0 % 1024) // 128) + ft
                            for (t0, n) in TB:
                                if last and kind in ('q', 'g') and t0 >= SEQ:
                                    continue
                                _rd0 = _os.environ.get("ROPE_DBG", "")
                                if 'ctxonly' in _rd0 and t0 < SEQ:
                                    continue
                                if 'latonly' in _rd0 and t0 >= SEQ:
                                    continue
                                i = cnt[0] % 2
                                cnt[0] += 1
                                bank = i
                                proj(b, ft, t0, n, bank)
                                if kind == 'g':
                                    fw.op('act', lambda e, i=i, n=n, bank=bank: e.activation(
                                        out=gs[i][:, 0:n], in_=ps[bank][:, 0:n], func=AF.Silu),
                                        reads=[PSK[bank]], writes=[('gs', i)])
                                    fw.dma('sp', GA_s[hd, :, t0:t0 + n], gs[i][:, 0:n], reads=[('gs', i)],
                                           writes=[('GA', hd, t0)])
                                    continue
                                dst = KT_s if kind == 'k' else QT_s
                                if t0 >= SEQ or 'norope' in _rd0:
                                    fw.op('act', lambda e, i=i, n=n, bank=bank: e.activation(
                                        out=ob[i][:, 0:n], in_=ps[bank][:, 0:n], func=AF.Copy),
                                        reads=[PSK[bank]], writes=[('ob', i)])
                                else:
                                    fw.op('act', lambda e, i=i, n=n, bank=bank: e.activation(
                                        out=qb[i][:, 0:n], in_=ps[bank][:, 0:n], func=AF.Copy),
                                        reads=[PSK[bank]], writes=[('qb', i)])
                                    if 'cosconst' in _rd0:
                                        fw.op('dve', lambda e, i=i, n=n, bank=bank, t0=t0: e.tensor_tensor(
                                            out=t1[i][:, 0:n], in0=ps[bank][:, 0:n], in1=gs[0][:, 0:n], op=ALU.mult),
                                            reads=[PSK[bank], ('gs', 0)], writes=[('t1', i)])
                                    else:
                                      fw.op('dve', lambda e, i=i, n=n, bank=bank, t0=t0: e.tensor_tensor(
                                        out=t1[i][:, 0:n], in0=ps[bank][:, 0:n], in1=cosT[:, t0:t0 + n], op=ALU.mult),
                                        reads=[PSK[bank], 'cosT', ('qb', i)], writes=[('t1', i)])
                                    _rd = _os.environ.get("ROPE_DBG", "")
                                    if 'nomm' not in _rd:
                                        fw.op('pe', lambda e, i=i, n=n: e.matmul(
                                            ps[2 + i][:, 0:n], lhsT=rm_b[:], rhs=qb[i][:, 0:n], start=True, stop=True),
                                            reads=['rm_b', ('qb', i)], writes=[PSK[2 + i]])
                                    if 'not2' not in _rd:
                                        fw.op('dve', lambda e, i=i, n=n, t0=t0: e.tensor_tensor(
                                            out=t2[i][:, 0:n], in0=ps[2 + i][:, 0:n], in1=sinT[:, t0:t0 + n], op=ALU.mult),
                                            reads=[PSK[2 + i], 'sinT'], writes=[('t2', i)])
                                    else:
                                        fw.op('dve', lambda e, i=i, n=n, t0=t0: e.tensor_tensor(
                                            out=t2[i][:, 0:n], in0=t1[i][:, 0:n], in1=(gs[0][:, 0:n] if 'cosconst' in _rd0 else sinT[:, t0:t0 + n]), op=ALU.mult),
                                            reads=[('t1', i), 'sinT'], writes=[('t2', i)])
                                    if 'actob' in _rd0:
                                        fw.op('dve', lambda e, i=i, n=n: e.tensor_tensor(
                                            out=t1[i][:, 0:n], in0=t1[i][:, 0:n], in1=t2[i][:, 0:n], op=ALU.add),
                                            reads=[('t1', i), ('t2', i)], writes=[('t1', i)])
                                        fw.op('act', lambda e, i=i, n=n, bank=bank: e.activation(
                                            out=ob[i][:, 0:n], in_=ps[bank][:, 0:n], func=AF.Copy),
                                            reads=[PSK[bank]], writes=[('ob', i)])
                                    else:
                                      fw.op(_os.environ.get("ROPE_ADD_ENG", "pool"), lambda e, i=i, n=n: e.tensor_tensor(
                                        out=ob[i][:, 0:n], in0=t1[i][:, 0:n], in1=t2[i][:, 0:n], op=ALU.add),
                                        reads=[('t1', i), ('t2', i)], writes=[('ob', i)])
                                fw.dma('sp', dst[hd, :, t0:t0 + n], ob[i][:, 0:n], reads=[('ob', i)],
                                       writes=[(kind, hd, t0)])
                    bv = [load_w(1024), load_w(1536)]
                    for tt in range(NTT if (_kinds is None or 'v' in _kinds) else 0):
                        i = tt % 2
                        for nb in range(2):
                            bank = 4 + 2 * i + nb

                            def mm(e, tt=tt, nb=nb, bank=bank):
                                r = None
                                for kc in range(8):
                                    r = e.matmul(ps[bank][:, :], lhsT=hT[:, kc, tt * 128:(tt + 1) * 128],
                                                 rhs=wt[bv[nb]][:, kc, :], start=(kc == 0), stop=(kc == 7))
                                return r
                            fw.op('pe', mm, reads=[('wt', bv[nb])] + [('hT', kc, tt) for kc in range(8)],
                                  writes=[PSK[bank]])
                            eng = 'act' if nb == 0 else 'dve'
                            if eng == 'act':
                                fw.op('act', lambda e, i=i, nb=nb, bank=bank: e.activation(
                                    out=vs[i][:, nb * 512:(nb + 1) * 512], in_=ps[bank][:, :], func=AF.Copy),
                                    reads=[PSK[bank]], writes=[('vs', i, nb)])
                            else:
                                fw.op('dve', lambda e, i=i, nb=nb, bank=bank: e.tensor_copy(
                                    out=vs[i][:, nb * 512:(nb + 1) * 512], in_=ps[bank][:, :]),
                                    reads=[PSK[bank]], writes=[('vs', i, nb)])
                        fw.dma('sp', V_s[tt * 128:(tt + 1) * 128, :], vs[i][:], reads=[('vs', i, 0), ('vs', i, 1)],
                               writes=[('V', tt)])
                    fw.barrier()
                if chk('2a'):
                    return
                with ExitStack() as st:
                    wt = [sb("wtb%d" % i, [128, 8, 512], BF16, st) for i in range(2)]
                    vrow = sb("vrow", [128, T + 4], F32, st)
                    bbg = sb("bbg", [128, T], F32, st)
                    acc = sb("acc", [128, T], F32, st)
                    ubr = [sb("ubr0", [128, T], BF16, st)] * 2
                    hsb = [sb("hsb%d" % i, [128, 512], F32, st) for i in range(2)]
                    sg = [sb("sg%d" % i, [128, 512], F32, st) for i in range(2)]
                    fw.op('pool', lambda e: e.memset(vrow[:], 0.0), writes=['vrow_all'])
                    LOFF = 1
                    COFF = SEQ + 3

                    def load_wb(jf, b):
                        for gi, base in enumerate((4096, 5120, 6144, 7168)):
                            fw.dma('pool', wt[b][:, :, gi * 128:(gi + 1) * 128],
                                   w_in[:, :, base + jf * 128:base + (jf + 1) * 128], writes=[('wtb', b, gi)])
                    load_wb(0, 0)
                    cnt = 0
                    for jf in range(8):
                        b = jf % 2
                        if jf + 1 < 8:
                            load_wb(jf + 1, 1 - b)
                        for (t0, n) in TB:
                            if last and t0 >= SEQ:
                                continue
                            i = cnt % 2
                            cnt += 1
                            for gi in range(4):
                                bank = 4 * i + gi

                                def mm(e, gi=gi, bank=bank, t0=t0, n=n, b=b):
                                    r = None
                                    for kc in range(8):
                                        r = e.matmul(ps[bank][:, 0:n], lhsT=wt[b][:, kc, gi * 128:(gi + 1) * 128],
                                                     rhs=hT[:, kc, t0:t0 + n], start=(kc == 0), stop=(kc == 7))
                                    return r
                                fw.op('pe', mm, reads=[('wtb', b, gi)] + hT_keys(t0, n), writes=[PSK[bank]])
                            voff = (LOFF + t0) if t0 < SEQ else (COFF + t0 - SEQ)
                            fw.op('act', lambda e, i=i, n=n: e.activation(out=hsb[i][:, 0:n], in_=ps[4 * i + 0][:, 0:n],
                                                                          func=AF.Copy),
                                  reads=[PSK[4 * i + 0]], writes=[('hsb', i)])
                            fw.op('dve', lambda e, i=i, n=n, voff=voff: e.tensor_tensor(
                                out=vrow[:, voff:voff + n], in0=ps[4 * i + 2][:, 0:n], in1=hsb[i][:, 0:n], op=ALU.mult),
                                reads=[PSK[4 * i + 2], ('hsb', i), 'vrow_all'], writes=[('vrow', t0)])
                            fw.op('act', lambda e, i=i, n=n: e.activation(out=sg[i][:, 0:n], in_=ps[4 * i + 3][:, 0:n],
                                                                          func=AF.Silu),
                                  reads=[PSK[4 * i + 3]], writes=[('sg', i)])
                            fw.op('dve', lambda e, i=i, n=n, t0=t0: e.tensor_tensor(
                                out=bbg[:, t0:t0 + n], in0=ps[4 * i + 1][:, 0:n], in1=sg[i][:, 0:n], op=ALU.mult),
                                reads=[PSK[4 * i + 1], ('sg', i)], writes=[('bbg', t0)])
                        segs = [(LOFF, 0, SEQ)] + ([] if last else [(COFF, SEQ, CTX)])
                        vk = [('vrow', t0) for (t0, n) in TB if not (last and t0 >= SEQ)]
                        bk = [('bbg', t0) for (t0, n) in TB if not (last and t0 >= SEQ)]
                        u = ubr[b]
                        for (vo, o0, n) in segs:
                            wc = lambda tap: convc[:, je * 24 + tap * 8 + jf:je * 24 + tap * 8 + jf + 1]
                            fw.op('dve', lambda e, vo=vo, o0=o0, n=n, wc=wc: e.tensor_scalar(
                                out=acc[:, o0:o0 + n], in0=vrow[:, vo - 1:vo - 1 + n], scalar1=wc(0), scalar2=None,
                                op0=ALU.mult), reads=vk + ['convc', 'vrow_all'], writes=[('acc', o0)])
                            for tap in (1, 2):
                                fw.op('dve', lambda e, vo=vo, o0=o0, n=n, wc=wc, tap=tap: e.scalar_tensor_tensor(
                                    out=acc[:, o0:o0 + n], in0=vrow[:, vo - 1 + tap:vo - 1 + tap + n], scalar=wc(tap),
                                    in1=acc[:, o0:o0 + n], op0=ALU.mult, op1=ALU.add),
                                    reads=vk + [('acc', o0)], writes=[('acc', o0)])
                            fw.op('pool', lambda e, o0=o0, n=n, u=u: e.tensor_tensor(
                                out=u[:, o0:o0 + n], in0=acc[:, o0:o0 + n], in1=bbg[:, o0:o0 + n], op=ALU.mult),
                                reads=[('acc', o0)] + bk, writes=[('ubr', 0, o0)])
                            fw.dma('sp', UT_s[8 + jf, :, o0:o0 + n], u[:, o0:o0 + n], reads=[('ubr', 0, o0)],
                                   writes=[('UT', 8 + jf, o0)])
                    fw.barrier()
                if chk('2b'):
                    return
            with ExitStack() as st:
                KT = sb("KT", [128, 8, T], BF16, st)
                V = sb("V", [128, NTT, D], BF16, st)
                for hd in range(8):
                    fw.dma('sp', KT[:, hd, :], KT_s[hd, :, :], writes=[('KT', hd)])
                Vv = V_s.rearrange("(tt p) f -> p tt f", p=128)
                for g in range(NTT // 2):
                    fw.dma('sp', V[:, 2 * g:2 * g + 2, :], Vv[:, 2 * g:2 * g + 2, :], writes=[('Vt', 2 * g), ('Vt', 2 * g + 1)])
                qt = [sb("qt%d" % i, [128, 512], BF16, st) for i in range(2)]
                ga = [sb("ga%d" % i, [128, 512], F32, st) for i in range(2)]
                pT = [sb("pT%d" % i, [128, 512], BF16, st) for i in range(4)]
                r1 = sb("r1", [128, 512], F32, st)
                a1 = sb("a1", [128, 512], F32, st)
                a2 = sb("a2", [128, 512], F32, st)
                dd = sb("dd", [128, 512], F32, st)
                uo = [sb("uo%d" % i, [128, 512], BF16, st) for i in range(2)]
                work = [(hd, t0, n) for hd in range(8) for (t0, n) in TB if not (last and t0 >= SEQ)]

                def loadq(wi):
                    hd, t0, n = work[wi]
                    i = wi % 2
                    fw.dma('sp', qt[i][:, 0:n], QT_s[hd, :, t0:t0 + n], reads=[('q', hd, t0)], writes=[('qt', i)])
                    fw.dma('sp', ga[i][:, 0:n], GA_s[hd, :, t0:t0 + n], reads=[('GA', hd, t0)], writes=[('ga', i)])
                loadq(0)
                pcnt = 0
                for wi, (hd, t0, n) in enumerate(work):
                    i = wi % 2
                    if wi + 1 < len(work):
                        loadq(wi + 1)
                    kts = list(range(NTT)) if t0 < SEQ else [32, 33]
                    for m in range(2):
                        bo, bz = 4 + 2 * m, 5 + 2 * m
                        for ki, kt in enumerate(kts):
                            sbank = pcnt % 4
                            pi = pcnt % 4
                            pcnt += 1
                            fw.op('pe', lambda e, m=m, kt=kt, n=n, i=i, sbank=sbank, hd=hd: e.matmul(
                                ps[sbank][:, 0:n], lhsT=KT[m * 64:(m + 1) * 64, hd, kt * 128:(kt + 1) * 128],
                                rhs=qt[i][m * 64:(m + 1) * 64, 0:n], start=True, stop=True),
                                reads=[('KT', hd), ('qt', i)], writes=[PSK[sbank]])
                            fw.op('act', lambda e, pi=pi, n=n, sbank=sbank: e.activation(
                                out=pT[pi][:, 0:n], in_=ps[sbank][:, 0:n], func=AF.Exp, scale=0.125),
                                reads=[PSK[sbank]], writes=[('pT', pi)])
                            first = (ki == 0)
                            lastk = (ki == len(kts) - 1)
                            fw.op('pe', lambda e, pi=pi, n=n, kt=kt, hd=hd, bo=bo, first=first, lastk=lastk: e.matmul(
                                ps[bo][:, 0:n], lhsT=V[:, kt, hd * 128:(hd + 1) * 128], rhs=pT[pi][:, 0:n],
                                start=first, stop=lastk), reads=[('Vt', kt), ('pT', pi)], writes=[PSK[bo]])
                            fw.op('pe', lambda e, pi=pi, n=n, bz=bz, first=first, lastk=lastk: e.matmul(
                                ps[bz][:, 0:n], lhsT=ones_b[:], rhs=pT[pi][:, 0:n], start=first, stop=lastk),
                                reads=['ones_b', ('pT', pi)], writes=[PSK[bz]])
                    fw.op('dve', lambda e, n=n: e.reciprocal(out=r1[:, 0:n], in_=ps[5][:, 0:n]),
                          reads=[PSK[5]], writes=['r1'])
                    fw.op('dve', lambda e, n=n: e.tensor_tensor(out=a1[:, 0:n], in0=ps[4][:, 0:n], in1=r1[:, 0:n],
                                                                op=ALU.mult), reads=[PSK[4], 'r1'], writes=['a1'])
                    fw.op('dve', lambda e, n=n: e.reciprocal(out=r1[:, 0:n], in_=ps[7][:, 0:n]),
                          reads=[PSK[7]], writes=['r1'])
                    fw.op('dve', lambda e, n=n: e.tensor_tensor(out=a2[:, 0:n], in0=ps[6][:, 0:n], in1=r1[:, 0:n],
                                                                op=ALU.mult), reads=[PSK[6], 'r1'], writes=['a2'])
                    fw.op('dve', lambda e, n=n: e.scalar_tensor_tensor(
                        out=dd[:, 0:n], in0=a2[:, 0:n], scalar=lam_c[:, je:je + 1], in1=a1[:, 0:n],
                        op0=ALU.mult, op1=ALU.add), reads=['a1', 'a2', ('lam_c', je)], writes=['dd'])
                    fw.op('act', lambda e, n=n: e.activation(out=a2[:, 0:n], in_=dd[:, 0:n], func=AF.Square),
                          reads=['dd'], writes=['a2'])
                    fw.op('pe', lambda e, n=n: e.matmul(ps[4][:, 0:n], lhsT=ones_f[:], rhs=a2[:, 0:n],
                                                        start=True, stop=True),
                          reads=['ones_f', 'a2'], writes=[PSK[4]])
                    fw.op('act', lambda e, n=n: e.activation(out=a1[:, 0:n], in_=ps[4][:, 0:n], func=AF.Sqrt,
                                                             scale=1.0 / 128, bias=EPS),
                          reads=[PSK[4]], writes=['a1'])
                    fw.op('dve', lambda e, n=n: e.reciprocal(out=a1[:, 0:n], in_=a1[:, 0:n]), reads=['a1'], writes=['a1'])
                    fw.op('dve', lambda e, n=n: e.scalar_tensor_tensor(
                        out=dd[:, 0:n], in0=dd[:, 0:n], scalar=sublc[:, je:je + 1], in1=a1[:, 0:n],
                        op0=ALU.mult, op1=ALU.mult), reads=['dd', 'a1', 'sublc'], writes=['dd'])
                    fw.op('dve', lambda e, n=n, i=i: e.tensor_tensor(out=uo[i][:, 0:n], in0=dd[:, 0:n], in1=ga[i][:, 0:n],
                                                                     op=ALU.mult),
                          reads=['dd', ('ga', i)], writes=[('uo', i)])
                    fw.dma('sp', UT_s[hd, :, t0:t0 + n], uo[i][:, 0:n], reads=[('uo', i)], writes=[('UT', hd, t0)])
                fw.barrier()

        def emit_odd(l, last):
            jo = l // 2
            w_in = od_w_in[jo, :, :].rearrange("(kc p) f -> p kc f", p=128)
            C_CK, C_CV, C_DK, C_DV, C_LR, C_CQ, C_DQ, C_CG, C_DG = 0, 1024, 2048, 2560, 3584, 3616, 4640, 5152, 6176
            ps7b = ps[7].bitcast(BF16)
            with ExitStack() as stL:
                lrT = sb("lrT", [16, 2, T], BF16, stL)
                gub = sb("gub", [16, 2, 512], BF16, stL)
                fw.dma('pool', gub[:], gu_d[jo, :, :, :].rearrange("d r c -> r d c"), writes=['gub'])
                with ExitStack() as st1:
                    hT = sb("hT", [128, 8, T], BF16, st1)
                    with ExitStack() as st:
                        emit_hT(l, hT, st)
                        fw.barrier()
                    if chk('hT'):
                        return
                    with ExitStack() as st:
                        wt = [sb("wto%d" % i, [128, 8, 512], BF16, st) for i in range(4)]
                        wlr = sb("wlr", [128, 8, 32], BF16, st)
                        ob = [sb("oob%d" % i, [128, 512], BF16, st) for i in range(2)]
                        gs = [sb("ogs%d" % i, [128, 512], F32, st) for i in range(2)]
                        vsn = [sb("vsn%d" % i, [128, 16, 65], BF16, st) for i in range(2)]
                        vs = [sb("ovs%d" % i, [128, D], BF16, st) for i in range(2)]
                        gst = [sb("gst%d" % i, [128, D], F32, st) for i in range(2)]
                        for i in range(2):
                            fw.op('pool', lambda e, i=i: e.memset(vsn[i][:], 1.0), writes=[('vsn', i, 0), ('vsn', i, 1)])
                        wcnt = [0]

                        def load_w(c0, ncol=512):
                            b = wcnt[0] % 4
                            wcnt[0] += 1
                            fw.dma('pool', wt[b][:, :, 0:ncol], w_in[:, :, c0:c0 + ncol], writes=[('wt', b)])
                            return b
                        fw.dma('pool', wlr[:], w_in[:, :, C_LR:C_LR + 32], writes=['wlr'])
                        cnt = 0
                        for dr_ in range(2):
                            for (t0, n) in TB:
                                bank = cnt % 2
                                cnt += 1

                                def mm(e, dr_=dr_, t0=t0, n=n, bank=bank):
                                    r = None
                                    for kc in range(8):
                                        r = e.matmul(ps[bank][0:16, 0:n], lhsT=wlr[:, kc, dr_ * 16:(dr_ + 1) * 16],
                                                     rhs=hT[:, kc, t0:t0 + n], start=(kc == 0), stop=(kc == 7))
                                    return r
                                fw.op('pe', mm, reads=['wlr'] + hT_keys(t0, n), writes=[PSK[bank]])
                                fw.op('act', lambda e, dr_=dr_, t0=t0, n=n, bank=bank: e.activation(
                                    out=lrT[0:16, dr_, t0:t0 + n], in_=ps[bank][0:16, 0:n], func=AF.Copy),
                                    reads=[PSK[bank]], writes=[('lrT', dr_, t0)])
                        plan = [('ck', C_CK, 0), ('ck', C_CK + 512, 4), ('cq', C_CQ, 0), ('cq', C_CQ + 512, 4),
                                ('dk', C_DK, 0), ('dq', C_DQ, 0)]
                        nextb = load_w(plan[0][1])
                        cnt = 0
                        for pi, (kind, c0, f0) in enumerate(plan):
                            b = nextb
                            if pi + 1 < len(plan):
                                nextb = load_w(plan[pi + 1][1])
                            for ft in range(4):
                                fidx = f0 + ft
                                for (t0, n) in TB:
                                    if last and kind in ('cq', 'dq') and t0 >= SEQ:
                                        continue
                                    i = cnt % 2
                                    cnt += 1
                                    bank = i

                                    def mm(e, b=b, ft=ft, t0=t0, n=n, bank=bank):
                                        r = None
                                        for kc in range(8):
                                            r = e.matmul(ps[bank][:, 0:n], lhsT=wt[b][:, kc, ft * 128:(ft + 1) * 128],
                                                         rhs=hT[:, kc, t0:t0 + n], start=(kc == 0), stop=(kc == 7))
                                        return r
                                    fw.op('pe', mm, reads=[('wt', b)] + hT_keys(t0, n), writes=[PSK[bank]])
                                    if kind in ('ck', 'cq'):
                                        sc = 1.0 if kind == 'ck' else 0.125
                                        dst = KT_s if kind == 'ck' else QT_s
                                        fw.op('act', lambda e, i=i, n=n, bank=bank, sc=sc: e.activation(
                                            out=ob[i][:, 0:n], in_=ps[bank][:, 0:n], func=AF.Copy, scale=sc),
                                            reads=[PSK[bank]], writes=[('ob', i)])
                                        fw.dma('sp', dst[fidx, :, t0:t0 + n], ob[i][:, 0:n], reads=[('ob', i)],
                                               writes=[(kind, fidx, t0)])
                                    else:
                                        sc = 1.0 if kind == 'dk' else (128.0 ** -0.5)
                                        gi = fidx if kind == 'dk' else 4 + fidx
                                        fw.op('act', lambda e, i=i, n=n, bank=bank, sc=sc: e.activation(
                                            out=gs[i][:, 0:n], in_=ps[bank][:, 0:n], func=AF.Copy, scale=sc),
                                            reads=[PSK[bank]], writes=[('gs', i)])
                                        fw.dma('sp', GA_s[gi, :, t0:t0 + n], gs[i][:, 0:n], reads=[('gs', i)],
                                               writes=[('GA', gi, t0)])
                        for (kind, c0) in (('cv', C_CV), ('dv', C_DV), ('cg', C_CG), ('dg', C_DG)):
                            bv = [load_w(c0), load_w(c0 + 512)]
                            for tt in range(NTT):
                                if last and kind in ('cg', 'dg') and tt >= 32:
                                    continue
                                i = tt % 2
                                for nb in range(2):
                                    bank = 2 + 2 * i + nb

                                    def mm(e, tt=tt, nb=nb, bank=bank, bv=bv):
                                        r = None
                                        for kc in range(8):
                                            r = e.matmul(ps[bank][:, :], lhsT=hT[:, kc, tt * 128:(tt + 1) * 128],
                                                         rhs=wt[bv[nb]][:, kc, :], start=(kc == 0), stop=(kc == 7))
                                        return r
                                    fw.op('pe', mm, reads=[('wt', bv[nb])] + [('hT', kc, tt) for kc in range(8)],
                                          writes=[PSK[bank]])
                                    if kind == 'cv':
                                        fw.op('act', lambda e, i=i, nb=nb, bank=bank: e.activation(
                                            out=vsn[i][:, nb * 8:(nb + 1) * 8, 0:64],
                                            in_=ps[bank][:, :].rearrange("p (h d) -> p h d", d=64), func=AF.Copy),
                                            reads=[PSK[bank]], writes=[('vsn', i, nb)])
                                    elif kind == 'dv':
                                        fw.op('act', lambda e, i=i, nb=nb, bank=bank: e.activation(
                                            out=vs[i][:, nb * 512:(nb + 1) * 512], in_=ps[bank][:, :], func=AF.Copy),
                                            reads=[PSK[bank]], writes=[('vs', i, nb)])
                                    else:
                                        fw.op('act', lambda e, i=i, nb=nb, bank=bank: e.activation(
                                            out=gst[i][:, nb * 512:(nb + 1) * 512], in_=ps[bank][:, :], func=AF.Silu),
                                            reads=[PSK[bank]], writes=[('gst', i, nb)])
                                rows = slice(tt * 128, (tt + 1) * 128)
                                if kind == 'cv':
                                    fw.dma('sp', VN_s[rows, :], vsn[i][:].rearrange("p h d -> p (h d)"),
                                           reads=[('vsn', i, 0), ('vsn', i, 1)], writes=[('VN', tt)])
                                elif kind == 'dv':
                                    fw.dma('sp', V_s[rows, :], vs[i][:], reads=[('vs', i, 0), ('vs', i, 1)], writes=[('DV', tt)])
                                else:
                                    fw.dma('sp', (CG_s if kind == 'cg' else DG_s)[rows, :], gst[i][:],
                                           reads=[('gst', i, 0), ('gst', i, 1)], writes=[(kind, tt)])
                        fw.barrier()
                if chk('o2'):
                    return
                order_f = [64, 65, 66, 67] + list(range(64))
                order_b = [67, 66, 65, 64] + list(range(63, -1, -1))
                NCH = T // 64
                for hp in range(2):
                    with ExitStack() as st:
                        chains = [(d_, 2 * hp + hl) for d_ in range(2) for hl in range(2)]
                        qtl = [sb("qtl%d" % ci, [128, T], BF16, st) for ci in range(4)]
                        ktl = [sb("ktl%d" % ci, [128, T], BF16, st) for ci in range(4)]
                        Et = [sb("Et%d" % ci, [128, NCH], F32, st) for ci in range(4)]
                        S = [sb("S%d" % ci, [128, 256], F32, st) for ci in range(4)]
                        Sb = [sb("Sb%d" % ci, [128, 256], BF16, st) for ci in range(4)]
                        qf = [sb("qf%d" % i, [128, 512], F32, st) for i in range(2)]
                        kf = [sb("kf%d" % i, [128, 512], F32, st) for i in range(2)]
                        e1 = [sb("e1_%d" % i, [128, 512], F32, st) for i in range(2)]
                        spt = [sb("spt%d" % i, [128, 512], F32, st) for i in range(2)]
                        pit = [sb("pit%d" % i, [128, 512], F32, st) for i in range(2)]
                        eq = [sb("eq%d" % i, [128, 512], F32, st) for i in range(2)]
                        ek = [sb("ek%d" % i, [128, 512], F32, st) for i in range(2)]
                        attb = [sb("attb%d" % ci, [64, 64], BF16, st) for ci in range(4)]
                        ktm = [sb("ktm%d" % ci, [64, 128], BF16, st) for ci in range(4)]
                        vch = [[sb("vch%d_%d" % (d_, i), [64, 512], BF16, st) for i in range(2)] for d_ in range(2)]
                        ost = [[sb("ost%d_%d" % (d_, i), [64, 512], F32, st) for i in range(2)] for d_ in range(2)]
                        cnt = 0
                        for ci, (d_, hh) in enumerate(chains):
                            for (t0, n) in TB:
                                i = cnt % 2
                                cnt += 1
                                nch = n // 64
                                c0 = t0 // 64
                                fw.dma('sp', qf[i][:, 0:n], GA_s[4 + hh, :, t0:t0 + n], reads=[('GA', 4 + hh, t0)],
                                       writes=[('qf', i)])
                                fw.dma('sp', kf[i][:, 0:n], GA_s[hh, :, t0:t0 + n], reads=[('GA', hh, t0)],
                                       writes=[('kf', i)])
                                fw.op('pe', lambda e, d_=d_, hh=hh, t0=t0, n=n: e.matmul(
                                    ps[6][:, 0:n], lhsT=gub[0:16, d_, hh * 128:(hh + 1) * 128], rhs=lrT[0:16, d_, t0:t0 + n],
                                    start=True, stop=True), reads=['gub', ('lrT', d_, t0)], writes=[PSK[6]])
                                gcol = jo * 8 + d_ * 4 + hh
                                fw.op('act', lambda e, i=i, n=n, gcol=gcol: e.activation(
                                    out=e1[i][:, 0:n], in_=ps[6][:, 0:n], func=AF.Exp, scale=-1.0, bias=gbc[:, gcol:gcol + 1]),
                                    reads=[PSK[6], 'gbc'], writes=[('e1', i)])
                                fw.op('act', lambda e, i=i, n=n: e.activation(
                                    out=spt[i][:, 0:n], in_=e1[i][:, 0:n], func=AF.Ln, bias=1.0),
                                    reads=[('e1', i)], writes=[('spt', i)])
                                fw.op('dve', lambda e, i=i, n=n: e.tensor_tensor_scan(
                                    out=pit[i][:, 0:n], data0=cmask[:, 0:n], data1=spt[i][:, 0:n], initial=0.0,
                                    op0=ALU.mult, op1=ALU.add), reads=['cmask', ('spt', i)], writes=[('pit', i)])
                                pv3 = pit[i][:, 0:n].rearrange("p (c s) -> p c s", s=64)
                                fw.op('act', lambda e, ci=ci, c0=c0, nch=nch, pv3=pv3: e.activation(
                                    out=Et[ci][:, c0:c0 + nch], in_=pv3[:, :, 63], func=AF.Exp, scale=-1.0 / 16),
                                    reads=[('pit', i)], writes=[('Et', ci, t0)])
                                if d_ == 1:
                                    fw.op('dve', lambda e, i=i, n=n: e.tensor_tensor(
                                        out=pit[i][:, 0:n], in0=pit[i][:, 0:n], in1=spt[i][:, 0:n], op=ALU.subtract),
                                        reads=[('pit', i), ('spt', i)], writes=[('pit', i)])
                                sq_ = (-1.0 / 16) if d_ == 0 else (1.0 / 16)
                                fw.op('act', lambda e, i=i, n=n, sq_=sq_: e.activation(
                                    out=eq[i][:, 0:n], in_=pit[i][:, 0:n], func=AF.Exp, scale=sq_),
                                    reads=[('pit', i)], writes=[('eq', i)])
                                fw.op('act', lambda e, i=i, n=n, sq_=sq_: e.activation(
                                    out=ek[i][:, 0:n], in_=pit[i][:, 0:n], func=AF.Exp, scale=-sq_),
                                    reads=[('pit', i)], writes=[('ek', i)])
                                fw.op('dve', lambda e, i=i, n=n, ci=ci, t0=t0: e.tensor_tensor(
                                    out=qtl[ci][:, t0:t0 + n], in0=qf[i][:, 0:n], in1=eq[i][:, 0:n], op=ALU.mult),
                                    reads=[('qf', i), ('eq', i)], writes=[('qtl', ci, t0)])
                                fw.op('pool', lambda e, i=i, n=n, ci=ci, t0=t0: e.tensor_tensor(
                                    out=ktl[ci][:, t0:t0 + n], in0=kf[i][:, 0:n], in1=ek[i][:, 0:n], op=ALU.mult),
                                    reads=[('kf', i), ('ek', i)], writes=[('ktl', ci, t0)])
                        for ci in range(4):
                            fw.op('dve', lambda e, ci=ci: e.memset(S[ci][:], 0.0), writes=[('S', ci)])
                            fw.op('pool', lambda e, ci=ci: e.memset(Sb[ci][:], 0.0), writes=[('Sb', ci)])
                        for step in range(NCH):
                            sb_i = step % 2
                            for d_ in range(2):
                                c = (order_f if d_ == 0 else order_b)[step]
                                tk = c * 64
                                blk = (tk // 512) * 512 if tk < SEQ else SEQ
                                want = not (last and c >= 64)
                                fw.dma('sp', vch[d_][sb_i][:], V_s[tk:tk + 64, hp * 512:(hp + 1) * 512],
                                       reads=[('DV', tk // 128)], writes=[('vch', d_, sb_i)])
                                for hl in range(2):
                                    ci = d_ * 2 + hl
                                    vv = vch[d_][sb_i][:, hl * 256:(hl + 1) * 256]
                                    kq_reads = [('qtl', ci, blk), ('ktl', ci, blk)]
                                    if d_ == 1:
                                        fw.op('dve', lambda e, ci=ci, c=c: e.tensor_scalar(
                                            out=S[ci][:], in0=S[ci][:], scalar1=Et[ci][:, c:c + 1], scalar2=None, op0=ALU.mult),
                                            reads=[('S', ci), ('Et', ci, blk)], writes=[('S', ci)])
                                        fw.op('pool', lambda e, ci=ci: e.tensor_copy(out=Sb[ci][:], in_=S[ci][:]),
                                              reads=[('S', ci)], writes=[('Sb', ci)])
                                    if want:
                                        fw.op('pe', lambda e, ci=ci, tk=tk: e.matmul(
                                            ps[ci][0:64, 0:64], lhsT=ktl[ci][:, tk:tk + 64], rhs=qtl[ci][:, tk:tk + 64],
                                            start=True, stop=True), reads=kq_reads, writes=[('psa', ci)])
                                        fw.op('dve', lambda e, ci=ci, d_=d_: e.tensor_tensor(
                                            out=attb[ci][:], in0=ps[ci][0:64, 0:64], in1=tri[:, d_ * 64:(d_ + 1) * 64], op=ALU.mult),
                                            reads=[('psa', ci), 'tri'], writes=[('attb', ci)])

                                        def omm(e, ci=ci, tk=tk, vv=vv):
                                            e.matmul(ps[ci][0:64, 128:384], lhsT=qtl[ci][:, tk:tk + 64], rhs=Sb[ci][:],
                                                     start=True, stop=False)
                                            return e.matmul(ps[ci][0:64, 128:384], lhsT=attb[ci][:], rhs=vv,
                                                            start=False, stop=True)
                                        fw.op('pe', omm, reads=kq_reads + [('Sb', ci), ('attb', ci), ('vch', d_, sb_i)],
                                              writes=[('pso', ci)])
                                        fw.op('dve', lambda e, ci=ci, d_=d_, hl=hl, sb_i=sb_i: e.tensor_copy(
                                            out=ost[d_][sb_i][:, hl * 256:(hl + 1) * 256], in_=ps[ci][0:64, 128:384]),
                                            reads=[('pso', ci)], writes=[('ost', d_, sb_i, hl)])
                                    fw.op('pe', lambda e, ci=ci, tk=tk: e.transpose(
                                        out=ps7b[0:64, ci * 128:(ci + 1) * 128], in_=ktl[ci][:, tk:tk + 64], identity=identb[:]),
                                        reads=[('ktl', ci, blk), 'identb'], writes=[('ps7', ci)])
                                    fw.op('act', lambda e, ci=ci: e.activation(
                                        out=ktm[ci][:], in_=ps7b[0:64, ci * 128:(ci + 1) * 128], func=AF.Copy),
                                        reads=[('ps7', ci)], writes=[('ktm', ci)])
                                    kvb = 4 + ci // 2
                                    kvc = (ci % 2) * 256
                                    fw.op('pe', lambda e, ci=ci, vv=vv, kvb=kvb, kvc=kvc: e.matmul(
                                        ps[kvb][:, kvc:kvc + 256], lhsT=ktm[ci][:], rhs=vv, start=True, stop=True),
                                        reads=[('ktm', ci), ('vch', d_, sb_i)], writes=[('pskv', ci)])
                                    fw.op('dve', lambda e, ci=ci, kvb=kvb, kvc=kvc: e.tensor_tensor(
                                        out=S[ci][:], in0=ps[kvb][:, kvc:kvc + 256], in1=S[ci][:], op=ALU.add),
                                        reads=[('pskv', ci), ('S', ci)], writes=[('S', ci)])
                                    if d_ == 0:
                                        fw.op('dve', lambda e, ci=ci, c=c: e.tensor_scalar(
                                            out=S[ci][:], in0=S[ci][:], scalar1=Et[ci][:, c:c + 1], scalar2=None, op0=ALU.mult),
                                            reads=[('S', ci), ('Et', ci, blk)], writes=[('S', ci)])
                                        fw.op('pool', lambda e, ci=ci: e.tensor_copy(out=Sb[ci][:], in_=S[ci][:]),
                                              reads=[('S', ci)], writes=[('Sb', ci)])
                                if want:
                                    dst = (OF_s if d_ == 0 else OB_s)[tk:tk + 64, hp * 512:(hp + 1) * 512]
                                    fw.dma('sp', dst, ost[d_][sb_i][:], reads=[('ost', d_, sb_i, 0), ('ost', d_, sb_i, 1)],
                                           writes=[('O', d_, c, hp)])
                        fw.barrier()
                if chk('o3'):
                    return
            with ExitStack() as st:
                gnb = sb("gnb", [128, 256], F32, st)
                fw.dma('sp', gnb[:], gnorm_b[jo, :, :], writes=['gnb'])
                oft = [sb("oft%d" % i, [128, D], F32, st) for i in range(2)]
                obt = [sb("obt%d" % i, [128, D], F32, st) for i in range(2)]
                dgt = [sb("dgt%d" % i, [128, D], F32, st) for i in range(2)]
                junk = sb("junk3", [128, 256], BF16, st)
                ssq = sb("ssq3", [128, 8], F32, st)
                ugb = [sb("ugb%d" % i, [128, 8, 128], BF16, st) for i in range(2)]
                ntt = 32 if last else NTT

                def loadc(tt):
                    i = tt % 2
                    rows = slice(tt * 128, (tt + 1) * 128)
                    fw.dma('sp', oft[i][:], OF_s[rows, :], writes=[('oft', i)])
                    fw.dma('sp', obt[i][:], OB_s[rows, :], writes=[('obt', i)])
                    fw.dma('sp', dgt[i][:], DG_s[rows, :], writes=[('dgt', i)])
                loadc(0)
                for tt in range(ntt):
                    i = tt % 2
                    if tt + 1 < ntt:
                        loadc(tt + 1)
                    fw.op('pool', lambda e, i=i: e.tensor_tensor(out=oft[i][:], in0=oft[i][:], in1=obt[i][:], op=ALU.add),
                          reads=[('oft', i), ('obt', i)], writes=[('oft', i)])
                    for hh in range(4):
                        fw.op('act', lambda e, i=i, hh=hh: e.activation(
                            out=junk[:], in_=oft[i][:, hh * 256:(hh + 1) * 256], func=AF.Square,
                            accum_out=ssq[:, i * 4 + hh:i * 4 + hh + 1]), reads=[('oft', i)], writes=['junk3', ('ssq3', i, hh)])
                    fw.op('act', lambda e, i=i: e.activation(out=ssq[:, i * 4:i * 4 + 4], in_=ssq[:, i * 4:i * 4 + 4],
                                                             func=AF.Sqrt, scale=1.0 / 256, bias=EPS),
                          reads=[('ssq3', i, hh) for hh in range(4)], writes=[('rs3', i)])
                    fw.op('dve', lambda e, i=i: e.reciprocal(out=ssq[:, i * 4:i * 4 + 4], in_=ssq[:, i * 4:i * 4 + 4]),
                          reads=[('rs3', i)], writes=[('rs3', i)])
                    for hh in range(4):
                        fw.op('dve', lambda e, i=i, hh=hh: e.scalar_tensor_tensor(
                            out=oft[i][:, hh * 256:(hh + 1) * 256], in0=oft[i][:, hh * 256:(hh + 1) * 256],
                            scalar=ssq[:, i * 4 + hh:i * 4 + hh + 1], in1=gnb[:], op0=ALU.mult, op1=ALU.mult),
                            reads=[('oft', i), ('rs3', i), 'gnb'], writes=[('oft', i)])
                    fw.op('pool', lambda e, i=i: e.tensor_tensor(out=oft[i][:], in0=oft[i][:], in1=dgt[i][:], op=ALU.mult),
                          reads=[('oft', i), ('dgt', i)], writes=[('oft', i)])
                    for half in range(2):
                        bank = 2 * i + half

                        def tr(e, i=i, half=half, bank=bank):
                            r = None
                            for q4 in range(4):
                                fc = half * 4 + q4
                                r = e.transpose(out=ps[bank][:, q4 * 128:(q4 + 1) * 128],
                                                in_=oft[i][:, fc * 128:(fc + 1) * 128], identity=ident[:])
                            return r
                        fw.op('pe', tr, reads=[('oft', i), 'ident'], writes=[PSK[bank]])
                        fw.op('dve', lambda e, i=i, half=half, bank=bank: e.tensor_copy(
                            out=ugb[i][:, half * 4:(half + 1) * 4, :],
                            in_=ps[bank][:, :].rearrange("p (f t) -> p f t", t=128)),
                            reads=[PSK[bank]], writes=[('ugb', i, half)])
                    fw.dma('sp', UT_s[8:16, :, tt * 128:(tt + 1) * 128].rearrange("f p t -> p f t"), ugb[i][:],
                           reads=[('ugb', i, 0), ('ugb', i, 1)], writes=[('UTg', tt)])
                fw.barrier()
            if chk('o3b'):
                return
            for hf in range(2):
                with ExitStack() as st:
                    KN = sb("KN", [128, 4, T], BF16, st)
                    VN = sb("VN", [128, NTT, 8 * 65], BF16, st)
                    TTt = sb("TTt", [128, 8, 16 * 64], BF16, st)
                    for j in range(4):
                        fw.dma('sp', KN[:, j, :], KT_s[hf * 4 + j, :, :], writes=[('KN', j)])
                    VNv = VN_s.rearrange("(tt p) f -> p tt f", p=128)
                    for g in range(NTT // 2):
                        fw.dma('sp', VN[:, 2 * g:2 * g + 2, :], VNv[:, 2 * g:2 * g + 2, hf * 520:(hf + 1) * 520],
                               writes=[('VNt', 2 * g), ('VNt', 2 * g + 1)])
                    for hl in range(8):
                        fw.dma('pool', TTt[:, hl, :], tt_d[jo, hf * 8 + hl, :, :], writes=[('TT', hl)])
                    qn = [sb("qn%d" % i, [128, 4, 512], BF16, st) for i in range(2)]
                    cgt = [sb("cgt%d" % i, [64, 512], F32, st) for i in range(2)]
                    pT = [sb("pTn%d" % i, [128, 448], BF16, st) for i in range(3)]
                    un = [sb("un%d" % i, [64, 512], F32, st) for i in range(2)]
                    rz = sb("rz", [64, 16], F32, st)
                    ust = [sb("ust%d" % i, [128, 4, 512], BF16, st) for i in range(2)]
                    rblocks = list(range(8)) + ([] if last else [8])
                    QTv = QT_s.rearrange("f p t -> p f t")

                    def loadq(bi):
                        rb = rblocks[bi]
                        i = bi % 2
                        n = 512 if rb < 8 else CTX
                        fw.dma('sp', qn[i][:, :, 0:n], QTv[:, hf * 4:(hf + 1) * 4, rb * 512:rb * 512 + n], writes=[('qn', i)])
                    loadq(0)
                    pcnt = 0
                    rcnt = 0
                    for bi, rb in enumerate(rblocks):
                        qi = bi % 2
                        if bi + 1 < len(rblocks):
                            loadq(bi + 1)
                        nrows = 8 if rb < 8 else 4
                        for rr in range(nrows):
                            ri = rcnt % 2
                            rcnt += 1
                            tok0 = rb * 512 + rr * 64
                            fw.dma('sp', cgt[ri][:], CG_s[tok0:tok0 + 64, hf * 512:(hf + 1) * 512], writes=[('cgt', ri)])
                            if rb < 8:
                                r = rb * 8 + rr
                                rs_ = min(max(r - 4, 0), 56)
                                if rs_ % 2 == 0:
                                    tiles = [(rs_ // 2 + k, 2 * (rs_ // 2 + k) - r + 7) for k in range(4)]
                                else:
                                    a0 = (rs_ - 1) // 2
                                    tiles = [(a0, 14)] + [(a0 + k, 2 * (a0 + k) - r + 7) for k in (1, 2, 3)] + [(a0 + 4, 15)]
                            else:
                                tiles = []
                            alltiles = tiles + [(32, None), (33, None)]
                            ntl = len(alltiles)
                            for hl in range(8):
                                j = hl // 2
                                p0 = (hl % 2) * 64
                                sbank = pcnt % 3
                                obank = 3 + pcnt % 3
                                pi_ = pcnt % 3
                                pcnt += 1

                                def smm(e, j=j, p0=p0, sbank=sbank, alltiles=alltiles, qi=qi, rr=rr, hl=hl):
                                    r_ = None
                                    for k, (a, ti) in enumerate(alltiles):
                                        r_ = e.matmul(ps[sbank][:, k * 64:(k + 1) * 64],
                                                      lhsT=KN[p0:p0 + 64, j, a * 128:(a + 1) * 128],
                                                      rhs=qn[qi][p0:p0 + 64, j, rr * 64:(rr + 1) * 64],
                                                      start=True, stop=(ti is None))
                                        if ti is not None:
                                            r_ = e.matmul(ps[sbank][:, k * 64:(k + 1) * 64], lhsT=identb[:],
                                                          rhs=TTt[:, hl, ti * 64:(ti + 1) * 64], start=False, stop=True)
                                    return r_
                                fw.op('pe', smm, reads=[('KN', j), ('qn', qi), ('TT', hl), 'identb'], writes=[PSK[sbank]])
                                fw.op('act', lambda e, pi_=pi_, sbank=sbank, ntl=ntl: e.activation(
                                    out=pT[pi_][:, 0:ntl * 64], in_=ps[sbank][:, 0:ntl * 64], func=AF.Exp),
                                    reads=[PSK[sbank]], writes=[('pTn', pi_)])

                                def pvm(e, pi_=pi_, obank=obank, alltiles=alltiles, hl=hl):
                                    r_ = None
                                    for k, (a, ti) in enumerate(alltiles):
                                        r_ = e.matmul(ps[obank][0:64, 0:65], lhsT=pT[pi_][:, k * 64:(k + 1) * 64],
                                                      rhs=VN[:, a, hl * 65:(hl + 1) * 65], start=(k == 0),
                                                      stop=(k == len(alltiles) - 1))
                                    return r_
                                fw.op('pe', pvm, reads=[('pTn', pi_)] + [('VNt', a) for (a, ti) in alltiles], writes=[PSK[obank]])
                                fw.op('dve', lambda e, obank=obank, hl=hl, ri=ri: e.reciprocal(
                                    out=rz[:, ri * 8 + hl:ri * 8 + hl + 1], in_=ps[obank][0:64, 64:65]),
                                    reads=[PSK[obank]], writes=[('rz', ri, hl)])
                                fw.op('dve', lambda e, obank=obank, hl=hl, ri=ri: e.tensor_scalar(
                                    out=un[ri][:, hl * 64:(hl + 1) * 64], in0=ps[obank][0:64, 0:64],
                                    scalar1=rz[:, ri * 8 + hl:ri * 8 + hl + 1], scalar2=None, op0=ALU.mult),
                                    reads=[PSK[obank], ('rz', ri, hl)], writes=[('un', ri, hl)])
                            unk = [('un', ri, hl) for hl in range(8)]
                            fw.op('pool', lambda e, ri=ri: e.tensor_tensor(out=un[ri][:], in0=un[ri][:], in1=cgt[ri][:], op=ALU.mult),
                                  reads=unk + [('cgt', ri)], writes=[('ung', ri)] + unk)

                            def tr(e, ri=ri):
                                r_ = None
                                for j in range(4):
                                    r_ = e.transpose(out=ps[6][:, j * 64:(j + 1) * 64], in_=un[ri][:, j * 128:(j + 1) * 128],
                                                     identity=ident[0:64, 0:64])
                                return r_
                            fw.op('pe', tr, reads=[('ung', ri), 'ident'] + unk, writes=[PSK[6]])
                            fw.op('dve', lambda e, qi=qi, rr=rr: e.tensor_copy(
                                out=ust[qi][:, :, rr * 64:(rr + 1) * 64], in_=ps[6][:, 0:256].rearrange("p (f t) -> p f t", t=64)),
                                reads=[PSK[6]], writes=[('ust', qi, rr)])
                        n = 512 if rb < 8 else CTX
                        fw.dma('sp', UT_s[hf * 4:(hf + 1) * 4, :, rb * 512:rb * 512 + n].rearrange("f p t -> p f t"),
                               ust[qi][:, :, 0:n], reads=[('ust', qi, rr) for rr in range(nrows)], writes=[('UTn', hf, rb)])
                    fw.barrier()

        def emit_outproj(l, last):
            with ExitStack() as st:
                wo = sb("wo", [128, 16, D], BF16, st)
                wov = w_out[l, :, :].rearrange("(fc p) n -> p fc n", p=128)
                for g in range(4):
                    fw.dma('pool', wo[:, 4 * g:4 * g + 4, :], wov[:, 4 * g:4 * g + 4, :], writes=[('wo', g)])
                wok = [('wo', g) for g in range(4)]
                ut = [sb("ut%d" % i, [128, 16, 512], BF16, st) for i in range(2)]
                xr = [sb("xr%d" % i, [128, D], F32, st) for i in range(2)]
                tn = [sb("tn%d" % i, [128, D], F32, st) for i in range(2)]
                xo = [sb("xo%d" % i, [128, D], F32, st) for i in range(2)]
                junk = sb("junk2", [128, 512], BF16, st)
                ss2 = sb("ss2", [128, 4], F32, st)
                rr = sb("rr", [128, 2], F32, st)
                acs2 = sb("acs2", [128, 2], F32, st)
                blocks = [tb for tb in TB if not (last and tb[0] >= SEQ)]
                UTv = UT_s.rearrange("f p t -> p f t")

                def loadu(bi):
                    t0, n = blocks[bi]
                    i = bi % 2
                    fw.dma('sp', ut[i][:, :, 0:n], UTv[:, :, t0:t0 + n],
                           reads=[('UT', f, t0) for f in range(16)] + [('UT', f, 0) for f in range(8, 16)] +
                           [('UT', f, SEQ) for f in range(8, 16)], writes=[('ut', i)])
                loadu(0)
                tcnt = 0
                for bi, (t0, n) in enumerate(blocks):
                    i = bi % 2
                    if bi + 1 < len(blocks):
                        loadu(bi + 1)
                    for stl in range(n // 128):
                        tt = (t0 // 128) + stl
                        j = 0 if tt < 32 else 1
                        c = tcnt % 2
                        tcnt += 1
                        src = (x_in if l == first_layer else out_d)[tt * 128:(tt + 1) * 128, :] if tt < 32 else \
                            (ctx_in if l == first_layer else xc_d)[(tt - 32) * 128:(tt - 31) * 128, :]
                        dst = out_d[tt * 128:(tt + 1) * 128, :] if tt < 32 else xc_d[(tt - 32) * 128:(tt - 31) * 128, :]
                        fw.dma('sp', xr[c][:], src, reads=[('X', tt)], writes=[('xr', c)])
                        for nb in range(2):
                            bank = 2 * c + nb

                            def mm(e, i=i, stl=stl, nb=nb, bank=bank):
                                r = None
                                for fc in range(16):
                                    r = e.matmul(ps[bank][:, :], lhsT=ut[i][:, fc, stl * 128:(stl + 1) * 128],
                                                 rhs=wo[:, fc, nb * 512:(nb + 1) * 512], start=(fc == 0), stop=(fc == 15))
                                return r
                            fw.op('pe', mm, reads=wok + [('ut', i)], writes=[PSK[bank]])
                            def sqa(e, bank=bank, c=c, nb=nb):
                                return e.activation(out=junk[:], in_=ps[bank][:, :], func=AF.Square,
                                                    accum_out=ss2[:, 2 * c + nb:2 * c + nb + 1])
                            fw.op('act', sqa, reads=[PSK[bank]], writes=['junk2', ('ss2', c, nb)])
                        fw.op('dve', lambda e, c=c: e.tensor_tensor(out=rr[:, c:c + 1], in0=ss2[:, 2 * c:2 * c + 1],
                                                                    in1=ss2[:, 2 * c + 1:2 * c + 2], op=ALU.add),
                              reads=[('ss2', c, 0), ('ss2', c, 1)], writes=[('rr', c)])
                        fw.op('act', lambda e, c=c: e.activation(out=rr[:, c:c + 1], in_=rr[:, c:c + 1], func=AF.Sqrt,
                                                                 scale=1.0 / D, bias=EPS),
                              reads=[('rr', c)], writes=[('rr', c)])
                        fw.op('dve', lambda e, c=c: e.reciprocal(out=rr[:, c:c + 1], in_=rr[:, c:c + 1]),
                              reads=[('rr', c)], writes=[('rr', c)])
                        for nb in range(2):
                            bank = 2 * c + nb
                            fw.op('dve', lambda e, c=c, nb=nb, bank=bank, j=j: e.scalar_tensor_tensor(
                                out=tn[c][:, nb * 512:(nb + 1) * 512], in0=ps[bank][:, :], scalar=rr[:, c:c + 1],
                                in1=GG[:, j, nb * 512:(nb + 1) * 512], op0=ALU.mult, op1=ALU.mult),
                                reads=[PSK[bank], ('rr', c), ('GG', j, nb)], writes=[('tn', c, nb)])
                        fw.op('pool', lambda e, c=c: e.tensor_tensor(out=xo[c][:], in0=tn[c][:], in1=xr[c][:], op=ALU.add),
                              reads=[('tn', c, 0), ('tn', c, 1), ('xr', c)], writes=[('xo', c)])
                        fw.dma('sp', dst, xo[c][:], reads=[('xo', c)], writes=[('X', tt)])
                fw.barrier()

        stopped = [False]

        def chk(name):
            if stop_after == name:
                stopped[0] = True
            return stopped[0]

        if True:
          for l in range(first_layer, n_layers):
            last = (l == DEPTH - 1)
            emit_mod(l)
            if chk('mod'):
                break
            if l % 2 == 0:
                emit_even(l, last)
            else:
                emit_odd(l, last)
            if stopped[0]:
                break
            emit_outproj(l, last)
        fw.barrier()
        print("ops emitted:", fw.nops, "sems:", fw.nsem)
    return nc


_NC_CACHE = {}


def _prep_inputs(inp, b):
    cosT, sinT, rm = _rope_tables()
    m = {}
    m["x"] = np.ascontiguousarray(inp["x"][b])
    m["ctx"] = np.ascontiguousarray(inp["ctx"][b])
    m["cvec"] = np.ascontiguousarray(np.concatenate([_col(inp["c"][b]), _col(inp["c_ctx"])], axis=1))
    m["w_mod"] = inp["w_mod"]
    m["bmod_col"] = np.ascontiguousarray(np.concatenate([_col(inp["b_mod"][l]) for l in range(DEPTH)], axis=1))
    m["bmod_gate"] = np.ascontiguousarray(np.broadcast_to(inp["b_mod"][:, None, 2 * D:], (DEPTH, 128, D)))
    m["gpre_col"] = np.ascontiguousarray(np.concatenate([_col(inp["g_pre"][l]) for l in range(DEPTH)], axis=1))
    m["gpost_b"] = np.ascontiguousarray(np.broadcast_to(inp["g_post"][:, None, :], (DEPTH, 128, D)))
    m["w_out"] = inp["w_out"]
    m["ev_w_in"] = inp["ev_w_in"]
    m["lam_b"] = np.ascontiguousarray(np.broadcast_to(inp["ev_lambda"].reshape(1, 512), (128, 512)))
    m["subln_col"] = np.ascontiguousarray(inp["ev_subln"].T)
    m["conv_col"] = np.ascontiguousarray(
        np.concatenate([_col(inp["ev_conv"][j, tap]) for j in range(2) for tap in range(3)], axis=1))
    m["od_w_in"] = inp["od_w_in"]
    m["tt_tab"] = _na_tables(inp["od_rpb"])
    m["gate_up"] = inp["od_gate_up"]
    m["gb_col"] = np.ascontiguousarray(np.concatenate(
        [_col(inp["od_gate_bias"][j, d_]) for j in range(2) for d_ in range(2)], axis=1))
    m["gnorm_b"] = np.ascontiguousarray(np.broadcast_to(inp["od_gnorm"][:, None, :], (2, 128, 256)))
    cm = np.ones((128, 512), np.float32)
    cm[:, ::64] = 0.0
    m["cmask"] = cm
    si = np.arange(64)
    m["tri"] = np.ascontiguousarray(np.concatenate([(si[:, None] <= si[None, :]), (si[:, None] >= si[None, :])],
                                                  axis=1).astype(np.float32))
    m["cosT"] = cosT
    m["sinT"] = sinT
    m["rm"] = rm
    m["ident"] = np.eye(128, dtype=np.float32)
    return m


def kernel(**inputs):
    inp = {k: np.asarray(v) for k, v in inputs.items()}
    if "nc" not in _NC_CACHE:
        _NC_CACHE["nc"] = build()
    nc = _NC_CACHE["nc"]
    in_maps = [_prep_inputs(inp, b) for b in range(8)]
    res = run_bass_kernel_spmd(nc, in_maps, core_ids=list(range(8)))
    return np.stack([r["out"] for r in res.results], axis=0).astype(np.float32)
```

```python
import math
from contextlib import ExitStack
import numpy as np
import concourse.bass as bass
import concourse.mybir as mybir
from concourse.bass_utils import run_bass_kernel_spmd

F32 = mybir.dt.float32
BF16 = mybir.dt.bfloat16
AF = mybir.ActivationFunctionType
ALU = mybir.AluOpType
AX = mybir.AxisListType

D = 1024
SEQ = 4096
CTX = 256
T = SEQ + CTX
NTT = T // 128
DEPTH = 4
EPS = 1e-6
GRID = 64
SAME_ENG_SYNC = True


class Fw:
    SEM_LIMIT = 30000

    def __init__(self, nc, es, n_dma_slots=12):
        self.nc = nc
        self.es = es
        self.E = {'pe': nc.tensor, 'dve': nc.vector, 'act': nc.scalar, 'pool': nc.gpsimd, 'sp': nc.sync}
        self.cur = {}
        self.nsem = 0
        for e in self.E:
            self.cur[e] = [self._newsem(e), 0]
        self.known = {e: {} for e in self.E}
        self.last_w = {}
        self.readers = {}
        self.slots = {}
        for q in ('sp', 'pool', 'act'):
            n = n_dma_slots if q != 'act' else 4
            self.slots[q] = [[self._newsem('d' + q), 0] for _ in range(n)]
        self.slot_rr = {q: 0 for q in self.slots}
        self.nops = 0

    def _newsem(self, tag):
        self.nsem += 1
        return self.es.enter_context(self.nc.semaphore("s%s%d" % (tag, self.nsem)))

    def _wait(self, eng, tok):
        sem, val, teng = tok
        kn = self.known[eng]
        k = id(sem)
        if kn.get(k, 0) >= val:
            return
        self.E[eng].wait_ge(sem, val)
        kn[k] = val

    def _deps(self, eng, reads, writes):
        deps = {}

        def addtok(t, hazard):
            if t is None:
                return
            sem, val, teng = t
            if teng == eng and teng != 'dma':
                if eng == 'pe' or not SAME_ENG_SYNC:
                    return
            k = id(sem)
            if k not in deps or deps[k][1] < val:
                deps[k] = t
        for r in reads:
            addtok(self.last_w.get(r), 'raw')
        for w in writes:
            addtok(self.last_w.get(w), 'waw')
            for t in self.readers.get(w, ()):
                addtok(t, 'war')
        return deps.values()

    def _commit(self, tok, reads, writes):
        for r in reads:
            self.readers.setdefault(r, []).append(tok)
        for w in writes:
            self.last_w[w] = tok
            self.readers[w] = []

    def op(self, eng, fn, reads=(), writes=()):
        for t in self._deps(eng, reads, writes):
            self._wait(eng, t)
        ins = fn(self.E[eng])
        c = self.cur[eng]
        if c[1] >= self.SEM_LIMIT:
            c[0] = self._newsem(eng)
            c[1] = 0
        c[1] += 1
        ins.then_inc(c[0], 1)
        tok = (c[0], c[1], eng)
        self._commit(tok, reads, writes)
        self.nops += 1
        return tok

    def dma(self, q, out, in_, reads=(), writes=()):
        for t in self._deps(q, reads, writes):
            self._wait(q, t)
        sl = self.slots[q]
        i = self.slot_rr[q]
        self.slot_rr[q] = (i + 1) % len(sl)
        s = sl[i]
        if s[1] > 0:
            self._wait(q, (s[0], s[1], 'dma'))
        if s[1] >= self.SEM_LIMIT:
            s[0] = self._newsem('d' + q)
            s[1] = 0
        ins = self.E[q].dma_start(out=out, in_=in_)
        s[1] += 16
        ins.then_inc(s[0], 16)
        tok = (s[0], s[1], 'dma')
        self._commit(tok, reads, writes)
        self.nops += 1
        return tok

    def barrier(self):
        toks = []
        for e in self.E:
            c = self.cur[e]
            if c[1] > 0:
                toks.append((c[0], c[1], e))
        for q in self.slots:
            for s in self.slots[q]:
                if s[1] > 0:
                    toks.append((s[0], s[1], 'dma'))
        for e in self.E:
            for t in toks:
                if t[2] == e:
                    continue
                self._wait(e, t)
        self.last_w = {}
        self.readers = {}


def _rope_tables():
    half = 32
    inv = (1.0 / (10000.0 ** (np.arange(0, half, 2, dtype=np.float32) / np.float32(half)))).astype(np.float32)
    t = np.arange(SEQ)
    row = (t // GRID).astype(np.float32)[:, None] * inv
    col = (t % GRID).astype(np.float32)[:, None] * inv
    ang = np.concatenate([row, row, col, col], axis=-1).astype(np.float32)
    cos = np.cos(ang).astype(np.float32).T
    sin = np.sin(ang).astype(np.float32).T
    sign = np.ones(64, np.float32)
    src = np.zeros(64, np.int64)
    for f in range(64):
        blk = (f // 32) * 32
        o = f % 32
        if o < 16:
            src[f] = blk + o + 16
            sign[f] = -1.0
        else:
            src[f] = blk + o - 16
            sign[f] = 1.0
    sin_s = sin * sign[:, None]
    cosT = np.concatenate([cos, cos], axis=0)
    sinT = np.concatenate([sin_s, sin_s], axis=0)
    rm = np.zeros((128, 128), np.float32)
    for m in range(2):
        for f in range(64):
            rm[m * 64 + src[f], m * 64 + f] = 1.0
    return np.ascontiguousarray(cosT), np.ascontiguousarray(sinT), rm


def _na_tables(rpb):
    NEG = np.float32(-30000.0)
    kc = np.arange(64)[:, None]
    c = np.arange(64)[None, :]
    ws = np.clip(c - 8, 0, 48)
    valid = (kc >= ws) & (kc < ws + 16)
    coff = np.clip(kc - c + 15, 0, 30)
    out = np.full((2, 16, 2, 64, 16, 64), NEG, np.float32)
    for ti in range(16):
        for par in range(2):
            if ti < 14:
                dr = ti + par
            elif ti == 14:
                dr = 3 if par == 1 else None
            else:
                dr = 10 if par == 0 else None
            if dr is None or dr > 14:
                continue
            g = rpb[:, :, dr, :][:, :, coff]
            out[:, :, par, :, ti, :] = np.where(valid[None, None], g, NEG)
    return np.ascontiguousarray(out.reshape(2, 16, 128, 16 * 64))


def _col(v):
    return np.ascontiguousarray(v.reshape(-1, 128).T)


class _Stop(Exception):
    pass


def build(n_layers=DEPTH, dbg=False, stop_after=None, first_layer=0):
    nc = bass.Bass("TRN2", target_bir_lowering=False)

    def din(name, shape, dt=F32):
        return nc.dram_tensor(name, list(shape), dt, kind="ExternalInput").ap()

    def dscr(name, shape, dt):
        return nc.dram_tensor(name, list(shape), dt, kind="Internal").ap()

    x_in = din("x", [SEQ, D])
    ctx_in = din("ctx", [CTX, D])
    cvec = din("cvec", [128, 16])
    w_mod = din("w_mod", [DEPTH, D, 3 * D])
    bmod_col = din("bmod_col", [128, DEPTH * 24])
    bmod_gate = din("bmod_gate", [DEPTH, 128, D])
    gpre_col = din("gpre_col", [128, DEPTH * 8])
    gpost_b = din("gpost_b", [DEPTH, 128, D])
    w_out = din("w_out", [DEPTH, 2 * D, D])
    ev_w_in = din("ev_w_in", [2, D, 8 * D])
    lam_b = din("lam_b", [128, 2 * 256])
    subln_col = din("subln_col", [128, 2])
    conv_col = din("conv_col", [128, 2 * 3 * 8])
    cosT_d = din("cosT", [128, SEQ])
    sinT_d = din("sinT", [128, SEQ])
    rm_d = din("rm", [128, 128])
    ident_d = din("ident", [128, 128])
    od_w_in = din("od_w_in", [2, D, 7200])
    tt_d = din("tt_tab", [2, 16, 128, 16 * 64])
    gu_d = din("gate_up", [2, 2, 16, 512])
    gb_col = din("gb_col", [128, 16])
    gnorm_b = din("gnorm_b", [2, 128, 256])
    cmask_d = din("cmask", [128, 512])
    tri_d = din("tri", [64, 128])
    out_d = nc.dram_tensor("out", [SEQ, D], F32, kind="ExternalOutput").ap()
    xc_d = nc.dram_tensor("xc_out", [CTX, D], F32, kind="ExternalOutput" if dbg else "Internal").ap()

    KT_s = dscr("KT_s", [8, 128, T], BF16)
    QT_s = dscr("QT_s", [8, 128, T], BF16)
    V_s = dscr("V_s", [T, D], BF16)
    GA_s = dscr("GA_s", [8, 128, T], F32)
    UT_s = dscr("UT_s", [16, 128, T], BF16)
    VN_s = dscr("VN_s", [T, 16 * 65], BF16)
    CG_s = dscr("CG_s", [T, D], F32)
    DG_s = dscr("DG_s", [T, D], F32)
    OF_s = dscr("OF_s", [T, D], F32)
    OB_s = dscr("OB_s", [T, D], F32)

    es_top = ExitStack()
    with es_top as es:
        fw = Fw(nc, es)

        uid = [0]

        def sb(name, shape, dt, stack=None):
            uid[0] += 1
            return (stack or es).enter_context(nc.sbuf_tensor("sb_%s_%d" % (name, uid[0]), list(shape), dt))

        ps = [es.enter_context(nc.psum_tensor("ps%d" % i, [128, 512], F32)) for i in range(8)]
        PSK = [('ps', i) for i in range(8)]

        ident = sb("ident", [128, 128], F32)
        ones_f = sb("ones_f", [128, 128], F32)
        ones_b = sb("ones_b", [128, 128], BF16)
        rm_b = sb("rm_b", [128, 128], BF16)
        cs = sb("cs", [128, 16], F32)
        csb = sb("csb", [128, 16, 128], F32)
        bmodc = sb("bmodc", [128, DEPTH * 24], F32)
        gprec = sb("gprec", [128, DEPTH * 8], F32)
        lamt = sb("lamt", [128, 512], F32)
        lam2 = sb("lam2", [128, 4, 64], F32)
        lsum = sb("lsum", [128, 4], F32)
        lam_c = sb("lam_c", [128, 2], F32)
        sublc = sb("sublc", [128, 2], F32)
        convc = sb("convc", [128, 48], F32)
        A_m = sb("A_m", [128, 2, 8], F32)
        B_m = sb("B_m", [128, 2, 8], F32)
        sc_m = sb("sc_m", [128, 2, 8], F32)
        GG = sb("GG", [128, 2, D], F32)

        identb = sb("identb", [128, 128], BF16)
        cmask = sb("cmask", [128, 512], F32)
        tri = sb("tri", [64, 128], F32)
        gbc = sb("gbc", [128, 16], F32)
        fw.dma('pool', identb[:], ident_d[:, :], writes=['identb'])
        fw.dma('sp', cmask[:], cmask_d[:, :], writes=['cmask'])
        fw.dma('sp', tri[:], tri_d[:, :], writes=['tri'])
        fw.dma('sp', gbc[:], gb_col[:, :], writes=['gbc'])
        fw.op('dve', lambda e: e.tensor_scalar(out=gbc[:], in0=gbc[:], scalar1=-1.0, scalar2=None, op0=ALU.mult),
              reads=['gbc'], writes=['gbc'])
        fw.dma('sp', ident[:], ident_d[:, :], writes=['ident'])
        fw.dma('pool', rm_b[:], rm_d[:, :], writes=['rm_b'])
        fw.dma('sp', cs[:], cvec[:, :], writes=['cs'])
        fw.dma('sp', bmodc[:], bmod_col[:, :], writes=['bmodc'])
        fw.dma('sp', gprec[:], gpre_col[:, :], writes=['gprec'])
        fw.dma('sp', lamt[:], lam_b[:, :], writes=['lamt'])
        fw.dma('sp', sublc[:], subln_col[:, :], writes=['sublc'])
        fw.dma('sp', convc[:], conv_col[:, :], writes=['convc'])
        fw.op('dve', lambda e: e.memset(ones_f[:], 1.0), writes=['ones_f'])
        fw.op('dve', lambda e: e.memset(ones_b[:], 1.0), writes=['ones_b'])
        fw.op('act', lambda e: e.activation(out=cs[:], in_=cs[:], func=AF.Silu), reads=['cs'], writes=['cs'])
        for k in range(16):
            fw.op('dve', lambda e, k=k: e.tensor_scalar(out=csb[:, k, :], in0=ones_f[:], scalar1=cs[:, k:k + 1],
                                                        scalar2=None, op0=ALU.mult),
                  reads=['ones_f', 'cs'], writes=[('csb', k)])
        lt = lamt[:].rearrange("p (j a d) -> p j a d", j=2, a=4)
        for j in range(2):
            for a in range(2):
                fw.op('dve', lambda e, j=j, a=a: e.tensor_tensor(out=lam2[:, j * 2 + a, :], in0=lt[:, j, 2 * a, :],
                                                                 in1=lt[:, j, 2 * a + 1, :], op=ALU.mult),
                      reads=['lamt'], writes=[('lam2', j, a)])
                fw.op('dve', lambda e, j=j, a=a: e.tensor_reduce(out=lsum[:, j * 2 + a:j * 2 + a + 1],
                                                                 in_=lam2[:, j * 2 + a, :], axis=AX.X, op=ALU.add),
                      reads=[('lam2', j, a)], writes=[('lsum', j, a)])
        fw.op('act', lambda e: e.activation(out=lsum[:], in_=lsum[:], func=AF.Exp),
              reads=[('lsum', j, a) for j in range(2) for a in range(2)], writes=['lsume'])
        for j in range(2):
            lam_init = 0.8 - 0.6 * math.exp(-0.3 * (2 * j))
            fw.op('dve', lambda e, j=j, li=lam_init: e.scalar_tensor_tensor(
                out=lam_c[:, j:j + 1], in0=lsum[:, 2 * j + 1:2 * j + 2], scalar=-li, in1=lsum[:, 2 * j:2 * j + 1],
                op0=ALU.add, op1=ALU.subtract), reads=['lsume'], writes=[('lam_c', j)])
            fw.op('dve', lambda e, j=j, li=lam_init: e.tensor_scalar(
                out=sublc[:, j:j + 1], in0=sublc[:, j:j + 1], scalar1=(1.0 - li), scalar2=None, op0=ALU.mult),
                reads=['sublc'], writes=['sublc'])

        def emit_mod(l):
            with ExitStack() as st:
                wm = [sb("wm%d" % i, [128, 8, 512], F32, st) for i in range(2)]
                bg = sb("bg", [128, D], F32, st)
                gp = sb("gp", [128, D], F32, st)
                tmpg = sb("tmpg", [128, 512], F32, st)
                fw.dma('sp', bg[:], bmod_gate[l, :, :], writes=['bg'])
                fw.dma('sp', gp[:], gpost_b[l, :, :], writes=['gp'])
                wv = w_mod[l, :, :].rearrange("(kc p) n -> p kc n", p=128)
                for blk in range(6):
                    b = blk % 2
                    fw.dma('sp', wm[b][:], wv[:, :, blk * 512:(blk + 1) * 512], writes=[('wm', b)])
                    if blk < 4:
                        for n4 in range(4):
                            n = blk * 4 + n4

                            def mm(e, n=n, n4=n4, b=b):
                                r = None
                                for kc in range(8):
                                    r = e.matmul(ps[0][:, 2 * n:2 * n + 2], lhsT=wm[b][:, kc, n4 * 128:(n4 + 1) * 128],
                                                 rhs=cs[:, kc:16:8], start=(kc == 0), stop=(kc == 7))
                                return r
                            fw.op('pe', mm, reads=[('wm', b), 'cs'], writes=[('modps', n)] + ([PSK[0]] if n == 0 else []))
                    else:
                        nb = blk - 4
                        for j in range(2):
                            bank = 1 + j

                            def mm(e, j=j, b=b, bank=bank):
                                r = None
                                for kc in range(8):
                                    r = e.matmul(ps[bank][:, :], lhsT=csb[:, j * 8 + kc, :], rhs=wm[b][:, kc, :],
                                                 start=(kc == 0), stop=(kc == 7))
                                return r
                            fw.op('pe', mm, reads=[('wm', b)] + [('csb', j * 8 + kc) for kc in range(8)], writes=[PSK[bank]])
                            fw.op('dve', lambda e, nb=nb, bank=bank: e.tensor_tensor(
                                out=tmpg[:], in0=ps[bank][:, :], in1=bg[:, nb * 512:(nb + 1) * 512], op=ALU.add),
                                reads=[PSK[bank], 'bg'], writes=['tmpg'])
                            fw.op('dve', lambda e, nb=nb, j=j: e.tensor_tensor(
                                out=GG[:, j, nb * 512:(nb + 1) * 512], in0=tmpg[:], in1=gp[:, nb * 512:(nb + 1) * 512],
                                op=ALU.mult), reads=['tmpg', 'gp'], writes=[('GG', j, nb)])
                pv = ps[0][:, 0:32].rearrange("p (n j) -> p n j", j=2)
                allmod = [('modps', n) for n in range(16)]
                for j in range(2):
                    fw.op('dve', lambda e, j=j: e.tensor_tensor(out=B_m[:, j, :], in0=pv[:, 0:8, j],
                                                                in1=bmodc[:, l * 24:l * 24 + 8], op=ALU.add),
                          reads=allmod + ['bmodc'], writes=[('B_m', j)])
                    fw.op('dve', lambda e, j=j: e.tensor_tensor(out=sc_m[:, j, :], in0=pv[:, 8:16, j],
                                                                in1=bmodc[:, l * 24 + 8:l * 24 + 16], op=ALU.add),
                          reads=allmod + ['bmodc'], writes=[('sc_m', j)])
                    fw.op('dve', lambda e, j=j: e.scalar_tensor_tensor(
                        out=A_m[:, j, :], in0=sc_m[:, j, :], scalar=1.0, in1=gprec[:, l * 8:(l + 1) * 8],
                        op0=ALU.add, op1=ALU.mult), reads=[('sc_m', j), 'gprec'], writes=[('A_m', j)])
                fw.barrier()

        def emit_hT(l, hT, st):
            xt = [sb("xt%d" % i, [128, D], F32, st) for i in range(2)]
            xn = [sb("xn%d" % i, [128, D], F32, st) for i in range(2)]
            junk = sb("junk", [128, D], BF16, st)
            ssq = sb("ssq", [128, 2], F32, st)
            rsd = sb("rsd", [128, 2], F32, st)
            acs = sb("acs", [128, 2], F32, st)

            def load(tt):
                b = tt % 2
                if tt < 32:
                    src = (x_in if l == first_layer else out_d)[tt * 128:(tt + 1) * 128, :]
                else:
                    src = (ctx_in if l == first_layer else xc_d)[(tt - 32) * 128:(tt - 31) * 128, :]
                fw.dma('sp', xt[b][:], src, reads=[('X', tt)], writes=[('xt', b)])
            load(0)
            for tt in range(NTT):
                b = tt % 2
                j = 0 if tt < 32 else 1
                if tt + 1 < NTT:
                    load(tt + 1)
                def sqa(e, b=b):
                    return e.activation(out=junk[:], in_=xt[b][:], func=AF.Square, accum_out=ssq[:, b:b + 1])
                fw.op('act', sqa, reads=[('xt', b)], writes=['junk', ('ssq', b)])
                fw.op('act', lambda e, b=b: e.activation(out=rsd[:, b:b + 1], in_=ssq[:, b:b + 1], func=AF.Sqrt,
                                                         scale=1.0 / D, bias=EPS),
                      reads=[('ssq', b)], writes=[('rsd', b)])
                fw.op('dve', lambda e, b=b: e.reciprocal(out=rsd[:, b:b + 1], in_=rsd[:, b:b + 1]),
                      reads=[('rsd', b)], writes=[('rsd', b)])
                fw.op('dve', lambda e, b=b: e.tensor_scalar(out=xn[b][:], in0=xt[b][:], scalar1=rsd[:, b:b + 1],
                                                            scalar2=None, op0=ALU.mult),
                      reads=[('xt', b), ('rsd', b)], writes=[('xn', b)])
                for half in range(2):
                    bank = 2 * b + half

                    def tr(e, b=b, half=half, bank=bank):
                        r = None
                        for q4 in range(4):
                            kc = half * 4 + q4
                            r = e.transpose(out=ps[bank][:, q4 * 128:(q4 + 1) * 128],
                                            in_=xn[b][:, kc * 128:(kc + 1) * 128], identity=ident[:])
                        return r
                    fw.op('pe', tr, reads=[('xn', b), 'ident'], writes=[PSK[bank]])
                    for q4 in range(4):
                        kc = half * 4 + q4
                        fw.op('dve', lambda e, kc=kc, q4=q4, bank=bank, tt=tt, j=j: e.tensor_scalar(
                            out=hT[:, kc, tt * 128:(tt + 1) * 128], in0=ps[bank][:, q4 * 128:(q4 + 1) * 128],
                            scalar1=A_m[:, j, kc:kc + 1], scalar2=B_m[:, j, kc:kc + 1], op0=ALU.mult, op1=ALU.add),
                            reads=[PSK[bank], ('A_m', j), ('B_m', j)], writes=[('hT', kc, tt)])

        TB = [(i * 512, 512) for i in range(8)] + [(SEQ, CTX)]

        def hT_keys(t0, n):
            return [('hT', kc, tt) for kc in range(8) for tt in range(t0 // 128, (t0 + n) // 128)]

        def emit_even(l, last):
            je = l // 2
            w_in = ev_w_in[je, :, :].rearrange("(kc p) f -> p kc f", p=128)
            with ExitStack() as st1:
                hT = sb("hT", [128, 8, T], BF16, st1)
                with ExitStack() as st:
                    emit_hT(l, hT, st)
                    fw.barrier()
                if chk('hT'):
                    return
                with ExitStack() as st:
                    cosT = sb("cosT", [128, SEQ], F32, st)
                    sinT = sb("sinT", [128, SEQ], F32, st)
                    fw.dma('sp', cosT[:], cosT_d[:, :], writes=['cosT'])
                    fw.dma('sp', sinT[:], sinT_d[:, :], writes=['sinT'])
                    wt = [sb("wt%d" % i, [128, 8, 512], BF16, st) for i in range(2)]
                    qb = [sb("qb%d" % i, [128, 512], BF16, st) for i in range(2)]
                    t1 = [sb("t1_%d" % i, [128, 512], F32, st) for i in range(2)]
                    t2 = [sb("t2_%d" % i, [128, 512], F32, st) for i in range(2)]
                    ob = [sb("ob%d" % i, [128, 512], BF16, st) for i in range(2)]
                    gs = [sb("gs%d" % i, [128, 512], F32, st) for i in range(2)]
                    vs = [sb("vs%d" % i, [128, D], BF16, st) for i in range(2)]
                    cnt = [0]
                    wcnt = [0]

                    def load_w(c0):
                        b = wcnt[0] % 2
                        wcnt[0] += 1
                        fw.dma('pool', wt[b][:], w_in[:, :, c0:c0 + 512], writes=[('wt', b)])
                        return b

                    def proj(b, ft, t0, n, bank):
                        def mm(e):
                            r = None
                            for kc in range(8):
                                r = e.matmul(ps[bank][:, 0:n], lhsT=wt[b][:, kc, ft * 128:(ft + 1) * 128],
                                             rhs=hT[:, kc, t0:t0 + n], start=(kc == 0), stop=(kc == 7))
                            return r
                        fw.op('pe', mm, reads=[('wt', b)] + hT_keys(t0, n), writes=[PSK[bank]])

                    plan = [('k', 0), ('k', 512), ('q', 2048), ('q', 2560), ('g', 3072), ('g', 3584)]
                    import os as _os
                    _kinds = _os.environ.get("KINDS")
                    if _kinds is not None:
                        plan = [p for p in plan if p[0] in _kinds]
                    nextb = load_w(plan[0][1])
                    for pi, (kind, c0) in enumerate(plan):
                        b = nextb
                        if pi + 1 < len(plan):
                            nextb = load_w(plan[pi + 1][1])
                        for ft in range(4):
                            hd = ((c0 % 1024) // 128) + ft
                            for (t0, n) in TB:
                                if last and kind in ('q', 'g') and t0 >= SEQ:
                                    continue
                                _rd0 = _os.environ.get("ROPE_DBG", "")
                                if 'ctxonly' in _rd0 and t0 < SEQ:
                                    continue
                                if 'latonly' in _rd0 and t0 >= SEQ:
                                    continue
                                i = cnt[0] % 2
                                cnt[0] += 1
                                bank = i
                                proj(b, ft, t0, n, bank)
                                if kind == 'g':
                                    fw.op('act', lambda e, i=i, n=n, bank=bank: e.activation(
                                        out=gs[i][:, 0:n], in_=ps[bank][:, 0:n], func=AF.Silu),
                                        reads=[PSK[bank]], writes=[('gs', i)])
                                    fw.dma('sp', GA_s[hd, :, t0:t0 + n], gs[i][:, 0:n], reads=[('gs', i)],
                                           writes=[('GA', hd, t0)])
                                    continue
                                dst = KT_s if kind == 'k' else QT_s
                                if t0 >= SEQ or 'norope' in _rd0:
                                    fw.op('act', lambda e, i=i, n=n, bank=bank: e.activation(
                                        out=ob[i][:, 0:n], in_=ps[bank][:, 0:n], func=AF.Copy),
                                        reads=[PSK[bank]], writes=[('ob', i)])
                                else:
                                    fw.op('act', lambda e, i=i, n=n, bank=bank: e.activation(
                                        out=qb[i][:, 0:n], in_=ps[bank][:, 0:n], func=AF.Copy),
                                        reads=[PSK[bank]], writes=[('qb', i)])
                                    if 'cosconst' in _rd0:
                                        fw.op('dve', lambda e, i=i, n=n, bank=bank, t0=t0: e.tensor_tensor(
                                            out=t1[i][:, 0:n], in0=ps[bank][:, 0:n], in1=gs[0][:, 0:n], op=ALU.mult),
                                            reads=[PSK[bank], ('gs', 0)], writes=[('t1', i)])
                                    else:
                                      fw.op('dve', lambda e, i=i, n=n, bank=bank, t0=t0: e.tensor_tensor(
                                        out=t1[i][:, 0:n], in0=ps[bank][:, 0:n], in1=cosT[:, t0:t0 + n], op=ALU.mult),
                                        reads=[PSK[bank], 'cosT', ('qb', i)], writes=[('t1', i)])
                                    _rd = _os.environ.get("ROPE_DBG", "")
                                    if 'nomm' not in _rd:
                                        fw.op('pe', lambda e, i=i, n=n: e.matmul(
                                            ps[2 + i][:, 0:n], lhsT=rm_b[:], rhs=qb[i][:, 0:n], start=True, stop=True),
                                            reads=['rm_b', ('qb', i)], writes=[PSK[2 + i]])
                                    if 'not2' not in _rd:
                                        fw.op('dve', lambda e, i=i, n=n, t0=t0: e.tensor_tensor(
                                            out=t2[i][:, 0:n], in0=ps[2 + i][:, 0:n], in1=sinT[:, t0:t0 + n], op=ALU.mult),
                                            reads=[PSK[2 + i], 'sinT'], writes=[('t2', i)])
                                    else:
                                        fw.op('dve', lambda e, i=i, n=n, t0=t0: e.tensor_tensor(
                                            out=t2[i][:, 0:n], in0=t1[i][:, 0:n], in1=(gs[0][:, 0:n] if 'cosconst' in _rd0 else sinT[:, t0:t0 + n]), op=ALU.mult),
                                            reads=[('t1', i), 'sinT'], writes=[('t2', i)])
                                    if 'actob' in _rd0:
                                        fw.op('dve', lambda e, i=i, n=n: e.tensor_tensor(
                                            out=t1[i][:, 0:n], in0=t1[i][:, 0:n], in1=t2[i][:, 0:n], op=ALU.add),
                                            reads=[('t1', i), ('t2', i)], writes=[('t1', i)])
                                        fw.op('act', lambda e, i=i, n=n, bank=bank: e.activation(
                                            out=ob[i][:, 0:n], in_=ps[bank][:, 0:n], func=AF.Copy),
                                            reads=[PSK[bank]], writes=[('ob', i)])
                                    else:
                                      fw.op(_os.environ.get("ROPE_ADD_ENG", "pool"), lambda e, i=i, n=n: e.tensor_tensor(
                                        out=ob[i][:, 0:n], in0=t1[i][:, 0:n], in1=t2[i][:, 0:n], op=ALU.add),
                                        reads=[('t1', i), ('t2', i)], writes=[('ob', i)])
                                fw.dma('sp', dst[hd, :, t0:t0 + n], ob[i][:, 0:n], reads=[('ob', i)],
                                       writes=[(kind, hd, t0)])
                    bv = [load_w(1024), load_w(1536)]
                    for tt in range(NTT if (_kinds is None or 'v' in _kinds) else 0):
                        i = tt % 2
                        for nb in range(2):
                            bank = 4 + 2 * i + nb

                            def mm(e, tt=tt, nb=nb, bank=bank):
                                r = None
                                for kc in range(8):
                                    r = e.matmul(ps[bank][:, :], lhsT=hT[:, kc, tt * 128:(tt + 1) * 128],
                                                 rhs=wt[bv[nb]][:, kc, :], start=(kc == 0), stop=(kc == 7))
                                return r
                            fw.op('pe', mm, reads=[('wt', bv[nb])] + [('hT', kc, tt) for kc in range(8)],
                                  writes=[PSK[bank]])
                            eng = 'act' if nb == 0 else 'dve'
                            if eng == 'act':
                                fw.op('act', lambda e, i=i, nb=nb, bank=bank: e.activation(
                                    out=vs[i][:, nb * 512:(nb + 1) * 512], in_=ps[bank][:, :], func=AF.Copy),
                                    reads=[PSK[bank]], writes=[('vs', i, nb)])
                            else:
                                fw.op('dve', lambda e, i=i, nb=nb, bank=bank: e.tensor_copy(
                                    out=vs[i][:, nb * 512:(nb + 1) * 512], in_=ps[bank][:, :]),
                                    reads=[PSK[bank]], writes=[('vs', i, nb)])
                        fw.dma('sp', V_s[tt * 128:(tt + 1) * 128, :], vs[i][:], reads=[('vs', i, 0), ('vs', i, 1)],
                               writes=[('V', tt)])
                    fw.barrier()
                if chk('2a'):
                    return
                with ExitStack() as st:
                    wt = [sb("wtb%d" % i, [128, 8, 512], BF16, st) for i in range(2)]
                    vrow = sb("vrow", [128, T + 4], F32, st)
                    bbg = sb("bbg", [128, T], F32, st)
                    acc = sb("acc", [128, T], F32, st)
                    ubr = [sb("ubr0", [128, T], BF16, st)] * 2
                    hsb = [sb("hsb%d" % i, [128, 512], F32, st) for i in range(2)]
                    sg = [sb("sg%d" % i, [128, 512], F32, st) for i in range(2)]
                    fw.op('pool', lambda e: e.memset(vrow[:], 0.0), writes=['vrow_all'])
                    LOFF = 1
                    COFF = SEQ + 3

                    def load_wb(jf, b):
                        for gi, base in enumerate((4096, 5120, 6144, 7168)):
                            fw.dma('pool', wt[b][:, :, gi * 128:(gi + 1) * 128],
                                   w_in[:, :, base + jf * 128:base + (jf + 1) * 128], writes=[('wtb', b, gi)])
                    load_wb(0, 0)
                    cnt = 0
                    for jf in range(8):
                        b = jf % 2
                        if jf + 1 < 8:
                            load_wb(jf + 1, 1 - b)
                        for (t0, n) in TB:
                            if last and t0 >= SEQ:
                                continue
                            i = cnt % 2
                            cnt += 1
                            for gi in range(4):
                                bank = 4 * i + gi

                                def mm(e, gi=gi, bank=bank, t0=t0, n=n, b=b):
                                    r = None
                                    for kc in range(8):
                                        r = e.matmul(ps[bank][:, 0:n], lhsT=wt[b][:, kc, gi * 128:(gi + 1) * 128],
                                                     rhs=hT[:, kc, t0:t0 + n], start=(kc == 0), stop=(kc == 7))
                                    return r
                                fw.op('pe', mm, reads=[('wtb', b, gi)] + hT_keys(t0, n), writes=[PSK[bank]])
                            voff = (LOFF + t0) if t0 < SEQ else (COFF + t0 - SEQ)
                            fw.op('act', lambda e, i=i, n=n: e.activation(out=hsb[i][:, 0:n], in_=ps[4 * i + 0][:, 0:n],
                                                                          func=AF.Copy),
                                  reads=[PSK[4 * i + 0]], writes=[('hsb', i)])
                            fw.op('dve', lambda e, i=i, n=n, voff=voff: e.tensor_tensor(
                                out=vrow[:, voff:voff + n], in0=ps[4 * i + 2][:, 0:n], in1=hsb[i][:, 0:n], op=ALU.mult),
                                reads=[PSK[4 * i + 2], ('hsb', i), 'vrow_all'], writes=[('vrow', t0)])
                            fw.op('act', lambda e, i=i, n=n: e.activation(out=sg[i][:, 0:n], in_=ps[4 * i + 3][:, 0:n],
                                                                          func=AF.Silu),
                                  reads=[PSK[4 * i + 3]], writes=[('sg', i)])
                            fw.op('dve', lambda e, i=i, n=n, t0=t0: e.tensor_tensor(
                                out=bbg[:, t0:t0 + n], in0=ps[4 * i + 1][:, 0:n], in1=sg[i][:, 0:n], op=ALU.mult),
                                reads=[PSK[4 * i + 1], ('sg', i)], writes=[('bbg', t0)])
                        segs = [(LOFF, 0, SEQ)] + ([] if last else [(COFF, SEQ, CTX)])
                        vk = [('vrow', t0) for (t0, n) in TB if not (last and t0 >= SEQ)]
                        bk = [('bbg', t0) for (t0, n) in TB if not (last and t0 >= SEQ)]
                        u = ubr[b]
                        for (vo, o0, n) in segs:
                            wc = lambda tap: convc[:, je * 24 + tap * 8 + jf:je * 24 + tap * 8 + jf + 1]
                            fw.op('dve', lambda e, vo=vo, o0=o0, n=n, wc=wc: e.tensor_scalar(
                                out=acc[:, o0:o0 + n], in0=vrow[:, vo - 1:vo - 1 + n], scalar1=wc(0), scalar2=None,
                                op0=ALU.mult), reads=vk + ['convc', 'vrow_all'], writes=[('acc', o0)])
                            for tap in (1, 2):
                                fw.op('dve', lambda e, vo=vo, o0=o0, n=n, wc=wc, tap=tap: e.scalar_tensor_tensor(
                                    out=acc[:, o0:o0 + n], in0=vrow[:, vo - 1 + tap:vo - 1 + tap + n], scalar=wc(tap),
                                    in1=acc[:, o0:o0 + n], op0=ALU.mult, op1=ALU.add),
                                    reads=vk + [('acc', o0)], writes=[('acc', o0)])
                            fw.op('pool', lambda e, o0=o0, n=n, u=u: e.tensor_tensor(
                                out=u[:, o0:o0 + n], in0=acc[:, o0:o0 + n], in1=bbg[:, o0:o0 + n], op=ALU.mult),
                                reads=[('acc', o0)] + bk, writes=[('ubr', 0, o0)])
                            fw.dma('sp', UT_s[8 + jf, :, o0:o0 + n], u[:, o0:o0 + n], reads=[('ubr', 0, o0)],
                                   writes=[('UT', 8 + jf, o0)])
                    fw.barrier()
                if chk('2b'):
                    return
            with ExitStack() as st:
                KT = sb("KT", [128, 8, T], BF16, st)
                V = sb("V", [128, NTT, D], BF16, st)
                for hd in range(8):
                    fw.dma('sp', KT[:, hd, :], KT_s[hd, :, :], writes=[('KT', hd)])
                Vv = V_s.rearrange("(tt p) f -> p tt f", p=128)
                for g in range(NTT // 2):
                    fw.dma('sp', V[:, 2 * g:2 * g + 2, :], Vv[:, 2 * g:2 * g + 2, :], writes=[('Vt', 2 * g), ('Vt', 2 * g + 1)])
                qt = [sb("qt%d" % i, [128, 512], BF16, st) for i in range(2)]
                ga = [sb("ga%d" % i, [128, 512], F32, st) for i in range(2)]
                pT = [sb("pT%d" % i, [128, 512], BF16, st) for i in range(4)]
                r1 = sb("r1", [128, 512], F32, st)
                a1 = sb("a1", [128, 512], F32, st)
                a2 = sb("a2", [128, 512], F32, st)
                dd = sb("dd", [128, 512], F32, st)
                uo = [sb("uo%d" % i, [128, 512], BF16, st) for i in range(2)]
                work = [(hd, t0, n) for hd in range(8) for (t0, n) in TB if not (last and t0 >= SEQ)]

                def loadq(wi):
                    hd, t0, n = work[wi]
                    i = wi % 2
                    fw.dma('sp', qt[i][:, 0:n], QT_s[hd, :, t0:t0 + n], reads=[('q', hd, t0)], writes=[('qt', i)])

                def loadg(wi):
                    if wi >= len(work):
                        return
                    hd, t0, n = work[wi]
                    i = wi % 2
                    fw.dma('sp', ga[i][:, 0:n], GA_s[hd, :, t0:t0 + n], reads=[('GA', hd, t0)], writes=[('ga', i)])
                loadq(0)
                zacc = [[sb("zacc%d_%d" % (w_, m_), [128, 512], F32, st) for m_ in range(2)] for w_ in range(2)]
                units = []
                for wi, (hd, t0, n) in enumerate(work):
                    kts = list(range(NTT)) if t0 < SEQ else [32, 33]
                    for ki, kt in enumerate(kts):
                        units.append((wi, ki, kt, len(kts)))

                def emit_qk(u):
                    wi, ki, kt, nk = units[u]
                    hd, t0, n = work[wi]
                    i = wi % 2
                    for m in range(2):
                        sbank = 2 * (u % 2) + m
                        fw.op('pe', lambda e, m=m, sbank=sbank: e.matmul(
                            ps[sbank][:, 0:n], lhsT=KT[m * 64:(m + 1) * 64, hd, kt * 128:(kt + 1) * 128],
                            rhs=qt[i][m * 64:(m + 1) * 64, 0:n], start=True, stop=True),
                            reads=[('KT', hd), ('qt', i)], writes=[PSK[sbank]])

                def emit_rest(u):
                    wi, ki, kt, nk = units[u]
                    hd, t0, n = work[wi]
                    wz = wi % 2
                    for m in range(2):
                        sbank = 2 * (u % 2) + m
                        fw.op('act', lambda e, sbank=sbank: e.activation(
                            out=pT[sbank][:, 0:n], in_=ps[sbank][:, 0:n], func=AF.Exp, scale=0.125),
                            reads=[PSK[sbank]], writes=[('pT', sbank)])
                    if u + 1 < len(units):
                        wj = units[u + 1][0]
                        if wj != wi and wj + 1 < len(work):
                            loadq(wj + 1)
                        emit_qk(u + 1)
                    first = (ki == 0)
                    lastk = (ki == nk - 1)
                    for m in range(2):
                        pi = 2 * (u % 2) + m
                        bo = 4 + 2 * m
                        fw.op('pe', lambda e, pi=pi, bo=bo: e.matmul(
                            ps[bo][:, 0:n], lhsT=V[:, kt, hd * 128:(hd + 1) * 128], rhs=pT[pi][:, 0:n],
                            start=first, stop=lastk), reads=[('Vt', kt), ('pT', pi)], writes=[PSK[bo]])
                        zeng = 'dve'
                        if m == 1:
                            fw.op('pe', lambda e, pi=pi: e.matmul(
                                ps[7][:, 0:n], lhsT=ones_b[:], rhs=pT[pi][:, 0:n], start=first, stop=lastk),
                                reads=['ones_b', ('pT', pi)], writes=[PSK[7]])
                        elif first:
                            fw.op(zeng, lambda e, pi=pi, m=m: e.tensor_copy(out=zacc[wz][m][:, 0:n], in_=pT[pi][:, 0:n]),
                                  reads=[('pT', pi)], writes=[('zacc', wz, m)])
                        else:
                            fw.op(zeng, lambda e, pi=pi, m=m: e.tensor_tensor(
                                out=zacc[wz][m][:, 0:n], in0=zacc[wz][m][:, 0:n], in1=pT[pi][:, 0:n], op=ALU.add),
                                reads=[('pT', pi), ('zacc', wz, m)], writes=[('zacc', wz, m)])

                if len(work) > 1:
                    loadq(1)
                loadg(0)
                loadg(1)
                emit_qk(0)
                for u in range(len(units)):
                    emit_rest(u)
                    wi, ki, kt, nk = units[u]
                    if ki != nk - 1:
                        continue
                    hd, t0, n = work[wi]
                    i = wi % 2
                    wz = wi % 2
                    for m in range(1):
                        fw.op('pe', lambda e, m=m: e.matmul(ps[5 + 2 * m][:, 0:n], lhsT=ones_f[:], rhs=zacc[wz][m][:, 0:n],
                                                            start=True, stop=True),
                              reads=['ones_f', ('zacc', wz, m)], writes=[PSK[5 + 2 * m]])
                    fw.op('dve', lambda e, n=n: e.reciprocal(out=r1[:, 0:n], in_=ps[5][:, 0:n]),
                          reads=[PSK[5]], writes=['r1'])
                    fw.op('dve', lambda e, n=n: e.tensor_tensor(out=a1[:, 0:n], in0=ps[4][:, 0:n], in1=r1[:, 0:n],
                                                                op=ALU.mult), reads=[PSK[4], 'r1'], writes=['a1'])
                    fw.op('dve', lambda e, n=n: e.reciprocal(out=r1[:, 0:n], in_=ps[7][:, 0:n]),
                          reads=[PSK[7]], writes=['r1'])
                    fw.op('dve', lambda e, n=n: e.tensor_tensor(out=a2[:, 0:n], in0=ps[6][:, 0:n], in1=r1[:, 0:n],
                                                                op=ALU.mult), reads=[PSK[6], 'r1'], writes=['a2'])
                    fw.op('dve', lambda e, n=n: e.scalar_tensor_tensor(
                        out=dd[:, 0:n], in0=a2[:, 0:n], scalar=lam_c[:, je:je + 1], in1=a1[:, 0:n],
                        op0=ALU.mult, op1=ALU.add), reads=['a1', 'a2', ('lam_c', je)], writes=['dd'])
                    fw.op('act', lambda e, n=n: e.activation(out=a2[:, 0:n], in_=dd[:, 0:n], func=AF.Square),
                          reads=['dd'], writes=['a2'])
                    fw.op('pe', lambda e, n=n: e.matmul(ps[4][:, 0:n], lhsT=ones_f[:], rhs=a2[:, 0:n],
                                                        start=True, stop=True),
                          reads=['ones_f', 'a2'], writes=[PSK[4]])
                    fw.op('act', lambda e, n=n: e.activation(out=a1[:, 0:n], in_=ps[4][:, 0:n], func=AF.Sqrt,
                                                             scale=1.0 / 128, bias=EPS),
                          reads=[PSK[4]], writes=['a1'])
                    fw.op('dve', lambda e, n=n: e.reciprocal(out=a1[:, 0:n], in_=a1[:, 0:n]), reads=['a1'], writes=['a1'])
                    fw.op('dve', lambda e, n=n: e.scalar_tensor_tensor(
                        out=dd[:, 0:n], in0=dd[:, 0:n], scalar=sublc[:, je:je + 1], in1=a1[:, 0:n],
                        op0=ALU.mult, op1=ALU.mult), reads=['dd', 'a1', 'sublc'], writes=['dd'])
                    fw.op('dve', lambda e, n=n, i=i: e.tensor_tensor(out=uo[i][:, 0:n], in0=dd[:, 0:n], in1=ga[i][:, 0:n],
                                                                     op=ALU.mult),
                          reads=['dd', ('ga', i)], writes=[('uo', i)])
                    fw.dma('sp', UT_s[hd, :, t0:t0 + n], uo[i][:, 0:n], reads=[('uo', i)], writes=[('UT', hd, t0)])
                    loadg(wi + 2)
                fw.barrier()

        def emit_odd(l, last):
            jo = l // 2
            w_in = od_w_in[jo, :, :].rearrange("(kc p) f -> p kc f", p=128)
            C_CK, C_CV, C_DK, C_DV, C_LR, C_CQ, C_DQ, C_CG, C_DG = 0, 1024, 2048, 2560, 3584, 3616, 4640, 5152, 6176
            ps7b = ps[7].bitcast(BF16)
            with ExitStack() as stL:
                lrT = sb("lrT", [16, 2, T], BF16, stL)
                gub = sb("gub", [16, 2, 512], BF16, stL)
                fw.dma('pool', gub[:], gu_d[jo, :, :, :].rearrange("d r c -> r d c"), writes=['gub'])
                with ExitStack() as st1:
                    hT = sb("hT", [128, 8, T], BF16, st1)
                    with ExitStack() as st:
                        emit_hT(l, hT, st)
                        fw.barrier()
                    if chk('hT'):
                        return
                    with ExitStack() as st:
                        wt = [sb("wto%d" % i, [128, 8, 512], BF16, st) for i in range(4)]
                        wlr = sb("wlr", [128, 8, 32], BF16, st)
                        ob = [sb("oob%d" % i, [128, 512], BF16, st) for i in range(2)]
                        gs = [sb("ogs%d" % i, [128, 512], F32, st) for i in range(2)]
                        vsn = [sb("vsn%d" % i, [128, 16, 65], BF16, st) for i in range(2)]
                        vs = [sb("ovs%d" % i, [128, D], BF16, st) for i in range(2)]
                        gst = [sb("gst%d" % i, [128, D], F32, st) for i in range(2)]
                        for i in range(2):
                            fw.op('pool', lambda e, i=i: e.memset(vsn[i][:], 1.0), writes=[('vsn', i, 0), ('vsn', i, 1)])
                        wcnt = [0]

                        def load_w(c0, ncol=512):
                            b = wcnt[0] % 4
                            wcnt[0] += 1
                            fw.dma('pool', wt[b][:, :, 0:ncol], w_in[:, :, c0:c0 + ncol], writes=[('wt', b)])
                            return b
                        fw.dma('pool', wlr[:], w_in[:, :, C_LR:C_LR + 32], writes=['wlr'])
                        cnt = 0
                        for dr_ in range(2):
                            for (t0, n) in TB:
                                bank = cnt % 2
                                cnt += 1

                                def mm(e, dr_=dr_, t0=t0, n=n, bank=bank):
                                    r = None
                                    for kc in range(8):
                                        r = e.matmul(ps[bank][0:16, 0:n], lhsT=wlr[:, kc, dr_ * 16:(dr_ + 1) * 16],
                                                     rhs=hT[:, kc, t0:t0 + n], start=(kc == 0), stop=(kc == 7))
                                    return r
                                fw.op('pe', mm, reads=['wlr'] + hT_keys(t0, n), writes=[PSK[bank]])
                                fw.op('act', lambda e, dr_=dr_, t0=t0, n=n, bank=bank: e.activation(
                                    out=lrT[0:16, dr_, t0:t0 + n], in_=ps[bank][0:16, 0:n], func=AF.Copy),
                                    reads=[PSK[bank]], writes=[('lrT', dr_, t0)])
                        plan = [('ck', C_CK, 0), ('ck', C_CK + 512, 4), ('cq', C_CQ, 0), ('cq', C_CQ + 512, 4),
                                ('dk', C_DK, 0), ('dq', C_DQ, 0)]
                        nextb = load_w(plan[0][1])
                        cnt = 0
                        for pi, (kind, c0, f0) in enumerate(plan):
                            b = nextb
                            if pi + 1 < len(plan):
                                nextb = load_w(plan[pi + 1][1])
                            for ft in range(4):
                                fidx = f0 + ft
                                for (t0, n) in TB:
                                    if last and kind in ('cq', 'dq') and t0 >= SEQ:
                                        continue
                                    i = cnt % 2
                                    cnt += 1
                                    bank = i

                                    def mm(e, b=b, ft=ft, t0=t0, n=n, bank=bank):
                                        r = None
                                        for kc in range(8):
                                            r = e.matmul(ps[bank][:, 0:n], lhsT=wt[b][:, kc, ft * 128:(ft + 1) * 128],
                                                         rhs=hT[:, kc, t0:t0 + n], start=(kc == 0), stop=(kc == 7))
                                        return r
                                    fw.op('pe', mm, reads=[('wt', b)] + hT_keys(t0, n), writes=[PSK[bank]])
                                    if kind in ('ck', 'cq'):
                                        sc = 1.0 if kind == 'ck' else 0.125
                                        dst = KT_s if kind == 'ck' else QT_s
                                        fw.op('act', lambda e, i=i, n=n, bank=bank, sc=sc: e.activation(
                                            out=ob[i][:, 0:n], in_=ps[bank][:, 0:n], func=AF.Copy, scale=sc),
                                            reads=[PSK[bank]], writes=[('ob', i)])
                                        fw.dma('sp', dst[fidx, :, t0:t0 + n], ob[i][:, 0:n], reads=[('ob', i)],
                                               writes=[(kind, fidx, t0)])
                                    else:
                                        sc = 1.0 if kind == 'dk' else (128.0 ** -0.5)
                                        gi = fidx if kind == 'dk' else 4 + fidx
                                        fw.op('act', lambda e, i=i, n=n, bank=bank, sc=sc: e.activation(
                                            out=gs[i][:, 0:n], in_=ps[bank][:, 0:n], func=AF.Copy, scale=sc),
                                            reads=[PSK[bank]], writes=[('gs', i)])
                                        fw.dma('sp', GA_s[gi, :, t0:t0 + n], gs[i][:, 0:n], reads=[('gs', i)],
                                               writes=[('GA', gi, t0)])
                        for (kind, c0) in (('cv', C_CV), ('dv', C_DV), ('cg', C_CG), ('dg', C_DG)):
                            bv = [load_w(c0), load_w(c0 + 512)]
                            for tt in range(NTT):
                                if last and kind in ('cg', 'dg') and tt >= 32:
                                    continue
                                i = tt % 2
                                for nb in range(2):
                                    bank = 2 + 2 * i + nb

                                    def mm(e, tt=tt, nb=nb, bank=bank, bv=bv):
                                        r = None
                                        for kc in range(8):
                                            r = e.matmul(ps[bank][:, :], lhsT=hT[:, kc, tt * 128:(tt + 1) * 128],
                                                         rhs=wt[bv[nb]][:, kc, :], start=(kc == 0), stop=(kc == 7))
                                        return r
                                    fw.op('pe', mm, reads=[('wt', bv[nb])] + [('hT', kc, tt) for kc in range(8)],
                                          writes=[PSK[bank]])
                                    if kind == 'cv':
                                        fw.op('act', lambda e, i=i, nb=nb, bank=bank: e.activation(
                                            out=vsn[i][:, nb * 8:(nb + 1) * 8, 0:64],
                                            in_=ps[bank][:, :].rearrange("p (h d) -> p h d", d=64), func=AF.Copy),
                                            reads=[PSK[bank]], writes=[('vsn', i, nb)])
                                    elif kind == 'dv':
                                        fw.op('act', lambda e, i=i, nb=nb, bank=bank: e.activation(
                                            out=vs[i][:, nb * 512:(nb + 1) * 512], in_=ps[bank][:, :], func=AF.Copy),
                                            reads=[PSK[bank]], writes=[('vs', i, nb)])
                                    else:
                                        fw.op('act', lambda e, i=i, nb=nb, bank=bank: e.activation(
                                            out=gst[i][:, nb * 512:(nb + 1) * 512], in_=ps[bank][:, :], func=AF.Silu),
                                            reads=[PSK[bank]], writes=[('gst', i, nb)])
                                rows = slice(tt * 128, (tt + 1) * 128)
                                if kind == 'cv':
                                    fw.dma('sp', VN_s[rows, :], vsn[i][:].rearrange("p h d -> p (h d)"),
                                           reads=[('vsn', i, 0), ('vsn', i, 1)], writes=[('VN', tt)])
                                elif kind == 'dv':
                                    fw.dma('sp', V_s[rows, :], vs[i][:], reads=[('vs', i, 0), ('vs', i, 1)], writes=[('DV', tt)])
                                else:
                                    fw.dma('sp', (CG_s if kind == 'cg' else DG_s)[rows, :], gst[i][:],
                                           reads=[('gst', i, 0), ('gst', i, 1)], writes=[(kind, tt)])
                        fw.barrier()
                if chk('o2'):
                    return
                order_f = [64, 65, 66, 67] + list(range(64))
                order_b = [67, 66, 65, 64] + list(range(63, -1, -1))
                NCH = T // 64
                for hp in range(2):
                    with ExitStack() as st:
                        chains = [(d_, 2 * hp + hl) for d_ in range(2) for hl in range(2)]
                        qtl = [sb("qtl%d" % ci, [128, T], BF16, st) for ci in range(4)]
                        ktl = [sb("ktl%d" % ci, [128, T], BF16, st) for ci in range(4)]
                        Et = [sb("Et%d" % ci, [128, NCH], F32, st) for ci in range(4)]
                        S = [sb("S%d" % ci, [128, 256], F32, st) for ci in range(4)]
                        Sb = [sb("Sb%d" % ci, [128, 256], BF16, st) for ci in range(4)]
                        qf = [sb("qf%d" % i, [128, 512], F32, st) for i in range(2)]
                        kf = [sb("kf%d" % i, [128, 512], F32, st) for i in range(2)]
                        e1 = [sb("e1_%d" % i, [128, 512], F32, st) for i in range(2)]
                        spt = [sb("spt%d" % i, [128, 512], F32, st) for i in range(2)]
                        pit = [sb("pit%d" % i, [128, 512], F32, st) for i in range(2)]
                        eq = [sb("eq%d" % i, [128, 512], F32, st) for i in range(2)]
                        ek = [sb("ek%d" % i, [128, 512], F32, st) for i in range(2)]
                        attb = [sb("attb%d" % ci, [64, 64], BF16, st) for ci in range(4)]
                        ktm = [sb("ktm%d" % ci, [64, 128], BF16, st) for ci in range(4)]
                        vch = [[sb("vch%d_%d" % (d_, i), [64, 512], BF16, st) for i in range(2)] for d_ in range(2)]
                        ost = [[sb("ost%d_%d" % (d_, i), [64, 512], F32, st) for i in range(2)] for d_ in range(2)]
                        cnt = 0
                        for ci, (d_, hh) in enumerate(chains):
                            for (t0, n) in TB:
                                i = cnt % 2
                                cnt += 1
                                nch = n // 64
                                c0 = t0 // 64
                                fw.dma('sp', qf[i][:, 0:n], GA_s[4 + hh, :, t0:t0 + n], reads=[('GA', 4 + hh, t0)],
                                       writes=[('qf', i)])
                                fw.dma('sp', kf[i][:, 0:n], GA_s[hh, :, t0:t0 + n], reads=[('GA', hh, t0)],
                                       writes=[('kf', i)])
                                fw.op('pe', lambda e, d_=d_, hh=hh, t0=t0, n=n: e.matmul(
                                    ps[6][:, 0:n], lhsT=gub[0:16, d_, hh * 128:(hh + 1) * 128], rhs=lrT[0:16, d_, t0:t0 + n],
                                    start=True, stop=True), reads=['gub', ('lrT', d_, t0)], writes=[PSK[6]])
                                gcol = jo * 8 + d_ * 4 + hh
                                fw.op('act', lambda e, i=i, n=n, gcol=gcol: e.activation(
                                    out=e1[i][:, 0:n], in_=ps[6][:, 0:n], func=AF.Exp, scale=-1.0, bias=gbc[:, gcol:gcol + 1]),
                                    reads=[PSK[6], 'gbc'], writes=[('e1', i)])
                                fw.op('act', lambda e, i=i, n=n: e.activation(
                                    out=spt[i][:, 0:n], in_=e1[i][:, 0:n], func=AF.Ln, bias=1.0),
                                    reads=[('e1', i)], writes=[('spt', i)])
                                fw.op('dve', lambda e, i=i, n=n: e.tensor_tensor_scan(
                                    out=pit[i][:, 0:n], data0=cmask[:, 0:n], data1=spt[i][:, 0:n], initial=0.0,
                                    op0=ALU.mult, op1=ALU.add), reads=['cmask', ('spt', i)], writes=[('pit', i)])
                                pv3 = pit[i][:, 0:n].rearrange("p (c s) -> p c s", s=64)
                                fw.op('act', lambda e, ci=ci, c0=c0, nch=nch, pv3=pv3: e.activation(
                                    out=Et[ci][:, c0:c0 + nch], in_=pv3[:, :, 63], func=AF.Exp, scale=-1.0 / 16),
                                    reads=[('pit', i)], writes=[('Et', ci, t0)])
                                if d_ == 1:
                                    fw.op('dve', lambda e, i=i, n=n: e.tensor_tensor(
                                        out=pit[i][:, 0:n], in0=pit[i][:, 0:n], in1=spt[i][:, 0:n], op=ALU.subtract),
                                        reads=[('pit', i), ('spt', i)], writes=[('pit', i)])
                                sq_ = (-1.0 / 16) if d_ == 0 else (1.0 / 16)
                                fw.op('act', lambda e, i=i, n=n, sq_=sq_: e.activation(
                                    out=eq[i][:, 0:n], in_=pit[i][:, 0:n], func=AF.Exp, scale=sq_),
                                    reads=[('pit', i)], writes=[('eq', i)])
                                fw.op('act', lambda e, i=i, n=n, sq_=sq_: e.activation(
                                    out=ek[i][:, 0:n], in_=pit[i][:, 0:n], func=AF.Exp, scale=-sq_),
                                    reads=[('pit', i)], writes=[('ek', i)])
                                fw.op('dve', lambda e, i=i, n=n, ci=ci, t0=t0: e.tensor_tensor(
                                    out=qtl[ci][:, t0:t0 + n], in0=qf[i][:, 0:n], in1=eq[i][:, 0:n], op=ALU.mult),
                                    reads=[('qf', i), ('eq', i)], writes=[('qtl', ci, t0)])
                                fw.op('pool', lambda e, i=i, n=n, ci=ci, t0=t0: e.tensor_tensor(
                                    out=ktl[ci][:, t0:t0 + n], in0=kf[i][:, 0:n], in1=ek[i][:, 0:n], op=ALU.mult),
                                    reads=[('kf', i), ('ek', i)], writes=[('ktl', ci, t0)])
                        for ci in range(4):
                            fw.op('dve', lambda e, ci=ci: e.memset(S[ci][:], 0.0), writes=[('S', ci)])
                            fw.op('pool', lambda e, ci=ci: e.memset(Sb[ci][:], 0.0), writes=[('Sb', ci)])
                        for step in range(NCH):
                            sb_i = step % 2
                            for d_ in range(2):
                                c = (order_f if d_ == 0 else order_b)[step]
                                tk = c * 64
                                blk = (tk // 512) * 512 if tk < SEQ else SEQ
                                want = not (last and c >= 64)
                                fw.dma('sp', vch[d_][sb_i][:], V_s[tk:tk + 64, hp * 512:(hp + 1) * 512],
                                       reads=[('DV', tk // 128)], writes=[('vch', d_, sb_i)])
                                for hl in range(2):
                                    ci = d_ * 2 + hl
                                    vv = vch[d_][sb_i][:, hl * 256:(hl + 1) * 256]
                                    kq_reads = [('qtl', ci, blk), ('ktl', ci, blk)]
                                    if d_ == 1:
                                        fw.op('dve', lambda e, ci=ci, c=c: e.tensor_scalar(
                                            out=S[ci][:], in0=S[ci][:], scalar1=Et[ci][:, c:c + 1], scalar2=None, op0=ALU.mult),
                                            reads=[('S', ci), ('Et', ci, blk)], writes=[('S', ci)])
                                        fw.op('pool', lambda e, ci=ci: e.tensor_copy(out=Sb[ci][:], in_=S[ci][:]),
                                              reads=[('S', ci)], writes=[('Sb', ci)])
                                    if want:
                                        fw.op('pe', lambda e, ci=ci, tk=tk: e.matmul(
                                            ps[ci][0:64, 0:64], lhsT=ktl[ci][:, tk:tk + 64], rhs=qtl[ci][:, tk:tk + 64],
                                            start=True, stop=True), reads=kq_reads, writes=[('psa', ci)])
                                        fw.op('dve', lambda e, ci=ci, d_=d_: e.tensor_tensor(
                                            out=attb[ci][:], in0=ps[ci][0:64, 0:64], in1=tri[:, d_ * 64:(d_ + 1) * 64], op=ALU.mult),
                                            reads=[('psa', ci), 'tri'], writes=[('attb', ci)])

                                        def omm(e, ci=ci, tk=tk, vv=vv):
                                            e.matmul(ps[ci][0:64, 128:384], lhsT=qtl[ci][:, tk:tk + 64], rhs=Sb[ci][:],
                                                     start=True, stop=False)
                                            return e.matmul(ps[ci][0:64, 128:384], lhsT=attb[ci][:], rhs=vv,
                                                            start=False, stop=True)
                                        fw.op('pe', omm, reads=kq_reads + [('Sb', ci), ('attb', ci), ('vch', d_, sb_i)],
                                              writes=[('pso', ci)])
                                        fw.op('dve', lambda e, ci=ci, d_=d_, hl=hl, sb_i=sb_i: e.tensor_copy(
                                            out=ost[d_][sb_i][:, hl * 256:(hl + 1) * 256], in_=ps[ci][0:64, 128:384]),
                                            reads=[('pso', ci)], writes=[('ost', d_, sb_i, hl)])
                                    fw.op('pe', lambda e, ci=ci, tk=tk: e.transpose(
                                        out=ps7b[0:64, ci * 128:(ci + 1) * 128], in_=ktl[ci][:, tk:tk + 64], identity=identb[:]),
                                        reads=[('ktl', ci, blk), 'identb'], writes=[('ps7', ci)])
                                    fw.op('act', lambda e, ci=ci: e.activation(
                                        out=ktm[ci][:], in_=ps7b[0:64, ci * 128:(ci + 1) * 128], func=AF.Copy),
                                        reads=[('ps7', ci)], writes=[('ktm', ci)])
                                    kvb = 4 + ci // 2
                                    kvc = (ci % 2) * 256
                                    fw.op('pe', lambda e, ci=ci, vv=vv, kvb=kvb, kvc=kvc: e.matmul(
                                        ps[kvb][:, kvc:kvc + 256], lhsT=ktm[ci][:], rhs=vv, start=True, stop=True),
                                        reads=[('ktm', ci), ('vch', d_, sb_i)], writes=[('pskv', ci)])
                                    fw.op('dve', lambda e, ci=ci, kvb=kvb, kvc=kvc: e.tensor_tensor(
                                        out=S[ci][:], in0=ps[kvb][:, kvc:kvc + 256], in1=S[ci][:], op=ALU.add),
                                        reads=[('pskv', ci), ('S', ci)], writes=[('S', ci)])
                                    if d_ == 0:
                                        fw.op('dve', lambda e, ci=ci, c=c: e.tensor_scalar(
                                            out=S[ci][:], in0=S[ci][:], scalar1=Et[ci][:, c:c + 1], scalar2=None, op0=ALU.mult),
                                            reads=[('S', ci), ('Et', ci, blk)], writes=[('S', ci)])
                                        fw.op('pool', lambda e, ci=ci: e.tensor_copy(out=Sb[ci][:], in_=S[ci][:]),
                                              reads=[('S', ci)], writes=[('Sb', ci)])
                                if want:
                                    dst = (OF_s if d_ == 0 else OB_s)[tk:tk + 64, hp * 512:(hp + 1) * 512]
                                    fw.dma('sp', dst, ost[d_][sb_i][:], reads=[('ost', d_, sb_i, 0), ('ost', d_, sb_i, 1)],
                                           writes=[('O', d_, c, hp)])
                        fw.barrier()
                if chk('o3'):
                    return
            with ExitStack() as st:
                gnb = sb("gnb", [128, 256], F32, st)
                fw.dma('sp', gnb[:], gnorm_b[jo, :, :], writes=['gnb'])
                oft = [sb("oft%d" % i, [128, D], F32, st) for i in range(2)]
                obt = [sb("obt%d" % i, [128, D], F32, st) for i in range(2)]
                dgt = [sb("dgt%d" % i, [128, D], F32, st) for i in range(2)]
                junk = sb("junk3", [128, 256], BF16, st)
                ssq = sb("ssq3", [128, 8], F32, st)
                ugb = [sb("ugb%d" % i, [128, 8, 128], BF16, st) for i in range(2)]
                ntt = 32 if last else NTT

                def loadc(tt):
                    i = tt % 2
                    rows = slice(tt * 128, (tt + 1) * 128)
                    fw.dma('sp', oft[i][:], OF_s[rows, :], writes=[('oft', i)])
                    fw.dma('sp', obt[i][:], OB_s[rows, :], writes=[('obt', i)])
                    fw.dma('sp', dgt[i][:], DG_s[rows, :], writes=[('dgt', i)])
                loadc(0)
                for tt in range(ntt):
                    i = tt % 2
                    if tt + 1 < ntt:
                        loadc(tt + 1)
                    fw.op('pool', lambda e, i=i: e.tensor_tensor(out=oft[i][:], in0=oft[i][:], in1=obt[i][:], op=ALU.add),
                          reads=[('oft', i), ('obt', i)], writes=[('oft', i)])
                    for hh in range(4):
                        fw.op('act', lambda e, i=i, hh=hh: e.activation(
                            out=junk[:], in_=oft[i][:, hh * 256:(hh + 1) * 256], func=AF.Square,
                            accum_out=ssq[:, i * 4 + hh:i * 4 + hh + 1]), reads=[('oft', i)], writes=['junk3', ('ssq3', i, hh)])
                    fw.op('act', lambda e, i=i: e.activation(out=ssq[:, i * 4:i * 4 + 4], in_=ssq[:, i * 4:i * 4 + 4],
                                                             func=AF.Sqrt, scale=1.0 / 256, bias=EPS),
                          reads=[('ssq3', i, hh) for hh in range(4)], writes=[('rs3', i)])
                    fw.op('dve', lambda e, i=i: e.reciprocal(out=ssq[:, i * 4:i * 4 + 4], in_=ssq[:, i * 4:i * 4 + 4]),
                          reads=[('rs3', i)], writes=[('rs3', i)])
                    for hh in range(4):
                        fw.op('dve', lambda e, i=i, hh=hh: e.scalar_tensor_tensor(
                            out=oft[i][:, hh * 256:(hh + 1) * 256], in0=oft[i][:, hh * 256:(hh + 1) * 256],
                            scalar=ssq[:, i * 4 + hh:i * 4 + hh + 1], in1=gnb[:], op0=ALU.mult, op1=ALU.mult),
                            reads=[('oft', i), ('rs3', i), 'gnb'], writes=[('oft', i)])
                    fw.op('pool', lambda e, i=i: e.tensor_tensor(out=oft[i][:], in0=oft[i][:], in1=dgt[i][:], op=ALU.mult),
                          reads=[('oft', i), ('dgt', i)], writes=[('oft', i)])
                    for half in range(2):
                        bank = 2 * i + half

                        def tr(e, i=i, half=half, bank=bank):
                            r = None
                            for q4 in range(4):
                                fc = half * 4 + q4
                                r = e.transpose(out=ps[bank][:, q4 * 128:(q4 + 1) * 128],
                                                in_=oft[i][:, fc * 128:(fc + 1) * 128], identity=ident[:])
                            return r
                        fw.op('pe', tr, reads=[('oft', i), 'ident'], writes=[PSK[bank]])
                        fw.op('dve', lambda e, i=i, half=half, bank=bank: e.tensor_copy(
                            out=ugb[i][:, half * 4:(half + 1) * 4, :],
                            in_=ps[bank][:, :].rearrange("p (f t) -> p f t", t=128)),
                            reads=[PSK[bank]], writes=[('ugb', i, half)])
                    fw.dma('sp', UT_s[8:16, :, tt * 128:(tt + 1) * 128].rearrange("f p t -> p f t"), ugb[i][:],
                           reads=[('ugb', i, 0), ('ugb', i, 1)], writes=[('UTg', tt)])
                fw.barrier()
            if chk('o3b'):
                return
            for hf in range(2):
                with ExitStack() as st:
                    KN = sb("KN", [128, 4, T], BF16, st)
                    VN = sb("VN", [128, NTT, 8 * 65], BF16, st)
                    TTt = sb("TTt", [128, 8, 16 * 64], BF16, st)
                    for j in range(4):
                        fw.dma('sp', KN[:, j, :], KT_s[hf * 4 + j, :, :], writes=[('KN', j)])
                    VNv = VN_s.rearrange("(tt p) f -> p tt f", p=128)
                    for g in range(NTT // 2):
                        fw.dma('sp', VN[:, 2 * g:2 * g + 2, :], VNv[:, 2 * g:2 * g + 2, hf * 520:(hf + 1) * 520],
                               writes=[('VNt', 2 * g), ('VNt', 2 * g + 1)])
                    for hl in range(8):
                        fw.dma('pool', TTt[:, hl, :], tt_d[jo, hf * 8 + hl, :, :], writes=[('TT', hl)])
                    qn = [sb("qn%d" % i, [128, 4, 512], BF16, st) for i in range(2)]
                    cgt = [sb("cgt%d" % i, [64, 512], F32, st) for i in range(2)]
                    pT = [sb("pTn%d" % i, [128, 448], BF16, st) for i in range(3)]
                    un = [sb("un%d" % i, [64, 512], F32, st) for i in range(2)]
                    rz = sb("rz", [64, 16], F32, st)
                    ust = [sb("ust%d" % i, [128, 4, 512], BF16, st) for i in range(2)]
                    rblocks = list(range(8)) + ([] if last else [8])
                    QTv = QT_s.rearrange("f p t -> p f t")

                    def loadq(bi):
                        rb = rblocks[bi]
                        i = bi % 2
                        n = 512 if rb < 8 else CTX
                        fw.dma('sp', qn[i][:, :, 0:n], QTv[:, hf * 4:(hf + 1) * 4, rb * 512:rb * 512 + n], writes=[('qn', i)])
                    loadq(0)
                    pcnt = 0
                    rcnt = 0
                    for bi, rb in enumerate(rblocks):
                        qi = bi % 2
                        if bi + 1 < len(rblocks):
                            loadq(bi + 1)
                        nrows = 8 if rb < 8 else 4
                        for rr in range(nrows):
                            ri = rcnt % 2
                            rcnt += 1
                            tok0 = rb * 512 + rr * 64
                            fw.dma('sp', cgt[ri][:], CG_s[tok0:tok0 + 64, hf * 512:(hf + 1) * 512], writes=[('cgt', ri)])
                            if rb < 8:
                                r = rb * 8 + rr
                                rs_ = min(max(r - 4, 0), 56)
                                if rs_ % 2 == 0:
                                    tiles = [(rs_ // 2 + k, 2 * (rs_ // 2 + k) - r + 7) for k in range(4)]
                                else:
                                    a0 = (rs_ - 1) // 2
                                    tiles = [(a0, 14)] + [(a0 + k, 2 * (a0 + k) - r + 7) for k in (1, 2, 3)] + [(a0 + 4, 15)]
                            else:
                                tiles = []
                            alltiles = tiles + [(32, None), (33, None)]
                            ntl = len(alltiles)
                            for hl in range(8):
                                j = hl // 2
                                p0 = (hl % 2) * 64
                                sbank = pcnt % 3
                                obank = 3 + pcnt % 3
                                pi_ = pcnt % 3
                                pcnt += 1

                                def smm(e, j=j, p0=p0, sbank=sbank, alltiles=alltiles, qi=qi, rr=rr, hl=hl):
                                    r_ = None
                                    for k, (a, ti) in enumerate(alltiles):
                                        r_ = e.matmul(ps[sbank][:, k * 64:(k + 1) * 64],
                                                      lhsT=KN[p0:p0 + 64, j, a * 128:(a + 1) * 128],
                                                      rhs=qn[qi][p0:p0 + 64, j, rr * 64:(rr + 1) * 64],
                                                      start=True, stop=(ti is None))
                                        if ti is not None:
                                            r_ = e.matmul(ps[sbank][:, k * 64:(k + 1) * 64], lhsT=identb[:],
                                                          rhs=TTt[:, hl, ti * 64:(ti + 1) * 64], start=False, stop=True)
                                    return r_
                                fw.op('pe', smm, reads=[('KN', j), ('qn', qi), ('TT', hl), 'identb'], writes=[PSK[sbank]])
                                fw.op('act', lambda e, pi_=pi_, sbank=sbank, ntl=ntl: e.activation(
                                    out=pT[pi_][:, 0:ntl * 64], in_=ps[sbank][:, 0:ntl * 64], func=AF.Exp),
                                    reads=[PSK[sbank]], writes=[('pTn', pi_)])

                                def pvm(e, pi_=pi_, obank=obank, alltiles=alltiles, hl=hl):
                                    r_ = None
                                    for k, (a, ti) in enumerate(alltiles):
                                        r_ = e.matmul(ps[obank][0:64, 0:65], lhsT=pT[pi_][:, k * 64:(k + 1) * 64],
                                                      rhs=VN[:, a, hl * 65:(hl + 1) * 65], start=(k == 0),
                                                      stop=(k == len(alltiles) - 1))
                                    return r_
                                fw.op('pe', pvm, reads=[('pTn', pi_)] + [('VNt', a) for (a, ti) in alltiles], writes=[PSK[obank]])
                                fw.op('dve', lambda e, obank=obank, hl=hl, ri=ri: e.reciprocal(
                                    out=rz[:, ri * 8 + hl:ri * 8 + hl + 1], in_=ps[obank][0:64, 64:65]),
                                    reads=[PSK[obank]], writes=[('rz', ri, hl)])
                                fw.op('dve', lambda e, obank=obank, hl=hl, ri=ri: e.tensor_scalar(
                                    out=un[ri][:, hl * 64:(hl + 1) * 64], in0=ps[obank][0:64, 0:64],
                                    scalar1=rz[:, ri * 8 + hl:ri * 8 + hl + 1], scalar2=None, op0=ALU.mult),
                                    reads=[PSK[obank], ('rz', ri, hl)], writes=[('un', ri, hl)])
                            unk = [('un', ri, hl) for hl in range(8)]
                            fw.op('pool', lambda e, ri=ri: e.tensor_tensor(out=un[ri][:], in0=un[ri][:], in1=cgt[ri][:], op=ALU.mult),
                                  reads=unk + [('cgt', ri)], writes=[('ung', ri)] + unk)

                            def tr(e, ri=ri):
                                r_ = None
                                for j in range(4):
                                    r_ = e.transpose(out=ps[6][:, j * 64:(j + 1) * 64], in_=un[ri][:, j * 128:(j + 1) * 128],
                                                     identity=ident[0:64, 0:64])
                                return r_
                            fw.op('pe', tr, reads=[('ung', ri), 'ident'] + unk, writes=[PSK[6]])
                            fw.op('dve', lambda e, qi=qi, rr=rr: e.tensor_copy(
                                out=ust[qi][:, :, rr * 64:(rr + 1) * 64], in_=ps[6][:, 0:256].rearrange("p (f t) -> p f t", t=64)),
                                reads=[PSK[6]], writes=[('ust', qi, rr)])
                        n = 512 if rb < 8 else CTX
                        fw.dma('sp', UT_s[hf * 4:(hf + 1) * 4, :, rb * 512:rb * 512 + n].rearrange("f p t -> p f t"),
                               ust[qi][:, :, 0:n], reads=[('ust', qi, rr) for rr in range(nrows)], writes=[('UTn', hf, rb)])
                    fw.barrier()

        def emit_outproj(l, last):
            with ExitStack() as st:
                wo = sb("wo", [128, 16, D], BF16, st)
                wov = w_out[l, :, :].rearrange("(fc p) n -> p fc n", p=128)
                for g in range(4):
                    fw.dma('pool', wo[:, 4 * g:4 * g + 4, :], wov[:, 4 * g:4 * g + 4, :], writes=[('wo', g)])
                wok = [('wo', g) for g in range(4)]
                ut = [sb("ut%d" % i, [128, 16, 512], BF16, st) for i in range(2)]
                xr = [sb("xr%d" % i, [128, D], F32, st) for i in range(2)]
                tn = [sb("tn%d" % i, [128, D], F32, st) for i in range(2)]
                xo = [sb("xo%d" % i, [128, D], F32, st) for i in range(2)]
                junk = sb("junk2", [128, 512], BF16, st)
                ss2 = sb("ss2", [128, 4], F32, st)
                rr = sb("rr", [128, 2], F32, st)
                acs2 = sb("acs2", [128, 2], F32, st)
                blocks = [tb for tb in TB if not (last and tb[0] >= SEQ)]
                UTv = UT_s.rearrange("f p t -> p f t")

                def loadu(bi):
                    t0, n = blocks[bi]
                    i = bi % 2
                    fw.dma('sp', ut[i][:, :, 0:n], UTv[:, :, t0:t0 + n],
                           reads=[('UT', f, t0) for f in range(16)] + [('UT', f, 0) for f in range(8, 16)] +
                           [('UT', f, SEQ) for f in range(8, 16)], writes=[('ut', i)])
                loadu(0)
                tcnt = 0
                for bi, (t0, n) in enumerate(blocks):
                    i = bi % 2
                    if bi + 1 < len(blocks):
                        loadu(bi + 1)
                    for stl in range(n // 128):
                        tt = (t0 // 128) + stl
                        j = 0 if tt < 32 else 1
                        c = tcnt % 2
                        tcnt += 1
                        src = (x_in if l == first_layer else out_d)[tt * 128:(tt + 1) * 128, :] if tt < 32 else \
                            (ctx_in if l == first_layer else xc_d)[(tt - 32) * 128:(tt - 31) * 128, :]
                        dst = out_d[tt * 128:(tt + 1) * 128, :] if tt < 32 else xc_d[(tt - 32) * 128:(tt - 31) * 128, :]
                        fw.dma('sp', xr[c][:], src, reads=[('X', tt)], writes=[('xr', c)])
                        for nb in range(2):
                            bank = 2 * c + nb

                            def mm(e, i=i, stl=stl, nb=nb, bank=bank):
                                r = None
                                for fc in range(16):
                                    r = e.matmul(ps[bank][:, :], lhsT=ut[i][:, fc, stl * 128:(stl + 1) * 128],
                                                 rhs=wo[:, fc, nb * 512:(nb + 1) * 512], start=(fc == 0), stop=(fc == 15))
                                return r
                            fw.op('pe', mm, reads=wok + [('ut', i)], writes=[PSK[bank]])
                            def sqa(e, bank=bank, c=c, nb=nb):
                                return e.activation(out=junk[:], in_=ps[bank][:, :], func=AF.Square,
                                                    accum_out=ss2[:, 2 * c + nb:2 * c + nb + 1])
                            fw.op('act', sqa, reads=[PSK[bank]], writes=['junk2', ('ss2', c, nb)])
                        fw.op('dve', lambda e, c=c: e.tensor_tensor(out=rr[:, c:c + 1], in0=ss2[:, 2 * c:2 * c + 1],
                                                                    in1=ss2[:, 2 * c + 1:2 * c + 2], op=ALU.add),
                              reads=[('ss2', c, 0), ('ss2', c, 1)], writes=[('rr', c)])
                        fw.op('act', lambda e, c=c: e.activation(out=rr[:, c:c + 1], in_=rr[:, c:c + 1], func=AF.Sqrt,
                                                                 scale=1.0 / D, bias=EPS),
                              reads=[('rr', c)], writes=[('rr', c)])
                        fw.op('dve', lambda e, c=c: e.reciprocal(out=rr[:, c:c + 1], in_=rr[:, c:c + 1]),
                              reads=[('rr', c)], writes=[('rr', c)])
                        for nb in range(2):
                            bank = 2 * c + nb
                            fw.op('dve', lambda e, c=c, nb=nb, bank=bank, j=j: e.scalar_tensor_tensor(
                                out=tn[c][:, nb * 512:(nb + 1) * 512], in0=ps[bank][:, :], scalar=rr[:, c:c + 1],
                                in1=GG[:, j, nb * 512:(nb + 1) * 512], op0=ALU.mult, op1=ALU.mult),
                                reads=[PSK[bank], ('rr', c), ('GG', j, nb)], writes=[('tn', c, nb)])
                        fw.op('pool', lambda e, c=c: e.tensor_tensor(out=xo[c][:], in0=tn[c][:], in1=xr[c][:], op=ALU.add),
                              reads=[('tn', c, 0), ('tn', c, 1), ('xr', c)], writes=[('xo', c)])
                        fw.dma('sp', dst, xo[c][:], reads=[('xo', c)], writes=[('X', tt)])
                fw.barrier()

        stopped = [False]

        def chk(name):
            if stop_after == name:
                stopped[0] = True
            return stopped[0]

        if True:
          for l in range(first_layer, n_layers):
            last = (l == DEPTH - 1)
            emit_mod(l)
            if chk('mod'):
                break
            if l % 2 == 0:
                emit_even(l, last)
            else:
                emit_odd(l, last)
            if stopped[0]:
                break
            emit_outproj(l, last)
        fw.barrier()
        print("ops emitted:", fw.nops, "sems:", fw.nsem)
    return nc


_NC_CACHE = {}


def _prep_inputs(inp, b):
    cosT, sinT, rm = _rope_tables()
    m = {}
    m["x"] = np.ascontiguousarray(inp["x"][b])
    m["ctx"] = np.ascontiguousarray(inp["ctx"][b])
    m["cvec"] = np.ascontiguousarray(np.concatenate([_col(inp["c"][b]), _col(inp["c_ctx"])], axis=1))
    m["w_mod"] = inp["w_mod"]
    m["bmod_col"] = np.ascontiguousarray(np.concatenate([_col(inp["b_mod"][l]) for l in range(DEPTH)], axis=1))
    m["bmod_gate"] = np.ascontiguousarray(np.broadcast_to(inp["b_mod"][:, None, 2 * D:], (DEPTH, 128, D)))
    m["gpre_col"] = np.ascontiguousarray(np.concatenate([_col(inp["g_pre"][l]) for l in range(DEPTH)], axis=1))
    m["gpost_b"] = np.ascontiguousarray(np.broadcast_to(inp["g_post"][:, None, :], (DEPTH, 128, D)))
    m["w_out"] = inp["w_out"]
    m["ev_w_in"] = inp["ev_w_in"]
    m["lam_b"] = np.ascontiguousarray(np.broadcast_to(inp["ev_lambda"].reshape(1, 512), (128, 512)))
    m["subln_col"] = np.ascontiguousarray(inp["ev_subln"].T)
    m["conv_col"] = np.ascontiguousarray(
        np.concatenate([_col(inp["ev_conv"][j, tap]) for j in range(2) for tap in range(3)], axis=1))
    m["od_w_in"] = inp["od_w_in"]
    m["tt_tab"] = _na_tables(inp["od_rpb"])
    m["gate_up"] = inp["od_gate_up"]
    m["gb_col"] = np.ascontiguousarray(np.concatenate(
        [_col(inp["od_gate_bias"][j, d_]) for j in range(2) for d_ in range(2)], axis=1))
    m["gnorm_b"] = np.ascontiguousarray(np.broadcast_to(inp["od_gnorm"][:, None, :], (2, 128, 256)))
    cm = np.ones((128, 512), np.float32)
    cm[:, ::64] = 0.0
    m["cmask"] = cm
    si = np.arange(64)
    m["tri"] = np.ascontiguousarray(np.concatenate([(si[:, None] <= si[None, :]), (si[:, None] >= si[None, :])],
                                                  axis=1).astype(np.float32))
    m["cosT"] = cosT
    m["sinT"] = sinT
    m["rm"] = rm
    m["ident"] = np.eye(128, dtype=np.float32)
    return m


def kernel(**inputs):
    inp = {k: np.asarray(v) for k, v in inputs.items()}
    if "nc" not in _NC_CACHE:
        _NC_CACHE["nc"] = build()
    nc = _NC_CACHE["nc"]
    in_maps = [_prep_inputs(inp, b) for b in range(8)]
    res = run_bass_kernel_spmd(nc, in_maps, core_ids=list(range(8)))
    return np.stack([r["out"] for r in res.results], axis=0).astype(np.float32)
```

```python
import math
from contextlib import ExitStack
import numpy as np
import concourse.bass as bass
import concourse.mybir as mybir
from concourse.bass_utils import run_bass_kernel_spmd

F32 = mybir.dt.float32
BF16 = mybir.dt.bfloat16
AF = mybir.ActivationFunctionType
ALU = mybir.AluOpType
AX = mybir.AxisListType

D = 1024
SEQ = 4096
CTX = 256
T = SEQ + CTX
NTT = T // 128
DEPTH = 4
EPS = 1e-6
GRID = 64
SAME_ENG_SYNC = True


class Fw:
    SEM_LIMIT = 30000

    def __init__(self, nc, es, n_dma_slots=12):
        self.nc = nc
        self.es = es
        self.E = {'pe': nc.tensor, 'dve': nc.vector, 'act': nc.scalar, 'pool': nc.gpsimd, 'sp': nc.sync}
        self.cur = {}
        self.nsem = 0
        for e in self.E:
            self.cur[e] = [self._newsem(e), 0]
        self.known = {e: {} for e in self.E}
        self.last_w = {}
        self.readers = {}
        self.slots = {}
        for q in ('sp', 'pool', 'act'):
            n = n_dma_slots if q != 'act' else 4
            self.slots[q] = [[self._newsem('d' + q), 0] for _ in range(n)]
        self.slot_rr = {q: 0 for q in self.slots}
        self.nops = 0

    def _newsem(self, tag):
        self.nsem += 1
        return self.es.enter_context(self.nc.semaphore("s%s%d" % (tag, self.nsem)))

    def _wait(self, eng, tok):
        sem, val, teng = tok
        kn = self.known[eng]
        k = id(sem)
        if kn.get(k, 0) >= val:
            return
        self.E[eng].wait_ge(sem, val)
        kn[k] = val

    def _deps(self, eng, reads, writes):
        deps = {}

        def addtok(t, hazard):
            if t is None:
                return
            sem, val, teng = t
            if teng == eng and teng != 'dma':
                if eng == 'pe' or not SAME_ENG_SYNC:
                    return
            k = id(sem)
            if k not in deps or deps[k][1] < val:
                deps[k] = t
        for r in reads:
            addtok(self.last_w.get(r), 'raw')
        for w in writes:
            addtok(self.last_w.get(w), 'waw')
            for t in self.readers.get(w, ()):
                addtok(t, 'war')
        return deps.values()

    def _commit(self, tok, reads, writes):
        for r in reads:
            self.readers.setdefault(r, []).append(tok)
        for w in writes:
            self.last_w[w] = tok
            self.readers[w] = []

    def op(self, eng, fn, reads=(), writes=()):
        for t in self._deps(eng, reads, writes):
            self._wait(eng, t)
        ins = fn(self.E[eng])
        c = self.cur[eng]
        if c[1] >= self.SEM_LIMIT:
            c[0] = self._newsem(eng)
            c[1] = 0
        c[1] += 1
        ins.then_inc(c[0], 1)
        tok = (c[0], c[1], eng)
        self._commit(tok, reads, writes)
        self.nops += 1
        return tok

    def dma(self, q, out, in_, reads=(), writes=()):
        for t in self._deps(q, reads, writes):
            self._wait(q, t)
        sl = self.slots[q]
        i = self.slot_rr[q]
        self.slot_rr[q] = (i + 1) % len(sl)
        s = sl[i]
        if s[1] > 0:
            self._wait(q, (s[0], s[1], 'dma'))
        if s[1] >= self.SEM_LIMIT:
            s[0] = self._newsem('d' + q)
            s[1] = 0
        ins = self.E[q].dma_start(out=out, in_=in_)
        s[1] += 16
        ins.then_inc(s[0], 16)
        tok = (s[0], s[1], 'dma')
        self._commit(tok, reads, writes)
        self.nops += 1
        return tok

    def barrier(self):
        toks = []
        for e in self.E:
            c = self.cur[e]
            if c[1] > 0:
                toks.append((c[0], c[1], e))
        for q in self.slots:
            for s in self.slots[q]:
                if s[1] > 0:
                    toks.append((s[0], s[1], 'dma'))
        for e in self.E:
            for t in toks:
                if t[2] == e:
                    continue
                self._wait(e, t)
        self.last_w = {}
        self.readers = {}


def _rope_tables():
    half = 32
    inv = (1.0 / (10000.0 ** (np.arange(0, half, 2, dtype=np.float32) / np.float32(half)))).astype(np.float32)
    t = np.arange(SEQ)
    row = (t // GRID).astype(np.float32)[:, None] * inv
    col = (t % GRID).astype(np.float32)[:, None] * inv
    ang = np.concatenate([row, row, col, col], axis=-1).astype(np.float32)
    cos = np.cos(ang).astype(np.float32).T
    sin = np.sin(ang).astype(np.float32).T
    sign = np.ones(64, np.float32)
    src = np.zeros(64, np.int64)
    for f in range(64):
        blk = (f // 32) * 32
        o = f % 32
        if o < 16:
            src[f] = blk + o + 16
            sign[f] = -1.0
        else:
            src[f] = blk + o - 16
            sign[f] = 1.0
    sin_s = sin * sign[:, None]
    cosT = np.concatenate([cos, cos], axis=0)
    sinT = np.concatenate([sin_s, sin_s], axis=0)
    rm = np.zeros((128, 128), np.float32)
    for m in range(2):
        for f in range(64):
            rm[m * 64 + src[f], m * 64 + f] = 1.0
    return np.ascontiguousarray(cosT), np.ascontiguousarray(sinT), rm


def _na_tables(rpb):
    NEG = np.float32(-30000.0)
    kc = np.arange(64)[:, None]
    c = np.arange(64)[None, :]
    ws = np.clip(c - 8, 0, 48)
    valid = (kc >= ws) & (kc < ws + 16)
    coff = np.clip(kc - c + 15, 0, 30)
    out = np.full((2, 16, 2, 64, 16, 64), NEG, np.float32)
    for ti in range(16):
        for par in range(2):
            if ti < 14:
                dr = ti + par
            elif ti == 14:
                dr = 3 if par == 1 else None
            else:
                dr = 10 if par == 0 else None
            if dr is None or dr > 14:
                continue
            g = rpb[:, :, dr, :][:, :, coff]
            out[:, :, par, :, ti, :] = np.where(valid[None, None], g, NEG)
    seq = np.stack([out[..., 14, :], out[..., 4, :], out[..., 6, :], out[..., 8, :], out[..., 15, :]], axis=-2)
    out = np.concatenate([out, seq], axis=-2)
    return np.ascontiguousarray(out.reshape(2, 16, 128, 21 * 64))


def _col(v):
    return np.ascontiguousarray(v.reshape(-1, 128).T)


class _Stop(Exception):
    pass


def build(n_layers=DEPTH, dbg=False, stop_after=None, first_layer=0):
    nc = bass.Bass("TRN2", target_bir_lowering=False)

    def din(name, shape, dt=F32):
        return nc.dram_tensor(name, list(shape), dt, kind="ExternalInput").ap()

    def dscr(name, shape, dt):
        return nc.dram_tensor(name, list(shape), dt, kind="Internal").ap()

    x_in = din("x", [SEQ, D])
    ctx_in = din("ctx", [CTX, D])
    cvec = din("cvec", [128, 16])
    w_mod = din("w_mod", [DEPTH, D, 3 * D])
    bmod_col = din("bmod_col", [128, DEPTH * 24])
    bmod_gate = din("bmod_gate", [DEPTH, 128, D])
    gpre_col = din("gpre_col", [128, DEPTH * 8])
    gpost_b = din("gpost_b", [DEPTH, 128, D])
    w_out = din("w_out", [DEPTH, 2 * D, D])
    ev_w_in = din("ev_w_in", [2, D, 8 * D])
    lam_b = din("lam_b", [128, 2 * 256])
    subln_col = din("subln_col", [128, 2])
    conv_col = din("conv_col", [128, 2 * 3 * 8])
    cosT_d = din("cosT", [128, SEQ])
    sinT_d = din("sinT", [128, SEQ])
    rm_d = din("rm", [128, 128])
    ident_d = din("ident", [128, 128])
    od_w_in = din("od_w_in", [2, D, 7200])
    tt_d = din("tt_tab", [2, 16, 128, 21 * 64])
    gu_d = din("gate_up", [2, 2, 16, 512])
    gb_col = din("gb_col", [128, 16])
    gnorm_b = din("gnorm_b", [2, 128, 256])
    cmask_d = din("cmask", [128, 512])
    tri_d = din("tri", [64, 128])
    out_d = nc.dram_tensor("out", [SEQ, D], F32, kind="ExternalOutput").ap()
    xc_d = nc.dram_tensor("xc_out", [CTX, D], F32, kind="ExternalOutput" if dbg else "Internal").ap()

    KT_s = dscr("KT_s", [8, 128, T], BF16)
    QT_s = dscr("QT_s", [8, 128, T], BF16)
    V_s = dscr("V_s", [T, D], BF16)
    GA_s = dscr("GA_s", [8, 128, T], F32)
    UT_s = dscr("UT_s", [16, 128, T], BF16)
    VN_s = dscr("VN_s", [T, 16 * 65], BF16)
    CG_s = dscr("CG_s", [T, D], F32)
    DG_s = dscr("DG_s", [T, D], F32)
    OF_s = dscr("OF_s", [T, D], F32)
    OB_s = dscr("OB_s", [T, D], F32)

    es_top = ExitStack()
    with es_top as es:
        fw = Fw(nc, es)

        uid = [0]

        def sb(name, shape, dt, stack=None):
            uid[0] += 1
            return (stack or es).enter_context(nc.sbuf_tensor("sb_%s_%d" % (name, uid[0]), list(shape), dt))

        ps = [es.enter_context(nc.psum_tensor("ps%d" % i, [128, 512], F32)) for i in range(8)]
        PSK = [('ps', i) for i in range(8)]

        ident = sb("ident", [128, 128], F32)
        ones_f = sb("ones_f", [128, 128], F32)
        ones_b = sb("ones_b", [128, 128], BF16)
        rm_b = sb("rm_b", [128, 128], BF16)
        cs = sb("cs", [128, 16], F32)
        csb = sb("csb", [128, 16, 128], F32)
        bmodc = sb("bmodc", [128, DEPTH * 24], F32)
        gprec = sb("gprec", [128, DEPTH * 8], F32)
        lamt = sb("lamt", [128, 512], F32)
        lam2 = sb("lam2", [128, 4, 64], F32)
        lsum = sb("lsum", [128, 4], F32)
        lam_c = sb("lam_c", [128, 2], F32)
        sublc = sb("sublc", [128, 2], F32)
        convc = sb("convc", [128, 48], F32)
        A_m = sb("A_m", [128, 2, 8], F32)
        B_m = sb("B_m", [128, 2, 8], F32)
        sc_m = sb("sc_m", [128, 2, 8], F32)
        GG = sb("GG", [128, 2, D], F32)

        identb = sb("identb", [128, 128], BF16)
        cmask = sb("cmask", [128, 512], F32)
        tri = sb("tri", [64, 128], F32)
        gbc = sb("gbc", [128, 16], F32)
        fw.dma('pool', identb[:], ident_d[:, :], writes=['identb'])
        fw.dma('sp', cmask[:], cmask_d[:, :], writes=['cmask'])
        fw.dma('sp', tri[:], tri_d[:, :], writes=['tri'])
        fw.dma('sp', gbc[:], gb_col[:, :], writes=['gbc'])
        fw.op('dve', lambda e: e.tensor_scalar(out=gbc[:], in0=gbc[:], scalar1=-1.0, scalar2=None, op0=ALU.mult),
              reads=['gbc'], writes=['gbc'])
        fw.dma('sp', ident[:], ident_d[:, :], writes=['ident'])
        fw.dma('pool', rm_b[:], rm_d[:, :], writes=['rm_b'])
        fw.dma('sp', cs[:], cvec[:, :], writes=['cs'])
        fw.dma('sp', bmodc[:], bmod_col[:, :], writes=['bmodc'])
        fw.dma('sp', gprec[:], gpre_col[:, :], writes=['gprec'])
        fw.dma('sp', lamt[:], lam_b[:, :], writes=['lamt'])
        fw.dma('sp', sublc[:], subln_col[:, :], writes=['sublc'])
        fw.dma('sp', convc[:], conv_col[:, :], writes=['convc'])
        fw.op('dve', lambda e: e.memset(ones_f[:], 1.0), writes=['ones_f'])
        fw.op('dve', lambda e: e.memset(ones_b[:], 1.0), writes=['ones_b'])
        fw.op('act', lambda e: e.activation(out=cs[:], in_=cs[:], func=AF.Silu), reads=['cs'], writes=['cs'])
        for k in range(16):
            fw.op('dve', lambda e, k=k: e.tensor_scalar(out=csb[:, k, :], in0=ones_f[:], scalar1=cs[:, k:k + 1],
                                                        scalar2=None, op0=ALU.mult),
                  reads=['ones_f', 'cs'], writes=[('csb', k)])
        lt = lamt[:].rearrange("p (j a d) -> p j a d", j=2, a=4)
        for j in range(2):
            for a in range(2):
                fw.op('dve', lambda e, j=j, a=a: e.tensor_tensor(out=lam2[:, j * 2 + a, :], in0=lt[:, j, 2 * a, :],
                                                                 in1=lt[:, j, 2 * a + 1, :], op=ALU.mult),
                      reads=['lamt'], writes=[('lam2', j, a)])
                fw.op('dve', lambda e, j=j, a=a: e.tensor_reduce(out=lsum[:, j * 2 + a:j * 2 + a + 1],
                                                                 in_=lam2[:, j * 2 + a, :], axis=AX.X, op=ALU.add),
                      reads=[('lam2', j, a)], writes=[('lsum', j, a)])
        fw.op('act', lambda e: e.activation(out=lsum[:], in_=lsum[:], func=AF.Exp),
              reads=[('lsum', j, a) for j in range(2) for a in range(2)], writes=['lsume'])
        for j in range(2):
            lam_init = 0.8 - 0.6 * math.exp(-0.3 * (2 * j))
            fw.op('dve', lambda e, j=j, li=lam_init: e.scalar_tensor_tensor(
                out=lam_c[:, j:j + 1], in0=lsum[:, 2 * j + 1:2 * j + 2], scalar=-li, in1=lsum[:, 2 * j:2 * j + 1],
                op0=ALU.add, op1=ALU.subtract), reads=['lsume'], writes=[('lam_c', j)])
            fw.op('dve', lambda e, j=j, li=lam_init: e.tensor_scalar(
                out=sublc[:, j:j + 1], in0=sublc[:, j:j + 1], scalar1=(1.0 - li), scalar2=None, op0=ALU.mult),
                reads=['sublc'], writes=['sublc'])

        def emit_mod(l):
            with ExitStack() as st:
                wm = [sb("wm%d" % i, [128, 8, 512], F32, st) for i in range(2)]
                bg = sb("bg", [128, D], F32, st)
                gp = sb("gp", [128, D], F32, st)
                tmpg = sb("tmpg", [128, 512], F32, st)
                fw.dma('sp', bg[:], bmod_gate[l, :, :], writes=['bg'])
                fw.dma('sp', gp[:], gpost_b[l, :, :], writes=['gp'])
                wv = w_mod[l, :, :].rearrange("(kc p) n -> p kc n", p=128)
                for blk in range(6):
                    b = blk % 2
                    fw.dma('sp', wm[b][:], wv[:, :, blk * 512:(blk + 1) * 512], writes=[('wm', b)])
                    if blk < 4:
                        for n4 in range(4):
                            n = blk * 4 + n4

                            def mm(e, n=n, n4=n4, b=b):
                                r = None
                                for kc in range(8):
                                    r = e.matmul(ps[0][:, 2 * n:2 * n + 2], lhsT=wm[b][:, kc, n4 * 128:(n4 + 1) * 128],
                                                 rhs=cs[:, kc:16:8], start=(kc == 0), stop=(kc == 7))
                                return r
                            fw.op('pe', mm, reads=[('wm', b), 'cs'], writes=[('modps', n)] + ([PSK[0]] if n == 0 else []))
                    else:
                        nb = blk - 4
                        for j in range(2):
                            bank = 1 + j

                            def mm(e, j=j, b=b, bank=bank):
                                r = None
                                for kc in range(8):
                                    r = e.matmul(ps[bank][:, :], lhsT=csb[:, j * 8 + kc, :], rhs=wm[b][:, kc, :],
                                                 start=(kc == 0), stop=(kc == 7))
                                return r
                            fw.op('pe', mm, reads=[('wm', b)] + [('csb', j * 8 + kc) for kc in range(8)], writes=[PSK[bank]])
                            fw.op('dve', lambda e, nb=nb, bank=bank: e.tensor_tensor(
                                out=tmpg[:], in0=ps[bank][:, :], in1=bg[:, nb * 512:(nb + 1) * 512], op=ALU.add),
                                reads=[PSK[bank], 'bg'], writes=['tmpg'])
                            fw.op('dve', lambda e, nb=nb, j=j: e.tensor_tensor(
                                out=GG[:, j, nb * 512:(nb + 1) * 512], in0=tmpg[:], in1=gp[:, nb * 512:(nb + 1) * 512],
                                op=ALU.mult), reads=['tmpg', 'gp'], writes=[('GG', j, nb)])
                pv = ps[0][:, 0:32].rearrange("p (n j) -> p n j", j=2)
                allmod = [('modps', n) for n in range(16)]
                for j in range(2):
                    fw.op('dve', lambda e, j=j: e.tensor_tensor(out=B_m[:, j, :], in0=pv[:, 0:8, j],
                                                                in1=bmodc[:, l * 24:l * 24 + 8], op=ALU.add),
                          reads=allmod + ['bmodc'], writes=[('B_m', j)])
                    fw.op('dve', lambda e, j=j: e.tensor_tensor(out=sc_m[:, j, :], in0=pv[:, 8:16, j],
                                                                in1=bmodc[:, l * 24 + 8:l * 24 + 16], op=ALU.add),
                          reads=allmod + ['bmodc'], writes=[('sc_m', j)])
                    fw.op('dve', lambda e, j=j: e.scalar_tensor_tensor(
                        out=A_m[:, j, :], in0=sc_m[:, j, :], scalar=1.0, in1=gprec[:, l * 8:(l + 1) * 8],
                        op0=ALU.add, op1=ALU.mult), reads=[('sc_m', j), 'gprec'], writes=[('A_m', j)])
                fw.barrier()

        def emit_hT(l, hT, st):
            xt = [sb("xt%d" % i, [128, D], F32, st) for i in range(2)]
            xn = [sb("xn%d" % i, [128, D], F32, st) for i in range(2)]
            junk = sb("junk", [128, D], BF16, st)
            ssq = sb("ssq", [128, 2], F32, st)
            rsd = sb("rsd", [128, 2], F32, st)
            acs = sb("acs", [128, 2], F32, st)

            def load(tt):
                b = tt % 2
                if tt < 32:
                    src = (x_in if l == first_layer else out_d)[tt * 128:(tt + 1) * 128, :]
                else:
                    src = (ctx_in if l == first_layer else xc_d)[(tt - 32) * 128:(tt - 31) * 128, :]
                fw.dma('sp', xt[b][:], src, reads=[('X', tt)], writes=[('xt', b)])
            load(0)
            for tt in range(NTT):
                b = tt % 2
                j = 0 if tt < 32 else 1
                if tt + 1 < NTT:
                    load(tt + 1)
                def sqa(e, b=b):
                    return e.activation(out=junk[:], in_=xt[b][:], func=AF.Square, accum_out=ssq[:, b:b + 1])
                fw.op('act', sqa, reads=[('xt', b)], writes=['junk', ('ssq', b)])
                fw.op('act', lambda e, b=b: e.activation(out=rsd[:, b:b + 1], in_=ssq[:, b:b + 1], func=AF.Sqrt,
                                                         scale=1.0 / D, bias=EPS),
                      reads=[('ssq', b)], writes=[('rsd', b)])
                fw.op('dve', lambda e, b=b: e.reciprocal(out=rsd[:, b:b + 1], in_=rsd[:, b:b + 1]),
                      reads=[('rsd', b)], writes=[('rsd', b)])
                fw.op('dve', lambda e, b=b: e.tensor_scalar(out=xn[b][:], in0=xt[b][:], scalar1=rsd[:, b:b + 1],
                                                            scalar2=None, op0=ALU.mult),
                      reads=[('xt', b), ('rsd', b)], writes=[('xn', b)])
                for half in range(2):
                    bank = 2 * b + half

                    def tr(e, b=b, half=half, bank=bank):
                        r = None
                        for q4 in range(4):
                            kc = half * 4 + q4
                            r = e.transpose(out=ps[bank][:, q4 * 128:(q4 + 1) * 128],
                                            in_=xn[b][:, kc * 128:(kc + 1) * 128], identity=ident[:])
                        return r
                    fw.op('pe', tr, reads=[('xn', b), 'ident'], writes=[PSK[bank]])
                    for q4 in range(4):
                        kc = half * 4 + q4
                        fw.op('dve', lambda e, kc=kc, q4=q4, bank=bank, tt=tt, j=j: e.tensor_scalar(
                            out=hT[:, kc, tt * 128:(tt + 1) * 128], in0=ps[bank][:, q4 * 128:(q4 + 1) * 128],
                            scalar1=A_m[:, j, kc:kc + 1], scalar2=B_m[:, j, kc:kc + 1], op0=ALU.mult, op1=ALU.add),
                            reads=[PSK[bank], ('A_m', j), ('B_m', j)], writes=[('hT', kc, tt)])

        TB = [(i * 512, 512) for i in range(8)] + [(SEQ, CTX)]

        def hT_keys(t0, n):
            return [('hT', kc, tt) for kc in range(8) for tt in range(t0 // 128, (t0 + n) // 128)]

        def emit_even(l, last):
            je = l // 2
            w_in = ev_w_in[je, :, :].rearrange("(kc p) f -> p kc f", p=128)
            with ExitStack() as st1:
                hT = sb("hT", [128, 8, T], BF16, st1)
                with ExitStack() as st:
                    emit_hT(l, hT, st)
                    fw.barrier()
                if chk('hT'):
                    return
                with ExitStack() as st:
                    cosT = sb("cosT", [128, SEQ], F32, st)
                    sinT = sb("sinT", [128, SEQ], F32, st)
                    fw.dma('sp', cosT[:], cosT_d[:, :], writes=['cosT'])
                    fw.dma('sp', sinT[:], sinT_d[:, :], writes=['sinT'])
                    wt = [sb("wt%d" % i, [128, 8, 512], BF16, st) for i in range(2)]
                    qb = [sb("qb%d" % i, [128, 512], BF16, st) for i in range(2)]
                    t1 = [sb("t1_%d" % i, [128, 512], F32, st) for i in range(2)]
                    t2 = [sb("t2_%d" % i, [128, 512], F32, st) for i in range(2)]
                    ob = [sb("ob%d" % i, [128, 512], BF16, st) for i in range(2)]
                    gs = [sb("gs%d" % i, [128, 512], F32, st) for i in range(2)]
                    vs = [sb("vs%d" % i, [128, D], BF16, st) for i in range(2)]
                    cnt = [0]
                    wcnt = [0]

                    def load_w(c0):
                        b = wcnt[0] % 2
                        wcnt[0] += 1
                        fw.dma('pool', wt[b][:], w_in[:, :, c0:c0 + 512], writes=[('wt', b)])
                        return b

                    def proj(b, ft, t0, n, bank):
                        def mm(e):
                            r = None
                            for kc in range(8):
                                r = e.matmul(ps[bank][:, 0:n], lhsT=wt[b][:, kc, ft * 128:(ft + 1) * 128],
                                             rhs=hT[:, kc, t0:t0 + n], start=(kc == 0), stop=(kc == 7))
                            return r
                        fw.op('pe', mm, reads=[('wt', b)] + hT_keys(t0, n), writes=[PSK[bank]])

                    plan = [('k', 0), ('k', 512), ('q', 2048), ('q', 2560), ('g', 3072), ('g', 3584)]
                    import os as _os
                    _kinds = _os.environ.get("KINDS")
                    if _kinds is not None:
                        plan = [p for p in plan if p[0] in _kinds]
                    nextb = load_w(plan[0][1])
                    for pi, (kind, c0) in enumerate(plan):
                        b = nextb
                        if pi + 1 < len(plan):
                            nextb = load_w(plan[pi + 1][1])
                        for ft in range(4):
                            hd = ((c0 % 1024) // 128) + ft
                            for (t0, n) in TB:
                                if last and kind in ('q', 'g') and t0 >= SEQ:
                                    continue
                                _rd0 = _os.environ.get("ROPE_DBG", "")
                                if 'ctxonly' in _rd0 and t0 < SEQ:
                                    continue
                                if 'latonly' in _rd0 and t0 >= SEQ:
                                    continue
                                i = cnt[0] % 2
                                cnt[0] += 1
                                bank = i
                                proj(b, ft, t0, n, bank)
                                if kind == 'g':
                                    fw.op('act', lambda e, i=i, n=n, bank=bank: e.activation(
                                        out=gs[i][:, 0:n], in_=ps[bank][:, 0:n], func=AF.Silu),
                                        reads=[PSK[bank]], writes=[('gs', i)])
                                    fw.dma('sp', GA_s[hd, :, t0:t0 + n], gs[i][:, 0:n], reads=[('gs', i)],
                                           writes=[('GA', hd, t0)])
                                    continue
                                dst = KT_s if kind == 'k' else QT_s
                                if t0 >= SEQ or 'norope' in _rd0:
                                    fw.op('act', lambda e, i=i, n=n, bank=bank: e.activation(
                                        out=ob[i][:, 0:n], in_=ps[bank][:, 0:n], func=AF.Copy),
                                        reads=[PSK[bank]], writes=[('ob', i)])
                                else:
                                    fw.op('act', lambda e, i=i, n=n, bank=bank: e.activation(
                                        out=qb[i][:, 0:n], in_=ps[bank][:, 0:n], func=AF.Copy),
                                        reads=[PSK[bank]], writes=[('qb', i)])
                                    if 'cosconst' in _rd0:
                                        fw.op('dve', lambda e, i=i, n=n, bank=bank, t0=t0: e.tensor_tensor(
                                            out=t1[i][:, 0:n], in0=ps[bank][:, 0:n], in1=gs[0][:, 0:n], op=ALU.mult),
                                            reads=[PSK[bank], ('gs', 0)], writes=[('t1', i)])
                                    else:
                                      fw.op('dve', lambda e, i=i, n=n, bank=bank, t0=t0: e.tensor_tensor(
                                        out=t1[i][:, 0:n], in0=ps[bank][:, 0:n], in1=cosT[:, t0:t0 + n], op=ALU.mult),
                                        reads=[PSK[bank], 'cosT', ('qb', i)], writes=[('t1', i)])
                                    _rd = _os.environ.get("ROPE_DBG", "")
                                    if 'nomm' not in _rd:
                                        fw.op('pe', lambda e, i=i, n=n: e.matmul(
                                            ps[2 + i][:, 0:n], lhsT=rm_b[:], rhs=qb[i][:, 0:n], start=True, stop=True),
                                            reads=['rm_b', ('qb', i)], writes=[PSK[2 + i]])
                                    if 'not2' not in _rd:
                                        fw.op('dve', lambda e, i=i, n=n, t0=t0: e.tensor_tensor(
                                            out=t2[i][:, 0:n], in0=ps[2 + i][:, 0:n], in1=sinT[:, t0:t0 + n], op=ALU.mult),
                                            reads=[PSK[2 + i], 'sinT'], writes=[('t2', i)])
                                    else:
                                        fw.op('dve', lambda e, i=i, n=n, t0=t0: e.tensor_tensor(
                                            out=t2[i][:, 0:n], in0=t1[i][:, 0:n], in1=(gs[0][:, 0:n] if 'cosconst' in _rd0 else sinT[:, t0:t0 + n]), op=ALU.mult),
                                            reads=[('t1', i), 'sinT'], writes=[('t2', i)])
                                    if 'actob' in _rd0:
                                        fw.op('dve', lambda e, i=i, n=n: e.tensor_tensor(
                                            out=t1[i][:, 0:n], in0=t1[i][:, 0:n], in1=t2[i][:, 0:n], op=ALU.add),
                                            reads=[('t1', i), ('t2', i)], writes=[('t1', i)])
                                        fw.op('act', lambda e, i=i, n=n, bank=bank: e.activation(
                                            out=ob[i][:, 0:n], in_=ps[bank][:, 0:n], func=AF.Copy),
                                            reads=[PSK[bank]], writes=[('ob', i)])
                                    else:
                                      fw.op(_os.environ.get("ROPE_ADD_ENG", "pool"), lambda e, i=i, n=n: e.tensor_tensor(
                                        out=ob[i][:, 0:n], in0=t1[i][:, 0:n], in1=t2[i][:, 0:n], op=ALU.add),
                                        reads=[('t1', i), ('t2', i)], writes=[('ob', i)])
                                fw.dma('sp', dst[hd, :, t0:t0 + n], ob[i][:, 0:n], reads=[('ob', i)],
                                       writes=[(kind, hd, t0)])
                    bv = [load_w(1024), load_w(1536)]
                    for tt in range(NTT if (_kinds is None or 'v' in _kinds) else 0):
                        i = tt % 2
                        for nb in range(2):
                            bank = 4 + 2 * i + nb

                            def mm(e, tt=tt, nb=nb, bank=bank):
                                r = None
                                for kc in range(8):
                                    r = e.matmul(ps[bank][:, :], lhsT=hT[:, kc, tt * 128:(tt + 1) * 128],
                                                 rhs=wt[bv[nb]][:, kc, :], start=(kc == 0), stop=(kc == 7))
                                return r
                            fw.op('pe', mm, reads=[('wt', bv[nb])] + [('hT', kc, tt) for kc in range(8)],
                                  writes=[PSK[bank]])
                            eng = 'act' if nb == 0 else 'dve'
                            if eng == 'act':
                                fw.op('act', lambda e, i=i, nb=nb, bank=bank: e.activation(
                                    out=vs[i][:, nb * 512:(nb + 1) * 512], in_=ps[bank][:, :], func=AF.Copy),
                                    reads=[PSK[bank]], writes=[('vs', i, nb)])
                            else:
                                fw.op('dve', lambda e, i=i, nb=nb, bank=bank: e.tensor_copy(
                                    out=vs[i][:, nb * 512:(nb + 1) * 512], in_=ps[bank][:, :]),
                                    reads=[PSK[bank]], writes=[('vs', i, nb)])
                        fw.dma('sp', V_s[tt * 128:(tt + 1) * 128, :], vs[i][:], reads=[('vs', i, 0), ('vs', i, 1)],
                               writes=[('V', tt)])
                    fw.barrier()
                if chk('2a'):
                    return
                with ExitStack() as st:
                    wt = [sb("wtb%d" % i, [128, 8, 512], BF16, st) for i in range(2)]
                    vrow = sb("vrow", [128, T + 4], F32, st)
                    bbg = sb("bbg", [128, T], F32, st)
                    acc = sb("acc", [128, T], F32, st)
                    ubr = [sb("ubr0", [128, T], BF16, st)] * 2
                    hsb = [sb("hsb%d" % i, [128, 512], F32, st) for i in range(2)]
                    sg = [sb("sg%d" % i, [128, 512], F32, st) for i in range(2)]
                    fw.op('pool', lambda e: e.memset(vrow[:], 0.0), writes=['vrow_all'])
                    LOFF = 1
                    COFF = SEQ + 3

                    def load_wb(jf, b):
                        for gi, base in enumerate((4096, 5120, 6144, 7168)):
                            fw.dma('pool', wt[b][:, :, gi * 128:(gi + 1) * 128],
                                   w_in[:, :, base + jf * 128:base + (jf + 1) * 128], writes=[('wtb', b, gi)])
                    load_wb(0, 0)
                    cnt = 0
                    for jf in range(8):
                        b = jf % 2
                        if jf + 1 < 8:
                            load_wb(jf + 1, 1 - b)
                        for (t0, n) in TB:
                            if last and t0 >= SEQ:
                                continue
                            i = cnt % 2
                            cnt += 1
                            for gi in range(4):
                                bank = 4 * i + gi

                                def mm(e, gi=gi, bank=bank, t0=t0, n=n, b=b):
                                    r = None
                                    for kc in range(8):
                                        r = e.matmul(ps[bank][:, 0:n], lhsT=wt[b][:, kc, gi * 128:(gi + 1) * 128],
                                                     rhs=hT[:, kc, t0:t0 + n], start=(kc == 0), stop=(kc == 7))
                                    return r
                                fw.op('pe', mm, reads=[('wtb', b, gi)] + hT_keys(t0, n), writes=[PSK[bank]])
                            voff = (LOFF + t0) if t0 < SEQ else (COFF + t0 - SEQ)
                            fw.op('act', lambda e, i=i, n=n: e.activation(out=hsb[i][:, 0:n], in_=ps[4 * i + 0][:, 0:n],
                                                                          func=AF.Copy),
                                  reads=[PSK[4 * i + 0]], writes=[('hsb', i)])
                            fw.op('dve', lambda e, i=i, n=n, voff=voff: e.tensor_tensor(
                                out=vrow[:, voff:voff + n], in0=ps[4 * i + 2][:, 0:n], in1=hsb[i][:, 0:n], op=ALU.mult),
                                reads=[PSK[4 * i + 2], ('hsb', i), 'vrow_all'], writes=[('vrow', t0)])
                            fw.op('act', lambda e, i=i, n=n: e.activation(out=sg[i][:, 0:n], in_=ps[4 * i + 3][:, 0:n],
                                                                          func=AF.Silu),
                                  reads=[PSK[4 * i + 3]], writes=[('sg', i)])
                            fw.op('dve', lambda e, i=i, n=n, t0=t0: e.tensor_tensor(
                                out=bbg[:, t0:t0 + n], in0=ps[4 * i + 1][:, 0:n], in1=sg[i][:, 0:n], op=ALU.mult),
                                reads=[PSK[4 * i + 1], ('sg', i)], writes=[('bbg', t0)])
                        segs = [(LOFF, 0, SEQ)] + ([] if last else [(COFF, SEQ, CTX)])
                        vk = [('vrow', t0) for (t0, n) in TB if not (last and t0 >= SEQ)]
                        bk = [('bbg', t0) for (t0, n) in TB if not (last and t0 >= SEQ)]
                        u = ubr[b]
                        for (vo, o0, n) in segs:
                            wc = lambda tap: convc[:, je * 24 + tap * 8 + jf:je * 24 + tap * 8 + jf + 1]
                            fw.op('dve', lambda e, vo=vo, o0=o0, n=n, wc=wc: e.tensor_scalar(
                                out=acc[:, o0:o0 + n], in0=vrow[:, vo - 1:vo - 1 + n], scalar1=wc(0), scalar2=None,
                                op0=ALU.mult), reads=vk + ['convc', 'vrow_all'], writes=[('acc', o0)])
                            for tap in (1, 2):
                                fw.op('dve', lambda e, vo=vo, o0=o0, n=n, wc=wc, tap=tap: e.scalar_tensor_tensor(
                                    out=acc[:, o0:o0 + n], in0=vrow[:, vo - 1 + tap:vo - 1 + tap + n], scalar=wc(tap),
                                    in1=acc[:, o0:o0 + n], op0=ALU.mult, op1=ALU.add),
                                    reads=vk + [('acc', o0)], writes=[('acc', o0)])
                            fw.op('pool', lambda e, o0=o0, n=n, u=u: e.tensor_tensor(
                                out=u[:, o0:o0 + n], in0=acc[:, o0:o0 + n], in1=bbg[:, o0:o0 + n], op=ALU.mult),
                                reads=[('acc', o0)] + bk, writes=[('ubr', 0, o0)])
                            fw.dma('sp', UT_s[8 + jf, :, o0:o0 + n], u[:, o0:o0 + n], reads=[('ubr', 0, o0)],
                                   writes=[('UT', 8 + jf, o0)])
                    fw.barrier()
                if chk('2b'):
                    return
            with ExitStack() as st:
                KT = sb("KT", [128, 8, T], BF16, st)
                V = sb("V", [128, NTT, D], BF16, st)
                for hd in range(8):
                    fw.dma('sp', KT[:, hd, :], KT_s[hd, :, :], writes=[('KT', hd)])
                Vv = V_s.rearrange("(tt p) f -> p tt f", p=128)
                for g in range(NTT // 2):
                    fw.dma('sp', V[:, 2 * g:2 * g + 2, :], Vv[:, 2 * g:2 * g + 2, :], writes=[('Vt', 2 * g), ('Vt', 2 * g + 1)])
                qt = [sb("qt%d" % i, [128, 512], BF16, st) for i in range(2)]
                ga = [sb("ga%d" % i, [128, 512], F32, st) for i in range(2)]
                pT = [sb("pT%d" % i, [128, 512], BF16, st) for i in range(4)]
                r1 = sb("r1", [128, 512], F32, st)
                a1 = sb("a1", [128, 512], F32, st)
                a2 = sb("a2", [128, 512], F32, st)
                dd = sb("dd", [128, 512], F32, st)
                uo = [sb("uo%d" % i, [128, 512], BF16, st) for i in range(2)]
                work = [(hd, t0, n) for hd in range(8) for (t0, n) in TB if not (last and t0 >= SEQ)]

                def loadq(wi):
                    hd, t0, n = work[wi]
                    i = wi % 2
                    fw.dma('sp', qt[i][:, 0:n], QT_s[hd, :, t0:t0 + n], reads=[('q', hd, t0)], writes=[('qt', i)])

                def loadg(wi):
                    if wi >= len(work):
                        return
                    hd, t0, n = work[wi]
                    i = wi % 2
                    fw.dma('sp', ga[i][:, 0:n], GA_s[hd, :, t0:t0 + n], reads=[('GA', hd, t0)], writes=[('ga', i)])
                loadq(0)
                zacc = [[sb("zacc%d_%d" % (w_, m_), [128, 512], F32, st) for m_ in range(2)] for w_ in range(2)]
                units = []
                for wi, (hd, t0, n) in enumerate(work):
                    kts = list(range(NTT)) if t0 < SEQ else [32, 33]
                    for ki, kt in enumerate(kts):
                        units.append((wi, ki, kt, len(kts)))

                def emit_qk(u):
                    wi, ki, kt, nk = units[u]
                    hd, t0, n = work[wi]
                    i = wi % 2
                    for m in range(2):
                        sbank = 2 * (u % 2) + m
                        fw.op('pe', lambda e, m=m, sbank=sbank: e.matmul(
                            ps[sbank][:, 0:n], lhsT=KT[m * 64:(m + 1) * 64, hd, kt * 128:(kt + 1) * 128],
                            rhs=qt[i][m * 64:(m + 1) * 64, 0:n], start=True, stop=True),
                            reads=[('KT', hd), ('qt', i)], writes=[PSK[sbank]])

                def emit_rest(u):
                    wi, ki, kt, nk = units[u]
                    hd, t0, n = work[wi]
                    wz = wi % 2
                    for m in range(2):
                        sbank = 2 * (u % 2) + m
                        fw.op('act', lambda e, sbank=sbank: e.activation(
                            out=pT[sbank][:, 0:n], in_=ps[sbank][:, 0:n], func=AF.Exp, scale=0.125),
                            reads=[PSK[sbank]], writes=[('pT', sbank)])
                    if u + 1 < len(units):
                        wj = units[u + 1][0]
                        if wj != wi and wj + 1 < len(work):
                            loadq(wj + 1)
                        emit_qk(u + 1)
                    first = (ki == 0)
                    lastk = (ki == nk - 1)
                    for m in range(2):
                        pi = 2 * (u % 2) + m
                        bo = 4 + 2 * m
                        fw.op('pe', lambda e, pi=pi, bo=bo: e.matmul(
                            ps[bo][:, 0:n], lhsT=V[:, kt, hd * 128:(hd + 1) * 128], rhs=pT[pi][:, 0:n],
                            start=first, stop=lastk), reads=[('Vt', kt), ('pT', pi)], writes=[PSK[bo]])
                        zeng = 'dve'
                        if m == 1:
                            fw.op('pe', lambda e, pi=pi: e.matmul(
                                ps[7][:, 0:n], lhsT=ones_b[:], rhs=pT[pi][:, 0:n], start=first, stop=lastk),
                                reads=['ones_b', ('pT', pi)], writes=[PSK[7]])
                        elif first:
                            fw.op(zeng, lambda e, pi=pi, m=m: e.tensor_copy(out=zacc[wz][m][:, 0:n], in_=pT[pi][:, 0:n]),
                                  reads=[('pT', pi)], writes=[('zacc', wz, m)])
                        else:
                            fw.op(zeng, lambda e, pi=pi, m=m: e.tensor_tensor(
                                out=zacc[wz][m][:, 0:n], in0=zacc[wz][m][:, 0:n], in1=pT[pi][:, 0:n], op=ALU.add),
                                reads=[('pT', pi), ('zacc', wz, m)], writes=[('zacc', wz, m)])

                if len(work) > 1:
                    loadq(1)
                loadg(0)
                loadg(1)
                emit_qk(0)
                for u in range(len(units)):
                    emit_rest(u)
                    wi, ki, kt, nk = units[u]
                    if ki != nk - 1:
                        continue
                    hd, t0, n = work[wi]
                    i = wi % 2
                    wz = wi % 2
                    for m in range(1):
                        fw.op('pe', lambda e, m=m: e.matmul(ps[5 + 2 * m][:, 0:n], lhsT=ones_f[:], rhs=zacc[wz][m][:, 0:n],
                                                            start=True, stop=True),
                              reads=['ones_f', ('zacc', wz, m)], writes=[PSK[5 + 2 * m]])
                    fw.op('dve', lambda e, n=n: e.reciprocal(out=r1[:, 0:n], in_=ps[5][:, 0:n]),
                          reads=[PSK[5]], writes=['r1'])
                    fw.op('dve', lambda e, n=n: e.tensor_tensor(out=a1[:, 0:n], in0=ps[4][:, 0:n], in1=r1[:, 0:n],
                                                                op=ALU.mult), reads=[PSK[4], 'r1'], writes=['a1'])
                    fw.op('dve', lambda e, n=n: e.reciprocal(out=r1[:, 0:n], in_=ps[7][:, 0:n]),
                          reads=[PSK[7]], writes=['r1'])
                    fw.op('dve', lambda e, n=n: e.tensor_tensor(out=a2[:, 0:n], in0=ps[6][:, 0:n], in1=r1[:, 0:n],
                                                                op=ALU.mult), reads=[PSK[6], 'r1'], writes=['a2'])
                    fw.op('dve', lambda e, n=n: e.scalar_tensor_tensor(
                        out=dd[:, 0:n], in0=a2[:, 0:n], scalar=lam_c[:, je:je + 1], in1=a1[:, 0:n],
                        op0=ALU.mult, op1=ALU.add), reads=['a1', 'a2', ('lam_c', je)], writes=['dd'])
                    fw.op('act', lambda e, n=n: e.activation(out=a2[:, 0:n], in_=dd[:, 0:n], func=AF.Square),
                          reads=['dd'], writes=['a2'])
                    fw.op('pe', lambda e, n=n: e.matmul(ps[4][:, 0:n], lhsT=ones_f[:], rhs=a2[:, 0:n],
                                                        start=True, stop=True),
                          reads=['ones_f', 'a2'], writes=[PSK[4]])
                    fw.op('act', lambda e, n=n: e.activation(out=a1[:, 0:n], in_=ps[4][:, 0:n], func=AF.Sqrt,
                                                             scale=1.0 / 128, bias=EPS),
                          reads=[PSK[4]], writes=['a1'])
                    fw.op('dve', lambda e, n=n: e.reciprocal(out=a1[:, 0:n], in_=a1[:, 0:n]), reads=['a1'], writes=['a1'])
                    fw.op('dve', lambda e, n=n: e.scalar_tensor_tensor(
                        out=dd[:, 0:n], in0=dd[:, 0:n], scalar=sublc[:, je:je + 1], in1=a1[:, 0:n],
                        op0=ALU.mult, op1=ALU.mult), reads=['dd', 'a1', 'sublc'], writes=['dd'])
                    fw.op('dve', lambda e, n=n, i=i: e.tensor_tensor(out=uo[i][:, 0:n], in0=dd[:, 0:n], in1=ga[i][:, 0:n],
                                                                     op=ALU.mult),
                          reads=['dd', ('ga', i)], writes=[('uo', i)])
                    fw.dma('sp', UT_s[hd, :, t0:t0 + n], uo[i][:, 0:n], reads=[('uo', i)], writes=[('UT', hd, t0)])
                    loadg(wi + 2)
                fw.barrier()

        def emit_odd(l, last):
            jo = l // 2
            w_in = od_w_in[jo, :, :].rearrange("(kc p) f -> p kc f", p=128)
            C_CK, C_CV, C_DK, C_DV, C_LR, C_CQ, C_DQ, C_CG, C_DG = 0, 1024, 2048, 2560, 3584, 3616, 4640, 5152, 6176
            ps7b = ps[7].bitcast(BF16)
            with ExitStack() as stL:
                lrT = sb("lrT", [16, 2, T], BF16, stL)
                gub = sb("gub", [16, 2, 512], BF16, stL)
                fw.dma('pool', gub[:], gu_d[jo, :, :, :].rearrange("d r c -> r d c"), writes=['gub'])
                with ExitStack() as st1:
                    hT = sb("hT", [128, 8, T], BF16, st1)
                    with ExitStack() as st:
                        emit_hT(l, hT, st)
                        fw.barrier()
                    if chk('hT'):
                        return
                    with ExitStack() as st:
                        wt = [sb("wto%d" % i, [128, 8, 512], BF16, st) for i in range(4)]
                        wlr = sb("wlr", [128, 8, 32], BF16, st)
                        ob = [sb("oob%d" % i, [128, 512], BF16, st) for i in range(2)]
                        gs = [sb("ogs%d" % i, [128, 512], F32, st) for i in range(2)]
                        vsn = [sb("vsn%d" % i, [128, 16, 65], BF16, st) for i in range(2)]
                        vs = [sb("ovs%d" % i, [128, D], BF16, st) for i in range(2)]
                        gst = [sb("gst%d" % i, [128, D], F32, st) for i in range(2)]
                        for i in range(2):
                            fw.op('pool', lambda e, i=i: e.memset(vsn[i][:], 1.0), writes=[('vsn', i, 0), ('vsn', i, 1)])
                        wcnt = [0]

                        def load_w(c0, ncol=512):
                            b = wcnt[0] % 4
                            wcnt[0] += 1
                            fw.dma('pool', wt[b][:, :, 0:ncol], w_in[:, :, c0:c0 + ncol], writes=[('wt', b)])
                            return b
                        fw.dma('pool', wlr[:], w_in[:, :, C_LR:C_LR + 32], writes=['wlr'])
                        cnt = 0
                        for dr_ in range(2):
                            for (t0, n) in TB:
                                bank = cnt % 2
                                cnt += 1

                                def mm(e, dr_=dr_, t0=t0, n=n, bank=bank):
                                    r = None
                                    for kc in range(8):
                                        r = e.matmul(ps[bank][0:16, 0:n], lhsT=wlr[:, kc, dr_ * 16:(dr_ + 1) * 16],
                                                     rhs=hT[:, kc, t0:t0 + n], start=(kc == 0), stop=(kc == 7))
                                    return r
                                fw.op('pe', mm, reads=['wlr'] + hT_keys(t0, n), writes=[PSK[bank]])
                                fw.op('act', lambda e, dr_=dr_, t0=t0, n=n, bank=bank: e.activation(
                                    out=lrT[0:16, dr_, t0:t0 + n], in_=ps[bank][0:16, 0:n], func=AF.Copy),
                                    reads=[PSK[bank]], writes=[('lrT', dr_, t0)])
                        plan = [('ck', C_CK, 0), ('ck', C_CK + 512, 4), ('cq', C_CQ, 0), ('cq', C_CQ + 512, 4),
                                ('dk', C_DK, 0), ('dq', C_DQ, 0)]
                        nextb = load_w(plan[0][1])
                        cnt = 0
                        for pi, (kind, c0, f0) in enumerate(plan):
                            b = nextb
                            if pi + 1 < len(plan):
                                nextb = load_w(plan[pi + 1][1])
                            for ft in range(4):
                                fidx = f0 + ft
                                for (t0, n) in TB:
                                    if last and kind in ('cq', 'dq') and t0 >= SEQ:
                                        continue
                                    i = cnt % 2
                                    cnt += 1
                                    bank = i

                                    def mm(e, b=b, ft=ft, t0=t0, n=n, bank=bank):
                                        r = None
                                        for kc in range(8):
                                            r = e.matmul(ps[bank][:, 0:n], lhsT=wt[b][:, kc, ft * 128:(ft + 1) * 128],
                                                         rhs=hT[:, kc, t0:t0 + n], start=(kc == 0), stop=(kc == 7))
                                        return r
                                    fw.op('pe', mm, reads=[('wt', b)] + hT_keys(t0, n), writes=[PSK[bank]])
                                    if kind in ('ck', 'cq'):
                                        sc = 1.0 if kind == 'ck' else 0.125
                                        dst = KT_s if kind == 'ck' else QT_s
                                        fw.op('act', lambda e, i=i, n=n, bank=bank, sc=sc: e.activation(
                                            out=ob[i][:, 0:n], in_=ps[bank][:, 0:n], func=AF.Copy, scale=sc),
                                            reads=[PSK[bank]], writes=[('ob', i)])
                                        fw.dma('sp', dst[fidx, :, t0:t0 + n], ob[i][:, 0:n], reads=[('ob', i)],
                                               writes=[(kind, fidx, t0)])
                                    else:
                                        sc = 1.0 if kind == 'dk' else (128.0 ** -0.5)
                                        gi = fidx if kind == 'dk' else 4 + fidx
                                        fw.op('act', lambda e, i=i, n=n, bank=bank, sc=sc: e.activation(
                                            out=gs[i][:, 0:n], in_=ps[bank][:, 0:n], func=AF.Copy, scale=sc),
                                            reads=[PSK[bank]], writes=[('gs', i)])
                                        fw.dma('sp', GA_s[gi, :, t0:t0 + n], gs[i][:, 0:n], reads=[('gs', i)],
                                               writes=[('GA', gi, t0)])
                        for (kind, c0) in (('cv', C_CV), ('dv', C_DV), ('cg', C_CG), ('dg', C_DG)):
                            bv = [load_w(c0), load_w(c0 + 512)]
                            for tt in range(NTT):
                                if last and kind in ('cg', 'dg') and tt >= 32:
                                    continue
                                i = tt % 2
                                for nb in range(2):
                                    bank = 2 + 2 * i + nb

                                    def mm(e, tt=tt, nb=nb, bank=bank, bv=bv):
                                        r = None
                                        for kc in range(8):
                                            r = e.matmul(ps[bank][:, :], lhsT=hT[:, kc, tt * 128:(tt + 1) * 128],
                                                         rhs=wt[bv[nb]][:, kc, :], start=(kc == 0), stop=(kc == 7))
                                        return r
                                    fw.op('pe', mm, reads=[('wt', bv[nb])] + [('hT', kc, tt) for kc in range(8)],
                                          writes=[PSK[bank]])
                                    if kind == 'cv':
                                        fw.op('act', lambda e, i=i, nb=nb, bank=bank: e.activation(
                                            out=vsn[i][:, nb * 8:(nb + 1) * 8, 0:64],
                                            in_=ps[bank][:, :].rearrange("p (h d) -> p h d", d=64), func=AF.Copy),
                                            reads=[PSK[bank]], writes=[('vsn', i, nb)])
                                    elif kind == 'dv':
                                        fw.op('act', lambda e, i=i, nb=nb, bank=bank: e.activation(
                                            out=vs[i][:, nb * 512:(nb + 1) * 512], in_=ps[bank][:, :], func=AF.Copy),
                                            reads=[PSK[bank]], writes=[('vs', i, nb)])
                                    else:
                                        fw.op('act', lambda e, i=i, nb=nb, bank=bank: e.activation(
                                            out=gst[i][:, nb * 512:(nb + 1) * 512], in_=ps[bank][:, :], func=AF.Silu),
                                            reads=[PSK[bank]], writes=[('gst', i, nb)])
                                rows = slice(tt * 128, (tt + 1) * 128)
                                if kind == 'cv':
                                    fw.dma('sp', VN_s[rows, :], vsn[i][:].rearrange("p h d -> p (h d)"),
                                           reads=[('vsn', i, 0), ('vsn', i, 1)], writes=[('VN', tt)])
                                elif kind == 'dv':
                                    fw.dma('sp', V_s[rows, :], vs[i][:], reads=[('vs', i, 0), ('vs', i, 1)], writes=[('DV', tt)])
                                else:
                                    fw.dma('sp', (CG_s if kind == 'cg' else DG_s)[rows, :], gst[i][:],
                                           reads=[('gst', i, 0), ('gst', i, 1)], writes=[(kind, tt)])
                        fw.barrier()
                if chk('o2'):
                    return
                order_f = [64, 65, 66, 67] + list(range(64))
                order_b = [67, 66, 65, 64] + list(range(63, -1, -1))
                NCH = T // 64
                for hp in range(2):
                    with ExitStack() as st:
                        chains = [(d_, 2 * hp + hl) for d_ in range(2) for hl in range(2)]
                        qtl = [sb("qtl%d" % ci, [128, T], BF16, st) for ci in range(4)]
                        ktl = [sb("ktl%d" % ci, [128, T], BF16, st) for ci in range(4)]
                        Et = [sb("Et%d" % ci, [128, NCH], F32, st) for ci in range(4)]
                        S = [sb("S%d" % ci, [128, 256], F32, st) for ci in range(4)]
                        Sb = [sb("Sb%d" % ci, [128, 256], BF16, st) for ci in range(4)]
                        qf = [sb("qf%d" % i, [128, 512], F32, st) for i in range(2)]
                        kf = [sb("kf%d" % i, [128, 512], F32, st) for i in range(2)]
                        e1 = [sb("e1_%d" % i, [128, 512], F32, st) for i in range(2)]
                        spt = [sb("spt%d" % i, [128, 512], F32, st) for i in range(2)]
                        pit = [sb("pit%d" % i, [128, 512], F32, st) for i in range(2)]
                        eq = [sb("eq%d" % i, [128, 512], F32, st) for i in range(2)]
                        ek = [sb("ek%d" % i, [128, 512], F32, st) for i in range(2)]
                        attb = [sb("attb%d" % ci, [64, 64], BF16, st) for ci in range(4)]
                        ktm = [sb("ktm%d" % ci, [64, 128], BF16, st) for ci in range(4)]
                        vch = [[sb("vch%d_%d" % (d_, i), [64, 512], BF16, st) for i in range(2)] for d_ in range(2)]
                        ost = [[sb("ost%d_%d" % (d_, i), [64, 512], F32, st) for i in range(2)] for d_ in range(2)]
                        cnt = 0
                        for ci, (d_, hh) in enumerate(chains):
                            for (t0, n) in TB:
                                i = cnt % 2
                                cnt += 1
                                nch = n // 64
                                c0 = t0 // 64
                                fw.dma('sp', qf[i][:, 0:n], GA_s[4 + hh, :, t0:t0 + n], reads=[('GA', 4 + hh, t0)],
                                       writes=[('qf', i)])
                                fw.dma('sp', kf[i][:, 0:n], GA_s[hh, :, t0:t0 + n], reads=[('GA', hh, t0)],
                                       writes=[('kf', i)])
                                fw.op('pe', lambda e, d_=d_, hh=hh, t0=t0, n=n: e.matmul(
                                    ps[6][:, 0:n], lhsT=gub[0:16, d_, hh * 128:(hh + 1) * 128], rhs=lrT[0:16, d_, t0:t0 + n],
                                    start=True, stop=True), reads=['gub', ('lrT', d_, t0)], writes=[PSK[6]])
                                gcol = jo * 8 + d_ * 4 + hh
                                fw.op('act', lambda e, i=i, n=n, gcol=gcol: e.activation(
                                    out=e1[i][:, 0:n], in_=ps[6][:, 0:n], func=AF.Exp, scale=-1.0, bias=gbc[:, gcol:gcol + 1]),
                                    reads=[PSK[6], 'gbc'], writes=[('e1', i)])
                                fw.op('act', lambda e, i=i, n=n: e.activation(
                                    out=spt[i][:, 0:n], in_=e1[i][:, 0:n], func=AF.Ln, bias=1.0),
                                    reads=[('e1', i)], writes=[('spt', i)])
                                fw.op('dve', lambda e, i=i, n=n: e.tensor_tensor_scan(
                                    out=pit[i][:, 0:n], data0=cmask[:, 0:n], data1=spt[i][:, 0:n], initial=0.0,
                                    op0=ALU.mult, op1=ALU.add), reads=['cmask', ('spt', i)], writes=[('pit', i)])
                                pv3 = pit[i][:, 0:n].rearrange("p (c s) -> p c s", s=64)
                                fw.op('act', lambda e, ci=ci, c0=c0, nch=nch, pv3=pv3: e.activation(
                                    out=Et[ci][:, c0:c0 + nch], in_=pv3[:, :, 63], func=AF.Exp, scale=-1.0 / 16),
                                    reads=[('pit', i)], writes=[('Et', ci, t0)])
                                if d_ == 1:
                                    fw.op('dve', lambda e, i=i, n=n: e.tensor_tensor(
                                        out=pit[i][:, 0:n], in0=pit[i][:, 0:n], in1=spt[i][:, 0:n], op=ALU.subtract),
                                        reads=[('pit', i), ('spt', i)], writes=[('pit', i)])
                                sq_ = (-1.0 / 16) if d_ == 0 else (1.0 / 16)
                                fw.op('act', lambda e, i=i, n=n, sq_=sq_: e.activation(
                                    out=eq[i][:, 0:n], in_=pit[i][:, 0:n], func=AF.Exp, scale=sq_),
                                    reads=[('pit', i)], writes=[('eq', i)])
                                fw.op('act', lambda e, i=i, n=n, sq_=sq_: e.activation(
                                    out=ek[i][:, 0:n], in_=pit[i][:, 0:n], func=AF.Exp, scale=-sq_),
                                    reads=[('pit', i)], writes=[('ek', i)])
                                fw.op('dve', lambda e, i=i, n=n, ci=ci, t0=t0: e.tensor_tensor(
                                    out=qtl[ci][:, t0:t0 + n], in0=qf[i][:, 0:n], in1=eq[i][:, 0:n], op=ALU.mult),
                                    reads=[('qf', i), ('eq', i)], writes=[('qtl', ci, t0)])
                                fw.op('pool', lambda e, i=i, n=n, ci=ci, t0=t0: e.tensor_tensor(
                                    out=ktl[ci][:, t0:t0 + n], in0=kf[i][:, 0:n], in1=ek[i][:, 0:n], op=ALU.mult),
                                    reads=[('kf', i), ('ek', i)], writes=[('ktl', ci, t0)])
                        for ci in range(4):
                            fw.op('dve', lambda e, ci=ci: e.memset(S[ci][:], 0.0), writes=[('S', ci)])
                            fw.op('pool', lambda e, ci=ci: e.memset(Sb[ci][:], 0.0), writes=[('Sb', ci)])
                        def cinfo(step, d_):
                            c = (order_f if d_ == 0 else order_b)[step]
                            tk = c * 64
                            blk = (tk // 512) * 512 if tk < SEQ else SEQ
                            want = not (last and c >= 64)
                            return c, tk, blk, want

                        def phase_B(step):
                            for d_ in range(2):
                                c, tk, blk, want = cinfo(step, d_)
                                for hl in range(2):
                                    ci = d_ * 2 + hl
                                    if want:
                                        fw.op('pe', lambda e, ci=ci, tk=tk: e.matmul(
                                            ps[ci][0:64, 0:64], lhsT=ktl[ci][:, tk:tk + 64], rhs=qtl[ci][:, tk:tk + 64],
                                            start=True, stop=True), reads=[('qtl', ci, blk), ('ktl', ci, blk)],
                                            writes=[('psa', ci)])
                                    fw.op('pe', lambda e, ci=ci, tk=tk: e.transpose(
                                        out=ps7b[0:64, ci * 128:(ci + 1) * 128], in_=ktl[ci][:, tk:tk + 64], identity=identb[:]),
                                        reads=[('ktl', ci, blk), 'identb'], writes=[('ps7', ci)])

                        phase_B(0)
                        for step in range(NCH):
                            sb_i = step % 2
                            for d_ in range(2):
                                c, tk, blk, want = cinfo(step, d_)
                                fw.dma('sp', vch[d_][sb_i][:], V_s[tk:tk + 64, hp * 512:(hp + 1) * 512],
                                       reads=[('DV', tk // 128)], writes=[('vch', d_, sb_i)])
                                if d_ == 1:
                                    for hl in range(2):
                                        ci = 2 + hl
                                        fw.op('dve', lambda e, ci=ci, c=c: e.tensor_scalar(
                                            out=S[ci][:], in0=S[ci][:], scalar1=Et[ci][:, c:c + 1], scalar2=None, op0=ALU.mult),
                                            reads=[('S', ci), ('Et', ci, blk)], writes=[('S', ci)])
                                        fw.op('pool', lambda e, ci=ci: e.tensor_copy(out=Sb[ci][:], in_=S[ci][:]),
                                              reads=[('S', ci)], writes=[('Sb', ci)])
                            for d_ in range(2):
                                c, tk, blk, want = cinfo(step, d_)
                                for hl in range(2):
                                    ci = d_ * 2 + hl
                                    if want:
                                        fw.op('dve', lambda e, ci=ci, d_=d_: e.tensor_tensor(
                                            out=attb[ci][:], in0=ps[ci][0:64, 0:64], in1=tri[:, d_ * 64:(d_ + 1) * 64], op=ALU.mult),
                                            reads=[('psa', ci), 'tri'], writes=[('attb', ci)])
                                    fw.op('act', lambda e, ci=ci: e.activation(
                                        out=ktm[ci][:], in_=ps7b[0:64, ci * 128:(ci + 1) * 128], func=AF.Copy),
                                        reads=[('ps7', c_) for c_ in range(4)], writes=[('ktm', ci)])
                            if step + 1 < NCH:
                                phase_B(step + 1)
                            for d_ in range(2):
                                c, tk, blk, want = cinfo(step, d_)
                                for hl in range(2):
                                    ci = d_ * 2 + hl
                                    vv = vch[d_][sb_i][:, hl * 256:(hl + 1) * 256]
                                    if want:
                                        def omm(e, ci=ci, tk=tk, vv=vv):
                                            e.matmul(ps[ci][0:64, 128:384], lhsT=qtl[ci][:, tk:tk + 64], rhs=Sb[ci][:],
                                                     start=True, stop=False)
                                            return e.matmul(ps[ci][0:64, 128:384], lhsT=attb[ci][:], rhs=vv,
                                                            start=False, stop=True)
                                        fw.op('pe', omm, reads=[('qtl', ci, blk), ('Sb', ci), ('attb', ci), ('vch', d_, sb_i)],
                                              writes=[('pso', ci)])
                                    kvb = 4 + ci // 2
                                    kvc = (ci % 2) * 256
                                    fw.op('pe', lambda e, ci=ci, vv=vv, kvb=kvb, kvc=kvc: e.matmul(
                                        ps[kvb][:, kvc:kvc + 256], lhsT=ktm[ci][:], rhs=vv, start=True, stop=True),
                                        reads=[('ktm', ci), ('vch', d_, sb_i)], writes=[('pskv', ci)])
                            for d_ in range(2):
                                c, tk, blk, want = cinfo(step, d_)
                                for hl in range(2):
                                    ci = d_ * 2 + hl
                                    kvb = 4 + ci // 2
                                    kvc = (ci % 2) * 256
                                    if want:
                                        fw.op('dve', lambda e, ci=ci, d_=d_, hl=hl, sb_i=sb_i: e.tensor_copy(
                                            out=ost[d_][sb_i][:, hl * 256:(hl + 1) * 256], in_=ps[ci][0:64, 128:384]),
                                            reads=[('pso', ci)], writes=[('ost', d_, sb_i, hl)])
                                    fw.op('dve', lambda e, ci=ci, kvb=kvb, kvc=kvc: e.tensor_tensor(
                                        out=S[ci][:], in0=ps[kvb][:, kvc:kvc + 256], in1=S[ci][:], op=ALU.add),
                                        reads=[('pskv', 2 * (ci // 2)), ('pskv', 2 * (ci // 2) + 1), ('S', ci)], writes=[('S', ci)])
                                    if d_ == 0:
                                        fw.op('dve', lambda e, ci=ci, c=c: e.tensor_scalar(
                                            out=S[ci][:], in0=S[ci][:], scalar1=Et[ci][:, c:c + 1], scalar2=None, op0=ALU.mult),
                                            reads=[('S', ci), ('Et', ci, blk)], writes=[('S', ci)])
                                        fw.op('pool', lambda e, ci=ci: e.tensor_copy(out=Sb[ci][:], in_=S[ci][:]),
                                              reads=[('S', ci)], writes=[('Sb', ci)])
                                if want:
                                    dst = (OF_s if d_ == 0 else OB_s)[tk:tk + 64, hp * 512:(hp + 1) * 512]
                                    fw.dma('sp', dst, ost[d_][sb_i][:], reads=[('ost', d_, sb_i, 0), ('ost', d_, sb_i, 1)],
                                           writes=[('O', d_, c, hp)])
                        fw.barrier()
                if chk('o3'):
                    return
            with ExitStack() as st:
                gnb = sb("gnb", [128, 256], F32, st)
                fw.dma('sp', gnb[:], gnorm_b[jo, :, :], writes=['gnb'])
                oft = [sb("oft%d" % i, [128, D], F32, st) for i in range(2)]
                obt = [sb("obt%d" % i, [128, D], F32, st) for i in range(2)]
                dgt = [sb("dgt%d" % i, [128, D], F32, st) for i in range(2)]
                junk = sb("junk3", [128, 256], BF16, st)
                ssq = sb("ssq3", [128, 8], F32, st)
                ugb = [sb("ugb%d" % i, [128, 8, 128], BF16, st) for i in range(2)]
                ntt = 32 if last else NTT

                def loadc(tt):
                    i = tt % 2
                    rows = slice(tt * 128, (tt + 1) * 128)
                    fw.dma('sp', oft[i][:], OF_s[rows, :], writes=[('oft', i)])
                    fw.dma('sp', obt[i][:], OB_s[rows, :], writes=[('obt', i)])
                    fw.dma('sp', dgt[i][:], DG_s[rows, :], writes=[('dgt', i)])
                loadc(0)
                for tt in range(ntt):
                    i = tt % 2
                    if tt + 1 < ntt:
                        loadc(tt + 1)
                    fw.op('pool', lambda e, i=i: e.tensor_tensor(out=oft[i][:], in0=oft[i][:], in1=obt[i][:], op=ALU.add),
                          reads=[('oft', i), ('obt', i)], writes=[('oft', i)])
                    for hh in range(4):
                        fw.op('act', lambda e, i=i, hh=hh: e.activation(
                            out=junk[:], in_=oft[i][:, hh * 256:(hh + 1) * 256], func=AF.Square,
                            accum_out=ssq[:, i * 4 + hh:i * 4 + hh + 1]), reads=[('oft', i)], writes=['junk3', ('ssq3', i, hh)])
                    fw.op('act', lambda e, i=i: e.activation(out=ssq[:, i * 4:i * 4 + 4], in_=ssq[:, i * 4:i * 4 + 4],
                                                             func=AF.Sqrt, scale=1.0 / 256, bias=EPS),
                          reads=[('ssq3', i, hh) for hh in range(4)], writes=[('rs3', i)])
                    fw.op('dve', lambda e, i=i: e.reciprocal(out=ssq[:, i * 4:i * 4 + 4], in_=ssq[:, i * 4:i * 4 + 4]),
                          reads=[('rs3', i)], writes=[('rs3', i)])
                    for hh in range(4):
                        fw.op('dve', lambda e, i=i, hh=hh: e.scalar_tensor_tensor(
                            out=oft[i][:, hh * 256:(hh + 1) * 256], in0=oft[i][:, hh * 256:(hh + 1) * 256],
                            scalar=ssq[:, i * 4 + hh:i * 4 + hh + 1], in1=gnb[:], op0=ALU.mult, op1=ALU.mult),
                            reads=[('oft', i), ('rs3', i), 'gnb'], writes=[('oft', i)])
                    fw.op('pool', lambda e, i=i: e.tensor_tensor(out=oft[i][:], in0=oft[i][:], in1=dgt[i][:], op=ALU.mult),
                          reads=[('oft', i), ('dgt', i)], writes=[('oft', i)])
                    for half in range(2):
                        bank = 2 * i + half

                        def tr(e, i=i, half=half, bank=bank):
                            r = None
                            for q4 in range(4):
                                fc = half * 4 + q4
                                r = e.transpose(out=ps[bank][:, q4 * 128:(q4 + 1) * 128],
                                                in_=oft[i][:, fc * 128:(fc + 1) * 128], identity=ident[:])
                            return r
                        fw.op('pe', tr, reads=[('oft', i), 'ident'], writes=[PSK[bank]])
                        fw.op('dve', lambda e, i=i, half=half, bank=bank: e.tensor_copy(
                            out=ugb[i][:, half * 4:(half + 1) * 4, :],
                            in_=ps[bank][:, :].rearrange("p (f t) -> p f t", t=128)),
                            reads=[PSK[bank]], writes=[('ugb', i, half)])
                    fw.dma('sp', UT_s[8:16, :, tt * 128:(tt + 1) * 128].rearrange("f p t -> p f t"), ugb[i][:],
                           reads=[('ugb', i, 0), ('ugb', i, 1)], writes=[('UTg', tt)])
                fw.barrier()
            if chk('o3b'):
                return
            for hf in range(2):
                with ExitStack() as st:
                    KN = sb("KN", [128, 4, T], BF16, st)
                    VN = sb("VN", [128, NTT, 8 * 65], BF16, st)
                    TTt = sb("TTt", [128, 8, 21 * 64], BF16, st)
                    for j in range(4):
                        fw.dma('sp', KN[:, j, :], KT_s[hf * 4 + j, :, :], writes=[('KN', j)])
                    VNv = VN_s.rearrange("(tt p) f -> p tt f", p=128)
                    for g in range(NTT // 2):
                        fw.dma('sp', VN[:, 2 * g:2 * g + 2, :], VNv[:, 2 * g:2 * g + 2, hf * 520:(hf + 1) * 520],
                               writes=[('VNt', 2 * g), ('VNt', 2 * g + 1)])
                    for hl in range(8):
                        fw.dma('pool', TTt[:, hl, :], tt_d[jo, hf * 8 + hl, :, :], writes=[('TT', hl)])
                        fw.op('act', lambda e, hl=hl: e.activation(out=TTt[:, hl, :], in_=TTt[:, hl, :], func=AF.Exp),
                              reads=[('TT', hl)], writes=[('TT', hl)])
                    qn = [sb("qn%d" % i, [128, 4, 512], BF16, st) for i in range(2)]
                    cgt = [sb("cgt%d" % i, [64, 512], F32, st) for i in range(2)]
                    pT = [sb("pTn%d" % i, [128, 448], BF16, st) for i in range(3)]
                    un = [sb("un%d" % i, [64, 512], F32, st) for i in range(2)]
                    rz = sb("rz", [64, 16], F32, st)
                    ust = [sb("ust%d" % i, [128, 4, 512], BF16, st) for i in range(2)]
                    rblocks = list(range(8)) + ([] if last else [8])
                    QTv = QT_s.rearrange("f p t -> p f t")

                    def loadq(bi):
                        rb = rblocks[bi]
                        i = bi % 2
                        n = 512 if rb < 8 else CTX
                        fw.dma('sp', qn[i][:, :, 0:n], QTv[:, hf * 4:(hf + 1) * 4, rb * 512:rb * 512 + n], writes=[('qn', i)])
                    items = []
                    rcnt = 0
                    for bi, rb in enumerate(rblocks):
                        nrows = 8 if rb < 8 else 4
                        for rr in range(nrows):
                            ri = rcnt % 2
                            rcnt += 1
                            if rb < 8:
                                r = rb * 8 + rr
                                rs_ = min(max(r - 4, 0), 56)
                                if rs_ % 2 == 0:
                                    tiles = [(rs_ // 2 + k, 2 * (rs_ // 2 + k) - r + 7) for k in range(4)]
                                else:
                                    a0 = (rs_ - 1) // 2
                                    tiles = [(a0, 14)] + [(a0 + k, 2 * (a0 + k) - r + 7) for k in (1, 2, 3)] + [(a0 + 4, 15)]
                            else:
                                tiles = []
                            alltiles = tiles + [(32, None), (33, None)]
                            for hl in range(8):
                                items.append(dict(bi=bi, rb=rb, rr=rr, ri=ri, hl=hl, tiles=alltiles, nrows=nrows))

                    def emit_s(k):
                        it_ = items[k]
                        hl, qi, rr, alltiles = it_['hl'], it_['bi'] % 2, it_['rr'], it_['tiles']
                        j = hl // 2
                        p0 = (hl % 2) * 64
                        sbank = k % 3

                        def smm(e):
                            r_ = None
                            for q_, (a, ti) in enumerate(alltiles):
                                r_ = e.matmul(ps[sbank][:, q_ * 64:(q_ + 1) * 64],
                                              lhsT=KN[p0:p0 + 64, j, a * 128:(a + 1) * 128],
                                              rhs=qn[qi][p0:p0 + 64, j, rr * 64:(rr + 1) * 64],
                                              start=True, stop=True)
                            return r_
                        fw.op('pe', smm, reads=[('KN', j), ('qn', qi)], writes=[PSK[sbank]])

                    loadq(0)
                    if len(rblocks) > 1:
                        loadq(1)
                    PDN = 2
                    for k in range(min(PDN, len(items))):
                        emit_s(k)
                    for k, it_ in enumerate(items):
                        bi, rb, rr, ri, hl, alltiles, nrows = (it_['bi'], it_['rb'], it_['rr'], it_['ri'], it_['hl'],
                                                               it_['tiles'], it_['nrows'])
                        qi = bi % 2
                        ntl = len(alltiles)
                        sbank = k % 3
                        obank = 3 + k % 3
                        pi_ = k % 3
                        if hl == 0:
                            tok0 = rb * 512 + rr * 64
                            fw.dma('sp', cgt[ri][:], CG_s[tok0:tok0 + 64, hf * 512:(hf + 1) * 512], writes=[('cgt', ri)])
                            if rr == 0 and bi >= 1 and bi + 1 < len(rblocks):
                                loadq(bi + 1)
                        fw.op('act', lambda e: e.activation(
                            out=pT[pi_][:, 0:ntl * 64], in_=ps[sbank][:, 0:ntl * 64], func=AF.Exp),
                            reads=[PSK[sbank]], writes=[('pTn', pi_)])
                        nl_ = ntl - 2
                        if nl_ > 0:
                            tv = TTt[:, hl, :].rearrange("p (t c) -> p t c", c=64)
                            if nl_ == 4:
                                t0_ = alltiles[0][1]
                                ebv = tv[:, t0_:t0_ + 7:2, :]
                            else:
                                ebv = tv[:, 16:21, :]
                            pv_ = pT[pi_][:, 0:nl_ * 64].rearrange("p (t c) -> p t c", c=64)
                            fw.op('dve', lambda e: e.tensor_tensor(out=pv_, in0=pv_, in1=ebv, op=ALU.mult),
                                  reads=[('pTn', pi_), ('TT', hl)], writes=[('pTn', pi_)])
                        if k + PDN < len(items):
                            emit_s(k + PDN)

                        def pvm(e):
                            r_ = None
                            for q_, (a, ti) in enumerate(alltiles):
                                r_ = e.matmul(ps[obank][0:64, 0:65], lhsT=pT[pi_][:, q_ * 64:(q_ + 1) * 64],
                                              rhs=VN[:, a, hl * 65:(hl + 1) * 65], start=(q_ == 0),
                                              stop=(q_ == len(alltiles) - 1))
                            return r_
                        fw.op('pe', pvm, reads=[('pTn', pi_)] + [('VNt', a) for (a, ti) in alltiles], writes=[PSK[obank]])
                        fw.op('dve', lambda e: e.reciprocal(
                            out=rz[:, ri * 8 + hl:ri * 8 + hl + 1], in_=ps[obank][0:64, 64:65]),
                            reads=[PSK[obank]], writes=[('rz', ri, hl)])
                        fw.op('dve', lambda e: e.tensor_scalar(
                            out=un[ri][:, hl * 64:(hl + 1) * 64], in0=ps[obank][0:64, 0:64],
                            scalar1=rz[:, ri * 8 + hl:ri * 8 + hl + 1], scalar2=None, op0=ALU.mult),
                            reads=[PSK[obank], ('rz', ri, hl)], writes=[('un', ri, hl)])
                        if hl != 7:
                            continue
                        unk = [('un', ri, h_) for h_ in range(8)]
                        fw.op('pool', lambda e: e.tensor_tensor(out=un[ri][:], in0=un[ri][:], in1=cgt[ri][:], op=ALU.mult),
                              reads=unk + [('cgt', ri)], writes=[('ung', ri)] + unk)

                        def tr(e):
                            r_ = None
                            for j_ in range(4):
                                r_ = e.transpose(out=ps[6][:, j_ * 64:(j_ + 1) * 64], in_=un[ri][:, j_ * 128:(j_ + 1) * 128],
                                                 identity=ident[0:64, 0:64])
                            return r_
                        fw.op('pe', tr, reads=[('ung', ri), 'ident'] + unk, writes=[PSK[6]])
                        fw.op('dve', lambda e: e.tensor_copy(
                            out=ust[qi][:, :, rr * 64:(rr + 1) * 64], in_=ps[6][:, 0:256].rearrange("p (f t) -> p f t", t=64)),
                            reads=[PSK[6]], writes=[('ust', qi, rr)])
                        if rr != nrows - 1:
                            continue
                        n = 512 if rb < 8 else CTX
                        fw.dma('sp', UT_s[hf * 4:(hf + 1) * 4, :, rb * 512:rb * 512 + n].rearrange("f p t -> p f t"),
                               ust[qi][:, :, 0:n], reads=[('ust', qi, r_) for r_ in range(nrows)], writes=[('UTn', hf, rb)])
                    fw.barrier()

        def emit_outproj(l, last):
            with ExitStack() as st:
                wo = sb("wo", [128, 16, D], BF16, st)
                wov = w_out[l, :, :].rearrange("(fc p) n -> p fc n", p=128)
                for g in range(4):
                    fw.dma('pool', wo[:, 4 * g:4 * g + 4, :], wov[:, 4 * g:4 * g + 4, :], writes=[('wo', g)])
                wok = [('wo', g) for g in range(4)]
                ut = [sb("ut%d" % i, [128, 16, 512], BF16, st) for i in range(2)]
                xr = [sb("xr%d" % i, [128, D], F32, st) for i in range(2)]
                tn = [sb("tn%d" % i, [128, D], F32, st) for i in range(2)]
                xo = [sb("xo%d" % i, [128, D], F32, st) for i in range(2)]
                junk = sb("junk2", [128, 512], BF16, st)
                ss2 = sb("ss2", [128, 4], F32, st)
                rr = sb("rr", [128, 2], F32, st)
                acs2 = sb("acs2", [128, 2], F32, st)
                blocks = [tb for tb in TB if not (last and tb[0] >= SEQ)]
                UTv = UT_s.rearrange("f p t -> p f t")

                def loadu(bi):
                    t0, n = blocks[bi]
                    i = bi % 2
                    fw.dma('sp', ut[i][:, :, 0:n], UTv[:, :, t0:t0 + n],
                           reads=[('UT', f, t0) for f in range(16)] + [('UT', f, 0) for f in range(8, 16)] +
                           [('UT', f, SEQ) for f in range(8, 16)], writes=[('ut', i)])
                loadu(0)
                tcnt = 0
                for bi, (t0, n) in enumerate(blocks):
                    i = bi % 2
                    if bi + 1 < len(blocks):
                        loadu(bi + 1)
                    for stl in range(n // 128):
                        tt = (t0 // 128) + stl
                        j = 0 if tt < 32 else 1
                        c = tcnt % 2
                        tcnt += 1
                        src = (x_in if l == first_layer else out_d)[tt * 128:(tt + 1) * 128, :] if tt < 32 else \
                            (ctx_in if l == first_layer else xc_d)[(tt - 32) * 128:(tt - 31) * 128, :]
                        dst = out_d[tt * 128:(tt + 1) * 128, :] if tt < 32 else xc_d[(tt - 32) * 128:(tt - 31) * 128, :]
                        fw.dma('sp', xr[c][:], src, reads=[('X', tt)], writes=[('xr', c)])
                        for nb in range(2):
                            bank = 2 * c + nb

                            def mm(e, i=i, stl=stl, nb=nb, bank=bank):
                                r = None
                                for fc in range(16):
                                    r = e.matmul(ps[bank][:, :], lhsT=ut[i][:, fc, stl * 128:(stl + 1) * 128],
                                                 rhs=wo[:, fc, nb * 512:(nb + 1) * 512], start=(fc == 0), stop=(fc == 15))
                                return r
                            fw.op('pe', mm, reads=wok + [('ut', i)], writes=[PSK[bank]])
                            def sqa(e, bank=bank, c=c, nb=nb):
                                return e.activation(out=junk[:], in_=ps[bank][:, :], func=AF.Square,
                                                    accum_out=ss2[:, 2 * c + nb:2 * c + nb + 1])
                            fw.op('act', sqa, reads=[PSK[bank]], writes=['junk2', ('ss2', c, nb)])
                        fw.op('dve', lambda e, c=c: e.tensor_tensor(out=rr[:, c:c + 1], in0=ss2[:, 2 * c:2 * c + 1],
                                                                    in1=ss2[:, 2 * c + 1:2 * c + 2], op=ALU.add),
                              reads=[('ss2', c, 0), ('ss2', c, 1)], writes=[('rr', c)])
                        fw.op('act', lambda e, c=c: e.activation(out=rr[:, c:c + 1], in_=rr[:, c:c + 1], func=AF.Sqrt,
                                                                 scale=1.0 / D, bias=EPS),
                              reads=[('rr', c)], writes=[('rr', c)])
                        fw.op('dve', lambda e, c=c: e.reciprocal(out=rr[:, c:c + 1], in_=rr[:, c:c + 1]),
                              reads=[('rr', c)], writes=[('rr', c)])
                        for nb in range(2):
                            bank = 2 * c + nb
                            fw.op('dve', lambda e, c=c, nb=nb, bank=bank, j=j: e.scalar_tensor_tensor(
                                out=tn[c][:, nb * 512:(nb + 1) * 512], in0=ps[bank][:, :], scalar=rr[:, c:c + 1],
                                in1=GG[:, j, nb * 512:(nb + 1) * 512], op0=ALU.mult, op1=ALU.mult),
                                reads=[PSK[bank], ('rr', c), ('GG', j, nb)], writes=[('tn', c, nb)])
                        fw.op('pool', lambda e, c=c: e.tensor_tensor(out=xo[c][:], in0=tn[c][:], in1=xr[c][:], op=ALU.add),
                              reads=[('tn', c, 0), ('tn', c, 1), ('xr', c)], writes=[('xo', c)])
                        fw.dma('sp', dst, xo[c][:], reads=[('xo', c)], writes=[('X', tt)])
                fw.barrier()

        stopped = [False]

        def chk(name):
            if stop_after == name:
                stopped[0] = True
            return stopped[0]

        if True:
          for l in range(first_layer, n_layers):
            last = (l == DEPTH - 1)
            emit_mod(l)
            if chk('mod'):
                break
            if l % 2 == 0:
                emit_even(l, last)
            else:
                emit_odd(l, last)
            if stopped[0]:
                break
            emit_outproj(l, last)
        fw.barrier()
        print("ops emitted:", fw.nops, "sems:", fw.nsem)
    return nc


_NC_CACHE = {}


def _prep_inputs(inp, b):
    cosT, sinT, rm = _rope_tables()
    m = {}
    m["x"] = np.ascontiguousarray(inp["x"][b])
    m["ctx"] = np.ascontiguousarray(inp["ctx"][b])
    m["cvec"] = np.ascontiguousarray(np.concatenate([_col(inp["c"][b]), _col(inp["c_ctx"])], axis=1))
    m["w_mod"] = inp["w_mod"]
    m["bmod_col"] = np.ascontiguousarray(np.concatenate([_col(inp["b_mod"][l]) for l in range(DEPTH)], axis=1))
    m["bmod_gate"] = np.ascontiguousarray(np.broadcast_to(inp["b_mod"][:, None, 2 * D:], (DEPTH, 128, D)))
    m["gpre_col"] = np.ascontiguousarray(np.concatenate([_col(inp["g_pre"][l]) for l in range(DEPTH)], axis=1))
    m["gpost_b"] = np.ascontiguousarray(np.broadcast_to(inp["g_post"][:, None, :], (DEPTH, 128, D)))
    m["w_out"] = inp["w_out"]
    m["ev_w_in"] = inp["ev_w_in"]
    m["lam_b"] = np.ascontiguousarray(np.broadcast_to(inp["ev_lambda"].reshape(1, 512), (128, 512)))
    m["subln_col"] = np.ascontiguousarray(inp["ev_subln"].T)
    m["conv_col"] = np.ascontiguousarray(
        np.concatenate([_col(inp["ev_conv"][j, tap]) for j in range(2) for tap in range(3)], axis=1))
    m["od_w_in"] = inp["od_w_in"]
    m["tt_tab"] = _na_tables(inp["od_rpb"])
    m["gate_up"] = inp["od_gate_up"]
    m["gb_col"] = np.ascontiguousarray(np.concatenate(
        [_col(inp["od_gate_bias"][j, d_]) for j in range(2) for d_ in range(2)], axis=1))
    m["gnorm_b"] = np.ascontiguousarray(np.broadcast_to(inp["od_gnorm"][:, None, :], (2, 128, 256)))
    cm = np.ones((128, 512), np.float32)
    cm[:, ::64] = 0.0
    m["cmask"] = cm
    si = np.arange(64)
    m["tri"] = np.ascontiguousarray(np.concatenate([(si[:, None] <= si[None, :]), (si[:, None] >= si[None, :])],
                                                  axis=1).astype(np.float32))
    m["cosT"] = cosT
    m["sinT"] = sinT
    m["rm"] = rm
    m["ident"] = np.eye(128, dtype=np.float32)
    return m


def kernel(**inputs):
    inp = {k: np.asarray(v) for k, v in inputs.items()}
    if "nc" not in _NC_CACHE:
        _NC_CACHE["nc"] = build()
    nc = _NC_CACHE["nc"]
    in_maps = [_prep_inputs(inp, b) for b in range(8)]
    res = run_bass_kernel_spmd(nc, in_maps, core_ids=list(range(8)))
    return np.stack([r["out"] for r in res.results], axis=0).astype(np.float32)
```

```python
import math
from contextlib import ExitStack
import numpy as np
import concourse.bass as bass
import concourse.mybir as mybir
from concourse.bass_utils import run_bass_kernel_spmd

F32 = mybir.dt.float32
BF16 = mybir.dt.bfloat16
AF = mybir.ActivationFunctionType
ALU = mybir.AluOpType
AX = mybir.AxisListType

D = 1024
SEQ = 4096
CTX = 256
T = SEQ + CTX
NTT = T // 128
DEPTH = 4
EPS = 1e-6
GRID = 64
SAME_ENG_SYNC = True


class Fw:
    SEM_LIMIT = 30000

    def __init__(self, nc, es, n_dma_slots=12):
        self.nc = nc
        self.es = es
        self.E = {'pe': nc.tensor, 'dve': nc.vector, 'act': nc.scalar, 'pool': nc.gpsimd, 'sp': nc.sync}
        self.cur = {}
        self.nsem = 0
        for e in self.E:
            self.cur[e] = [self._newsem(e), 0]
        self.known = {e: {} for e in self.E}
        self.last_w = {}
        self.readers = {}
        self.slots = {}
        for q in ('sp', 'pool', 'act'):
            n = n_dma_slots if q != 'act' else 4
            self.slots[q] = [[self._newsem('d' + q), 0] for _ in range(n)]
        self.slot_rr = {q: 0 for q in self.slots}
        self.nops = 0

    def _newsem(self, tag):
        self.nsem += 1
        return self.es.enter_context(self.nc.semaphore("s%s%d" % (tag, self.nsem)))

    def _wait(self, eng, tok):
        sem, val, teng = tok
        kn = self.known[eng]
        k = id(sem)
        if kn.get(k, 0) >= val:
            return
        self.E[eng].wait_ge(sem, val)
        kn[k] = val

    def _deps(self, eng, reads, writes):
        deps = {}

        def addtok(t, hazard):
            if t is None:
                return
            sem, val, teng = t
            if teng == eng and teng != 'dma':
                if eng == 'pe' or not SAME_ENG_SYNC:
                    return
            k = id(sem)
            if k not in deps or deps[k][1] < val:
                deps[k] = t
        for r in reads:
            addtok(self.last_w.get(r), 'raw')
        for w in writes:
            addtok(self.last_w.get(w), 'waw')
            for t in self.readers.get(w, ()):
                addtok(t, 'war')
        return deps.values()

    def _commit(self, tok, reads, writes):
        for r in reads:
            self.readers.setdefault(r, []).append(tok)
        for w in writes:
            self.last_w[w] = tok
            self.readers[w] = []

    def op(self, eng, fn, reads=(), writes=()):
        for t in self._deps(eng, reads, writes):
            self._wait(eng, t)
        ins = fn(self.E[eng])
        c = self.cur[eng]
        if c[1] >= self.SEM_LIMIT:
            c[0] = self._newsem(eng)
            c[1] = 0
        c[1] += 1
        ins.then_inc(c[0], 1)
        tok = (c[0], c[1], eng)
        self._commit(tok, reads, writes)
        self.nops += 1
        return tok

    def dma(self, q, out, in_, reads=(), writes=()):
        for t in self._deps(q, reads, writes):
            self._wait(q, t)
        sl = self.slots[q]
        i = self.slot_rr[q]
        self.slot_rr[q] = (i + 1) % len(sl)
        s = sl[i]
        if s[1] > 0:
            self._wait(q, (s[0], s[1], 'dma'))
        if s[1] >= self.SEM_LIMIT:
            s[0] = self._newsem('d' + q)
            s[1] = 0
        ins = self.E[q].dma_start(out=out, in_=in_)
        s[1] += 16
        ins.then_inc(s[0], 16)
        tok = (s[0], s[1], 'dma')
        self._commit(tok, reads, writes)
        self.nops += 1
        return tok

    def barrier(self):
        toks = []
        for e in self.E:
            c = self.cur[e]
            if c[1] > 0:
                toks.append((c[0], c[1], e))
        for q in self.slots:
            for s in self.slots[q]:
                if s[1] > 0:
                    toks.append((s[0], s[1], 'dma'))
        for e in self.E:
            for t in toks:
                if t[2] == e:
                    continue
                self._wait(e, t)
        self.last_w = {}
        self.readers = {}


def _rope_tables():
    half = 32
    inv = (1.0 / (10000.0 ** (np.arange(0, half, 2, dtype=np.float32) / np.float32(half)))).astype(np.float32)
    t = np.arange(SEQ)
    row = (t // GRID).astype(np.float32)[:, None] * inv
    col = (t % GRID).astype(np.float32)[:, None] * inv
    ang = np.concatenate([row, row, col, col], axis=-1).astype(np.float32)
    cos = np.cos(ang).astype(np.float32).T
    sin = np.sin(ang).astype(np.float32).T
    sign = np.ones(64, np.float32)
    src = np.zeros(64, np.int64)
    for f in range(64):
        blk = (f // 32) * 32
        o = f % 32
        if o < 16:
            src[f] = blk + o + 16
            sign[f] = -1.0
        else:
            src[f] = blk + o - 16
            sign[f] = 1.0
    sin_s = sin * sign[:, None]
    cosT = np.concatenate([cos, cos], axis=0)
    sinT = np.concatenate([sin_s, sin_s], axis=0)
    rm = np.zeros((128, 128), np.float32)
    for m in range(2):
        for f in range(64):
            rm[m * 64 + src[f], m * 64 + f] = 1.0
    return np.ascontiguousarray(cosT), np.ascontiguousarray(sinT), rm


def _na_tables(rpb):
    NEG = np.float32(-30000.0)
    kc = np.arange(64)[:, None]
    c = np.arange(64)[None, :]
    ws = np.clip(c - 8, 0, 48)
    valid = (kc >= ws) & (kc < ws + 16)
    coff = np.clip(kc - c + 15, 0, 30)
    out = np.full((2, 16, 2, 64, 16, 64), NEG, np.float32)
    for ti in range(16):
        for par in range(2):
            if ti < 14:
                dr = ti + par
            elif ti == 14:
                dr = 3 if par == 1 else None
            else:
                dr = 10 if par == 0 else None
            if dr is None or dr > 14:
                continue
            g = rpb[:, :, dr, :][:, :, coff]
            out[:, :, par, :, ti, :] = np.where(valid[None, None], g, NEG)
    seq = np.stack([out[..., 14, :], out[..., 4, :], out[..., 6, :], out[..., 8, :], out[..., 15, :]], axis=-2)
    out = np.concatenate([out, seq], axis=-2)
    return np.ascontiguousarray(out.reshape(2, 16, 128, 21 * 64))


def _col(v):
    return np.ascontiguousarray(v.reshape(-1, 128).T)


class _Stop(Exception):
    pass


def build(n_layers=DEPTH, dbg=False, stop_after=None, first_layer=0):
    nc = bass.Bass("TRN2", target_bir_lowering=False)

    def din(name, shape, dt=F32):
        return nc.dram_tensor(name, list(shape), dt, kind="ExternalInput").ap()

    def dscr(name, shape, dt):
        return nc.dram_tensor(name, list(shape), dt, kind="Internal").ap()

    x_in = din("x", [SEQ, D])
    ctx_in = din("ctx", [CTX, D])
    cvec = din("cvec", [128, 16])
    w_mod = din("w_mod", [DEPTH, D, 3 * D])
    bmod_col = din("bmod_col", [128, DEPTH * 24])
    bmod_gate = din("bmod_gate", [DEPTH, 128, D])
    gpre_col = din("gpre_col", [128, DEPTH * 8])
    gpost_b = din("gpost_b", [DEPTH, 128, D])
    w_out = din("w_out", [DEPTH, 2 * D, D])
    ev_w_in = din("ev_w_in", [2, D, 8 * D])
    lam_b = din("lam_b", [128, 2 * 256])
    subln_col = din("subln_col", [128, 2])
    conv_col = din("conv_col", [128, 2 * 3 * 8])
    cosT_d = din("cosT", [128, SEQ])
    sinT_d = din("sinT", [128, SEQ])
    rm_d = din("rm", [128, 128])
    ident_d = din("ident", [128, 128])
    od_w_in = din("od_w_in", [2, D, 7200])
    tt_d = din("tt_tab", [2, 16, 128, 21 * 64])
    gu_d = din("gate_up", [2, 2, 16, 512])
    gb_col = din("gb_col", [128, 16])
    gnorm_b = din("gnorm_b", [2, 128, 256])
    cmask_d = din("cmask", [128, 512])
    tri_d = din("tri", [64, 128])
    out_d = nc.dram_tensor("out", [SEQ, D], F32, kind="ExternalOutput").ap()
    xc_d = nc.dram_tensor("xc_out", [CTX, D], F32, kind="ExternalOutput" if dbg else "Internal").ap()

    KT_s = dscr("KT_s", [8, 128, T], BF16)
    QT_s = dscr("QT_s", [8, 128, T], BF16)
    V_s = dscr("V_s", [T, D], BF16)
    GA_s = dscr("GA_s", [8, 128, T], F32)
    UT_s = dscr("UT_s", [16, 128, T], BF16)
    VN_s = dscr("VN_s", [T, 16 * 65], BF16)
    CG_s = dscr("CG_s", [T, D], F32)
    DG_s = dscr("DG_s", [T, D], F32)
    OF_s = dscr("OF_s", [T, D], F32)
    OB_s = dscr("OB_s", [T, D], F32)

    es_top = ExitStack()
    with es_top as es:
        fw = Fw(nc, es)

        uid = [0]

        def sb(name, shape, dt, stack=None):
            uid[0] += 1
            return (stack or es).enter_context(nc.sbuf_tensor("sb_%s_%d" % (name, uid[0]), list(shape), dt))

        ps = [es.enter_context(nc.psum_tensor("ps%d" % i, [128, 512], F32)) for i in range(8)]
        PSK = [('ps', i) for i in range(8)]

        ident = sb("ident", [128, 128], F32)
        ones_f = sb("ones_f", [128, 128], F32)
        ones_b = sb("ones_b", [128, 128], BF16)
        rm_b = sb("rm_b", [128, 128], BF16)
        cs = sb("cs", [128, 16], F32)
        csb = sb("csb", [128, 16, 128], F32)
        bmodc = sb("bmodc", [128, DEPTH * 24], F32)
        gprec = sb("gprec", [128, DEPTH * 8], F32)
        lamt = sb("lamt", [128, 512], F32)
        lam2 = sb("lam2", [128, 4, 64], F32)
        lsum = sb("lsum", [128, 4], F32)
        lam_c = sb("lam_c", [128, 2], F32)
        sublc = sb("sublc", [128, 2], F32)
        convc = sb("convc", [128, 48], F32)
        A_m = sb("A_m", [128, 2, 8], F32)
        B_m = sb("B_m", [128, 2, 8], F32)
        sc_m = sb("sc_m", [128, 2, 8], F32)
        GG = sb("GG", [128, 2, D], F32)

        identb = sb("identb", [128, 128], BF16)
        cmask = sb("cmask", [128, 512], F32)
        tri = sb("tri", [64, 128], F32)
        gbc = sb("gbc", [128, 16], F32)
        fw.dma('pool', identb[:], ident_d[:, :], writes=['identb'])
        fw.dma('sp', cmask[:], cmask_d[:, :], writes=['cmask'])
        fw.dma('sp', tri[:], tri_d[:, :], writes=['tri'])
        fw.dma('sp', gbc[:], gb_col[:, :], writes=['gbc'])
        fw.op('dve', lambda e: e.tensor_scalar(out=gbc[:], in0=gbc[:], scalar1=-1.0, scalar2=None, op0=ALU.mult),
              reads=['gbc'], writes=['gbc'])
        fw.dma('sp', ident[:], ident_d[:, :], writes=['ident'])
        fw.dma('pool', rm_b[:], rm_d[:, :], writes=['rm_b'])
        fw.dma('sp', cs[:], cvec[:, :], writes=['cs'])
        fw.dma('sp', bmodc[:], bmod_col[:, :], writes=['bmodc'])
        fw.dma('sp', gprec[:], gpre_col[:, :], writes=['gprec'])
        fw.dma('sp', lamt[:], lam_b[:, :], writes=['lamt'])
        fw.dma('sp', sublc[:], subln_col[:, :], writes=['sublc'])
        fw.dma('sp', convc[:], conv_col[:, :], writes=['convc'])
        epsc = sb("epsc", [128, 1], F32)
        fw.op('dve', lambda e: e.memset(epsc[:], EPS), writes=['epsc'])
        fw.op('dve', lambda e: e.memset(ones_f[:], 1.0), writes=['ones_f'])
        fw.op('dve', lambda e: e.memset(ones_b[:], 1.0), writes=['ones_b'])
        fw.op('act', lambda e: e.activation(out=cs[:], in_=cs[:], func=AF.Silu), reads=['cs'], writes=['cs'])
        for k in range(16):
            fw.op('dve', lambda e, k=k: e.tensor_scalar(out=csb[:, k, :], in0=ones_f[:], scalar1=cs[:, k:k + 1],
                                                        scalar2=None, op0=ALU.mult),
                  reads=['ones_f', 'cs'], writes=[('csb', k)])
        lt = lamt[:].rearrange("p (j a d) -> p j a d", j=2, a=4)
        for j in range(2):
            for a in range(2):
                fw.op('dve', lambda e, j=j, a=a: e.tensor_tensor(out=lam2[:, j * 2 + a, :], in0=lt[:, j, 2 * a, :],
                                                                 in1=lt[:, j, 2 * a + 1, :], op=ALU.mult),
                      reads=['lamt'], writes=[('lam2', j, a)])
                fw.op('dve', lambda e, j=j, a=a: e.tensor_reduce(out=lsum[:, j * 2 + a:j * 2 + a + 1],
                                                                 in_=lam2[:, j * 2 + a, :], axis=AX.X, op=ALU.add),
                      reads=[('lam2', j, a)], writes=[('lsum', j, a)])
        fw.op('act', lambda e: e.activation(out=lsum[:], in_=lsum[:], func=AF.Exp),
              reads=[('lsum', j, a) for j in range(2) for a in range(2)], writes=['lsume'])
        for j in range(2):
            lam_init = 0.8 - 0.6 * math.exp(-0.3 * (2 * j))
            fw.op('dve', lambda e, j=j, li=lam_init: e.scalar_tensor_tensor(
                out=lam_c[:, j:j + 1], in0=lsum[:, 2 * j + 1:2 * j + 2], scalar=-li, in1=lsum[:, 2 * j:2 * j + 1],
                op0=ALU.add, op1=ALU.subtract), reads=['lsume'], writes=[('lam_c', j)])
            fw.op('dve', lambda e, j=j, li=lam_init: e.tensor_scalar(
                out=sublc[:, j:j + 1], in0=sublc[:, j:j + 1], scalar1=(1.0 - li), scalar2=None, op0=ALU.mult),
                reads=['sublc'], writes=['sublc'])

        def emit_mod(l):
            with ExitStack() as st:
                wm = [sb("wm%d" % i, [128, 8, 512], F32, st) for i in range(2)]
                bg = sb("bg", [128, D], F32, st)
                gp = sb("gp", [128, D], F32, st)
                tmpg = sb("tmpg", [128, 512], F32, st)
                fw.dma('sp', bg[:], bmod_gate[l, :, :], writes=['bg'])
                fw.dma('sp', gp[:], gpost_b[l, :, :], writes=['gp'])
                wv = w_mod[l, :, :].rearrange("(kc p) n -> p kc n", p=128)
                for blk in range(6):
                    b = blk % 2
                    fw.dma('sp', wm[b][:], wv[:, :, blk * 512:(blk + 1) * 512], writes=[('wm', b)])
                    if blk < 4:
                        for n4 in range(4):
                            n = blk * 4 + n4

                            def mm(e, n=n, n4=n4, b=b):
                                r = None
                                for kc in range(8):
                                    r = e.matmul(ps[0][:, 2 * n:2 * n + 2], lhsT=wm[b][:, kc, n4 * 128:(n4 + 1) * 128],
                                                 rhs=cs[:, kc:16:8], start=(kc == 0), stop=(kc == 7))
                                return r
                            fw.op('pe', mm, reads=[('wm', b), 'cs'], writes=[('modps', n)] + ([PSK[0]] if n == 0 else []))
                    else:
                        nb = blk - 4
                        for j in range(2):
                            bank = 1 + j

                            def mm(e, j=j, b=b, bank=bank):
                                r = None
                                for kc in range(8):
                                    r = e.matmul(ps[bank][:, :], lhsT=csb[:, j * 8 + kc, :], rhs=wm[b][:, kc, :],
                                                 start=(kc == 0), stop=(kc == 7))
                                return r
                            fw.op('pe', mm, reads=[('wm', b)] + [('csb', j * 8 + kc) for kc in range(8)], writes=[PSK[bank]])
                            fw.op('dve', lambda e, nb=nb, bank=bank: e.tensor_tensor(
                                out=tmpg[:], in0=ps[bank][:, :], in1=bg[:, nb * 512:(nb + 1) * 512], op=ALU.add),
                                reads=[PSK[bank], 'bg'], writes=['tmpg'])
                            fw.op('dve', lambda e, nb=nb, j=j: e.tensor_tensor(
                                out=GG[:, j, nb * 512:(nb + 1) * 512], in0=tmpg[:], in1=gp[:, nb * 512:(nb + 1) * 512],
                                op=ALU.mult), reads=['tmpg', 'gp'], writes=[('GG', j, nb)])
                pv = ps[0][:, 0:32].rearrange("p (n j) -> p n j", j=2)
                allmod = [('modps', n) for n in range(16)]
                for j in range(2):
                    fw.op('dve', lambda e, j=j: e.tensor_tensor(out=B_m[:, j, :], in0=pv[:, 0:8, j],
                                                                in1=bmodc[:, l * 24:l * 24 + 8], op=ALU.add),
                          reads=allmod + ['bmodc'], writes=[('B_m', j)])
                    fw.op('dve', lambda e, j=j: e.tensor_tensor(out=sc_m[:, j, :], in0=pv[:, 8:16, j],
                                                                in1=bmodc[:, l * 24 + 8:l * 24 + 16], op=ALU.add),
                          reads=allmod + ['bmodc'], writes=[('sc_m', j)])
                    fw.op('dve', lambda e, j=j: e.scalar_tensor_tensor(
                        out=A_m[:, j, :], in0=sc_m[:, j, :], scalar=1.0, in1=gprec[:, l * 8:(l + 1) * 8],
                        op0=ALU.add, op1=ALU.mult), reads=[('sc_m', j), 'gprec'], writes=[('A_m', j)])
                fw.barrier()

        def emit_hT(l, hT, st):
            xt = [sb("xt%d" % i, [128, D], F32, st) for i in range(2)]
            xn = [sb("xn%d" % i, [128, D], F32, st) for i in range(2)]
            junk = sb("junk", [128, D], BF16, st)
            ssq = sb("ssq", [128, 2], F32, st)
            rsd = sb("rsd", [128, 2], F32, st)
            acs = sb("acs", [128, 2], F32, st)

            def load(tt):
                b = tt % 2
                if tt < 32:
                    src = (x_in if l == first_layer else out_d)[tt * 128:(tt + 1) * 128, :]
                else:
                    src = (ctx_in if l == first_layer else xc_d)[(tt - 32) * 128:(tt - 31) * 128, :]
                fw.dma('sp', xt[b][:], src, reads=[('X', tt)], writes=[('xt', b)])
            load(0)
            for tt in range(NTT):
                b = tt % 2
                j = 0 if tt < 32 else 1
                if tt + 1 < NTT:
                    load(tt + 1)
                def sqa(e, b=b):
                    return e.activation(out=junk[:], in_=xt[b][:], func=AF.Square, accum_out=ssq[:, b:b + 1])
                fw.op('act', sqa, reads=[('xt', b)], writes=['junk', ('ssq', b)])
                fw.op('act', lambda e, b=b: e.activation(out=rsd[:, b:b + 1], in_=ssq[:, b:b + 1], func=AF.Sqrt,
                                                         scale=1.0 / D, bias=EPS),
                      reads=[('ssq', b)], writes=[('rsd', b)])
                fw.op('dve', lambda e, b=b: e.reciprocal(out=rsd[:, b:b + 1], in_=rsd[:, b:b + 1]),
                      reads=[('rsd', b)], writes=[('rsd', b)])
                fw.op('dve', lambda e, b=b: e.tensor_scalar(out=xn[b][:], in0=xt[b][:], scalar1=rsd[:, b:b + 1],
                                                            scalar2=None, op0=ALU.mult),
                      reads=[('xt', b), ('rsd', b)], writes=[('xn', b)])
                for half in range(2):
                    bank = 2 * b + half

                    def tr(e, b=b, half=half, bank=bank):
                        r = None
                        for q4 in range(4):
                            kc = half * 4 + q4
                            r = e.transpose(out=ps[bank][:, q4 * 128:(q4 + 1) * 128],
                                            in_=xn[b][:, kc * 128:(kc + 1) * 128], identity=ident[:])
                        return r
                    fw.op('pe', tr, reads=[('xn', b), 'ident'], writes=[PSK[bank]])
                    for q4 in range(4):
                        kc = half * 4 + q4
                        fw.op('dve', lambda e, kc=kc, q4=q4, bank=bank, tt=tt, j=j: e.tensor_scalar(
                            out=hT[:, kc, tt * 128:(tt + 1) * 128], in0=ps[bank][:, q4 * 128:(q4 + 1) * 128],
                            scalar1=A_m[:, j, kc:kc + 1], scalar2=B_m[:, j, kc:kc + 1], op0=ALU.mult, op1=ALU.add),
                            reads=[PSK[bank], ('A_m', j), ('B_m', j)], writes=[('hT', kc, tt)])

        TB = [(i * 512, 512) for i in range(8)] + [(SEQ, CTX)]

        def hT_keys(t0, n):
            return [('hT', kc, tt) for kc in range(8) for tt in range(t0 // 128, (t0 + n) // 128)]

        def emit_even(l, last):
            je = l // 2
            w_in = ev_w_in[je, :, :].rearrange("(kc p) f -> p kc f", p=128)
            with ExitStack() as st1:
                hT = sb("hT", [128, 8, T], BF16, st1)
                with ExitStack() as st:
                    emit_hT(l, hT, st)
                    fw.barrier()
                if chk('hT'):
                    return
                with ExitStack() as st:
                    cosT = sb("cosT", [128, SEQ], F32, st)
                    sinT = sb("sinT", [128, SEQ], F32, st)
                    fw.dma('sp', cosT[:], cosT_d[:, :], writes=['cosT'])
                    fw.dma('sp', sinT[:], sinT_d[:, :], writes=['sinT'])
                    wt = [sb("wt%d" % i, [128, 8, 512], BF16, st) for i in range(2)]
                    qb = [sb("qb%d" % i, [128, 512], BF16, st) for i in range(2)]
                    t1 = [sb("t1_%d" % i, [128, 512], F32, st) for i in range(2)]
                    t2 = [sb("t2_%d" % i, [128, 512], F32, st) for i in range(2)]
                    ob = [sb("ob%d" % i, [128, 512], BF16, st) for i in range(2)]
                    gs = [sb("gs%d" % i, [128, 512], F32, st) for i in range(2)]
                    vs = [sb("vs%d" % i, [128, D], BF16, st) for i in range(2)]
                    cnt = [0]
                    wcnt = [0]

                    def load_w(c0):
                        b = wcnt[0] % 2
                        wcnt[0] += 1
                        fw.dma('pool', wt[b][:], w_in[:, :, c0:c0 + 512], writes=[('wt', b)])
                        return b

                    def proj(b, ft, t0, n, bank):
                        def mm(e):
                            r = None
                            for kc in range(8):
                                r = e.matmul(ps[bank][:, 0:n], lhsT=wt[b][:, kc, ft * 128:(ft + 1) * 128],
                                             rhs=hT[:, kc, t0:t0 + n], start=(kc == 0), stop=(kc == 7))
                            return r
                        fw.op('pe', mm, reads=[('wt', b)] + hT_keys(t0, n), writes=[PSK[bank]])

                    plan = [('k', 0), ('k', 512), ('q', 2048), ('q', 2560), ('g', 3072), ('g', 3584)]
                    import os as _os
                    _kinds = _os.environ.get("KINDS")
                    if _kinds is not None:
                        plan = [p for p in plan if p[0] in _kinds]
                    nextb = load_w(plan[0][1])
                    for pi, (kind, c0) in enumerate(plan):
                        b = nextb
                        if pi + 1 < len(plan):
                            nextb = load_w(plan[pi + 1][1])
                        for ft in range(4):
                            hd = ((c0 % 1024) // 128) + ft
                            for (t0, n) in TB:
                                if last and kind in ('q', 'g') and t0 >= SEQ:
                                    continue
                                _rd0 = _os.environ.get("ROPE_DBG", "")
                                if 'ctxonly' in _rd0 and t0 < SEQ:
                                    continue
                                if 'latonly' in _rd0 and t0 >= SEQ:
                                    continue
                                i = cnt[0] % 2
                                cnt[0] += 1
                                bank = i
                                proj(b, ft, t0, n, bank)
                                if kind == 'g':
                                    fw.op('act', lambda e, i=i, n=n, bank=bank: e.activation(
                                        out=gs[i][:, 0:n], in_=ps[bank][:, 0:n], func=AF.Silu),
                                        reads=[PSK[bank]], writes=[('gs', i)])
                                    fw.dma('sp', GA_s[hd, :, t0:t0 + n], gs[i][:, 0:n], reads=[('gs', i)],
                                           writes=[('GA', hd, t0)])
                                    continue
                                dst = KT_s if kind == 'k' else QT_s
                                if t0 >= SEQ or 'norope' in _rd0:
                                    fw.op('act', lambda e, i=i, n=n, bank=bank: e.activation(
                                        out=ob[i][:, 0:n], in_=ps[bank][:, 0:n], func=AF.Copy),
                                        reads=[PSK[bank]], writes=[('ob', i)])
                                else:
                                    fw.op('act', lambda e, i=i, n=n, bank=bank: e.activation(
                                        out=qb[i][:, 0:n], in_=ps[bank][:, 0:n], func=AF.Copy),
                                        reads=[PSK[bank]], writes=[('qb', i)])
                                    if 'cosconst' in _rd0:
                                        fw.op('dve', lambda e, i=i, n=n, bank=bank, t0=t0: e.tensor_tensor(
                                            out=t1[i][:, 0:n], in0=ps[bank][:, 0:n], in1=gs[0][:, 0:n], op=ALU.mult),
                                            reads=[PSK[bank], ('gs', 0)], writes=[('t1', i)])
                                    else:
                                      fw.op('dve', lambda e, i=i, n=n, bank=bank, t0=t0: e.tensor_tensor(
                                        out=t1[i][:, 0:n], in0=ps[bank][:, 0:n], in1=cosT[:, t0:t0 + n], op=ALU.mult),
                                        reads=[PSK[bank], 'cosT', ('qb', i)], writes=[('t1', i)])
                                    _rd = _os.environ.get("ROPE_DBG", "")
                                    if 'nomm' not in _rd:
                                        fw.op('pe', lambda e, i=i, n=n: e.matmul(
                                            ps[2 + i][:, 0:n], lhsT=rm_b[:], rhs=qb[i][:, 0:n], start=True, stop=True),
                                            reads=['rm_b', ('qb', i)], writes=[PSK[2 + i]])
                                    if 'not2' not in _rd:
                                        fw.op('dve', lambda e, i=i, n=n, t0=t0: e.tensor_tensor(
                                            out=t2[i][:, 0:n], in0=ps[2 + i][:, 0:n], in1=sinT[:, t0:t0 + n], op=ALU.mult),
                                            reads=[PSK[2 + i], 'sinT'], writes=[('t2', i)])
                                    else:
                                        fw.op('dve', lambda e, i=i, n=n, t0=t0: e.tensor_tensor(
                                            out=t2[i][:, 0:n], in0=t1[i][:, 0:n], in1=(gs[0][:, 0:n] if 'cosconst' in _rd0 else sinT[:, t0:t0 + n]), op=ALU.mult),
                                            reads=[('t1', i), 'sinT'], writes=[('t2', i)])
                                    if 'actob' in _rd0:
                                        fw.op('dve', lambda e, i=i, n=n: e.tensor_tensor(
                                            out=t1[i][:, 0:n], in0=t1[i][:, 0:n], in1=t2[i][:, 0:n], op=ALU.add),
                                            reads=[('t1', i), ('t2', i)], writes=[('t1', i)])
                                        fw.op('act', lambda e, i=i, n=n, bank=bank: e.activation(
                                            out=ob[i][:, 0:n], in_=ps[bank][:, 0:n], func=AF.Copy),
                                            reads=[PSK[bank]], writes=[('ob', i)])
                                    else:
                                      fw.op(_os.environ.get("ROPE_ADD_ENG", "pool"), lambda e, i=i, n=n: e.tensor_tensor(
                                        out=ob[i][:, 0:n], in0=t1[i][:, 0:n], in1=t2[i][:, 0:n], op=ALU.add),
                                        reads=[('t1', i), ('t2', i)], writes=[('ob', i)])
                                fw.dma('sp', dst[hd, :, t0:t0 + n], ob[i][:, 0:n], reads=[('ob', i)],
                                       writes=[(kind, hd, t0)])
                    bv = [load_w(1024), load_w(1536)]
                    for tt in range(NTT if (_kinds is None or 'v' in _kinds) else 0):
                        i = tt % 2
                        for nb in range(2):
                            bank = 4 + 2 * i + nb

                            def mm(e, tt=tt, nb=nb, bank=bank):
                                r = None
                                for kc in range(8):
                                    r = e.matmul(ps[bank][:, :], lhsT=hT[:, kc, tt * 128:(tt + 1) * 128],
                                                 rhs=wt[bv[nb]][:, kc, :], start=(kc == 0), stop=(kc == 7))
                                return r
                            fw.op('pe', mm, reads=[('wt', bv[nb])] + [('hT', kc, tt) for kc in range(8)],
                                  writes=[PSK[bank]])
                            eng = 'act' if nb == 0 else 'dve'
                            if eng == 'act':
                                fw.op('act', lambda e, i=i, nb=nb, bank=bank: e.activation(
                                    out=vs[i][:, nb * 512:(nb + 1) * 512], in_=ps[bank][:, :], func=AF.Copy),
                                    reads=[PSK[bank]], writes=[('vs', i, nb)])
                            else:
                                fw.op('dve', lambda e, i=i, nb=nb, bank=bank: e.tensor_copy(
                                    out=vs[i][:, nb * 512:(nb + 1) * 512], in_=ps[bank][:, :]),
                                    reads=[PSK[bank]], writes=[('vs', i, nb)])
                        fw.dma('sp', V_s[tt * 128:(tt + 1) * 128, :], vs[i][:], reads=[('vs', i, 0), ('vs', i, 1)],
                               writes=[('V', tt)])
                    fw.barrier()
                if chk('2a'):
                    return
                with ExitStack() as st:
                    wt = [sb("wtb%d" % i, [128, 8, 512], BF16, st) for i in range(2)]
                    vrow = sb("vrow", [128, T + 4], F32, st)
                    bbg = sb("bbg", [128, T], F32, st)
                    acc = sb("acc", [128, T], F32, st)
                    ubr = [sb("ubr0", [128, T], BF16, st)] * 2
                    hsb = [sb("hsb%d" % i, [128, 512], F32, st) for i in range(2)]
                    sg = [sb("sg%d" % i, [128, 512], F32, st) for i in range(2)]
                    fw.op('pool', lambda e: e.memset(vrow[:], 0.0), writes=['vrow_all'])
                    LOFF = 1
                    COFF = SEQ + 3

                    def load_wb(jf, b):
                        for gi, base in enumerate((4096, 5120, 6144, 7168)):
                            fw.dma('pool', wt[b][:, :, gi * 128:(gi + 1) * 128],
                                   w_in[:, :, base + jf * 128:base + (jf + 1) * 128], writes=[('wtb', b, gi)])
                    load_wb(0, 0)
                    cnt = 0
                    for jf in range(8):
                        b = jf % 2
                        if jf + 1 < 8:
                            load_wb(jf + 1, 1 - b)
                        for (t0, n) in TB:
                            if last and t0 >= SEQ:
                                continue
                            i = cnt % 2
                            cnt += 1
                            for gi in range(4):
                                bank = 4 * i + gi

                                def mm(e, gi=gi, bank=bank, t0=t0, n=n, b=b):
                                    r = None
                                    for kc in range(8):
                                        r = e.matmul(ps[bank][:, 0:n], lhsT=wt[b][:, kc, gi * 128:(gi + 1) * 128],
                                                     rhs=hT[:, kc, t0:t0 + n], start=(kc == 0), stop=(kc == 7))
                                    return r
                                fw.op('pe', mm, reads=[('wtb', b, gi)] + hT_keys(t0, n), writes=[PSK[bank]])
                            voff = (LOFF + t0) if t0 < SEQ else (COFF + t0 - SEQ)
                            fw.op('act', lambda e, i=i, n=n: e.activation(out=hsb[i][:, 0:n], in_=ps[4 * i + 0][:, 0:n],
                                                                          func=AF.Copy),
                                  reads=[PSK[4 * i + 0]], writes=[('hsb', i)])
                            fw.op('dve', lambda e, i=i, n=n, voff=voff: e.tensor_tensor(
                                out=vrow[:, voff:voff + n], in0=ps[4 * i + 2][:, 0:n], in1=hsb[i][:, 0:n], op=ALU.mult),
                                reads=[PSK[4 * i + 2], ('hsb', i), 'vrow_all'], writes=[('vrow', t0)])
                            fw.op('act', lambda e, i=i, n=n: e.activation(out=sg[i][:, 0:n], in_=ps[4 * i + 3][:, 0:n],
                                                                          func=AF.Silu),
                                  reads=[PSK[4 * i + 3]], writes=[('sg', i)])
                            fw.op('dve', lambda e, i=i, n=n, t0=t0: e.tensor_tensor(
                                out=bbg[:, t0:t0 + n], in0=ps[4 * i + 1][:, 0:n], in1=sg[i][:, 0:n], op=ALU.mult),
                                reads=[PSK[4 * i + 1], ('sg', i)], writes=[('bbg', t0)])
                        segs = [(LOFF, 0, SEQ)] + ([] if last else [(COFF, SEQ, CTX)])
                        vk = [('vrow', t0) for (t0, n) in TB if not (last and t0 >= SEQ)]
                        bk = [('bbg', t0) for (t0, n) in TB if not (last and t0 >= SEQ)]
                        u = ubr[b]
                        for (vo, o0, n) in segs:
                            wc = lambda tap: convc[:, je * 24 + tap * 8 + jf:je * 24 + tap * 8 + jf + 1]
                            fw.op('dve', lambda e, vo=vo, o0=o0, n=n, wc=wc: e.tensor_scalar(
                                out=acc[:, o0:o0 + n], in0=vrow[:, vo - 1:vo - 1 + n], scalar1=wc(0), scalar2=None,
                                op0=ALU.mult), reads=vk + ['convc', 'vrow_all'], writes=[('acc', o0)])
                            for tap in (1, 2):
                                fw.op('dve', lambda e, vo=vo, o0=o0, n=n, wc=wc, tap=tap: e.scalar_tensor_tensor(
                                    out=acc[:, o0:o0 + n], in0=vrow[:, vo - 1 + tap:vo - 1 + tap + n], scalar=wc(tap),
                                    in1=acc[:, o0:o0 + n], op0=ALU.mult, op1=ALU.add),
                                    reads=vk + [('acc', o0)], writes=[('acc', o0)])
                            fw.op('pool', lambda e, o0=o0, n=n, u=u: e.tensor_tensor(
                                out=u[:, o0:o0 + n], in0=acc[:, o0:o0 + n], in1=bbg[:, o0:o0 + n], op=ALU.mult),
                                reads=[('acc', o0)] + bk, writes=[('ubr', 0, o0)])
                            fw.dma('sp', UT_s[8 + jf, :, o0:o0 + n], u[:, o0:o0 + n], reads=[('ubr', 0, o0)],
                                   writes=[('UT', 8 + jf, o0)])
                    fw.barrier()
                if chk('2b'):
                    return
            with ExitStack() as st:
                KT = sb("KT", [128, 8, T], BF16, st)
                V = sb("V", [128, NTT, D], BF16, st)
                for hd in range(8):
                    fw.dma('sp', KT[:, hd, :], KT_s[hd, :, :], writes=[('KT', hd)])
                Vv = V_s.rearrange("(tt p) f -> p tt f", p=128)
                for g in range(NTT // 2):
                    fw.dma('sp', V[:, 2 * g:2 * g + 2, :], Vv[:, 2 * g:2 * g + 2, :], writes=[('Vt', 2 * g), ('Vt', 2 * g + 1)])
                qt = [sb("qt%d" % i, [128, 512], BF16, st) for i in range(2)]
                ga = [sb("ga%d" % i, [128, 512], F32, st) for i in range(2)]
                pT = [sb("pT%d" % i, [128, 512], BF16, st) for i in range(4)]
                r1 = sb("r1", [128, 512], F32, st)
                a1 = sb("a1", [128, 512], F32, st)
                a2 = sb("a2", [128, 512], F32, st)
                dd = sb("dd", [128, 512], F32, st)
                uo = [sb("uo%d" % i, [128, 512], BF16, st) for i in range(2)]
                work = [(hd, t0, n) for hd in range(8) for (t0, n) in TB if not (last and t0 >= SEQ)]

                def loadq(wi):
                    hd, t0, n = work[wi]
                    i = wi % 2
                    fw.dma('sp', qt[i][:, 0:n], QT_s[hd, :, t0:t0 + n], reads=[('q', hd, t0)], writes=[('qt', i)])

                def loadg(wi):
                    if wi >= len(work):
                        return
                    hd, t0, n = work[wi]
                    i = wi % 2
                    fw.dma('sp', ga[i][:, 0:n], GA_s[hd, :, t0:t0 + n], reads=[('GA', hd, t0)], writes=[('ga', i)])
                loadq(0)
                zacc = [[sb("zacc%d_%d" % (w_, m_), [128, 512], F32, st) for m_ in range(2)] for w_ in range(2)]
                units = []
                for wi, (hd, t0, n) in enumerate(work):
                    kts = list(range(NTT)) if t0 < SEQ else [32, 33]
                    for ki, kt in enumerate(kts):
                        units.append((wi, ki, kt, len(kts)))

                def emit_qk(u):
                    wi, ki, kt, nk = units[u]
                    hd, t0, n = work[wi]
                    i = wi % 2
                    for m in range(2):
                        sbank = 2 * (u % 2) + m
                        fw.op('pe', lambda e, m=m, sbank=sbank: e.matmul(
                            ps[sbank][:, 0:n], lhsT=KT[m * 64:(m + 1) * 64, hd, kt * 128:(kt + 1) * 128],
                            rhs=qt[i][m * 64:(m + 1) * 64, 0:n], start=True, stop=True),
                            reads=[('KT', hd), ('qt', i)], writes=[PSK[sbank]])

                def emit_rest(u):
                    wi, ki, kt, nk = units[u]
                    hd, t0, n = work[wi]
                    wz = wi % 2
                    for m in range(2):
                        sbank = 2 * (u % 2) + m
                        fw.op('act', lambda e, sbank=sbank: e.activation(
                            out=pT[sbank][:, 0:n], in_=ps[sbank][:, 0:n], func=AF.Exp, scale=0.125),
                            reads=[PSK[sbank]], writes=[('pT', sbank)])
                    if u + 1 < len(units):
                        wj = units[u + 1][0]
                        if wj != wi and wj + 1 < len(work):
                            loadq(wj + 1)
                        emit_qk(u + 1)
                    first = (ki == 0)
                    lastk = (ki == nk - 1)
                    for m in range(2):
                        pi = 2 * (u % 2) + m
                        bo = 4 + 2 * m
                        fw.op('pe', lambda e, pi=pi, bo=bo: e.matmul(
                            ps[bo][:, 0:n], lhsT=V[:, kt, hd * 128:(hd + 1) * 128], rhs=pT[pi][:, 0:n],
                            start=first, stop=lastk), reads=[('Vt', kt), ('pT', pi)], writes=[PSK[bo]])
                        zeng = 'dve'
                        if m == 1:
                            fw.op('pe', lambda e, pi=pi: e.matmul(
                                ps[7][:, 0:n], lhsT=ones_b[:], rhs=pT[pi][:, 0:n], start=first, stop=lastk),
                                reads=['ones_b', ('pT', pi)], writes=[PSK[7]])
                        elif first:
                            fw.op(zeng, lambda e, pi=pi, m=m: e.tensor_copy(out=zacc[wz][m][:, 0:n], in_=pT[pi][:, 0:n]),
                                  reads=[('pT', pi)], writes=[('zacc', wz, m)])
                        else:
                            fw.op(zeng, lambda e, pi=pi, m=m: e.tensor_tensor(
                                out=zacc[wz][m][:, 0:n], in0=zacc[wz][m][:, 0:n], in1=pT[pi][:, 0:n], op=ALU.add),
                                reads=[('pT', pi), ('zacc', wz, m)], writes=[('zacc', wz, m)])

                if len(work) > 1:
                    loadq(1)
                loadg(0)
                loadg(1)
                emit_qk(0)
                for u in range(len(units)):
                    emit_rest(u)
                    wi, ki, kt, nk = units[u]
                    if ki != nk - 1:
                        continue
                    hd, t0, n = work[wi]
                    i = wi % 2
                    wz = wi % 2
                    for m in range(1):
                        fw.op('pe', lambda e, m=m: e.matmul(ps[5 + 2 * m][:, 0:n], lhsT=ones_f[:], rhs=zacc[wz][m][:, 0:n],
                                                            start=True, stop=True),
                              reads=['ones_f', ('zacc', wz, m)], writes=[PSK[5 + 2 * m]])
                    fw.op('dve', lambda e: e.tensor_copy(out=a1[:, 0:n], in_=ps[4][:, 0:n]), reads=[PSK[4]], writes=['a1'])
                    fw.op('dve', lambda e: e.tensor_copy(out=a2[:, 0:n], in_=ps[6][:, 0:n]), reads=[PSK[6]], writes=['a2'])
                    fw.op('dve', lambda e: e.tensor_copy(out=r1[:, 0:n], in_=ps[7][:, 0:n]), reads=[PSK[7]], writes=['r1'])
                    fw.op('dve', lambda e: e.reciprocal(out=r1[:, 0:n], in_=r1[:, 0:n]), reads=['r1'], writes=['r1'])
                    fw.op('dve', lambda e: e.tensor_tensor(out=a2[:, 0:n], in0=a2[:, 0:n], in1=r1[:, 0:n], op=ALU.mult),
                          reads=['a2', 'r1'], writes=['a2'])
                    fw.op('dve', lambda e: e.reciprocal(out=r1[:, 0:n], in_=ps[5][:, 0:n]), reads=[PSK[5]], writes=['r1'])
                    fw.op('dve', lambda e: e.tensor_tensor(out=a1[:, 0:n], in0=a1[:, 0:n], in1=r1[:, 0:n], op=ALU.mult),
                          reads=['a1', 'r1'], writes=['a1'])
                    fw.op('dve', lambda e: e.scalar_tensor_tensor(
                        out=dd[:, 0:n], in0=a2[:, 0:n], scalar=lam_c[:, je:je + 1], in1=a1[:, 0:n],
                        op0=ALU.mult, op1=ALU.add), reads=['a1', 'a2', ('lam_c', je)], writes=['dd'])
                    fw.op('dve', lambda e: e.tensor_tensor(out=a2[:, 0:n], in0=dd[:, 0:n], in1=dd[:, 0:n], op=ALU.mult),
                          reads=['dd'], writes=['a2'])
                    fw.op('pe', lambda e: e.matmul(ps[5][:, 0:n], lhsT=ones_f[:], rhs=a2[:, 0:n], start=True, stop=True),
                          reads=['ones_f', 'a2'], writes=[PSK[5]])
                    fw.op('act', lambda e: e.activation(out=a1[:, 0:n], in_=ps[5][:, 0:n], func=AF.Ln,
                                                        scale=1.0 / 128, bias=epsc[:, 0:1]),
                          reads=[PSK[5], 'epsc'], writes=['a1'])
                    fw.op('act', lambda e: e.activation(out=a1[:, 0:n], in_=a1[:, 0:n], func=AF.Exp, scale=-0.5),
                          reads=['a1'], writes=['a1'])
                    fw.op('dve', lambda e: e.scalar_tensor_tensor(
                        out=dd[:, 0:n], in0=dd[:, 0:n], scalar=sublc[:, je:je + 1], in1=a1[:, 0:n],
                        op0=ALU.mult, op1=ALU.mult), reads=['dd', 'a1', 'sublc'], writes=['dd'])
                    fw.op('dve', lambda e: e.tensor_tensor(out=uo[i][:, 0:n], in0=dd[:, 0:n], in1=ga[i][:, 0:n], op=ALU.mult),
                          reads=['dd', ('ga', i)], writes=[('uo', i)])
                    fw.dma('sp', UT_s[hd, :, t0:t0 + n], uo[i][:, 0:n], reads=[('uo', i)], writes=[('UT', hd, t0)])
                    loadg(wi + 2)
                fw.barrier()

        def emit_odd(l, last):
            jo = l // 2
            w_in = od_w_in[jo, :, :].rearrange("(kc p) f -> p kc f", p=128)
            C_CK, C_CV, C_DK, C_DV, C_LR, C_CQ, C_DQ, C_CG, C_DG = 0, 1024, 2048, 2560, 3584, 3616, 4640, 5152, 6176
            ps7b = ps[7].bitcast(BF16)
            with ExitStack() as stL:
                lrT = sb("lrT", [16, 2, T], BF16, stL)
                gub = sb("gub", [16, 2, 512], BF16, stL)
                fw.dma('pool', gub[:], gu_d[jo, :, :, :].rearrange("d r c -> r d c"), writes=['gub'])
                with ExitStack() as st1:
                    hT = sb("hT", [128, 8, T], BF16, st1)
                    with ExitStack() as st:
                        emit_hT(l, hT, st)
                        fw.barrier()
                    if chk('hT'):
                        return
                    with ExitStack() as st:
                        wt = [sb("wto%d" % i, [128, 8, 512], BF16, st) for i in range(4)]
                        wlr = sb("wlr", [128, 8, 32], BF16, st)
                        ob = [sb("oob%d" % i, [128, 512], BF16, st) for i in range(2)]
                        gs = [sb("ogs%d" % i, [128, 512], F32, st) for i in range(2)]
                        vsn = [sb("vsn%d" % i, [128, 16, 65], BF16, st) for i in range(2)]
                        vs = [sb("ovs%d" % i, [128, D], BF16, st) for i in range(2)]
                        gst = [sb("gst%d" % i, [128, D], F32, st) for i in range(2)]
                        for i in range(2):
                            fw.op('pool', lambda e, i=i: e.memset(vsn[i][:], 1.0), writes=[('vsn', i, 0), ('vsn', i, 1)])
                        wcnt = [0]

                        def load_w(c0, ncol=512):
                            b = wcnt[0] % 4
                            wcnt[0] += 1
                            fw.dma('pool', wt[b][:, :, 0:ncol], w_in[:, :, c0:c0 + ncol], writes=[('wt', b)])
                            return b
                        fw.dma('pool', wlr[:], w_in[:, :, C_LR:C_LR + 32], writes=['wlr'])
                        cnt = 0
                        for dr_ in range(2):
                            for (t0, n) in TB:
                                bank = cnt % 2
                                cnt += 1

                                def mm(e, dr_=dr_, t0=t0, n=n, bank=bank):
                                    r = None
                                    for kc in range(8):
                                        r = e.matmul(ps[bank][0:16, 0:n], lhsT=wlr[:, kc, dr_ * 16:(dr_ + 1) * 16],
                                                     rhs=hT[:, kc, t0:t0 + n], start=(kc == 0), stop=(kc == 7))
                                    return r
                                fw.op('pe', mm, reads=['wlr'] + hT_keys(t0, n), writes=[PSK[bank]])
                                fw.op('act', lambda e, dr_=dr_, t0=t0, n=n, bank=bank: e.activation(
                                    out=lrT[0:16, dr_, t0:t0 + n], in_=ps[bank][0:16, 0:n], func=AF.Copy),
                                    reads=[PSK[bank]], writes=[('lrT', dr_, t0)])
                        plan = [('ck', C_CK, 0), ('ck', C_CK + 512, 4), ('cq', C_CQ, 0), ('cq', C_CQ + 512, 4),
                                ('dk', C_DK, 0), ('dq', C_DQ, 0)]
                        nextb = load_w(plan[0][1])
                        cnt = 0
                        for pi, (kind, c0, f0) in enumerate(plan):
                            b = nextb
                            if pi + 1 < len(plan):
                                nextb = load_w(plan[pi + 1][1])
                            for ft in range(4):
                                fidx = f0 + ft
                                for (t0, n) in TB:
                                    if last and kind in ('cq', 'dq') and t0 >= SEQ:
                                        continue
                                    i = cnt % 2
                                    cnt += 1
                                    bank = i

                                    def mm(e, b=b, ft=ft, t0=t0, n=n, bank=bank):
                                        r = None
                                        for kc in range(8):
                                            r = e.matmul(ps[bank][:, 0:n], lhsT=wt[b][:, kc, ft * 128:(ft + 1) * 128],
                                                         rhs=hT[:, kc, t0:t0 + n], start=(kc == 0), stop=(kc == 7))
                                        return r
                                    fw.op('pe', mm, reads=[('wt', b)] + hT_keys(t0, n), writes=[PSK[bank]])
                                    if kind in ('ck', 'cq'):
                                        sc = 1.0 if kind == 'ck' else 0.125
                                        dst = KT_s if kind == 'ck' else QT_s
                                        fw.op('act', lambda e, i=i, n=n, bank=bank, sc=sc: e.activation(
                                            out=ob[i][:, 0:n], in_=ps[bank][:, 0:n], func=AF.Copy, scale=sc),
                                            reads=[PSK[bank]], writes=[('ob', i)])
                                        fw.dma('sp', dst[fidx, :, t0:t0 + n], ob[i][:, 0:n], reads=[('ob', i)],
                                               writes=[(kind, fidx, t0)])
                                    else:
                                        sc = 1.0 if kind == 'dk' else (128.0 ** -0.5)
                                        gi = fidx if kind == 'dk' else 4 + fidx
                                        fw.op('act', lambda e, i=i, n=n, bank=bank, sc=sc: e.activation(
                                            out=gs[i][:, 0:n], in_=ps[bank][:, 0:n], func=AF.Copy, scale=sc),
                                            reads=[PSK[bank]], writes=[('gs', i)])
                                        fw.dma('sp', GA_s[gi, :, t0:t0 + n], gs[i][:, 0:n], reads=[('gs', i)],
                                               writes=[('GA', gi, t0)])
                        for (kind, c0) in (('cv', C_CV), ('dv', C_DV), ('cg', C_CG), ('dg', C_DG)):
                            bv = [load_w(c0), load_w(c0 + 512)]
                            for tt in range(NTT):
                                if last and kind in ('cg', 'dg') and tt >= 32:
                                    continue
                                i = tt % 2
                                for nb in range(2):
                                    bank = 2 + 2 * i + nb

                                    def mm(e, tt=tt, nb=nb, bank=bank, bv=bv):
                                        r = None
                                        for kc in range(8):
                                            r = e.matmul(ps[bank][:, :], lhsT=hT[:, kc, tt * 128:(tt + 1) * 128],
                                                         rhs=wt[bv[nb]][:, kc, :], start=(kc == 0), stop=(kc == 7))
                                        return r
                                    fw.op('pe', mm, reads=[('wt', bv[nb])] + [('hT', kc, tt) for kc in range(8)],
                                          writes=[PSK[bank]])
                                    if kind == 'cv':
                                        fw.op('act', lambda e, i=i, nb=nb, bank=bank: e.activation(
                                            out=vsn[i][:, nb * 8:(nb + 1) * 8, 0:64],
                                            in_=ps[bank][:, :].rearrange("p (h d) -> p h d", d=64), func=AF.Copy),
                                            reads=[PSK[bank]], writes=[('vsn', i, nb)])
                                    elif kind == 'dv':
                                        fw.op('act', lambda e, i=i, nb=nb, bank=bank: e.activation(
                                            out=vs[i][:, nb * 512:(nb + 1) * 512], in_=ps[bank][:, :], func=AF.Copy),
                                            reads=[PSK[bank]], writes=[('vs', i, nb)])
                                    else:
                                        fw.op('act', lambda e, i=i, nb=nb, bank=bank: e.activation(
                                            out=gst[i][:, nb * 512:(nb + 1) * 512], in_=ps[bank][:, :], func=AF.Silu),
                                            reads=[PSK[bank]], writes=[('gst', i, nb)])
                                rows = slice(tt * 128, (tt + 1) * 128)
                                if kind == 'cv':
                                    fw.dma('sp', VN_s[rows, :], vsn[i][:].rearrange("p h d -> p (h d)"),
                                           reads=[('vsn', i, 0), ('vsn', i, 1)], writes=[('VN', tt)])
                                elif kind == 'dv':
                                    fw.dma('sp', V_s[rows, :], vs[i][:], reads=[('vs', i, 0), ('vs', i, 1)], writes=[('DV', tt)])
                                else:
                                    fw.dma('sp', (CG_s if kind == 'cg' else DG_s)[rows, :], gst[i][:],
                                           reads=[('gst', i, 0), ('gst', i, 1)], writes=[(kind, tt)])
                        fw.barrier()
                if chk('o2'):
                    return
                order_f = [64, 65, 66, 67] + list(range(64))
                order_b = [67, 66, 65, 64] + list(range(63, -1, -1))
                NCH = T // 64
                for hp in range(2):
                    with ExitStack() as st:
                        chains = [(d_, 2 * hp + hl) for d_ in range(2) for hl in range(2)]
                        qtl = [sb("qtl%d" % ci, [128, T], BF16, st) for ci in range(4)]
                        ktl = [sb("ktl%d" % ci, [128, T], BF16, st) for ci in range(4)]
                        Et = [sb("Et%d" % ci, [128, NCH], F32, st) for ci in range(4)]
                        S = [sb("S%d" % ci, [128, 256], F32, st) for ci in range(4)]
                        Sb = [sb("Sb%d" % ci, [128, 256], BF16, st) for ci in range(4)]
                        qf = [sb("qf%d" % i, [128, 512], F32, st) for i in range(2)]
                        kf = [sb("kf%d" % i, [128, 512], F32, st) for i in range(2)]
                        e1 = [sb("e1_%d" % i, [128, 512], F32, st) for i in range(2)]
                        spt = [sb("spt%d" % i, [128, 512], F32, st) for i in range(2)]
                        pit = [sb("pit%d" % i, [128, 512], F32, st) for i in range(2)]
                        eq = [sb("eq%d" % i, [128, 512], F32, st) for i in range(2)]
                        ek = [sb("ek%d" % i, [128, 512], F32, st) for i in range(2)]
                        attb = [sb("attb%d" % ci, [64, 64], BF16, st) for ci in range(4)]
                        ktm = [sb("ktm%d" % ci, [64, 128], BF16, st) for ci in range(4)]
                        vch = [[sb("vch%d_%d" % (d_, i), [64, 512], BF16, st) for i in range(2)] for d_ in range(2)]
                        ost = [[sb("ost%d_%d" % (d_, i), [64, 512], F32, st) for i in range(2)] for d_ in range(2)]
                        cnt = 0
                        for ci, (d_, hh) in enumerate(chains):
                            for (t0, n) in TB:
                                i = cnt % 2
                                cnt += 1
                                nch = n // 64
                                c0 = t0 // 64
                                fw.dma('sp', qf[i][:, 0:n], GA_s[4 + hh, :, t0:t0 + n], reads=[('GA', 4 + hh, t0)],
                                       writes=[('qf', i)])
                                fw.dma('sp', kf[i][:, 0:n], GA_s[hh, :, t0:t0 + n], reads=[('GA', hh, t0)],
                                       writes=[('kf', i)])
                                fw.op('pe', lambda e, d_=d_, hh=hh, t0=t0, n=n: e.matmul(
                                    ps[6][:, 0:n], lhsT=gub[0:16, d_, hh * 128:(hh + 1) * 128], rhs=lrT[0:16, d_, t0:t0 + n],
                                    start=True, stop=True), reads=['gub', ('lrT', d_, t0)], writes=[PSK[6]])
                                gcol = jo * 8 + d_ * 4 + hh
                                fw.op('act', lambda e, i=i, n=n, gcol=gcol: e.activation(
                                    out=e1[i][:, 0:n], in_=ps[6][:, 0:n], func=AF.Exp, scale=-1.0, bias=gbc[:, gcol:gcol + 1]),
                                    reads=[PSK[6], 'gbc'], writes=[('e1', i)])
                                fw.op('act', lambda e, i=i, n=n: e.activation(
                                    out=spt[i][:, 0:n], in_=e1[i][:, 0:n], func=AF.Ln, bias=1.0),
                                    reads=[('e1', i)], writes=[('spt', i)])
                                fw.op('dve', lambda e, i=i, n=n: e.tensor_tensor_scan(
                                    out=pit[i][:, 0:n], data0=cmask[:, 0:n], data1=spt[i][:, 0:n], initial=0.0,
                                    op0=ALU.mult, op1=ALU.add), reads=['cmask', ('spt', i)], writes=[('pit', i)])
                                pv3 = pit[i][:, 0:n].rearrange("p (c s) -> p c s", s=64)
                                fw.op('act', lambda e, ci=ci, c0=c0, nch=nch, pv3=pv3: e.activation(
                                    out=Et[ci][:, c0:c0 + nch], in_=pv3[:, :, 63], func=AF.Exp, scale=-1.0 / 16),
                                    reads=[('pit', i)], writes=[('Et', ci, t0)])
                                if d_ == 1:
                                    fw.op('dve', lambda e, i=i, n=n: e.tensor_tensor(
                                        out=pit[i][:, 0:n], in0=pit[i][:, 0:n], in1=spt[i][:, 0:n], op=ALU.subtract),
                                        reads=[('pit', i), ('spt', i)], writes=[('pit', i)])
                                sq_ = (-1.0 / 16) if d_ == 0 else (1.0 / 16)
                                fw.op('act', lambda e, i=i, n=n, sq_=sq_: e.activation(
                                    out=eq[i][:, 0:n], in_=pit[i][:, 0:n], func=AF.Exp, scale=sq_),
                                    reads=[('pit', i)], writes=[('eq', i)])
                                fw.op('act', lambda e, i=i, n=n, sq_=sq_: e.activation(
                                    out=ek[i][:, 0:n], in_=pit[i][:, 0:n], func=AF.Exp, scale=-sq_),
                                    reads=[('pit', i)], writes=[('ek', i)])
                                fw.op('dve', lambda e, i=i, n=n, ci=ci, t0=t0: e.tensor_tensor(
                                    out=qtl[ci][:, t0:t0 + n], in0=qf[i][:, 0:n], in1=eq[i][:, 0:n], op=ALU.mult),
                                    reads=[('qf', i), ('eq', i)], writes=[('qtl', ci, t0)])
                                fw.op('pool', lambda e, i=i, n=n, ci=ci, t0=t0: e.tensor_tensor(
                                    out=ktl[ci][:, t0:t0 + n], in0=kf[i][:, 0:n], in1=ek[i][:, 0:n], op=ALU.mult),
                                    reads=[('kf', i), ('ek', i)], writes=[('ktl', ci, t0)])
                        for ci in range(4):
                            fw.op('dve', lambda e, ci=ci: e.memset(S[ci][:], 0.0), writes=[('S', ci)])
                            fw.op('pool', lambda e, ci=ci: e.memset(Sb[ci][:], 0.0), writes=[('Sb', ci)])
                        def cinfo(step, d_):
                            c = (order_f if d_ == 0 else order_b)[step]
                            tk = c * 64
                            blk = (tk // 512) * 512 if tk < SEQ else SEQ
                            want = not (last and c >= 64)
                            return c, tk, blk, want

                        def phase_B(step):
                            for d_ in range(2):
                                c, tk, blk, want = cinfo(step, d_)
                                for hl in range(2):
                                    ci = d_ * 2 + hl
                                    if want:
                                        fw.op('pe', lambda e, ci=ci, tk=tk: e.matmul(
                                            ps[ci][0:64, 0:64], lhsT=ktl[ci][:, tk:tk + 64], rhs=qtl[ci][:, tk:tk + 64],
                                            start=True, stop=True), reads=[('qtl', ci, blk), ('ktl', ci, blk)],
                                            writes=[('psa', ci)])
                                    fw.op('pe', lambda e, ci=ci, tk=tk: e.transpose(
                                        out=ps7b[0:64, ci * 128:(ci + 1) * 128], in_=ktl[ci][:, tk:tk + 64], identity=identb[:]),
                                        reads=[('ktl', ci, blk), 'identb'], writes=[('ps7', ci)])

                        phase_B(0)
                        for step in range(NCH):
                            sb_i = step % 2
                            for d_ in range(2):
                                c, tk, blk, want = cinfo(step, d_)
                                fw.dma('sp', vch[d_][sb_i][:], V_s[tk:tk + 64, hp * 512:(hp + 1) * 512],
                                       reads=[('DV', tk // 128)], writes=[('vch', d_, sb_i)])
                                if d_ == 1:
                                    for hl in range(2):
                                        ci = 2 + hl
                                        fw.op('dve', lambda e, ci=ci, c=c: e.tensor_scalar(
                                            out=S[ci][:], in0=S[ci][:], scalar1=Et[ci][:, c:c + 1], scalar2=None, op0=ALU.mult),
                                            reads=[('S', ci), ('Et', ci, blk)], writes=[('S', ci)])
                                        fw.op('pool', lambda e, ci=ci: e.tensor_copy(out=Sb[ci][:], in_=S[ci][:]),
                                              reads=[('S', ci)], writes=[('Sb', ci)])
                            for d_ in range(2):
                                c, tk, blk, want = cinfo(step, d_)
                                for hl in range(2):
                                    ci = d_ * 2 + hl
                                    if want:
                                        fw.op('dve', lambda e, ci=ci, d_=d_: e.tensor_tensor(
                                            out=attb[ci][:], in0=ps[ci][0:64, 0:64], in1=tri[:, d_ * 64:(d_ + 1) * 64], op=ALU.mult),
                                            reads=[('psa', ci), 'tri'], writes=[('attb', ci)])
                                    fw.op('act', lambda e, ci=ci: e.activation(
                                        out=ktm[ci][:], in_=ps7b[0:64, ci * 128:(ci + 1) * 128], func=AF.Copy),
                                        reads=[('ps7', c_) for c_ in range(4)], writes=[('ktm', ci)])
                            if step + 1 < NCH:
                                phase_B(step + 1)
                            for d_ in range(2):
                                c, tk, blk, want = cinfo(step, d_)
                                for hl in range(2):
                                    ci = d_ * 2 + hl
                                    vv = vch[d_][sb_i][:, hl * 256:(hl + 1) * 256]
                                    if want:
                                        def omm(e, ci=ci, tk=tk, vv=vv):
                                            e.matmul(ps[ci][0:64, 128:384], lhsT=qtl[ci][:, tk:tk + 64], rhs=Sb[ci][:],
                                                     start=True, stop=False)
                                            return e.matmul(ps[ci][0:64, 128:384], lhsT=attb[ci][:], rhs=vv,
                                                            start=False, stop=True)
                                        fw.op('pe', omm, reads=[('qtl', ci, blk), ('Sb', ci), ('attb', ci), ('vch', d_, sb_i)],
                                              writes=[('pso', ci)])
                                    kvb = 4 + ci // 2
                                    kvc = (ci % 2) * 256
                                    fw.op('pe', lambda e, ci=ci, vv=vv, kvb=kvb, kvc=kvc: e.matmul(
                                        ps[kvb][:, kvc:kvc + 256], lhsT=ktm[ci][:], rhs=vv, start=True, stop=True),
                                        reads=[('ktm', ci), ('vch', d_, sb_i)], writes=[('pskv', ci)])
                            for d_ in range(2):
                                c, tk, blk, want = cinfo(step, d_)
                                for hl in range(2):
                                    ci = d_ * 2 + hl
                                    kvb = 4 + ci // 2
                                    kvc = (ci % 2) * 256
                                    if want:
                                        fw.op('dve', lambda e, ci=ci, d_=d_, hl=hl, sb_i=sb_i: e.tensor_copy(
                                            out=ost[d_][sb_i][:, hl * 256:(hl + 1) * 256], in_=ps[ci][0:64, 128:384]),
                                            reads=[('pso', ci)], writes=[('ost', d_, sb_i, hl)])
                                    fw.op('dve', lambda e, ci=ci, kvb=kvb, kvc=kvc: e.tensor_tensor(
                                        out=S[ci][:], in0=ps[kvb][:, kvc:kvc + 256], in1=S[ci][:], op=ALU.add),
                                        reads=[('pskv', 2 * (ci // 2)), ('pskv', 2 * (ci // 2) + 1), ('S', ci)], writes=[('S', ci)])
                                    if d_ == 0:
                                        fw.op('dve', lambda e, ci=ci, c=c: e.tensor_scalar(
                                            out=S[ci][:], in0=S[ci][:], scalar1=Et[ci][:, c:c + 1], scalar2=None, op0=ALU.mult),
                                            reads=[('S', ci), ('Et', ci, blk)], writes=[('S', ci)])
                                        fw.op('pool', lambda e, ci=ci: e.tensor_copy(out=Sb[ci][:], in_=S[ci][:]),
                                              reads=[('S', ci)], writes=[('Sb', ci)])
                                if want:
                                    dst = (OF_s if d_ == 0 else OB_s)[tk:tk + 64, hp * 512:(hp + 1) * 512]
                                    fw.dma('sp', dst, ost[d_][sb_i][:], reads=[('ost', d_, sb_i, 0), ('ost', d_, sb_i, 1)],
                                           writes=[('O', d_, c, hp)])
                        fw.barrier()
                if chk('o3'):
                    return
            with ExitStack() as st:
                gnb = sb("gnb", [128, 256], F32, st)
                fw.dma('sp', gnb[:], gnorm_b[jo, :, :], writes=['gnb'])
                oft = [sb("oft%d" % i, [128, D], F32, st) for i in range(2)]
                obt = [sb("obt%d" % i, [128, D], F32, st) for i in range(2)]
                dgt = [sb("dgt%d" % i, [128, D], F32, st) for i in range(2)]
                junk = sb("junk3", [128, 256], BF16, st)
                ssq = sb("ssq3", [128, 8], F32, st)
                ugb = [sb("ugb%d" % i, [128, 8, 128], BF16, st) for i in range(2)]
                ntt = 32 if last else NTT

                def loadc(tt):
                    i = tt % 2
                    rows = slice(tt * 128, (tt + 1) * 128)
                    fw.dma('sp', oft[i][:], OF_s[rows, :], writes=[('oft', i)])
                    fw.dma('sp', obt[i][:], OB_s[rows, :], writes=[('obt', i)])
                    fw.dma('sp', dgt[i][:], DG_s[rows, :], writes=[('dgt', i)])
                loadc(0)
                for tt in range(ntt):
                    i = tt % 2
                    if tt + 1 < ntt:
                        loadc(tt + 1)
                    fw.op('pool', lambda e, i=i: e.tensor_tensor(out=oft[i][:], in0=oft[i][:], in1=obt[i][:], op=ALU.add),
                          reads=[('oft', i), ('obt', i)], writes=[('oft', i)])
                    for hh in range(4):
                        fw.op('act', lambda e, i=i, hh=hh: e.activation(
                            out=junk[:], in_=oft[i][:, hh * 256:(hh + 1) * 256], func=AF.Square,
                            accum_out=ssq[:, i * 4 + hh:i * 4 + hh + 1]), reads=[('oft', i)], writes=['junk3', ('ssq3', i, hh)])
                    fw.op('act', lambda e, i=i: e.activation(out=ssq[:, i * 4:i * 4 + 4], in_=ssq[:, i * 4:i * 4 + 4],
                                                             func=AF.Sqrt, scale=1.0 / 256, bias=EPS),
                          reads=[('ssq3', i, hh) for hh in range(4)], writes=[('rs3', i)])
                    fw.op('dve', lambda e, i=i: e.reciprocal(out=ssq[:, i * 4:i * 4 + 4], in_=ssq[:, i * 4:i * 4 + 4]),
                          reads=[('rs3', i)], writes=[('rs3', i)])
                    for hh in range(4):
                        fw.op('dve', lambda e, i=i, hh=hh: e.scalar_tensor_tensor(
                            out=oft[i][:, hh * 256:(hh + 1) * 256], in0=oft[i][:, hh * 256:(hh + 1) * 256],
                            scalar=ssq[:, i * 4 + hh:i * 4 + hh + 1], in1=gnb[:], op0=ALU.mult, op1=ALU.mult),
                            reads=[('oft', i), ('rs3', i), 'gnb'], writes=[('oft', i)])
                    fw.op('pool', lambda e, i=i: e.tensor_tensor(out=oft[i][:], in0=oft[i][:], in1=dgt[i][:], op=ALU.mult),
                          reads=[('oft', i), ('dgt', i)], writes=[('oft', i)])
                    for half in range(2):
                        bank = 2 * i + half

                        def tr(e, i=i, half=half, bank=bank):
                            r = None
                            for q4 in range(4):
                                fc = half * 4 + q4
                                r = e.transpose(out=ps[bank][:, q4 * 128:(q4 + 1) * 128],
                                                in_=oft[i][:, fc * 128:(fc + 1) * 128], identity=ident[:])
                            return r
                        fw.op('pe', tr, reads=[('oft', i), 'ident'], writes=[PSK[bank]])
                        fw.op('dve', lambda e, i=i, half=half, bank=bank: e.tensor_copy(
                            out=ugb[i][:, half * 4:(half + 1) * 4, :],
                            in_=ps[bank][:, :].rearrange("p (f t) -> p f t", t=128)),
                            reads=[PSK[bank]], writes=[('ugb', i, half)])
                    fw.dma('sp', UT_s[8:16, :, tt * 128:(tt + 1) * 128].rearrange("f p t -> p f t"), ugb[i][:],
                           reads=[('ugb', i, 0), ('ugb', i, 1)], writes=[('UTg', tt)])
                fw.barrier()
            if chk('o3b'):
                return
            for hf in range(2):
                with ExitStack() as st:
                    KN = sb("KN", [128, 4, T], BF16, st)
                    VN = sb("VN", [128, NTT, 8 * 65], BF16, st)
                    TTt = sb("TTt", [128, 8, 21 * 64], BF16, st)
                    for j in range(4):
                        fw.dma('sp', KN[:, j, :], KT_s[hf * 4 + j, :, :], writes=[('KN', j)])
                    VNv = VN_s.rearrange("(tt p) f -> p tt f", p=128)
                    for g in range(NTT // 2):
                        fw.dma('sp', VN[:, 2 * g:2 * g + 2, :], VNv[:, 2 * g:2 * g + 2, hf * 520:(hf + 1) * 520],
                               writes=[('VNt', 2 * g), ('VNt', 2 * g + 1)])
                    for hl in range(8):
                        fw.dma('pool', TTt[:, hl, :], tt_d[jo, hf * 8 + hl, :, :], writes=[('TT', hl)])
                        fw.op('act', lambda e, hl=hl: e.activation(out=TTt[:, hl, :], in_=TTt[:, hl, :], func=AF.Exp),
                              reads=[('TT', hl)], writes=[('TT', hl)])
                    qn = [sb("qn%d" % i, [128, 4, 512], BF16, st) for i in range(2)]
                    cgt = [sb("cgt%d" % i, [64, 512], F32, st) for i in range(2)]
                    pT = [sb("pTn%d" % i, [128, 448], BF16, st) for i in range(3)]
                    un = [sb("un%d" % i, [64, 512], F32, st) for i in range(2)]
                    rz = sb("rz", [64, 16], F32, st)
                    ust = [sb("ust%d" % i, [128, 4, 512], BF16, st) for i in range(2)]
                    rblocks = list(range(8)) + ([] if last else [8])
                    QTv = QT_s.rearrange("f p t -> p f t")

                    def loadq(bi):
                        rb = rblocks[bi]
                        i = bi % 2
                        n = 512 if rb < 8 else CTX
                        fw.dma('sp', qn[i][:, :, 0:n], QTv[:, hf * 4:(hf + 1) * 4, rb * 512:rb * 512 + n], writes=[('qn', i)])
                    items = []
                    rcnt = 0
                    for bi, rb in enumerate(rblocks):
                        nrows = 8 if rb < 8 else 4
                        for rr in range(nrows):
                            ri = rcnt % 2
                            rcnt += 1
                            if rb < 8:
                                r = rb * 8 + rr
                                rs_ = min(max(r - 4, 0), 56)
                                if rs_ % 2 == 0:
                                    tiles = [(rs_ // 2 + k, 2 * (rs_ // 2 + k) - r + 7) for k in range(4)]
                                else:
                                    a0 = (rs_ - 1) // 2
                                    tiles = [(a0, 14)] + [(a0 + k, 2 * (a0 + k) - r + 7) for k in (1, 2, 3)] + [(a0 + 4, 15)]
                            else:
                                tiles = []
                            alltiles = tiles + [(32, None), (33, None)]
                            for hl in range(8):
                                items.append(dict(bi=bi, rb=rb, rr=rr, ri=ri, hl=hl, tiles=alltiles, nrows=nrows))

                    def emit_s(k):
                        it_ = items[k]
                        hl, qi, rr, alltiles = it_['hl'], it_['bi'] % 2, it_['rr'], it_['tiles']
                        j = hl // 2
                        p0 = (hl % 2) * 64
                        sbank = k % 3

                        def smm(e):
                            r_ = None
                            for q_, (a, ti) in enumerate(alltiles):
                                r_ = e.matmul(ps[sbank][:, q_ * 64:(q_ + 1) * 64],
                                              lhsT=KN[p0:p0 + 64, j, a * 128:(a + 1) * 128],
                                              rhs=qn[qi][p0:p0 + 64, j, rr * 64:(rr + 1) * 64],
                                              start=True, stop=True)
                            return r_
                        fw.op('pe', smm, reads=[('KN', j), ('qn', qi)], writes=[PSK[sbank]])

                    loadq(0)
                    if len(rblocks) > 1:
                        loadq(1)
                    PDN = 2
                    for k in range(min(PDN, len(items))):
                        emit_s(k)
                    for k, it_ in enumerate(items):
                        bi, rb, rr, ri, hl, alltiles, nrows = (it_['bi'], it_['rb'], it_['rr'], it_['ri'], it_['hl'],
                                                               it_['tiles'], it_['nrows'])
                        qi = bi % 2
                        ntl = len(alltiles)
                        sbank = k % 3
                        obank = 3 + k % 3
                        pi_ = k % 3
                        if hl == 0:
                            tok0 = rb * 512 + rr * 64
                            fw.dma('sp', cgt[ri][:], CG_s[tok0:tok0 + 64, hf * 512:(hf + 1) * 512], writes=[('cgt', ri)])
                            if rr == 0 and bi >= 1 and bi + 1 < len(rblocks):
                                loadq(bi + 1)
                        fw.op('act', lambda e: e.activation(
                            out=pT[pi_][:, 0:ntl * 64], in_=ps[sbank][:, 0:ntl * 64], func=AF.Exp),
                            reads=[PSK[sbank]], writes=[('pTn', pi_)])
                        nl_ = ntl - 2
                        if nl_ > 0:
                            tv = TTt[:, hl, :].rearrange("p (t c) -> p t c", c=64)
                            if nl_ == 4:
                                t0_ = alltiles[0][1]
                                ebv = tv[:, t0_:t0_ + 7:2, :]
                            else:
                                ebv = tv[:, 16:21, :]
                            pv_ = pT[pi_][:, 0:nl_ * 64].rearrange("p (t c) -> p t c", c=64)
                            fw.op('dve', lambda e: e.tensor_tensor(out=pv_, in0=pv_, in1=ebv, op=ALU.mult),
                                  reads=[('pTn', pi_), ('TT', hl)], writes=[('pTn', pi_)])
                        if k + PDN < len(items):
                            emit_s(k + PDN)

                        def pvm(e):
                            r_ = None
                            for q_, (a, ti) in enumerate(alltiles):
                                r_ = e.matmul(ps[obank][0:64, 0:65], lhsT=pT[pi_][:, q_ * 64:(q_ + 1) * 64],
                                              rhs=VN[:, a, hl * 65:(hl + 1) * 65], start=(q_ == 0),
                                              stop=(q_ == len(alltiles) - 1))
                            return r_
                        fw.op('pe', pvm, reads=[('pTn', pi_)] + [('VNt', a) for (a, ti) in alltiles], writes=[PSK[obank]])
                        fw.op('dve', lambda e: e.reciprocal(
                            out=rz[:, ri * 8 + hl:ri * 8 + hl + 1], in_=ps[obank][0:64, 64:65]),
                            reads=[PSK[obank]], writes=[('rz', ri, hl)])
                        fw.op('dve', lambda e: e.tensor_scalar(
                            out=un[ri][:, hl * 64:(hl + 1) * 64], in0=ps[obank][0:64, 0:64],
                            scalar1=rz[:, ri * 8 + hl:ri * 8 + hl + 1], scalar2=None, op0=ALU.mult),
                            reads=[PSK[obank], ('rz', ri, hl)], writes=[('un', ri, hl)])
                        if hl != 7:
                            continue
                        unk = [('un', ri, h_) for h_ in range(8)]
                        fw.op('pool', lambda e: e.tensor_tensor(out=un[ri][:], in0=un[ri][:], in1=cgt[ri][:], op=ALU.mult),
                              reads=unk + [('cgt', ri)], writes=[('ung', ri)] + unk)

                        def tr(e):
                            r_ = None
                            for j_ in range(4):
                                r_ = e.transpose(out=ps[6][:, j_ * 64:(j_ + 1) * 64], in_=un[ri][:, j_ * 128:(j_ + 1) * 128],
                                                 identity=ident[0:64, 0:64])
                            return r_
                        fw.op('pe', tr, reads=[('ung', ri), 'ident'] + unk, writes=[PSK[6]])
                        fw.op('dve', lambda e: e.tensor_copy(
                            out=ust[qi][:, :, rr * 64:(rr + 1) * 64], in_=ps[6][:, 0:256].rearrange("p (f t) -> p f t", t=64)),
                            reads=[PSK[6]], writes=[('ust', qi, rr)])
                        if rr != nrows - 1:
                            continue
                        n = 512 if rb < 8 else CTX
                        fw.dma('sp', UT_s[hf * 4:(hf + 1) * 4, :, rb * 512:rb * 512 + n].rearrange("f p t -> p f t"),
                               ust[qi][:, :, 0:n], reads=[('ust', qi, r_) for r_ in range(nrows)], writes=[('UTn', hf, rb)])
                    fw.barrier()

        def emit_outproj(l, last):
            with ExitStack() as st:
                wo = sb("wo", [128, 16, D], BF16, st)
                wov = w_out[l, :, :].rearrange("(fc p) n -> p fc n", p=128)
                for g in range(4):
                    fw.dma('pool', wo[:, 4 * g:4 * g + 4, :], wov[:, 4 * g:4 * g + 4, :], writes=[('wo', g)])
                wok = [('wo', g) for g in range(4)]
                ut = [sb("ut%d" % i, [128, 16, 512], BF16, st) for i in range(2)]
                xr = [sb("xr%d" % i, [128, D], F32, st) for i in range(2)]
                tn = [sb("tn%d" % i, [128, D], F32, st) for i in range(2)]
                xo = [sb("xo%d" % i, [128, D], F32, st) for i in range(2)]
                junk = sb("junk2", [128, 512], BF16, st)
                ss2 = sb("ss2", [128, 4], F32, st)
                rr = sb("rr", [128, 2], F32, st)
                acs2 = sb("acs2", [128, 2], F32, st)
                blocks = [tb for tb in TB if not (last and tb[0] >= SEQ)]
                UTv = UT_s.rearrange("f p t -> p f t")

                def loadu(bi):
                    t0, n = blocks[bi]
                    i = bi % 2
                    fw.dma('sp', ut[i][:, :, 0:n], UTv[:, :, t0:t0 + n],
                           reads=[('UT', f, t0) for f in range(16)] + [('UT', f, 0) for f in range(8, 16)] +
                           [('UT', f, SEQ) for f in range(8, 16)], writes=[('ut', i)])
                loadu(0)
                tcnt = 0
                for bi, (t0, n) in enumerate(blocks):
                    i = bi % 2
                    if bi + 1 < len(blocks):
                        loadu(bi + 1)
                    for stl in range(n // 128):
                        tt = (t0 // 128) + stl
                        j = 0 if tt < 32 else 1
                        c = tcnt % 2
                        tcnt += 1
                        src = (x_in if l == first_layer else out_d)[tt * 128:(tt + 1) * 128, :] if tt < 32 else \
                            (ctx_in if l == first_layer else xc_d)[(tt - 32) * 128:(tt - 31) * 128, :]
                        dst = out_d[tt * 128:(tt + 1) * 128, :] if tt < 32 else xc_d[(tt - 32) * 128:(tt - 31) * 128, :]
                        fw.dma('sp', xr[c][:], src, reads=[('X', tt)], writes=[('xr', c)])
                        for nb in range(2):
                            bank = 2 * c + nb

                            def mm(e, i=i, stl=stl, nb=nb, bank=bank):
                                r = None
                                for fc in range(16):
                                    r = e.matmul(ps[bank][:, :], lhsT=ut[i][:, fc, stl * 128:(stl + 1) * 128],
                                                 rhs=wo[:, fc, nb * 512:(nb + 1) * 512], start=(fc == 0), stop=(fc == 15))
                                return r
                            fw.op('pe', mm, reads=wok + [('ut', i)], writes=[PSK[bank]])
                            def sqa(e, bank=bank, c=c, nb=nb):
                                return e.activation(out=junk[:], in_=ps[bank][:, :], func=AF.Square,
                                                    accum_out=ss2[:, 2 * c + nb:2 * c + nb + 1])
                            fw.op('act', sqa, reads=[PSK[bank]], writes=['junk2', ('ss2', c, nb)])
                        fw.op('dve', lambda e, c=c: e.tensor_tensor(out=rr[:, c:c + 1], in0=ss2[:, 2 * c:2 * c + 1],
                                                                    in1=ss2[:, 2 * c + 1:2 * c + 2], op=ALU.add),
                              reads=[('ss2', c, 0), ('ss2', c, 1)], writes=[('rr', c)])
                        fw.op('act', lambda e, c=c: e.activation(out=rr[:, c:c + 1], in_=rr[:, c:c + 1], func=AF.Sqrt,
                                                                 scale=1.0 / D, bias=EPS),
                              reads=[('rr', c)], writes=[('rr', c)])
                        fw.op('dve', lambda e, c=c: e.reciprocal(out=rr[:, c:c + 1], in_=rr[:, c:c + 1]),
                              reads=[('rr', c)], writes=[('rr', c)])
                        for nb in range(2):
                            bank = 2 * c + nb
                            fw.op('dve', lambda e, c=c, nb=nb, bank=bank, j=j: e.scalar_tensor_tensor(
                                out=tn[c][:, nb * 512:(nb + 1) * 512], in0=ps[bank][:, :], scalar=rr[:, c:c + 1],
                                in1=GG[:, j, nb * 512:(nb + 1) * 512], op0=ALU.mult, op1=ALU.mult),
                                reads=[PSK[bank], ('rr', c), ('GG', j, nb)], writes=[('tn', c, nb)])
                        fw.op('pool', lambda e, c=c: e.tensor_tensor(out=xo[c][:], in0=tn[c][:], in1=xr[c][:], op=ALU.add),
                              reads=[('tn', c, 0), ('tn', c, 1), ('xr', c)], writes=[('xo', c)])
                        fw.dma('sp', dst, xo[c][:], reads=[('xo', c)], writes=[('X', tt)])
                fw.barrier()

        stopped = [False]

        def chk(name):
            if stop_after == name:
                stopped[0] = True
            return stopped[0]

        if True:
          for l in range(first_layer, n_layers):
            last = (l == DEPTH - 1)
            emit_mod(l)
            if chk('mod'):
                break
            if l % 2 == 0:
                emit_even(l, last)
            else:
                emit_odd(l, last)
            if stopped[0]:
                break
            emit_outproj(l, last)
        fw.barrier()
        print("ops emitted:", fw.nops, "sems:", fw.nsem)
    return nc


_NC_CACHE = {}


def _prep_inputs(inp, b):
    cosT, sinT, rm = _rope_tables()
    m = {}
    m["x"] = np.ascontiguousarray(inp["x"][b])
    m["ctx"] = np.ascontiguousarray(inp["ctx"][b])
    m["cvec"] = np.ascontiguousarray(np.concatenate([_col(inp["c"][b]), _col(inp["c_ctx"])], axis=1))
    m["w_mod"] = inp["w_mod"]
    m["bmod_col"] = np.ascontiguousarray(np.concatenate([_col(inp["b_mod"][l]) for l in range(DEPTH)], axis=1))
    m["bmod_gate"] = np.ascontiguousarray(np.broadcast_to(inp["b_mod"][:, None, 2 * D:], (DEPTH, 128, D)))
    m["gpre_col"] = np.ascontiguousarray(np.concatenate([_col(inp["g_pre"][l]) for l in range(DEPTH)], axis=1))
    m["gpost_b"] = np.ascontiguousarray(np.broadcast_to(inp["g_post"][:, None, :], (DEPTH, 128, D)))
    m["w_out"] = inp["w_out"]
    m["ev_w_in"] = inp["ev_w_in"]
    m["lam_b"] = np.ascontiguousarray(np.broadcast_to(inp["ev_lambda"].reshape(1, 512), (128, 512)))
    m["subln_col"] = np.ascontiguousarray(inp["ev_subln"].T)
    m["conv_col"] = np.ascontiguousarray(
        np.concatenate([_col(inp["ev_conv"][j, tap]) for j in range(2) for tap in range(3)], axis=1))
    m["od_w_in"] = inp["od_w_in"]
    m["tt_tab"] = _na_tables(inp["od_rpb"])
    m["gate_up"] = inp["od_gate_up"]
    m["gb_col"] = np.ascontiguousarray(np.concatenate(
        [_col(inp["od_gate_bias"][j, d_]) for j in range(2) for d_ in range(2)], axis=1))
    m["gnorm_b"] = np.ascontiguousarray(np.broadcast_to(inp["od_gnorm"][:, None, :], (2, 128, 256)))
    cm = np.ones((128, 512), np.float32)
    cm[:, ::64] = 0.0
    m["cmask"] = cm
    si = np.arange(64)
    m["tri"] = np.ascontiguousarray(np.concatenate([(si[:, None] <= si[None, :]), (si[:, None] >= si[None, :])],
                                                  axis=1).astype(np.float32))
    m["cosT"] = cosT
    m["sinT"] = sinT
    m["rm"] = rm
    m["ident"] = np.eye(128, dtype=np.float32)
    return m


def kernel(**inputs):
    inp = {k: np.asarray(v) for k, v in inputs.items()}
    if "nc" not in _NC_CACHE:
        _NC_CACHE["nc"] = build()
    nc = _NC_CACHE["nc"]
    in_maps = [_prep_inputs(inp, b) for b in range(8)]
    res = run_bass_kernel_spmd(nc, in_maps, core_ids=list(range(8)))
    return np.stack([r["out"] for r in res.results], axis=0).astype(np.float32)
```

```python
import math
from contextlib import ExitStack
import numpy as np
import concourse.bass as bass
import concourse.mybir as mybir
from concourse.bass_utils import run_bass_kernel_spmd

F32 = mybir.dt.float32
BF16 = mybir.dt.bfloat16
AF = mybir.ActivationFunctionType
ALU = mybir.AluOpType
AX = mybir.AxisListType

D = 1024
SEQ = 4096
CTX = 256
T = SEQ + CTX
NTT = T // 128
DEPTH = 4
EPS = 1e-6
GRID = 64
SAME_ENG_SYNC = True


class Fw:
    SEM_LIMIT = 30000

    def __init__(self, nc, es, n_dma_slots=12):
        self.nc = nc
        self.es = es
        self.E = {'pe': nc.tensor, 'dve': nc.vector, 'act': nc.scalar, 'pool': nc.gpsimd, 'sp': nc.sync}
        self.cur = {}
        self.nsem = 0
        for e in self.E:
            self.cur[e] = [self._newsem(e), 0]
        self.known = {e: {} for e in self.E}
        self.last_w = {}
        self.readers = {}
        self.slots = {}
        for q in ('sp', 'pool', 'act'):
            n = n_dma_slots if q != 'act' else 4
            self.slots[q] = [[self._newsem('d' + q), 0] for _ in range(n)]
        self.slot_rr = {q: 0 for q in self.slots}
        self.nops = 0

    def _newsem(self, tag):
        self.nsem += 1
        return self.es.enter_context(self.nc.semaphore("s%s%d" % (tag, self.nsem)))

    def _wait(self, eng, tok):
        sem, val, teng = tok
        kn = self.known[eng]
        k = id(sem)
        if kn.get(k, 0) >= val:
            return
        self.E[eng].wait_ge(sem, val)
        kn[k] = val

    def _deps(self, eng, reads, writes):
        deps = {}

        def addtok(t, hazard):
            if t is None:
                return
            sem, val, teng = t
            if teng == eng and teng != 'dma':
                if eng == 'pe' or not SAME_ENG_SYNC:
                    return
            k = id(sem)
            if k not in deps or deps[k][1] < val:
                deps[k] = t
        for r in reads:
            addtok(self.last_w.get(r), 'raw')
        for w in writes:
            addtok(self.last_w.get(w), 'waw')
            for t in self.readers.get(w, ()):
                addtok(t, 'war')
        return deps.values()

    def _commit(self, tok, reads, writes):
        for r in reads:
            self.readers.setdefault(r, []).append(tok)
        for w in writes:
            self.last_w[w] = tok
            self.readers[w] = []

    def op(self, eng, fn, reads=(), writes=()):
        for t in self._deps(eng, reads, writes):
            self._wait(eng, t)
        ins = fn(self.E[eng])
        c = self.cur[eng]
        if c[1] >= self.SEM_LIMIT:
            c[0] = self._newsem(eng)
            c[1] = 0
        c[1] += 1
        ins.then_inc(c[0], 1)
        tok = (c[0], c[1], eng)
        self._commit(tok, reads, writes)
        self.nops += 1
        return tok

    def dma(self, q, out, in_, reads=(), writes=()):
        for t in self._deps(q, reads, writes):
            self._wait(q, t)
        sl = self.slots[q]
        i = self.slot_rr[q]
        self.slot_rr[q] = (i + 1) % len(sl)
        s = sl[i]
        if s[1] > 0:
            self._wait(q, (s[0], s[1], 'dma'))
        if s[1] >= self.SEM_LIMIT:
            s[0] = self._newsem('d' + q)
            s[1] = 0
        ins = self.E[q].dma_start(out=out, in_=in_)
        s[1] += 16
        ins.then_inc(s[0], 16)
        tok = (s[0], s[1], 'dma')
        self._commit(tok, reads, writes)
        self.nops += 1
        return tok

    def barrier(self):
        toks = []
        for e in self.E:
            c = self.cur[e]
            if c[1] > 0:
                toks.append((c[0], c[1], e))
        for q in self.slots:
            for s in self.slots[q]:
                if s[1] > 0:
                    toks.append((s[0], s[1], 'dma'))
        for e in self.E:
            for t in toks:
                if t[2] == e:
                    continue
                self._wait(e, t)
        self.last_w = {}
        self.readers = {}


def _rope_tables():
    half = 32
    inv = (1.0 / (10000.0 ** (np.arange(0, half, 2, dtype=np.float32) / np.float32(half)))).astype(np.float32)
    t = np.arange(SEQ)
    row = (t // GRID).astype(np.float32)[:, None] * inv
    col = (t % GRID).astype(np.float32)[:, None] * inv
    ang = np.concatenate([row, row, col, col], axis=-1).astype(np.float32)
    cos = np.cos(ang).astype(np.float32).T
    sin = np.sin(ang).astype(np.float32).T
    sign = np.ones(64, np.float32)
    src = np.zeros(64, np.int64)
    for f in range(64):
        blk = (f // 32) * 32
        o = f % 32
        if o < 16:
            src[f] = blk + o + 16
            sign[f] = -1.0
        else:
            src[f] = blk + o - 16
            sign[f] = 1.0
    sin_s = sin * sign[:, None]
    cosT = np.concatenate([cos, cos], axis=0)
    sinT = np.concatenate([sin_s, sin_s], axis=0)
    rm = np.zeros((128, 128), np.float32)
    for m in range(2):
        for f in range(64):
            rm[m * 64 + src[f], m * 64 + f] = 1.0
    return np.ascontiguousarray(cosT), np.ascontiguousarray(sinT), rm


def _na_tables(rpb):
    NEG = np.float32(-30000.0)
    kc = np.arange(64)[:, None]
    c = np.arange(64)[None, :]
    ws = np.clip(c - 8, 0, 48)
    valid = (kc >= ws) & (kc < ws + 16)
    coff = np.clip(kc - c + 15, 0, 30)
    out = np.full((2, 16, 2, 64, 16, 64), NEG, np.float32)
    for ti in range(16):
        for par in range(2):
            if ti < 14:
                dr = ti + par
            elif ti == 14:
                dr = 3 if par == 1 else None
            else:
                dr = 10 if par == 0 else None
            if dr is None or dr > 14:
                continue
            g = rpb[:, :, dr, :][:, :, coff]
            out[:, :, par, :, ti, :] = np.where(valid[None, None], g, NEG)
    seq = np.stack([out[..., 14, :], out[..., 4, :], out[..., 6, :], out[..., 8, :], out[..., 15, :]], axis=-2)
    msk = np.full_like(out[..., 0, :], NEG)
    seqa = np.stack([out[..., 3, :], out[..., 5, :], out[..., 7, :], out[..., 9, :], msk], axis=-2)
    out = np.concatenate([out, seq, seqa], axis=-2)
    return np.ascontiguousarray(out.reshape(2, 16, 128, 26 * 64))


def _col(v):
    return np.ascontiguousarray(v.reshape(-1, 128).T)


class _Stop(Exception):
    pass


def build(n_layers=DEPTH, dbg=False, stop_after=None, first_layer=0):
    nc = bass.Bass("TRN2", target_bir_lowering=False)

    def din(name, shape, dt=F32):
        return nc.dram_tensor(name, list(shape), dt, kind="ExternalInput").ap()

    def dscr(name, shape, dt):
        return nc.dram_tensor(name, list(shape), dt, kind="Internal").ap()

    x_in = din("x", [SEQ, D])
    ctx_in = din("ctx", [CTX, D])
    cvec = din("cvec", [128, 16])
    w_mod = din("w_mod", [DEPTH, D, 3 * D])
    bmod_col = din("bmod_col", [128, DEPTH * 24])
    bmod_gate = din("bmod_gate", [DEPTH, 128, D])
    gpre_col = din("gpre_col", [128, DEPTH * 8])
    gpost_b = din("gpost_b", [DEPTH, 128, D])
    w_out = din("w_out", [DEPTH, 2 * D, D])
    ev_w_in = din("ev_w_in", [2, D, 8 * D])
    lam_b = din("lam_b", [128, 2 * 256])
    subln_col = din("subln_col", [128, 2])
    conv_col = din("conv_col", [128, 2 * 3 * 8])
    cosT_d = din("cosT", [128, SEQ])
    sinT_d = din("sinT", [128, SEQ])
    rm_d = din("rm", [128, 128])
    ident_d = din("ident", [128, 128])
    od_w_in = din("od_w_in", [2, D, 7200])
    tt_d = din("tt_tab", [2, 16, 128, 26 * 64])
    gu_d = din("gate_up", [2, 2, 16, 512])
    gb_col = din("gb_col", [128, 16])
    gnorm_b = din("gnorm_b", [2, 128, 256])
    cmask_d = din("cmask", [128, 512])
    tri_d = din("tri", [64, 128])
    out_d = nc.dram_tensor("out", [SEQ, D], F32, kind="ExternalOutput").ap()
    xc_d = nc.dram_tensor("xc_out", [CTX, D], F32, kind="ExternalOutput" if dbg else "Internal").ap()

    KT_s = dscr("KT_s", [8, 128, T], BF16)
    QT_s = dscr("QT_s", [8, 128, T], BF16)
    V_s = dscr("V_s", [T, D], BF16)
    GA_s = dscr("GA_s", [8, 128, T], F32)
    UT_s = dscr("UT_s", [16, 128, T], BF16)
    VN_s = dscr("VN_s", [T, 16 * 65], BF16)
    CG_s = dscr("CG_s", [T, D], F32)
    DG_s = dscr("DG_s", [T, D], F32)
    OF_s = dscr("OF_s", [T, D], F32)
    OB_s = dscr("OB_s", [T, D], F32)

    es_top = ExitStack()
    with es_top as es:
        fw = Fw(nc, es)

        uid = [0]

        def sb(name, shape, dt, stack=None):
            uid[0] += 1
            return (stack or es).enter_context(nc.sbuf_tensor("sb_%s_%d" % (name, uid[0]), list(shape), dt))

        ps = [es.enter_context(nc.psum_tensor("ps%d" % i, [128, 512], F32)) for i in range(8)]
        PSK = [('ps', i) for i in range(8)]

        ident = sb("ident", [128, 128], F32)
        ones_f = sb("ones_f", [128, 128], F32)
        ones_b = sb("ones_b", [128, 128], BF16)
        rm_b = sb("rm_b", [128, 128], BF16)
        cs = sb("cs", [128, 16], F32)
        csb = sb("csb", [128, 16, 128], F32)
        bmodc = sb("bmodc", [128, DEPTH * 24], F32)
        gprec = sb("gprec", [128, DEPTH * 8], F32)
        lamt = sb("lamt", [128, 512], F32)
        lam2 = sb("lam2", [128, 4, 64], F32)
        lsum = sb("lsum", [128, 4], F32)
        lam_c = sb("lam_c", [128, 2], F32)
        sublc = sb("sublc", [128, 2], F32)
        convc = sb("convc", [128, 48], F32)
        A_m = sb("A_m", [128, 2, 8], F32)
        B_m = sb("B_m", [128, 2, 8], F32)
        sc_m = sb("sc_m", [128, 2, 8], F32)
        GG = sb("GG", [128, 2, D], F32)

        identb = sb("identb", [128, 128], BF16)
        cmask = sb("cmask", [128, 512], F32)
        tri = sb("tri", [64, 128], F32)
        gbc = sb("gbc", [128, 16], F32)
        fw.dma('pool', identb[:], ident_d[:, :], writes=['identb'])
        fw.dma('sp', cmask[:], cmask_d[:, :], writes=['cmask'])
        fw.dma('sp', tri[:], tri_d[:, :], writes=['tri'])
        fw.dma('sp', gbc[:], gb_col[:, :], writes=['gbc'])
        fw.op('dve', lambda e: e.tensor_scalar(out=gbc[:], in0=gbc[:], scalar1=-1.0, scalar2=None, op0=ALU.mult),
              reads=['gbc'], writes=['gbc'])
        fw.dma('sp', ident[:], ident_d[:, :], writes=['ident'])
        fw.dma('pool', rm_b[:], rm_d[:, :], writes=['rm_b'])
        fw.dma('sp', cs[:], cvec[:, :], writes=['cs'])
        fw.dma('sp', bmodc[:], bmod_col[:, :], writes=['bmodc'])
        fw.dma('sp', gprec[:], gpre_col[:, :], writes=['gprec'])
        fw.dma('sp', lamt[:], lam_b[:, :], writes=['lamt'])
        fw.dma('sp', sublc[:], subln_col[:, :], writes=['sublc'])
        fw.dma('sp', convc[:], conv_col[:, :], writes=['convc'])
        epsc = sb("epsc", [128, 1], F32)
        fw.op('dve', lambda e: e.memset(epsc[:], EPS), writes=['epsc'])
        fw.op('dve', lambda e: e.memset(ones_f[:], 1.0), writes=['ones_f'])
        fw.op('dve', lambda e: e.memset(ones_b[:], 1.0), writes=['ones_b'])
        fw.op('act', lambda e: e.activation(out=cs[:], in_=cs[:], func=AF.Silu), reads=['cs'], writes=['cs'])
        for k in range(16):
            fw.op('dve', lambda e, k=k: e.tensor_scalar(out=csb[:, k, :], in0=ones_f[:], scalar1=cs[:, k:k + 1],
                                                        scalar2=None, op0=ALU.mult),
                  reads=['ones_f', 'cs'], writes=[('csb', k)])
        lt = lamt[:].rearrange("p (j a d) -> p j a d", j=2, a=4)
        for j in range(2):
            for a in range(2):
                fw.op('dve', lambda e, j=j, a=a: e.tensor_tensor(out=lam2[:, j * 2 + a, :], in0=lt[:, j, 2 * a, :],
                                                                 in1=lt[:, j, 2 * a + 1, :], op=ALU.mult),
                      reads=['lamt'], writes=[('lam2', j, a)])
                fw.op('dve', lambda e, j=j, a=a: e.tensor_reduce(out=lsum[:, j * 2 + a:j * 2 + a + 1],
                                                                 in_=lam2[:, j * 2 + a, :], axis=AX.X, op=ALU.add),
                      reads=[('lam2', j, a)], writes=[('lsum', j, a)])
        fw.op('act', lambda e: e.activation(out=lsum[:], in_=lsum[:], func=AF.Exp),
              reads=[('lsum', j, a) for j in range(2) for a in range(2)], writes=['lsume'])
        for j in range(2):
            lam_init = 0.8 - 0.6 * math.exp(-0.3 * (2 * j))
            fw.op('dve', lambda e, j=j, li=lam_init: e.scalar_tensor_tensor(
                out=lam_c[:, j:j + 1], in0=lsum[:, 2 * j + 1:2 * j + 2], scalar=-li, in1=lsum[:, 2 * j:2 * j + 1],
                op0=ALU.add, op1=ALU.subtract), reads=['lsume'], writes=[('lam_c', j)])
            fw.op('dve', lambda e, j=j, li=lam_init: e.tensor_scalar(
                out=sublc[:, j:j + 1], in0=sublc[:, j:j + 1], scalar1=(1.0 - li), scalar2=None, op0=ALU.mult),
                reads=['sublc'], writes=['sublc'])

        def emit_mod(l):
            with ExitStack() as st:
                wm = [sb("wm%d" % i, [128, 8, 512], F32, st) for i in range(2)]
                bg = sb("bg", [128, D], F32, st)
                gp = sb("gp", [128, D], F32, st)
                tmpg = sb("tmpg", [128, 512], F32, st)
                fw.dma('sp', bg[:], bmod_gate[l, :, :], writes=['bg'])
                fw.dma('sp', gp[:], gpost_b[l, :, :], writes=['gp'])
                wv = w_mod[l, :, :].rearrange("(kc p) n -> p kc n", p=128)
                for blk in range(6):
                    b = blk % 2
                    fw.dma('sp', wm[b][:], wv[:, :, blk * 512:(blk + 1) * 512], writes=[('wm', b)])
                    if blk < 4:
                        for n4 in range(4):
                            n = blk * 4 + n4

                            def mm(e, n=n, n4=n4, b=b):
                                r = None
                                for kc in range(8):
                                    r = e.matmul(ps[0][:, 2 * n:2 * n + 2], lhsT=wm[b][:, kc, n4 * 128:(n4 + 1) * 128],
                                                 rhs=cs[:, kc:16:8], start=(kc == 0), stop=(kc == 7))
                                return r
                            fw.op('pe', mm, reads=[('wm', b), 'cs'], writes=[('modps', n)] + ([PSK[0]] if n == 0 else []))
                    else:
                        nb = blk - 4
                        for j in range(2):
                            bank = 1 + j

                            def mm(e, j=j, b=b, bank=bank):
                                r = None
                                for kc in range(8):
                                    r = e.matmul(ps[bank][:, :], lhsT=csb[:, j * 8 + kc, :], rhs=wm[b][:, kc, :],
                                                 start=(kc == 0), stop=(kc == 7))
                                return r
                            fw.op('pe', mm, reads=[('wm', b)] + [('csb', j * 8 + kc) for kc in range(8)], writes=[PSK[bank]])
                            fw.op('dve', lambda e, nb=nb, bank=bank: e.tensor_tensor(
                                out=tmpg[:], in0=ps[bank][:, :], in1=bg[:, nb * 512:(nb + 1) * 512], op=ALU.add),
                                reads=[PSK[bank], 'bg'], writes=['tmpg'])
                            fw.op('dve', lambda e, nb=nb, j=j: e.tensor_tensor(
                                out=GG[:, j, nb * 512:(nb + 1) * 512], in0=tmpg[:], in1=gp[:, nb * 512:(nb + 1) * 512],
                                op=ALU.mult), reads=['tmpg', 'gp'], writes=[('GG', j, nb)])
                pv = ps[0][:, 0:32].rearrange("p (n j) -> p n j", j=2)
                allmod = [('modps', n) for n in range(16)]
                for j in range(2):
                    fw.op('dve', lambda e, j=j: e.tensor_tensor(out=B_m[:, j, :], in0=pv[:, 0:8, j],
                                                                in1=bmodc[:, l * 24:l * 24 + 8], op=ALU.add),
                          reads=allmod + ['bmodc'], writes=[('B_m', j)])
                    fw.op('dve', lambda e, j=j: e.tensor_tensor(out=sc_m[:, j, :], in0=pv[:, 8:16, j],
                                                                in1=bmodc[:, l * 24 + 8:l * 24 + 16], op=ALU.add),
                          reads=allmod + ['bmodc'], writes=[('sc_m', j)])
                    fw.op('dve', lambda e, j=j: e.scalar_tensor_tensor(
                        out=A_m[:, j, :], in0=sc_m[:, j, :], scalar=1.0, in1=gprec[:, l * 8:(l + 1) * 8],
                        op0=ALU.add, op1=ALU.mult), reads=[('sc_m', j), 'gprec'], writes=[('A_m', j)])
                fw.barrier()

        def emit_hT(l, hT, st):
            xt = [sb("xt%d" % i, [128, D], F32, st) for i in range(2)]
            xn = [sb("xn%d" % i, [128, D], F32, st) for i in range(2)]
            junk = sb("junk", [128, D], BF16, st)
            ssq = sb("ssq", [128, 2], F32, st)
            rsd = sb("rsd", [128, 2], F32, st)
            acs = sb("acs", [128, 2], F32, st)

            def load(tt):
                b = tt % 2
                if tt < 32:
                    src = (x_in if l == first_layer else out_d)[tt * 128:(tt + 1) * 128, :]
                else:
                    src = (ctx_in if l == first_layer else xc_d)[(tt - 32) * 128:(tt - 31) * 128, :]
                fw.dma('sp', xt[b][:], src, reads=[('X', tt)], writes=[('xt', b)])
            load(0)
            for tt in range(NTT):
                b = tt % 2
                j = 0 if tt < 32 else 1
                if tt + 1 < NTT:
                    load(tt + 1)
                def sqa(e, b=b):
                    return e.activation(out=junk[:], in_=xt[b][:], func=AF.Square, accum_out=ssq[:, b:b + 1])
                fw.op('act', sqa, reads=[('xt', b)], writes=['junk', ('ssq', b)])
                fw.op('act', lambda e, b=b: e.activation(out=rsd[:, b:b + 1], in_=ssq[:, b:b + 1], func=AF.Sqrt,
                                                         scale=1.0 / D, bias=EPS),
                      reads=[('ssq', b)], writes=[('rsd', b)])
                fw.op('dve', lambda e, b=b: e.reciprocal(out=rsd[:, b:b + 1], in_=rsd[:, b:b + 1]),
                      reads=[('rsd', b)], writes=[('rsd', b)])
                fw.op('dve', lambda e, b=b: e.tensor_scalar(out=xn[b][:], in0=xt[b][:], scalar1=rsd[:, b:b + 1],
                                                            scalar2=None, op0=ALU.mult),
                      reads=[('xt', b), ('rsd', b)], writes=[('xn', b)])
                for half in range(2):
                    bank = 2 * b + half

                    def tr(e, b=b, half=half, bank=bank):
                        r = None
                        for q4 in range(4):
                            kc = half * 4 + q4
                            r = e.transpose(out=ps[bank][:, q4 * 128:(q4 + 1) * 128],
                                            in_=xn[b][:, kc * 128:(kc + 1) * 128], identity=ident[:])
                        return r
                    fw.op('pe', tr, reads=[('xn', b), 'ident'], writes=[PSK[bank]])
                    for q4 in range(4):
                        kc = half * 4 + q4
                        fw.op('dve', lambda e, kc=kc, q4=q4, bank=bank, tt=tt, j=j: e.tensor_scalar(
                            out=hT[:, kc, tt * 128:(tt + 1) * 128], in0=ps[bank][:, q4 * 128:(q4 + 1) * 128],
                            scalar1=A_m[:, j, kc:kc + 1], scalar2=B_m[:, j, kc:kc + 1], op0=ALU.mult, op1=ALU.add),
                            reads=[PSK[bank], ('A_m', j), ('B_m', j)], writes=[('hT', kc, tt)])

        TB = [(i * 512, 512) for i in range(8)] + [(SEQ, CTX)]

        def hT_keys(t0, n):
            return [('hT', kc, tt) for kc in range(8) for tt in range(t0 // 128, (t0 + n) // 128)]

        def emit_even(l, last):
            je = l // 2
            w_in = ev_w_in[je, :, :].rearrange("(kc p) f -> p kc f", p=128)
            with ExitStack() as st1:
                hT = sb("hT", [128, 8, T], BF16, st1)
                with ExitStack() as st:
                    emit_hT(l, hT, st)
                    fw.barrier()
                if chk('hT'):
                    return
                with ExitStack() as st:
                    cosT = sb("cosT", [128, SEQ], F32, st)
                    sinT = sb("sinT", [128, SEQ], F32, st)
                    fw.dma('sp', cosT[:], cosT_d[:, :], writes=['cosT'])
                    fw.dma('sp', sinT[:], sinT_d[:, :], writes=['sinT'])
                    wt = [sb("wt%d" % i, [128, 8, 512], BF16, st) for i in range(2)]
                    qb = [sb("qb%d" % i, [128, 512], BF16, st) for i in range(2)]
                    t1 = [sb("t1_%d" % i, [128, 512], F32, st) for i in range(2)]
                    t2 = [sb("t2_%d" % i, [128, 512], F32, st) for i in range(2)]
                    ob = [sb("ob%d" % i, [128, 512], BF16, st) for i in range(2)]
                    gs = [sb("gs%d" % i, [128, 512], F32, st) for i in range(2)]
                    vs = [sb("vs%d" % i, [128, D], BF16, st) for i in range(2)]
                    cnt = [0]
                    wcnt = [0]

                    def load_w(c0):
                        b = wcnt[0] % 2
                        wcnt[0] += 1
                        fw.dma('pool', wt[b][:], w_in[:, :, c0:c0 + 512], writes=[('wt', b)])
                        return b

                    def proj(b, ft, t0, n, bank):
                        def mm(e):
                            r = None
                            for kc in range(8):
                                r = e.matmul(ps[bank][:, 0:n], lhsT=wt[b][:, kc, ft * 128:(ft + 1) * 128],
                                             rhs=hT[:, kc, t0:t0 + n], start=(kc == 0), stop=(kc == 7))
                            return r
                        fw.op('pe', mm, reads=[('wt', b)] + hT_keys(t0, n), writes=[PSK[bank]])

                    plan = [('k', 0), ('k', 512), ('q', 2048), ('q', 2560), ('g', 3072), ('g', 3584)]
                    import os as _os
                    _kinds = _os.environ.get("KINDS")
                    if _kinds is not None:
                        plan = [p for p in plan if p[0] in _kinds]
                    nextb = load_w(plan[0][1])
                    for pi, (kind, c0) in enumerate(plan):
                        b = nextb
                        if pi + 1 < len(plan):
                            nextb = load_w(plan[pi + 1][1])
                        for ft in range(4):
                            hd = ((c0 % 1024) // 128) + ft
                            for (t0, n) in TB:
                                if last and kind in ('q', 'g') and t0 >= SEQ:
                                    continue
                                _rd0 = _os.environ.get("ROPE_DBG", "")
                                if 'ctxonly' in _rd0 and t0 < SEQ:
                                    continue
                                if 'latonly' in _rd0 and t0 >= SEQ:
                                    continue
                                i = cnt[0] % 2
                                cnt[0] += 1
                                bank = i
                                proj(b, ft, t0, n, bank)
                                if kind == 'g':
                                    fw.op('act', lambda e, i=i, n=n, bank=bank: e.activation(
                                        out=gs[i][:, 0:n], in_=ps[bank][:, 0:n], func=AF.Silu),
                                        reads=[PSK[bank]], writes=[('gs', i)])
                                    fw.dma('sp', GA_s[hd, :, t0:t0 + n], gs[i][:, 0:n], reads=[('gs', i)],
                                           writes=[('GA', hd, t0)])
                                    continue
                                dst = KT_s if kind == 'k' else QT_s
                                if t0 >= SEQ or 'norope' in _rd0:
                                    fw.op('act', lambda e, i=i, n=n, bank=bank: e.activation(
                                        out=ob[i][:, 0:n], in_=ps[bank][:, 0:n], func=AF.Copy),
                                        reads=[PSK[bank]], writes=[('ob', i)])
                                else:
                                    fw.op('act', lambda e, i=i, n=n, bank=bank: e.activation(
                                        out=qb[i][:, 0:n], in_=ps[bank][:, 0:n], func=AF.Copy),
                                        reads=[PSK[bank]], writes=[('qb', i)])
                                    if 'cosconst' in _rd0:
                                        fw.op('dve', lambda e, i=i, n=n, bank=bank, t0=t0: e.tensor_tensor(
                                            out=t1[i][:, 0:n], in0=ps[bank][:, 0:n], in1=gs[0][:, 0:n], op=ALU.mult),
                                            reads=[PSK[bank], ('gs', 0)], writes=[('t1', i)])
                                    else:
                                      fw.op('dve', lambda e, i=i, n=n, bank=bank, t0=t0: e.tensor_tensor(
                                        out=t1[i][:, 0:n], in0=ps[bank][:, 0:n], in1=cosT[:, t0:t0 + n], op=ALU.mult),
                                        reads=[PSK[bank], 'cosT', ('qb', i)], writes=[('t1', i)])
                                    _rd = _os.environ.get("ROPE_DBG", "")
                                    if 'nomm' not in _rd:
                                        fw.op('pe', lambda e, i=i, n=n: e.matmul(
                                            ps[2 + i][:, 0:n], lhsT=rm_b[:], rhs=qb[i][:, 0:n], start=True, stop=True),
                                            reads=['rm_b', ('qb', i)], writes=[PSK[2 + i]])
                                    if 'not2' not in _rd:
                                        fw.op('dve', lambda e, i=i, n=n, t0=t0: e.tensor_tensor(
                                            out=t2[i][:, 0:n], in0=ps[2 + i][:, 0:n], in1=sinT[:, t0:t0 + n], op=ALU.mult),
                                            reads=[PSK[2 + i], 'sinT'], writes=[('t2', i)])
                                    else:
                                        fw.op('dve', lambda e, i=i, n=n, t0=t0: e.tensor_tensor(
                                            out=t2[i][:, 0:n], in0=t1[i][:, 0:n], in1=(gs[0][:, 0:n] if 'cosconst' in _rd0 else sinT[:, t0:t0 + n]), op=ALU.mult),
                                            reads=[('t1', i), 'sinT'], writes=[('t2', i)])
                                    if 'actob' in _rd0:
                                        fw.op('dve', lambda e, i=i, n=n: e.tensor_tensor(
                                            out=t1[i][:, 0:n], in0=t1[i][:, 0:n], in1=t2[i][:, 0:n], op=ALU.add),
                                            reads=[('t1', i), ('t2', i)], writes=[('t1', i)])
                                        fw.op('act', lambda e, i=i, n=n, bank=bank: e.activation(
                                            out=ob[i][:, 0:n], in_=ps[bank][:, 0:n], func=AF.Copy),
                                            reads=[PSK[bank]], writes=[('ob', i)])
                                    else:
                                      fw.op(_os.environ.get("ROPE_ADD_ENG", "pool"), lambda e, i=i, n=n: e.tensor_tensor(
                                        out=ob[i][:, 0:n], in0=t1[i][:, 0:n], in1=t2[i][:, 0:n], op=ALU.add),
                                        reads=[('t1', i), ('t2', i)], writes=[('ob', i)])
                                fw.dma('sp', dst[hd, :, t0:t0 + n], ob[i][:, 0:n], reads=[('ob', i)],
                                       writes=[(kind, hd, t0)])
                    bv = [load_w(1024), load_w(1536)]
                    for tt in range(NTT if (_kinds is None or 'v' in _kinds) else 0):
                        i = tt % 2
                        for nb in range(2):
                            bank = 4 + 2 * i + nb

                            def mm(e, tt=tt, nb=nb, bank=bank):
                                r = None
                                for kc in range(8):
                                    r = e.matmul(ps[bank][:, :], lhsT=hT[:, kc, tt * 128:(tt + 1) * 128],
                                                 rhs=wt[bv[nb]][:, kc, :], start=(kc == 0), stop=(kc == 7))
                                return r
                            fw.op('pe', mm, reads=[('wt', bv[nb])] + [('hT', kc, tt) for kc in range(8)],
                                  writes=[PSK[bank]])
                            eng = 'act' if nb == 0 else 'dve'
                            if eng == 'act':
                                fw.op('act', lambda e, i=i, nb=nb, bank=bank: e.activation(
                                    out=vs[i][:, nb * 512:(nb + 1) * 512], in_=ps[bank][:, :], func=AF.Copy),
                                    reads=[PSK[bank]], writes=[('vs', i, nb)])
                            else:
                                fw.op('dve', lambda e, i=i, nb=nb, bank=bank: e.tensor_copy(
                                    out=vs[i][:, nb * 512:(nb + 1) * 512], in_=ps[bank][:, :]),
                                    reads=[PSK[bank]], writes=[('vs', i, nb)])
                        fw.dma('sp', V_s[tt * 128:(tt + 1) * 128, :], vs[i][:], reads=[('vs', i, 0), ('vs', i, 1)],
                               writes=[('V', tt)])
                    fw.barrier()
                if chk('2a'):
                    return
                with ExitStack() as st:
                    wt = [sb("wtb%d" % i, [128, 8, 512], BF16, st) for i in range(2)]
                    vrow = sb("vrow", [128, T + 4], F32, st)
                    bbg = sb("bbg", [128, T], F32, st)
                    acc = sb("acc", [128, T], F32, st)
                    ubr = [sb("ubr0", [128, T], BF16, st)] * 2
                    hsb = [sb("hsb%d" % i, [128, 512], F32, st) for i in range(2)]
                    sg = [sb("sg%d" % i, [128, 512], F32, st) for i in range(2)]
                    fw.op('pool', lambda e: e.memset(vrow[:], 0.0), writes=['vrow_all'])
                    LOFF = 1
                    COFF = SEQ + 3

                    def load_wb(jf, b):
                        for gi, base in enumerate((4096, 5120, 6144, 7168)):
                            fw.dma('pool', wt[b][:, :, gi * 128:(gi + 1) * 128],
                                   w_in[:, :, base + jf * 128:base + (jf + 1) * 128], writes=[('wtb', b, gi)])
                    load_wb(0, 0)
                    cnt = 0
                    for jf in range(8):
                        b = jf % 2
                        if jf + 1 < 8:
                            load_wb(jf + 1, 1 - b)
                        for (t0, n) in TB:
                            if last and t0 >= SEQ:
                                continue
                            i = cnt % 2
                            cnt += 1
                            for gi in range(4):
                                bank = 4 * i + gi

                                def mm(e, gi=gi, bank=bank, t0=t0, n=n, b=b):
                                    r = None
                                    for kc in range(8):
                                        r = e.matmul(ps[bank][:, 0:n], lhsT=wt[b][:, kc, gi * 128:(gi + 1) * 128],
                                                     rhs=hT[:, kc, t0:t0 + n], start=(kc == 0), stop=(kc == 7))
                                    return r
                                fw.op('pe', mm, reads=[('wtb', b, gi)] + hT_keys(t0, n), writes=[PSK[bank]])
                            voff = (LOFF + t0) if t0 < SEQ else (COFF + t0 - SEQ)
                            fw.op('act', lambda e, i=i, n=n: e.activation(out=hsb[i][:, 0:n], in_=ps[4 * i + 0][:, 0:n],
                                                                          func=AF.Copy),
                                  reads=[PSK[4 * i + 0]], writes=[('hsb', i)])
                            fw.op('dve', lambda e, i=i, n=n, voff=voff: e.tensor_tensor(
                                out=vrow[:, voff:voff + n], in0=ps[4 * i + 2][:, 0:n], in1=hsb[i][:, 0:n], op=ALU.mult),
                                reads=[PSK[4 * i + 2], ('hsb', i), 'vrow_all'], writes=[('vrow', t0)])
                            fw.op('act', lambda e, i=i, n=n: e.activation(out=sg[i][:, 0:n], in_=ps[4 * i + 3][:, 0:n],
                                                                          func=AF.Silu),
                                  reads=[PSK[4 * i + 3]], writes=[('sg', i)])
                            fw.op('dve', lambda e, i=i, n=n, t0=t0: e.tensor_tensor(
                                out=bbg[:, t0:t0 + n], in0=ps[4 * i + 1][:, 0:n], in1=sg[i][:, 0:n], op=ALU.mult),
                                reads=[PSK[4 * i + 1], ('sg', i)], writes=[('bbg', t0)])
                        segs = [(LOFF, 0, SEQ)] + ([] if last else [(COFF, SEQ, CTX)])
                        vk = [('vrow', t0) for (t0, n) in TB if not (last and t0 >= SEQ)]
                        bk = [('bbg', t0) for (t0, n) in TB if not (last and t0 >= SEQ)]
                        u = ubr[b]
                        for (vo, o0, n) in segs:
                            wc = lambda tap: convc[:, je * 24 + tap * 8 + jf:je * 24 + tap * 8 + jf + 1]
                            fw.op('dve', lambda e, vo=vo, o0=o0, n=n, wc=wc: e.tensor_scalar(
                                out=acc[:, o0:o0 + n], in0=vrow[:, vo - 1:vo - 1 + n], scalar1=wc(0), scalar2=None,
                                op0=ALU.mult), reads=vk + ['convc', 'vrow_all'], writes=[('acc', o0)])
                            for tap in (1, 2):
                                fw.op('dve', lambda e, vo=vo, o0=o0, n=n, wc=wc, tap=tap: e.scalar_tensor_tensor(
                                    out=acc[:, o0:o0 + n], in0=vrow[:, vo - 1 + tap:vo - 1 + tap + n], scalar=wc(tap),
                                    in1=acc[:, o0:o0 + n], op0=ALU.mult, op1=ALU.add),
                                    reads=vk + [('acc', o0)], writes=[('acc', o0)])
                            fw.op('pool', lambda e, o0=o0, n=n, u=u: e.tensor_tensor(
                                out=u[:, o0:o0 + n], in0=acc[:, o0:o0 + n], in1=bbg[:, o0:o0 + n], op=ALU.mult),
                                reads=[('acc', o0)] + bk, writes=[('ubr', 0, o0)])
                            fw.dma('sp', UT_s[8 + jf, :, o0:o0 + n], u[:, o0:o0 + n], reads=[('ubr', 0, o0)],
                                   writes=[('UT', 8 + jf, o0)])
                    fw.barrier()
                if chk('2b'):
                    return
            with ExitStack() as st:
                KT = sb("KT", [128, 8, T], BF16, st)
                V = sb("V", [128, NTT, D], BF16, st)
                for hd in range(8):
                    fw.dma('sp', KT[:, hd, :], KT_s[hd, :, :], writes=[('KT', hd)])
                Vv = V_s.rearrange("(tt p) f -> p tt f", p=128)
                for g in range(NTT // 2):
                    fw.dma('sp', V[:, 2 * g:2 * g + 2, :], Vv[:, 2 * g:2 * g + 2, :], writes=[('Vt', 2 * g), ('Vt', 2 * g + 1)])
                qt = [sb("qt%d" % i, [128, 512], BF16, st) for i in range(2)]
                ga = [sb("ga%d" % i, [128, 512], F32, st) for i in range(2)]
                pT = [sb("pT%d" % i, [128, 512], BF16, st) for i in range(4)]
                r1 = sb("r1", [128, 512], F32, st)
                a1 = sb("a1", [128, 512], F32, st)
                a2 = sb("a2", [128, 512], F32, st)
                dd = sb("dd", [128, 512], F32, st)
                uo = [sb("uo%d" % i, [128, 512], BF16, st) for i in range(2)]
                work = [(hd, t0, n) for hd in range(8) for (t0, n) in TB if not (last and t0 >= SEQ)]

                def loadq(wi):
                    hd, t0, n = work[wi]
                    i = wi % 2
                    fw.dma('sp', qt[i][:, 0:n], QT_s[hd, :, t0:t0 + n], reads=[('q', hd, t0)], writes=[('qt', i)])

                def loadg(wi):
                    if wi >= len(work):
                        return
                    hd, t0, n = work[wi]
                    i = wi % 2
                    fw.dma('sp', ga[i][:, 0:n], GA_s[hd, :, t0:t0 + n], reads=[('GA', hd, t0)], writes=[('ga', i)])
                loadq(0)
                zacc = [[sb("zacc%d_%d" % (w_, m_), [128, 512], F32, st) for m_ in range(2)] for w_ in range(2)]
                units = []
                for wi, (hd, t0, n) in enumerate(work):
                    kts = list(range(NTT)) if t0 < SEQ else [32, 33]
                    for ki, kt in enumerate(kts):
                        units.append((wi, ki, kt, len(kts)))

                def emit_qk(u):
                    wi, ki, kt, nk = units[u]
                    hd, t0, n = work[wi]
                    i = wi % 2
                    for m in range(2):
                        sbank = 2 * (u % 2) + m
                        fw.op('pe', lambda e, m=m, sbank=sbank: e.matmul(
                            ps[sbank][:, 0:n], lhsT=KT[m * 64:(m + 1) * 64, hd, kt * 128:(kt + 1) * 128],
                            rhs=qt[i][m * 64:(m + 1) * 64, 0:n], start=True, stop=True),
                            reads=[('KT', hd), ('qt', i)], writes=[PSK[sbank]])

                def emit_rest(u):
                    wi, ki, kt, nk = units[u]
                    hd, t0, n = work[wi]
                    wz = wi % 2
                    for m in range(2):
                        sbank = 2 * (u % 2) + m
                        fw.op('act', lambda e, sbank=sbank: e.activation(
                            out=pT[sbank][:, 0:n], in_=ps[sbank][:, 0:n], func=AF.Exp, scale=0.125),
                            reads=[PSK[sbank]], writes=[('pT', sbank)])
                    if u + 1 < len(units):
                        wj = units[u + 1][0]
                        if wj != wi and wj + 1 < len(work):
                            loadq(wj + 1)
                        emit_qk(u + 1)
                    first = (ki == 0)
                    lastk = (ki == nk - 1)
                    for m in range(2):
                        pi = 2 * (u % 2) + m
                        bo = 4 + 2 * m
                        fw.op('pe', lambda e, pi=pi, bo=bo: e.matmul(
                            ps[bo][:, 0:n], lhsT=V[:, kt, hd * 128:(hd + 1) * 128], rhs=pT[pi][:, 0:n],
                            start=first, stop=lastk), reads=[('Vt', kt), ('pT', pi)], writes=[PSK[bo]])
                        zeng = 'dve'
                        if m == 1:
                            fw.op('pe', lambda e, pi=pi: e.matmul(
                                ps[7][:, 0:n], lhsT=ones_b[:], rhs=pT[pi][:, 0:n], start=first, stop=lastk),
                                reads=['ones_b', ('pT', pi)], writes=[PSK[7]])
                        elif first:
                            fw.op(zeng, lambda e, pi=pi, m=m: e.tensor_copy(out=zacc[wz][m][:, 0:n], in_=pT[pi][:, 0:n]),
                                  reads=[('pT', pi)], writes=[('zacc', wz, m)])
                        else:
                            fw.op(zeng, lambda e, pi=pi, m=m: e.tensor_tensor(
                                out=zacc[wz][m][:, 0:n], in0=zacc[wz][m][:, 0:n], in1=pT[pi][:, 0:n], op=ALU.add),
                                reads=[('pT', pi), ('zacc', wz, m)], writes=[('zacc', wz, m)])

                if len(work) > 1:
                    loadq(1)
                loadg(0)
                loadg(1)
                emit_qk(0)
                for u in range(len(units)):
                    emit_rest(u)
                    wi, ki, kt, nk = units[u]
                    if ki != nk - 1:
                        continue
                    hd, t0, n = work[wi]
                    i = wi % 2
                    wz = wi % 2
                    for m in range(1):
                        fw.op('pe', lambda e, m=m: e.matmul(ps[5 + 2 * m][:, 0:n], lhsT=ones_f[:], rhs=zacc[wz][m][:, 0:n],
                                                            start=True, stop=True),
                              reads=['ones_f', ('zacc', wz, m)], writes=[PSK[5 + 2 * m]])
                    fw.op('dve', lambda e: e.tensor_copy(out=a1[:, 0:n], in_=ps[4][:, 0:n]), reads=[PSK[4]], writes=['a1'])
                    fw.op('dve', lambda e: e.tensor_copy(out=a2[:, 0:n], in_=ps[6][:, 0:n]), reads=[PSK[6]], writes=['a2'])
                    fw.op('dve', lambda e: e.tensor_copy(out=r1[:, 0:n], in_=ps[7][:, 0:n]), reads=[PSK[7]], writes=['r1'])
                    fw.op('dve', lambda e: e.reciprocal(out=r1[:, 0:n], in_=r1[:, 0:n]), reads=['r1'], writes=['r1'])
                    fw.op('dve', lambda e: e.tensor_tensor(out=a2[:, 0:n], in0=a2[:, 0:n], in1=r1[:, 0:n], op=ALU.mult),
                          reads=['a2', 'r1'], writes=['a2'])
                    fw.op('dve', lambda e: e.reciprocal(out=r1[:, 0:n], in_=ps[5][:, 0:n]), reads=[PSK[5]], writes=['r1'])
                    fw.op('dve', lambda e: e.tensor_tensor(out=a1[:, 0:n], in0=a1[:, 0:n], in1=r1[:, 0:n], op=ALU.mult),
                          reads=['a1', 'r1'], writes=['a1'])
                    fw.op('dve', lambda e: e.scalar_tensor_tensor(
                        out=dd[:, 0:n], in0=a2[:, 0:n], scalar=lam_c[:, je:je + 1], in1=a1[:, 0:n],
                        op0=ALU.mult, op1=ALU.add), reads=['a1', 'a2', ('lam_c', je)], writes=['dd'])
                    fw.op('dve', lambda e: e.tensor_tensor(out=a2[:, 0:n], in0=dd[:, 0:n], in1=dd[:, 0:n], op=ALU.mult),
                          reads=['dd'], writes=['a2'])
                    fw.op('pe', lambda e: e.matmul(ps[5][:, 0:n], lhsT=ones_f[:], rhs=a2[:, 0:n], start=True, stop=True),
                          reads=['ones_f', 'a2'], writes=[PSK[5]])
                    fw.op('act', lambda e: e.activation(out=a1[:, 0:n], in_=ps[5][:, 0:n], func=AF.Ln,
                                                        scale=1.0 / 128, bias=epsc[:, 0:1]),
                          reads=[PSK[5], 'epsc'], writes=['a1'])
                    fw.op('act', lambda e: e.activation(out=a1[:, 0:n], in_=a1[:, 0:n], func=AF.Exp, scale=-0.5),
                          reads=['a1'], writes=['a1'])
                    fw.op('dve', lambda e: e.scalar_tensor_tensor(
                        out=dd[:, 0:n], in0=dd[:, 0:n], scalar=sublc[:, je:je + 1], in1=a1[:, 0:n],
                        op0=ALU.mult, op1=ALU.mult), reads=['dd', 'a1', 'sublc'], writes=['dd'])
                    fw.op('dve', lambda e: e.tensor_tensor(out=uo[i][:, 0:n], in0=dd[:, 0:n], in1=ga[i][:, 0:n], op=ALU.mult),
                          reads=['dd', ('ga', i)], writes=[('uo', i)])
                    fw.dma('sp', UT_s[hd, :, t0:t0 + n], uo[i][:, 0:n], reads=[('uo', i)], writes=[('UT', hd, t0)])
                    loadg(wi + 2)
                fw.barrier()

        def emit_odd(l, last):
            jo = l // 2
            w_in = od_w_in[jo, :, :].rearrange("(kc p) f -> p kc f", p=128)
            C_CK, C_CV, C_DK, C_DV, C_LR, C_CQ, C_DQ, C_CG, C_DG = 0, 1024, 2048, 2560, 3584, 3616, 4640, 5152, 6176
            ps7b = ps[7].bitcast(BF16)
            with ExitStack() as stL:
                lrT = sb("lrT", [16, 2, T], BF16, stL)
                gub = sb("gub", [16, 2, 512], BF16, stL)
                fw.dma('pool', gub[:], gu_d[jo, :, :, :].rearrange("d r c -> r d c"), writes=['gub'])
                with ExitStack() as st1:
                    hT = sb("hT", [128, 8, T], BF16, st1)
                    with ExitStack() as st:
                        emit_hT(l, hT, st)
                        fw.barrier()
                    if chk('hT'):
                        return
                    with ExitStack() as st:
                        wt = [sb("wto%d" % i, [128, 8, 512], BF16, st) for i in range(4)]
                        wlr = sb("wlr", [128, 8, 32], BF16, st)
                        ob = [sb("oob%d" % i, [128, 512], BF16, st) for i in range(2)]
                        gs = [sb("ogs%d" % i, [128, 512], F32, st) for i in range(2)]
                        vsn = [sb("vsn%d" % i, [128, 16, 65], BF16, st) for i in range(2)]
                        vs = [sb("ovs%d" % i, [128, D], BF16, st) for i in range(2)]
                        gst = [sb("gst%d" % i, [128, D], F32, st) for i in range(2)]
                        for i in range(2):
                            fw.op('pool', lambda e, i=i: e.memset(vsn[i][:], 1.0), writes=[('vsn', i, 0), ('vsn', i, 1)])
                        wcnt = [0]

                        def load_w(c0, ncol=512):
                            b = wcnt[0] % 4
                            wcnt[0] += 1
                            fw.dma('pool', wt[b][:, :, 0:ncol], w_in[:, :, c0:c0 + ncol], writes=[('wt', b)])
                            return b
                        fw.dma('pool', wlr[:], w_in[:, :, C_LR:C_LR + 32], writes=['wlr'])
                        cnt = 0
                        for dr_ in range(2):
                            for (t0, n) in TB:
                                bank = cnt % 2
                                cnt += 1

                                def mm(e, dr_=dr_, t0=t0, n=n, bank=bank):
                                    r = None
                                    for kc in range(8):
                                        r = e.matmul(ps[bank][0:16, 0:n], lhsT=wlr[:, kc, dr_ * 16:(dr_ + 1) * 16],
                                                     rhs=hT[:, kc, t0:t0 + n], start=(kc == 0), stop=(kc == 7))
                                    return r
                                fw.op('pe', mm, reads=['wlr'] + hT_keys(t0, n), writes=[PSK[bank]])
                                fw.op('act', lambda e, dr_=dr_, t0=t0, n=n, bank=bank: e.activation(
                                    out=lrT[0:16, dr_, t0:t0 + n], in_=ps[bank][0:16, 0:n], func=AF.Copy),
                                    reads=[PSK[bank]], writes=[('lrT', dr_, t0)])
                        plan = [('ck', C_CK, 0), ('ck', C_CK + 512, 4), ('cq', C_CQ, 0), ('cq', C_CQ + 512, 4),
                                ('dk', C_DK, 0), ('dq', C_DQ, 0)]
                        nextb = load_w(plan[0][1])
                        cnt = 0
                        for pi, (kind, c0, f0) in enumerate(plan):
                            b = nextb
                            if pi + 1 < len(plan):
                                nextb = load_w(plan[pi + 1][1])
                            for ft in range(4):
                                fidx = f0 + ft
                                for (t0, n) in TB:
                                    if last and kind in ('cq', 'dq') and t0 >= SEQ:
                                        continue
                                    i = cnt % 2
                                    cnt += 1
                                    bank = i

                                    def mm(e, b=b, ft=ft, t0=t0, n=n, bank=bank):
                                        r = None
                                        for kc in range(8):
                                            r = e.matmul(ps[bank][:, 0:n], lhsT=wt[b][:, kc, ft * 128:(ft + 1) * 128],
                                                         rhs=hT[:, kc, t0:t0 + n], start=(kc == 0), stop=(kc == 7))
                                        return r
                                    fw.op('pe', mm, reads=[('wt', b)] + hT_keys(t0, n), writes=[PSK[bank]])
                                    if kind in ('ck', 'cq'):
                                        sc = 1.0 if kind == 'ck' else 0.125
                                        dst = KT_s if kind == 'ck' else QT_s
                                        fw.op('act', lambda e, i=i, n=n, bank=bank, sc=sc: e.activation(
                                            out=ob[i][:, 0:n], in_=ps[bank][:, 0:n], func=AF.Copy, scale=sc),
                                            reads=[PSK[bank]], writes=[('ob', i)])
                                        fw.dma('sp', dst[fidx, :, t0:t0 + n], ob[i][:, 0:n], reads=[('ob', i)],
                                               writes=[(kind, fidx, t0)])
                                    else:
                                        sc = 1.0 if kind == 'dk' else (128.0 ** -0.5)
                                        gi = fidx if kind == 'dk' else 4 + fidx
                                        fw.op('act', lambda e, i=i, n=n, bank=bank, sc=sc: e.activation(
                                            out=gs[i][:, 0:n], in_=ps[bank][:, 0:n], func=AF.Copy, scale=sc),
                                            reads=[PSK[bank]], writes=[('gs', i)])
                                        fw.dma('sp', GA_s[gi, :, t0:t0 + n], gs[i][:, 0:n], reads=[('gs', i)],
                                               writes=[('GA', gi, t0)])
                        for (kind, c0) in (('cv', C_CV), ('dv', C_DV), ('cg', C_CG), ('dg', C_DG)):
                            bv = [load_w(c0), load_w(c0 + 512)]
                            for tt in range(NTT):
                                if last and kind in ('cg', 'dg') and tt >= 32:
                                    continue
                                i = tt % 2
                                for nb in range(2):
                                    bank = 2 + 2 * i + nb

                                    def mm(e, tt=tt, nb=nb, bank=bank, bv=bv):
                                        r = None
                                        for kc in range(8):
                                            r = e.matmul(ps[bank][:, :], lhsT=hT[:, kc, tt * 128:(tt + 1) * 128],
                                                         rhs=wt[bv[nb]][:, kc, :], start=(kc == 0), stop=(kc == 7))
                                        return r
                                    fw.op('pe', mm, reads=[('wt', bv[nb])] + [('hT', kc, tt) for kc in range(8)],
                                          writes=[PSK[bank]])
                                    if kind == 'cv':
                                        fw.op('act', lambda e, i=i, nb=nb, bank=bank: e.activation(
                                            out=vsn[i][:, nb * 8:(nb + 1) * 8, 0:64],
                                            in_=ps[bank][:, :].rearrange("p (h d) -> p h d", d=64), func=AF.Copy),
                                            reads=[PSK[bank]], writes=[('vsn', i, nb)])
                                    elif kind == 'dv':
                                        fw.op('act', lambda e, i=i, nb=nb, bank=bank: e.activation(
                                            out=vs[i][:, nb * 512:(nb + 1) * 512], in_=ps[bank][:, :], func=AF.Copy),
                                            reads=[PSK[bank]], writes=[('vs', i, nb)])
                                    else:
                                        fw.op('act', lambda e, i=i, nb=nb, bank=bank: e.activation(
                                            out=gst[i][:, nb * 512:(nb + 1) * 512], in_=ps[bank][:, :], func=AF.Silu),
                                            reads=[PSK[bank]], writes=[('gst', i, nb)])
                                rows = slice(tt * 128, (tt + 1) * 128)
                                if kind == 'cv':
                                    fw.dma('sp', VN_s[rows, :], vsn[i][:].rearrange("p h d -> p (h d)"),
                                           reads=[('vsn', i, 0), ('vsn', i, 1)], writes=[('VN', tt)])
                                elif kind == 'dv':
                                    fw.dma('sp', V_s[rows, :], vs[i][:], reads=[('vs', i, 0), ('vs', i, 1)], writes=[('DV', tt)])
                                else:
                                    fw.dma('sp', (CG_s if kind == 'cg' else DG_s)[rows, :], gst[i][:],
                                           reads=[('gst', i, 0), ('gst', i, 1)], writes=[(kind, tt)])
                        fw.barrier()
                if chk('o2'):
                    return
                order_f = [64, 65, 66, 67] + list(range(64))
                order_b = [67, 66, 65, 64] + list(range(63, -1, -1))
                NCH = T // 64
                for hp in range(2):
                    with ExitStack() as st:
                        chains = [(d_, 2 * hp + hl) for d_ in range(2) for hl in range(2)]
                        qtl = [sb("qtl%d" % ci, [128, T], BF16, st) for ci in range(4)]
                        ktl = [sb("ktl%d" % ci, [128, T], BF16, st) for ci in range(4)]
                        Et = [sb("Et%d" % ci, [128, NCH], F32, st) for ci in range(4)]
                        S = [sb("S%d" % ci, [128, 256], F32, st) for ci in range(4)]
                        Sb = [sb("Sb%d" % ci, [128, 256], BF16, st) for ci in range(4)]
                        qf = [sb("qf%d" % i, [128, 512], F32, st) for i in range(2)]
                        kf = [sb("kf%d" % i, [128, 512], F32, st) for i in range(2)]
                        e1 = [sb("e1_%d" % i, [128, 512], F32, st) for i in range(2)]
                        spt = [sb("spt%d" % i, [128, 512], F32, st) for i in range(2)]
                        pit = [sb("pit%d" % i, [128, 512], F32, st) for i in range(2)]
                        eq = [sb("eq%d" % i, [128, 512], F32, st) for i in range(2)]
                        ek = [sb("ek%d" % i, [128, 512], F32, st) for i in range(2)]
                        attb = [sb("attb%d" % ci, [64, 64], BF16, st) for ci in range(4)]
                        ktm = [sb("ktm%d" % ci, [64, 128], BF16, st) for ci in range(4)]
                        vch = [[sb("vch%d_%d" % (d_, i), [64, 512], BF16, st) for i in range(2)] for d_ in range(2)]
                        ost = [[sb("ost%d_%d" % (d_, i), [64, 512], F32, st) for i in range(2)] for d_ in range(2)]
                        cnt = 0
                        for ci, (d_, hh) in enumerate(chains):
                            for (t0, n) in TB:
                                i = cnt % 2
                                cnt += 1
                                nch = n // 64
                                c0 = t0 // 64
                                needq = not (last and t0 >= SEQ)
                                if needq:
                                    fw.dma('sp', qf[i][:, 0:n], GA_s[4 + hh, :, t0:t0 + n], reads=[('GA', 4 + hh, t0)],
                                           writes=[('qf', i)])
                                fw.dma('sp', kf[i][:, 0:n], GA_s[hh, :, t0:t0 + n], reads=[('GA', hh, t0)],
                                       writes=[('kf', i)])
                                fw.op('pe', lambda e, d_=d_, hh=hh, t0=t0, n=n: e.matmul(
                                    ps[6][:, 0:n], lhsT=gub[0:16, d_, hh * 128:(hh + 1) * 128], rhs=lrT[0:16, d_, t0:t0 + n],
                                    start=True, stop=True), reads=['gub', ('lrT', d_, t0)], writes=[PSK[6]])
                                gcol = jo * 8 + d_ * 4 + hh
                                fw.op('act', lambda e, i=i, n=n, gcol=gcol: e.activation(
                                    out=e1[i][:, 0:n], in_=ps[6][:, 0:n], func=AF.Exp, scale=-1.0, bias=gbc[:, gcol:gcol + 1]),
                                    reads=[PSK[6], 'gbc'], writes=[('e1', i)])
                                fw.op('act', lambda e, i=i, n=n: e.activation(
                                    out=spt[i][:, 0:n], in_=e1[i][:, 0:n], func=AF.Ln, bias=1.0),
                                    reads=[('e1', i)], writes=[('spt', i)])
                                fw.op('dve', lambda e, i=i, n=n: e.tensor_tensor_scan(
                                    out=pit[i][:, 0:n], data0=cmask[:, 0:n], data1=spt[i][:, 0:n], initial=0.0,
                                    op0=ALU.mult, op1=ALU.add), reads=['cmask', ('spt', i)], writes=[('pit', i)])
                                pv3 = pit[i][:, 0:n].rearrange("p (c s) -> p c s", s=64)
                                fw.op('act', lambda e, ci=ci, c0=c0, nch=nch, pv3=pv3: e.activation(
                                    out=Et[ci][:, c0:c0 + nch], in_=pv3[:, :, 63], func=AF.Exp, scale=-1.0 / 16),
                                    reads=[('pit', i)], writes=[('Et', ci, t0)])
                                if d_ == 1:
                                    fw.op('dve', lambda e, i=i, n=n: e.tensor_tensor(
                                        out=pit[i][:, 0:n], in0=pit[i][:, 0:n], in1=spt[i][:, 0:n], op=ALU.subtract),
                                        reads=[('pit', i), ('spt', i)], writes=[('pit', i)])
                                sq_ = (-1.0 / 16) if d_ == 0 else (1.0 / 16)
                                fw.op('act', lambda e, i=i, n=n, sq_=sq_: e.activation(
                                    out=eq[i][:, 0:n], in_=pit[i][:, 0:n], func=AF.Exp, scale=sq_),
                                    reads=[('pit', i)], writes=[('eq', i)])
                                fw.op('act', lambda e, i=i, n=n, sq_=sq_: e.activation(
                                    out=ek[i][:, 0:n], in_=pit[i][:, 0:n], func=AF.Exp, scale=-sq_),
                                    reads=[('pit', i)], writes=[('ek', i)])
                                if needq:
                                    fw.op('dve', lambda e, i=i, n=n, ci=ci, t0=t0: e.tensor_tensor(
                                        out=qtl[ci][:, t0:t0 + n], in0=qf[i][:, 0:n], in1=eq[i][:, 0:n], op=ALU.mult),
                                        reads=[('qf', i), ('eq', i)], writes=[('qtl', ci, t0)])
                                fw.op('pool', lambda e, i=i, n=n, ci=ci, t0=t0: e.tensor_tensor(
                                    out=ktl[ci][:, t0:t0 + n], in0=kf[i][:, 0:n], in1=ek[i][:, 0:n], op=ALU.mult),
                                    reads=[('kf', i), ('ek', i)], writes=[('ktl', ci, t0)])
                        for ci in range(4):
                            fw.op('dve', lambda e, ci=ci: e.memset(S[ci][:], 0.0), writes=[('S', ci)])
                            fw.op('pool', lambda e, ci=ci: e.memset(Sb[ci][:], 0.0), writes=[('Sb', ci)])
                        def cinfo(step, d_):
                            c = (order_f if d_ == 0 else order_b)[step]
                            tk = c * 64
                            blk = (tk // 512) * 512 if tk < SEQ else SEQ
                            want = not (last and c >= 64)
                            return c, tk, blk, want

                        def phase_B(step):
                            for d_ in range(2):
                                c, tk, blk, want = cinfo(step, d_)
                                for hl in range(2):
                                    ci = d_ * 2 + hl
                                    if want:
                                        fw.op('pe', lambda e, ci=ci, tk=tk: e.matmul(
                                            ps[ci][0:64, 0:64], lhsT=ktl[ci][:, tk:tk + 64], rhs=qtl[ci][:, tk:tk + 64],
                                            start=True, stop=True), reads=[('qtl', ci, blk), ('ktl', ci, blk)],
                                            writes=[('psa', ci)])
                                    fw.op('pe', lambda e, ci=ci, tk=tk: e.transpose(
                                        out=ps7b[0:64, ci * 128:(ci + 1) * 128], in_=ktl[ci][:, tk:tk + 64], identity=identb[:]),
                                        reads=[('ktl', ci, blk), 'identb'], writes=[('ps7', ci)])

                        phase_B(0)
                        for step in range(NCH):
                            sb_i = step % 2
                            for d_ in range(2):
                                c, tk, blk, want = cinfo(step, d_)
                                fw.dma('sp', vch[d_][sb_i][:], V_s[tk:tk + 64, hp * 512:(hp + 1) * 512],
                                       reads=[('DV', tk // 128)], writes=[('vch', d_, sb_i)])
                                if d_ == 1:
                                    for hl in range(2):
                                        ci = 2 + hl
                                        fw.op('dve', lambda e, ci=ci, c=c: e.tensor_scalar(
                                            out=S[ci][:], in0=S[ci][:], scalar1=Et[ci][:, c:c + 1], scalar2=None, op0=ALU.mult),
                                            reads=[('S', ci), ('Et', ci, blk)], writes=[('S', ci)])
                                        fw.op('pool', lambda e, ci=ci: e.tensor_copy(out=Sb[ci][:], in_=S[ci][:]),
                                              reads=[('S', ci)], writes=[('Sb', ci)])
                            for d_ in range(2):
                                c, tk, blk, want = cinfo(step, d_)
                                for hl in range(2):
                                    ci = d_ * 2 + hl
                                    if want:
                                        fw.op('dve', lambda e, ci=ci, d_=d_: e.tensor_tensor(
                                            out=attb[ci][:], in0=ps[ci][0:64, 0:64], in1=tri[:, d_ * 64:(d_ + 1) * 64], op=ALU.mult),
                                            reads=[('psa', ci), 'tri'], writes=[('attb', ci)])
                                    fw.op('act', lambda e, ci=ci: e.activation(
                                        out=ktm[ci][:], in_=ps7b[0:64, ci * 128:(ci + 1) * 128], func=AF.Copy),
                                        reads=[('ps7', c_) for c_ in range(4)], writes=[('ktm', ci)])
                            if step + 1 < NCH:
                                phase_B(step + 1)
                            for d_ in range(2):
                                c, tk, blk, want = cinfo(step, d_)
                                for hl in range(2):
                                    ci = d_ * 2 + hl
                                    vv = vch[d_][sb_i][:, hl * 256:(hl + 1) * 256]
                                    if want:
                                        def omm(e, ci=ci, tk=tk, vv=vv):
                                            e.matmul(ps[ci][0:64, 128:384], lhsT=qtl[ci][:, tk:tk + 64], rhs=Sb[ci][:],
                                                     start=True, stop=False)
                                            return e.matmul(ps[ci][0:64, 128:384], lhsT=attb[ci][:], rhs=vv,
                                                            start=False, stop=True)
                                        fw.op('pe', omm, reads=[('qtl', ci, blk), ('Sb', ci), ('attb', ci), ('vch', d_, sb_i)],
                                              writes=[('pso', ci)])
                                    kvb = 4 + ci // 2
                                    kvc = (ci % 2) * 256
                                    fw.op('pe', lambda e, ci=ci, vv=vv, kvb=kvb, kvc=kvc: e.matmul(
                                        ps[kvb][:, kvc:kvc + 256], lhsT=ktm[ci][:], rhs=vv, start=True, stop=True),
                                        reads=[('ktm', ci), ('vch', d_, sb_i)], writes=[('pskv', ci)])
                            for d_ in range(2):
                                c, tk, blk, want = cinfo(step, d_)
                                for hl in range(2):
                                    ci = d_ * 2 + hl
                                    kvb = 4 + ci // 2
                                    kvc = (ci % 2) * 256
                                    if want:
                                        fw.op('dve', lambda e, ci=ci, d_=d_, hl=hl, sb_i=sb_i: e.tensor_copy(
                                            out=ost[d_][sb_i][:, hl * 256:(hl + 1) * 256], in_=ps[ci][0:64, 128:384]),
                                            reads=[('pso', ci)], writes=[('ost', d_, sb_i, hl)])
                                    fw.op('dve', lambda e, ci=ci, kvb=kvb, kvc=kvc: e.tensor_tensor(
                                        out=S[ci][:], in0=ps[kvb][:, kvc:kvc + 256], in1=S[ci][:], op=ALU.add),
                                        reads=[('pskv', 2 * (ci // 2)), ('pskv', 2 * (ci // 2) + 1), ('S', ci)], writes=[('S', ci)])
                                    if d_ == 0:
                                        fw.op('dve', lambda e, ci=ci, c=c: e.tensor_scalar(
                                            out=S[ci][:], in0=S[ci][:], scalar1=Et[ci][:, c:c + 1], scalar2=None, op0=ALU.mult),
                                            reads=[('S', ci), ('Et', ci, blk)], writes=[('S', ci)])
                                        fw.op('pool', lambda e, ci=ci: e.tensor_copy(out=Sb[ci][:], in_=S[ci][:]),
                                              reads=[('S', ci)], writes=[('Sb', ci)])
                                if want:
                                    dst = (OF_s if d_ == 0 else OB_s)[tk:tk + 64, hp * 512:(hp + 1) * 512]
                                    fw.dma('sp', dst, ost[d_][sb_i][:], reads=[('ost', d_, sb_i, 0), ('ost', d_, sb_i, 1)],
                                           writes=[('O', d_, c, hp)])
                        fw.barrier()
                if chk('o3'):
                    return
            with ExitStack() as st:
                gnb = sb("gnb", [128, 256], F32, st)
                fw.dma('sp', gnb[:], gnorm_b[jo, :, :], writes=['gnb'])
                oft = [sb("oft%d" % i, [128, D], F32, st) for i in range(2)]
                obt = [sb("obt%d" % i, [128, D], F32, st) for i in range(2)]
                dgt = [sb("dgt%d" % i, [128, D], F32, st) for i in range(2)]
                junk = sb("junk3", [128, 256], BF16, st)
                ssq = sb("ssq3", [128, 8], F32, st)
                ugb = [sb("ugb%d" % i, [128, 8, 128], BF16, st) for i in range(2)]
                ntt = 32 if last else NTT

                def loadc(tt):
                    i = tt % 2
                    rows = slice(tt * 128, (tt + 1) * 128)
                    fw.dma('sp', oft[i][:], OF_s[rows, :], writes=[('oft', i)])
                    fw.dma('sp', obt[i][:], OB_s[rows, :], writes=[('obt', i)])
                    fw.dma('sp', dgt[i][:], DG_s[rows, :], writes=[('dgt', i)])
                loadc(0)
                for tt in range(ntt):
                    i = tt % 2
                    if tt + 1 < ntt:
                        loadc(tt + 1)
                    fw.op('pool', lambda e, i=i: e.tensor_tensor(out=oft[i][:], in0=oft[i][:], in1=obt[i][:], op=ALU.add),
                          reads=[('oft', i), ('obt', i)], writes=[('oft', i)])
                    for hh in range(4):
                        fw.op('act', lambda e, i=i, hh=hh: e.activation(
                            out=junk[:], in_=oft[i][:, hh * 256:(hh + 1) * 256], func=AF.Square,
                            accum_out=ssq[:, i * 4 + hh:i * 4 + hh + 1]), reads=[('oft', i)], writes=['junk3', ('ssq3', i, hh)])
                    fw.op('act', lambda e, i=i: e.activation(out=ssq[:, i * 4:i * 4 + 4], in_=ssq[:, i * 4:i * 4 + 4],
                                                             func=AF.Sqrt, scale=1.0 / 256, bias=EPS),
                          reads=[('ssq3', i, hh) for hh in range(4)], writes=[('rs3', i)])
                    fw.op('dve', lambda e, i=i: e.reciprocal(out=ssq[:, i * 4:i * 4 + 4], in_=ssq[:, i * 4:i * 4 + 4]),
                          reads=[('rs3', i)], writes=[('rs3', i)])
                    for hh in range(4):
                        fw.op('dve', lambda e, i=i, hh=hh: e.scalar_tensor_tensor(
                            out=oft[i][:, hh * 256:(hh + 1) * 256], in0=oft[i][:, hh * 256:(hh + 1) * 256],
                            scalar=ssq[:, i * 4 + hh:i * 4 + hh + 1], in1=gnb[:], op0=ALU.mult, op1=ALU.mult),
                            reads=[('oft', i), ('rs3', i), 'gnb'], writes=[('oft', i)])
                    fw.op('pool', lambda e, i=i: e.tensor_tensor(out=oft[i][:], in0=oft[i][:], in1=dgt[i][:], op=ALU.mult),
                          reads=[('oft', i), ('dgt', i)], writes=[('oft', i)])
                    for half in range(2):
                        bank = 2 * i + half

                        def tr(e, i=i, half=half, bank=bank):
                            r = None
                            for q4 in range(4):
                                fc = half * 4 + q4
                                r = e.transpose(out=ps[bank][:, q4 * 128:(q4 + 1) * 128],
                                                in_=oft[i][:, fc * 128:(fc + 1) * 128], identity=ident[:])
                            return r
                        fw.op('pe', tr, reads=[('oft', i), 'ident'], writes=[PSK[bank]])
                        fw.op('dve', lambda e, i=i, half=half, bank=bank: e.tensor_copy(
                            out=ugb[i][:, half * 4:(half + 1) * 4, :],
                            in_=ps[bank][:, :].rearrange("p (f t) -> p f t", t=128)),
                            reads=[PSK[bank]], writes=[('ugb', i, half)])
                    fw.dma('sp', UT_s[8:16, :, tt * 128:(tt + 1) * 128].rearrange("f p t -> p f t"), ugb[i][:],
                           reads=[('ugb', i, 0), ('ugb', i, 1)], writes=[('UTg', tt)])
                fw.barrier()
            if chk('o3b'):
                return
            for hf in range(2):
                with ExitStack() as st:
                    KN = sb("KN", [128, 4, T], BF16, st)
                    VN = sb("VN", [128, NTT, 8 * 65], BF16, st)
                    TTt = sb("TTt", [128, 8, 26 * 64], BF16, st)
                    for j in range(4):
                        fw.dma('sp', KN[:, j, :], KT_s[hf * 4 + j, :, :], writes=[('KN', j)])
                    VNv = VN_s.rearrange("(tt p) f -> p tt f", p=128)
                    for g in range(NTT // 2):
                        fw.dma('sp', VN[:, 2 * g:2 * g + 2, :], VNv[:, 2 * g:2 * g + 2, hf * 520:(hf + 1) * 520],
                               writes=[('VNt', 2 * g), ('VNt', 2 * g + 1)])
                    for hl in range(8):
                        fw.dma('pool', TTt[:, hl, :], tt_d[jo, hf * 8 + hl, :, :], writes=[('TT', hl)])
                        fw.op('act', lambda e, hl=hl: e.activation(out=TTt[:, hl, :], in_=TTt[:, hl, :], func=AF.Exp),
                              reads=[('TT', hl)], writes=[('TT', hl)])
                    qn = [sb("qn%d" % i, [128, 4, 512], BF16, st) for i in range(2)]
                    cgt = [sb("cgt%d" % i, [128, 512], F32, st) for i in range(2)]
                    pT = [sb("pTn%d" % i, [128, 896], BF16, st) for i in range(2)]
                    un = [sb("un%d" % i, [128, 512], F32, st) for i in range(2)]
                    rz = sb("rz", [128, 16], F32, st)
                    ust = [sb("ust%d" % i, [128, 4, 512], BF16, st) for i in range(2)]
                    rblocks = list(range(8)) + ([] if last else [8])
                    QTv = QT_s.rearrange("f p t -> p f t")

                    def loadq(bi):
                        rb = rblocks[bi]
                        i = bi % 2
                        n = 512 if rb < 8 else CTX
                        fw.dma('sp', qn[i][:, :, 0:n], QTv[:, hf * 4:(hf + 1) * 4, rb * 512:rb * 512 + n], writes=[('qn', i)])

                    items = []
                    pcnt_ = 0
                    for bi, rb in enumerate(rblocks):
                        npairs = 4 if rb < 8 else 2
                        for pp in range(npairs):
                            ri = pcnt_ % 2
                            pcnt_ += 1
                            if rb < 8:
                                r = rb * 8 + 2 * pp
                                if 4 <= r <= 58:
                                    a0 = (r - 4) // 2
                                    tiles = [a0 + k for k in range(5)]
                                    eb = ('seq',)
                                else:
                                    rs_ = min(max(r - 4, 0), 56)
                                    tiles = [rs_ // 2 + k for k in range(4)]
                                    eb = ('str', 2 * (rs_ // 2) - r + 7)
                            else:
                                tiles = []
                                eb = None
                            alltiles = tiles + [32, 33]
                            for hl in range(8):
                                items.append(dict(bi=bi, rb=rb, pp=pp, ri=ri, hl=hl, tiles=alltiles, nloc=len(tiles), eb=eb,
                                                  npairs=npairs))

                    def emit_s(k):
                        it_ = items[k]
                        hl, qi, pp, alltiles = it_['hl'], it_['bi'] % 2, it_['pp'], it_['tiles']
                        j = hl // 2
                        p0 = (hl % 2) * 64
                        b0 = 2 * (k % 2)

                        def smm(e):
                            r_ = None
                            for q_, a in enumerate(alltiles):
                                bank = b0 + (q_ // 4)
                                col = (q_ % 4) * 128
                                r_ = e.matmul(ps[bank][:, col:col + 128],
                                              lhsT=KN[p0:p0 + 64, j, a * 128:(a + 1) * 128],
                                              rhs=qn[qi][p0:p0 + 64, j, pp * 128:(pp + 1) * 128],
                                              start=True, stop=True)
                            return r_
                        fw.op('pe', smm, reads=[('KN', j), ('qn', qi)], writes=[PSK[b0], PSK[b0 + 1]])

                    loadq(0)
                    if len(rblocks) > 1:
                        loadq(1)
                    emit_s(0)
                    for k, it_ in enumerate(items):
                        bi, rb, pp, ri, hl, alltiles, nloc, eb, npairs = (it_['bi'], it_['rb'], it_['pp'], it_['ri'], it_['hl'],
                                                                          it_['tiles'], it_['nloc'], it_['eb'], it_['npairs'])
                        qi = bi % 2
                        ntl = len(alltiles)
                        b0 = 2 * (k % 2)
                        obank = 4 + k % 2
                        pi_ = k % 2
                        if hl == 0:
                            tok0 = rb * 512 + pp * 128
                            fw.dma('sp', cgt[ri][:], CG_s[tok0:tok0 + 128, hf * 512:(hf + 1) * 512], writes=[('cgt', ri)])
                            if pp == 0 and bi >= 1 and bi + 1 < len(rblocks):
                                loadq(bi + 1)
                        n0 = min(ntl, 4) * 128
                        fw.op('act', lambda e: e.activation(out=pT[pi_][:, 0:n0], in_=ps[b0][:, 0:n0], func=AF.Exp),
                              reads=[PSK[b0]], writes=[('pTn', pi_, 0)])
                        if ntl > 4:
                            n1 = (ntl - 4) * 128
                            fw.op('act', lambda e: e.activation(out=pT[pi_][:, 512:512 + n1], in_=ps[b0 + 1][:, 0:n1], func=AF.Exp),
                                  reads=[PSK[b0 + 1]], writes=[('pTn', pi_, 1)])
                        pkeys = [('pTn', pi_, 0)] + ([('pTn', pi_, 1)] if ntl > 4 else [])
                        if nloc > 0:
                            tv = TTt[:, hl, :].rearrange("p (t c) -> p t c", c=64)
                            pv4 = pT[pi_][:, 0:nloc * 128].rearrange("p (t w c) -> p t w c", w=2, c=64)
                            for w_ in range(2):
                                if eb[0] == 'seq':
                                    ebv = tv[:, 21:26, :] if w_ == 0 else tv[:, 16:21, :]
                                else:
                                    t0_ = eb[1] - w_
                                    ebv = tv[:, t0_:t0_ + 7:2, :]
                                fw.op('dve', lambda e, w_=w_, ebv=ebv: e.tensor_tensor(
                                    out=pv4[:, :, w_, :], in0=pv4[:, :, w_, :], in1=ebv, op=ALU.mult),
                                    reads=pkeys + [('TT', hl)], writes=pkeys)
                        if k + 1 < len(items):
                            emit_s(k + 1)

                        def pvm(e):
                            r_ = None
                            for q_, a in enumerate(alltiles):
                                r_ = e.matmul(ps[obank][:, 0:65], lhsT=pT[pi_][:, q_ * 128:(q_ + 1) * 128],
                                              rhs=VN[:, a, hl * 65:(hl + 1) * 65], start=(q_ == 0),
                                              stop=(q_ == len(alltiles) - 1))
                            return r_
                        fw.op('pe', pvm, reads=pkeys + [('VNt', a) for a in alltiles], writes=[PSK[obank]])
                        fw.op('dve', lambda e: e.reciprocal(
                            out=rz[:, ri * 8 + hl:ri * 8 + hl + 1], in_=ps[obank][:, 64:65]),
                            reads=[PSK[obank]], writes=[('rz', ri, hl)])
                        fw.op('dve', lambda e: e.tensor_scalar(
                            out=un[ri][:, hl * 64:(hl + 1) * 64], in0=ps[obank][:, 0:64],
                            scalar1=rz[:, ri * 8 + hl:ri * 8 + hl + 1], scalar2=None, op0=ALU.mult),
                            reads=[PSK[obank], ('rz', ri, hl)], writes=[('un', ri, hl)])
                        if hl != 7:
                            continue
                        unk = [('un', ri, h_) for h_ in range(8)]
                        fw.op('pool', lambda e: e.tensor_tensor(out=un[ri][:], in0=un[ri][:], in1=cgt[ri][:], op=ALU.mult),
                              reads=unk + [('cgt', ri)], writes=[('ung', ri)] + unk)

                        def tr(e):
                            r_ = None
                            for j_ in range(4):
                                r_ = e.transpose(out=ps[6][:, j_ * 128:(j_ + 1) * 128], in_=un[ri][:, j_ * 128:(j_ + 1) * 128],
                                                 identity=ident[:])
                            return r_
                        fw.op('pe', tr, reads=[('ung', ri), 'ident'] + unk, writes=[PSK[6]])
                        fw.op('dve', lambda e: e.tensor_copy(
                            out=ust[qi][:, :, pp * 128:(pp + 1) * 128], in_=ps[6][:, :].rearrange("p (f t) -> p f t", t=128)),
                            reads=[PSK[6]], writes=[('ust', qi, pp)])
                        if pp != npairs - 1:
                            continue
                        n = 512 if rb < 8 else CTX
                        fw.dma('sp', UT_s[hf * 4:(hf + 1) * 4, :, rb * 512:rb * 512 + n].rearrange("f p t -> p f t"),
                               ust[qi][:, :, 0:n], reads=[('ust', qi, p_) for p_ in range(npairs)], writes=[('UTn', hf, rb)])
                    fw.barrier()

        def emit_outproj(l, last):
            with ExitStack() as st:
                wo = sb("wo", [128, 16, D], BF16, st)
                wov = w_out[l, :, :].rearrange("(fc p) n -> p fc n", p=128)
                for g in range(4):
                    fw.dma('pool', wo[:, 4 * g:4 * g + 4, :], wov[:, 4 * g:4 * g + 4, :], writes=[('wo', g)])
                wok = [('wo', g) for g in range(4)]
                ut = [sb("ut%d" % i, [128, 16, 512], BF16, st) for i in range(2)]
                xr = [sb("xr%d" % i, [128, D], F32, st) for i in range(2)]
                tn = [sb("tn%d" % i, [128, D], F32, st) for i in range(2)]
                xo = [sb("xo%d" % i, [128, D], F32, st) for i in range(2)]
                junk = sb("junk2", [128, 512], BF16, st)
                ss2 = sb("ss2", [128, 4], F32, st)
                rr = sb("rr", [128, 2], F32, st)
                acs2 = sb("acs2", [128, 2], F32, st)
                blocks = [tb for tb in TB if not (last and tb[0] >= SEQ)]
                UTv = UT_s.rearrange("f p t -> p f t")

                def loadu(bi):
                    t0, n = blocks[bi]
                    i = bi % 2
                    fw.dma('sp', ut[i][:, :, 0:n], UTv[:, :, t0:t0 + n],
                           reads=[('UT', f, t0) for f in range(16)] + [('UT', f, 0) for f in range(8, 16)] +
                           [('UT', f, SEQ) for f in range(8, 16)], writes=[('ut', i)])
                loadu(0)
                tcnt = 0
                for bi, (t0, n) in enumerate(blocks):
                    i = bi % 2
                    if bi + 1 < len(blocks):
                        loadu(bi + 1)
                    for stl in range(n // 128):
                        tt = (t0 // 128) + stl
                        j = 0 if tt < 32 else 1
                        c = tcnt % 2
                        tcnt += 1
                        src = (x_in if l == first_layer else out_d)[tt * 128:(tt + 1) * 128, :] if tt < 32 else \
                            (ctx_in if l == first_layer else xc_d)[(tt - 32) * 128:(tt - 31) * 128, :]
                        dst = out_d[tt * 128:(tt + 1) * 128, :] if tt < 32 else xc_d[(tt - 32) * 128:(tt - 31) * 128, :]
                        fw.dma('sp', xr[c][:], src, reads=[('X', tt)], writes=[('xr', c)])
                        for nb in range(2):
                            bank = 2 * c + nb

                            def mm(e, i=i, stl=stl, nb=nb, bank=bank):
                                r = None
                                for fc in range(16):
                                    r = e.matmul(ps[bank][:, :], lhsT=ut[i][:, fc, stl * 128:(stl + 1) * 128],
                                                 rhs=wo[:, fc, nb * 512:(nb + 1) * 512], start=(fc == 0), stop=(fc == 15))
                                return r
                            fw.op('pe', mm, reads=wok + [('ut', i)], writes=[PSK[bank]])
                            def sqa(e, bank=bank, c=c, nb=nb):
                                return e.activation(out=junk[:], in_=ps[bank][:, :], func=AF.Square,
                                                    accum_out=ss2[:, 2 * c + nb:2 * c + nb + 1])
                            fw.op('act', sqa, reads=[PSK[bank]], writes=['junk2', ('ss2', c, nb)])
                        fw.op('dve', lambda e, c=c: e.tensor_tensor(out=rr[:, c:c + 1], in0=ss2[:, 2 * c:2 * c + 1],
                                                                    in1=ss2[:, 2 * c + 1:2 * c + 2], op=ALU.add),
                              reads=[('ss2', c, 0), ('ss2', c, 1)], writes=[('rr', c)])
                        fw.op('act', lambda e, c=c: e.activation(out=rr[:, c:c + 1], in_=rr[:, c:c + 1], func=AF.Sqrt,
                                                                 scale=1.0 / D, bias=EPS),
                              reads=[('rr', c)], writes=[('rr', c)])
                        fw.op('dve', lambda e, c=c: e.reciprocal(out=rr[:, c:c + 1], in_=rr[:, c:c + 1]),
                              reads=[('rr', c)], writes=[('rr', c)])
                        for nb in range(2):
                            bank = 2 * c + nb
                            fw.op('dve', lambda e, c=c, nb=nb, bank=bank, j=j: e.scalar_tensor_tensor(
                                out=tn[c][:, nb * 512:(nb + 1) * 512], in0=ps[bank][:, :], scalar=rr[:, c:c + 1],
                                in1=GG[:, j, nb * 512:(nb + 1) * 512], op0=ALU.mult, op1=ALU.mult),
                                reads=[PSK[bank], ('rr', c), ('GG', j, nb)], writes=[('tn', c, nb)])
                        fw.op('pool', lambda e, c=c: e.tensor_tensor(out=xo[c][:], in0=tn[c][:], in1=xr[c][:], op=ALU.add),
                              reads=[('tn', c, 0), ('tn', c, 1), ('xr', c)], writes=[('xo', c)])
                        fw.dma('sp', dst, xo[c][:], reads=[('xo', c)], writes=[('X', tt)])
                fw.barrier()

        stopped = [False]

        def chk(name):
            if stop_after == name:
                stopped[0] = True
            return stopped[0]

        if True:
          for l in range(first_layer, n_layers):
            last = (l == DEPTH - 1)
            emit_mod(l)
            if chk('mod'):
                break
            if l % 2 == 0:
                emit_even(l, last)
            else:
                emit_odd(l, last)
            if stopped[0]:
                break
            emit_outproj(l, last)
        fw.barrier()
        print("ops emitted:", fw.nops, "sems:", fw.nsem)
    return nc


_NC_CACHE = {}


def _prep_inputs(inp, b):
    cosT, sinT, rm = _rope_tables()
    m = {}
    m["x"] = np.ascontiguousarray(inp["x"][b])
    m["ctx"] = np.ascontiguousarray(inp["ctx"][b])
    m["cvec"] = np.ascontiguousarray(np.concatenate([_col(inp["c"][b]), _col(inp["c_ctx"])], axis=1))
    m["w_mod"] = inp["w_mod"]
    m["bmod_col"] = np.ascontiguousarray(np.concatenate([_col(inp["b_mod"][l]) for l in range(DEPTH)], axis=1))
    m["bmod_gate"] = np.ascontiguousarray(np.broadcast_to(inp["b_mod"][:, None, 2 * D:], (DEPTH, 128, D)))
    m["gpre_col"] = np.ascontiguousarray(np.concatenate([_col(inp["g_pre"][l]) for l in range(DEPTH)], axis=1))
    m["gpost_b"] = np.ascontiguousarray(np.broadcast_to(inp["g_post"][:, None, :], (DEPTH, 128, D)))
    m["w_out"] = inp["w_out"]
    m["ev_w_in"] = inp["ev_w_in"]
    m["lam_b"] = np.ascontiguousarray(np.broadcast_to(inp["ev_lambda"].reshape(1, 512), (128, 512)))
    m["subln_col"] = np.ascontiguousarray(inp["ev_subln"].T)
    m["conv_col"] = np.ascontiguousarray(
        np.concatenate([_col(inp["ev_conv"][j, tap]) for j in range(2) for tap in range(3)], axis=1))
    m["od_w_in"] = inp["od_w_in"]
    m["tt_tab"] = _na_tables(inp["od_rpb"])
    m["gate_up"] = inp["od_gate_up"]
    m["gb_col"] = np.ascontiguousarray(np.concatenate(
        [_col(inp["od_gate_bias"][j, d_]) for j in range(2) for d_ in range(2)], axis=1))
    m["gnorm_b"] = np.ascontiguousarray(np.broadcast_to(inp["od_gnorm"][:, None, :], (2, 128, 256)))
    cm = np.ones((128, 512), np.float32)
    cm[:, ::64] = 0.0
    m["cmask"] = cm
    si = np.arange(64)
    m["tri"] = np.ascontiguousarray(np.concatenate([(si[:, None] <= si[None, :]), (si[:, None] >= si[None, :])],
                                                  axis=1).astype(np.float32))
    m["cosT"] = cosT
    m["sinT"] = sinT
    m["rm"] = rm
    m["ident"] = np.eye(128, dtype=np.float32)
    return m


def kernel(**inputs):
    inp = {k: np.asarray(v) for k, v in inputs.items()}
    if "nc" not in _NC_CACHE:
        _NC_CACHE["nc"] = build()
    nc = _NC_CACHE["nc"]
    in_maps = [_prep_inputs(inp, b) for b in range(8)]
    res = run_bass_kernel_spmd(nc, in_maps, core_ids=list(range(8)))
    return np.stack([r["out"] for r in res.results], axis=0).astype(np.float32)
```
